# Optimizing a Trainium2 kernel written in Bass

```python
import math
import jax, jax.numpy as jnp
from jax import lax
import numpy as np

D_MODEL = 1024
BATCH = 4
SEQ = 8192
DEPTH = 2

CHUNK = 64
N_META = 16
Q_BLOCK = 128
DIFF_HEADS = 4
DIFF_HEAD_DIM = 64
DIFF_V_DIM = 2 * DIFF_HEAD_DIM
FOX_HEADS = 8
FOX_HEAD_DIM = 64
DIFF_QK_W = DIFF_HEADS * 2 * DIFF_HEAD_DIM
DIFF_V_W = DIFF_HEADS * DIFF_V_DIM
FOX_W = FOX_HEADS * FOX_HEAD_DIM
EVEN_IN_W = 2 * DIFF_QK_W + DIFF_V_W + 3 * FOX_W + FOX_HEADS
EVEN_MIX_W = DIFF_V_W + FOX_W
RET_HEADS = 4
RET_QK_DIM = D_MODEL // RET_HEADS
RET_V_DIM = 2 * RET_QK_DIM
RET_QK_W = RET_HEADS * RET_QK_DIM
RET_V_W = RET_HEADS * RET_V_DIM
RET_IN_W = 2 * RET_QK_W + 2 * RET_V_W
D_FF = 4 * D_MODEL
N_EVEN = (DEPTH + 1) // 2
N_ODD = DEPTH // 2
DEEPNORM_ALPHA = (2 * DEPTH) ** 0.25
DEEPNORM_BETA = (8 * DEPTH) ** -0.25
LN_EPS = 1e-5
RMS_EPS = 1e-6

kernel_name = "hybrid_diff_fox_retention_deepnorm"

F32 = jnp.float32


def _chunk_id(pos):
    return (pos + CHUNK - N_META) // CHUNK


def _layer_norm(x, g, b):
    xf = x.astype(F32)
    mu = jnp.mean(xf, axis=-1, keepdims=True)
    var = jnp.mean(jnp.square(xf - mu), axis=-1, keepdims=True)
    y = (xf - mu) * lax.rsqrt(var + LN_EPS)
    return (y * g.astype(F32) + b.astype(F32)).astype(x.dtype)


def _rms_norm(x):
    xf = x.astype(F32)
    return xf * lax.rsqrt(jnp.mean(jnp.square(xf), axis=-1, keepdims=True) + RMS_EPS)


def _even_mixer(h, w_in, f_bias, lam_vecs, subln_g, w_out, layer_idx):
    B, L, _ = h.shape
    Lp = -(-L // Q_BLOCK) * Q_BLOCK
    nblk = Lp // Q_BLOCK
    proj = h @ w_in
    offs = [int(o) for o in np.cumsum([DIFF_QK_W, DIFF_QK_W, DIFF_V_W, FOX_W, FOX_W, FOX_W])]
    qa, ka, va, qb, kb, vb, fb = jnp.split(proj, offs, axis=-1)
    pad = ((0, 0), (0, Lp - L), (0, 0))
    qa, ka, va, qb, kb, vb = (jnp.pad(t, pad) for t in (qa, ka, va, qb, kb, vb))
    log_f = jnp.pad(jax.nn.log_sigmoid((fb + f_bias).astype(F32)), pad)
    cum_f = jnp.cumsum(log_f, axis=1).transpose(0, 2, 1)

    qa = qa.reshape(B, nblk, Q_BLOCK, DIFF_HEADS, 2, DIFF_HEAD_DIM).transpose(1, 0, 3, 4, 2, 5)
    ka = ka.reshape(B, Lp, DIFF_HEADS, 2, DIFF_HEAD_DIM).transpose(0, 2, 3, 1, 4)
    va = va.reshape(B, Lp, DIFF_HEADS, DIFF_V_DIM).transpose(0, 2, 1, 3)
    qb = qb.reshape(B, nblk, Q_BLOCK, FOX_HEADS, FOX_HEAD_DIM).transpose(1, 0, 3, 2, 4)
    kb = kb.reshape(B, Lp, FOX_HEADS, FOX_HEAD_DIM).transpose(0, 2, 1, 3)
    vb = vb.reshape(B, Lp, FOX_HEADS, FOX_HEAD_DIM).transpose(0, 2, 1, 3)
    cq = cum_f.reshape(B, FOX_HEADS, nblk, Q_BLOCK).transpose(2, 0, 1, 3)

    lam_init = 0.8 - 0.6 * math.exp(-0.3 * layer_idx)
    lv = lam_vecs.astype(F32)
    lam = jnp.exp(jnp.sum(lv[0] * lv[1])) - jnp.exp(jnp.sum(lv[2] * lv[3])) + lam_init
    slopes = 2.0 ** (-8.0 * jnp.arange(1, DIFF_HEADS + 1, dtype=F32) / DIFF_HEADS)
    scale_a = DIFF_HEAD_DIM ** -0.5
    scale_b = FOX_HEAD_DIM ** -0.5
    k_pos = jnp.arange(Lp)
    k_chunk = _chunk_id(k_pos)
    q_pos_blk = k_pos.reshape(nblk, Q_BLOCK)

    def one_block(args):
        qa_i, qb_i, cq_i, q_pos = args
        dist = jnp.abs(q_pos[:, None] - k_pos[None, :]).astype(F32)
        chunk_vis = k_chunk[None, :] <= _chunk_id(q_pos)[:, None]
        frame_vis = k_pos[None, :] <= q_pos[:, None]
        s_a = jnp.einsum('bhmqd,bhmkd->bhmqk', qa_i, ka).astype(F32) * scale_a \
            - slopes[:, None, None, None] * dist
        p_a = jax.nn.softmax(jnp.where(chunk_vis, s_a, -jnp.inf), axis=-1)
        p_diff = p_a[:, :, 0] - lam * p_a[:, :, 1]
        o_a = jnp.einsum('bhqk,bhkd->bhqd', p_diff.astype(va.dtype), va)
        s_b = jnp.einsum('bhqd,bhkd->bhqk', qb_i, kb).astype(F32) * scale_b \
            + cq_i[..., :, None] - cum_f[:, :, None, :]
        p_b = jax.nn.softmax(jnp.where(frame_vis, s_b, -jnp.inf), axis=-1)
        o_b = jnp.einsum('bhqk,bhkd->bhqd', p_b.astype(vb.dtype), vb)
        return o_a, o_b

    o_a, o_b = lax.map(one_block, (qa, qb, cq, q_pos_blk))
    o_a = o_a.transpose(1, 0, 3, 2, 4).reshape(B, Lp, DIFF_HEADS, DIFF_V_DIM)[:, :L]
    o_a = (_rms_norm(o_a) * subln_g.astype(F32) * (1.0 - lam_init)).astype(h.dtype)
    o_b = o_b.transpose(1, 0, 3, 2, 4).reshape(B, Lp, FOX_W)[:, :L]
    o = jnp.concatenate([o_a.reshape(B, L, DIFF_V_W), o_b.astype(h.dtype)], axis=-1)
    return o @ w_out


def _retention(h, w_in, w_out):
    B, L, _ = h.shape
    proj = h @ w_in
    q, k, v, g = jnp.split(proj, [RET_QK_W, 2 * RET_QK_W, 2 * RET_QK_W + RET_V_W], axis=-1)
    k = k * RET_QK_DIM ** -0.5
    lpad = (-N_META) % CHUNK
    Lr = L + lpad
    nc = Lr // CHUNK

    def to_chunks(t, d):
        t = jnp.pad(t, ((0, 0), (lpad, 0), (0, 0)))
        return t.reshape(B, nc, CHUNK, RET_HEADS, d).transpose(1, 0, 3, 2, 4)

    qc, kc, vc = to_chunks(q, RET_QK_DIM), to_chunks(k, RET_QK_DIM), to_chunks(v, RET_V_DIM)
    log_gamma = jnp.log1p(-(2.0 ** (-5.0 - jnp.arange(RET_HEADS, dtype=F32))))
    idx = jnp.arange(CHUNK, dtype=F32)
    intra = jnp.exp(log_gamma[:, None, None] * jnp.abs(idx[:, None] - idx[None, :]))
    q_dec = jnp.exp(log_gamma[:, None] * (idx + 1.0))[:, :, None]
    k_dec = jnp.exp(log_gamma[:, None] * (CHUNK - 1.0 - idx))[:, :, None]
    c_dec = jnp.exp(log_gamma * CHUNK)[:, None, None]

    def step(state, inp):
        q_i, k_i, v_i = inp
        scores = jnp.einsum('bhid,bhjd->bhij', q_i, k_i) * intra
        o = jnp.einsum('bhij,bhje->bhie', scores, v_i) \
            + jnp.einsum('bhid,bhde->bhie', q_i * q_dec, state)
        state = c_dec * state + jnp.einsum('bhjd,bhje->bhde', k_i * k_dec, v_i)
        return state, o

    s0 = jnp.zeros((B, RET_HEADS, RET_QK_DIM, RET_V_DIM), F32)
    _, o = lax.scan(step, s0, (qc, kc, vc))
    o = o.transpose(1, 0, 3, 2, 4).reshape(B, Lr, RET_HEADS, RET_V_DIM)[:, lpad:]
    o = _rms_norm(o).reshape(B, L, RET_V_W).astype(h.dtype)
    y = jax.nn.silu(g) * o
    return y @ w_out


def _sq_relu_mlp(h, w1, w2):
    return jnp.square(jax.nn.relu(h @ w1)) @ w2


def setup_inputs(seed: int = 0) -> dict:
    key = jax.random.key(seed)
    ks = jax.random.split(key, 13)

    def nrm(k, shape, scale):
        return jax.random.normal(k, shape, F32) * scale

    beta = DEEPNORM_BETA
    x = nrm(ks[0], (BATCH, SEQ, D_MODEL), 1.0)
    meta_tokens = nrm(ks[1], (N_META, D_MODEL), 1.0)
    even_col_scale = jnp.concatenate([
        jnp.ones((2 * DIFF_QK_W,), F32), jnp.full((DIFF_V_W,), beta, F32),
        jnp.ones((2 * FOX_W,), F32), jnp.full((FOX_W,), beta, F32),
        jnp.ones((FOX_HEADS,), F32)])
    even_w_in = nrm(ks[2], (N_EVEN, D_MODEL, EVEN_IN_W), D_MODEL ** -0.5) * even_col_scale
    even_f_bias = jnp.linspace(1.0, 6.0, FOX_HEADS, dtype=F32) + nrm(ks[3], (N_EVEN, FOX_HEADS), 0.1)
    diff_lambda = nrm(ks[4], (N_EVEN, 4, DIFF_HEAD_DIM), 0.1)
    diff_subln_g = 1.0 + nrm(ks[5], (N_EVEN, DIFF_V_DIM), 0.02)
    even_w_out = nrm(ks[6], (N_EVEN, EVEN_MIX_W, D_MODEL), EVEN_MIX_W ** -0.5 * beta)
    ret_col_scale = jnp.concatenate([
        jnp.ones((2 * RET_QK_W,), F32), jnp.full((RET_V_W,), beta, F32), jnp.ones((RET_V_W,), F32)])
    ret_w_in = nrm(ks[7], (N_ODD, D_MODEL, RET_IN_W), D_MODEL ** -0.5) * ret_col_scale
    ret_w_out = nrm(ks[8], (N_ODD, RET_V_W, D_MODEL), RET_V_W ** -0.5 * beta)
    ln_g = 1.0 + nrm(ks[9], (DEPTH, 2, D_MODEL), 0.02)
    ln_b = nrm(ks[10], (DEPTH, 2, D_MODEL), 0.02)
    ffn_w1 = nrm(ks[11], (DEPTH, D_MODEL, D_FF), D_MODEL ** -0.5 * beta)
    ffn_w2 = nrm(ks[12], (DEPTH, D_FF, D_MODEL), D_FF ** -0.5 * beta)
    return {"x": x, "meta_tokens": meta_tokens, "even_w_in": even_w_in, "even_f_bias": even_f_bias,
            "diff_lambda": diff_lambda, "diff_subln_g": diff_subln_g, "even_w_out": even_w_out,
            "ret_w_in": ret_w_in, "ret_w_out": ret_w_out, "ln_g": ln_g, "ln_b": ln_b,
            "ffn_w1": ffn_w1, "ffn_w2": ffn_w2}


def reference(x, meta_tokens, even_w_in, even_f_bias, diff_lambda, diff_subln_g, even_w_out,
              ret_w_in, ret_w_out, ln_g, ln_b, ffn_w1, ffn_w2):
    B = x.shape[0]
    meta = jnp.broadcast_to(meta_tokens[None].astype(x.dtype), (B, N_META, D_MODEL))
    h = jnp.concatenate([meta, x], axis=1)
    for layer in range(DEPTH):
        i = layer // 2
        if layer % 2 == 0:
            mix = _even_mixer(h, even_w_in[i], even_f_bias[i], diff_lambda[i], diff_subln_g[i],
                              even_w_out[i], layer)
        else:
            mix = _retention(h, ret_w_in[i], ret_w_out[i])
        h = _layer_norm(DEEPNORM_ALPHA * h + mix, ln_g[layer, 0], ln_b[layer, 0])
        h = _layer_norm(DEEPNORM_ALPHA * h + _sq_relu_mlp(h, ffn_w1[layer], ffn_w2[layer]),
                        ln_g[layer, 1], ln_b[layer, 1])
    return h[:, N_META:, :]
```

```python
import contextlib
import math
import numpy as np
import concourse.bass as bass
import concourse.mybir as mybir
from concourse.bass_utils import run_bass_kernel_spmd

F32 = mybir.dt.float32
BF16 = mybir.dt.bfloat16
AF = mybir.ActivationFunctionType
ALU = mybir.AluOpType

D = 1024
SEQ = 8192
NMETA = 16
FPAD = 48
LR = 8448
TW = 384
NT = LR // TW
NB = LR // 128
HALF = LR // 2
NTH = HALF // TW
ALPHA = 4 ** 0.25
LN_EPS = 1e-5
RMS_EPS = 1e-6
LAM_INIT0 = 0.8 - 0.6 * math.exp(-0.3 * 0)
NEG = -30000.0
REAL_END = FPAD + NMETA + SEQ

ENGS = ("pe", "act", "dve", "pool", "sp")


class Buf:
    __slots__ = ("name", "w", "r", "ex")

    def __init__(self, name="", ex=False):
        self.name = name
        self.w = None
        self.r = []
        self.ex = ex


def PB():
    return Buf(ex=True)


class Sched:
    def __init__(self, nc, stack, n_dma_sems=16):
        self.nc = nc
        self.streams = {e: [] for e in ENGS}
        self.cnt = {e: 0 for e in ENGS}
        self.seen = {e: {} for e in ENGS}
        self.n_dma = n_dma_sems
        self.dma_k = 0
        self.sems = {e: stack.enter_context(nc.semaphore("s_" + e)) for e in ENGS}
        self.dsems = [stack.enter_context(nc.semaphore("d_%d" % i)) for i in range(n_dma_sems)]
        self.dlast = [0] * n_dma_sems

    def _need(self, eng, tok, waits):
        if tok is None:
            return
        kind, key, val = tok
        if kind == "e" and key == eng and eng in ("pe", "sp"):
            return
        k = (kind, key)
        if self.seen[eng].get(k, 0) >= val:
            return
        if val > waits.get(k, 0):
            waits[k] = val

    def _emit_waits(self, eng, waits):
        for k, val in waits.items():
            self.seen[eng][k] = val
            self.streams[eng].append(("wait", k, val))

    def _deps(self, eng, reads, writes):
        waits = {}
        for b in reads:
            self._need(eng, b.w, waits)
        for b in writes:
            self._need(eng, b.w, waits)
            for t in b.r:
                self._need(eng, t, waits)
        return waits

    def _mark(self, tok, reads, writes):
        for b in reads:
            b.r.append(tok)
            if len(b.r) > 24:
                b.r = b.r[-24:]
        for b in writes:
            b.w = tok
            b.r = []

    def op(self, eng, fn, reads=(), writes=(), inc=True):
        if any(b.ex for b in reads):
            writes = list(writes) + [b for b in reads if b.ex]
            reads = [b for b in reads if not b.ex]
        waits = self._deps(eng, reads, writes)
        self._emit_waits(eng, waits)
        if inc:
            self.cnt[eng] += 1
            tok = ("e", eng, self.cnt[eng])
        else:
            tok = ("e", eng, self.cnt[eng] + 1)
        self.streams[eng].append(("op", fn, inc))
        self._mark(tok, reads, writes)
        return tok

    def dma(self, q, out_ap, in_ap, reads=(), writes=(), **kw):
        waits = self._deps(q, reads, writes)
        s = self.dma_k % self.n_dma
        v = self.dlast[s] + 16
        self.dma_k += 1
        if v > 16:
            self._need(q, ("d", s, v - 16), waits)
        self._emit_waits(q, waits)
        self.dlast[s] = v
        tok = ("d", s, v)
        self.streams[q].append(("dma", out_ap, in_ap, s, kw))
        self._mark(tok, reads, writes)
        return tok

    def barrier(self):
        toks = [("e", e, self.cnt[e]) for e in ENGS if self.cnt[e] > 0]
        toks += [("d", s, self.dlast[s]) for s in range(self.n_dma) if self.dlast[s] > 0]
        for e in ENGS:
            waits = {}
            for t in toks:
                if t[0] == "e" and t[1] == e:
                    continue
                self._need(e, t, waits)
            self._emit_waits(e, waits)

    def flush(self):
        nc = self.nc
        with nc.Block() as block:
            def run(e, engobj):
                for item in self.streams[e]:
                    if item[0] == "wait":
                        (kind, key), val = item[1], item[2]
                        sem = self.sems[key] if kind == "e" else self.dsems[key]
                        engobj.wait_ge(sem, val)
                    elif item[0] == "op":
                        ins = item[1](engobj)
                        if item[2]:
                            ins.then_inc(self.sems[e], 1)
                    else:
                        _, o, i, s, kw = item
                        engobj.dma_start(out=o, in_=i, **kw).then_inc(self.dsems[s], 16)

            @block.tensor
            def _(eng):
                run("pe", eng)

            @block.scalar
            def _(eng):
                run("act", eng)

            @block.vector
            def _(eng):
                run("dve", eng)

            @block.gpsimd
            def _(eng):
                run("pool", eng)

            @block.sync
            def _(eng):
                run("sp", eng)
        self.streams = {e: [] for e in ENGS}


class Ctx:
    K = [0]

    def __init__(self, nc, stack):
        self.nc = nc
        self.st = stack

    def sb(self, shape, dt, name=None):
        Ctx.K[0] += 1
        return self.st.enter_context(self.nc.sbuf_tensor(name or ("t%d" % Ctx.K[0]), list(shape), dt))

    def ps(self, shape, dt=F32, name=None):
        Ctx.K[0] += 1
        return self.st.enter_context(self.nc.psum_tensor(name or ("p%d" % Ctx.K[0]), list(shape), dt))


SCALE = 0.125
DEBUG_A = False


def emit_l1(nc, S, T, Bo):
    xT = T["xT"]; wq = T["wq"]; wk = T["wk"]; wv = T["wv"]; wf = T["wf"]; fbias = T["fbias"]
    lamv = T["lamv"]; gsub = T["gsub"]; dbias = T["dbias"]; Fd = T["Fd"]; Ff = T["Ff"]; qaug = T["qaug"]
    ODT = T["odt"]
    QT_s = nc.dram_tensor("QT_s", [8, 64, LR], BF16).ap()
    KT_s = nc.dram_tensor("KT_s", [8, 64, LR], BF16).ap()
    V_s = nc.dram_tensor("V_s", [LR, 512], BF16).ap()
    AQ_s = nc.dram_tensor("AQ_s", [4, 3, LR], BF16).ap()
    AQd_s = nc.dram_tensor("AQd_s", [6, LR], BF16).ap()
    out_toks = []
    if True:
        BQT = [Buf() for _ in range(8)]
        BKT = [Buf() for _ in range(8)]
        BV = Buf()
        BAQ = Buf()
        with contextlib.ExitStack() as st:
            C = Ctx(nc, st)
            wq_sb = C.sb([128, 8, 512], BF16)
            wk_sb = C.sb([128, 8, 512], BF16)
            wv_sb = C.sb([128, 8, 512], BF16)
            wf_sb = C.sb([128, 8, 4], BF16)
            fb_sb = C.sb([4, 1], F32)
            nfb_sb = C.sb([4, 1], F32)
            ident = C.sb([128, 128], F32)
            xt = [C.sb([128, 8, TW], BF16) for _ in range(2)]
            stq = [C.sb([128, TW], BF16) for _ in range(4)]
            stv = [C.sb([128, 512], BF16) for _ in range(2)]
            lf = C.sb([4, LR], F32)
            cs = C.sb([4, LR], F32)
            ones4 = C.sb([4, LR // 4], F32)
            e_t = C.sb([4, TW], F32)
            hi = C.sb([4, LR], BF16)
            mid = C.sb([4, LR], BF16)
            lo = C.sb([4, LR], BF16)
            pq = [C.ps([128, 512]) for _ in range(4)]
            pv = [C.ps([128, 512]) for _ in range(2)]
            pf = C.ps([128, 512])
            Bw = Buf(); Bxt = [Buf(), Buf()]; Bstq = [Buf() for _ in range(4)]; Bstv = [Buf(), Buf()]
            Bpq = [PB() for _ in range(4)]; Bpv = [PB(), PB()]; Bpf = PB(); Blf = Buf(); Bcs = Buf()
            Bet = Buf(); Bfb = Buf(); Bo4 = Buf(); Bhi = Buf(); Bmid = Buf(); Blo = Buf()

            wr = "(c p) o -> p c o"
            S.dma("pool", wq_sb[:], wq.rearrange(wr, p=128), writes=[Bw])
            for fc in range(8):
                S.dma("pool", wk_sb[:, fc, :], wk[fc * 128:(fc + 1) * 128, :], writes=[Bw])
                S.dma("pool", wv_sb[:, fc, :], wv[fc * 128:(fc + 1) * 128, :], writes=[Bw])
            S.dma("pool", wf_sb[:], wf.rearrange(wr, p=128), writes=[Bw])
            S.dma("sp", fb_sb[:], fbias, writes=[Bfb])
            S.op("dve", lambda e: e.tensor_scalar(nfb_sb[:], fb_sb[:], -1.0, None, ALU.mult), reads=[Bfb], writes=[Bfb])
            S.op("dve", lambda e: e.memset(ones4[:], 1.0), writes=[Bo4])
            xTr = xT.rearrange("(c p) t -> p c t", p=128)
            qi = 0
            vi = 0
            for t in range(NT):
                c0 = t * TW
                X = xt[t % 2]; BX = Bxt[t % 2]
                S.dma("pool", X[:], xTr[:, :, c0:c0 + TW], writes=[BX])
                for which, (w_sb, dst, BD) in enumerate(((wq_sb, QT_s, BQT), (wk_sb, KT_s, BKT))):
                    for g in range(4):
                        P = pq[qi % 4]; BP = Bpq[qi % 4]; ST = stq[qi % 4]; BS = Bstq[qi % 4]
                        qi += 1
                        for c in range(8):
                            S.op("pe", lambda e, P=P, w_sb=w_sb, c=c, g=g, X=X: e.matmul(
                                P[:, 0:TW], w_sb[:, c, g * 128:(g + 1) * 128], X[:, c, :], start=(c == 0), stop=(c == 7)),
                                reads=[Bw, BX], writes=[BP], inc=(c == 7))
                        eng = "act" if (g % 2 == 0) else "dve"
                        if eng == "act":
                            S.op("act", lambda e, ST=ST, P=P: e.copy(ST[:], P[:, 0:TW]), reads=[BP], writes=[BS])
                        else:
                            S.op("dve", lambda e, ST=ST, P=P: e.tensor_copy(ST[:], P[:, 0:TW]), reads=[BP], writes=[BS])
                        S.dma("sp", dst[2 * g:2 * g + 2].rearrange("u r t -> (u r) t")[:, c0:c0 + TW], ST[:],
                              reads=[BS], writes=[BD[2 * g], BD[2 * g + 1]])
                for bl in range(3):
                    P = pv[vi % 2]; BP = Bpv[vi % 2]; ST = stv[vi % 2]; BS = Bstv[vi % 2]
                    vi += 1
                    for c in range(8):
                        S.op("pe", lambda e, P=P, c=c, bl=bl, X=X: e.matmul(
                            P[:], X[:, c, bl * 128:(bl + 1) * 128], wv_sb[:, c, :], start=(c == 0), stop=(c == 7)),
                            reads=[Bw, BX], writes=[BP], inc=(c == 7))
                    S.op("dve", lambda e, ST=ST, P=P: e.tensor_copy(ST[:], P[:]), reads=[BP], writes=[BS])
                    r0 = c0 + bl * 128
                    S.dma("sp", V_s[r0:r0 + 128, :], ST[:], reads=[BS], writes=[BV])
                for c in range(8):
                    S.op("pe", lambda e, c=c, X=X: e.matmul(pf[0:4, 0:TW], wf_sb[:, c, :], X[:, c, :], start=(c == 0), stop=(c == 7)),
                         reads=[Bw, BX], writes=[Bpf], inc=(c == 7))
                S.op("act", lambda e: e.activation(e_t[:], pf[0:4, 0:TW], AF.Exp, bias=nfb_sb[:, 0:1], scale=-1.0),
                     reads=[Bpf, Bfb], writes=[Bet])
                S.op("act", lambda e, c0=c0: e.activation(lf[:, c0:c0 + TW], e_t[:], AF.Ln, bias=1.0, scale=1.0),
                     reads=[Bet], writes=[Blf])
            S.op("dve", lambda e: e.memset(lf[:, 0:FPAD], 0.0), reads=[Blf], writes=[Blf])
            S.op("dve", lambda e: e.memset(lf[:, REAL_END:LR], 0.0), reads=[Blf], writes=[Blf])
            CH = LR // 4
            for k in range(4):
                a = k * CH
                if k == 0:
                    S.op("dve", lambda e, a=a: e.tensor_tensor_scan(cs[:, a:a + CH], ones4[:], lf[:, a:a + CH], 0.0, ALU.mult, ALU.add),
                         reads=[Blf, Bo4], writes=[Bcs])
                else:
                    S.op("dve", lambda e, a=a: e.tensor_tensor_scan(cs[:, a:a + CH], ones4[:], lf[:, a:a + CH], cs[:, a - 1:a], ALU.mult, ALU.add),
                         reads=[Blf, Bo4, Bcs], writes=[Bcs])
            CS_s = nc.dram_tensor("CS_s", [4, LR], F32).ap()
            BCS = Buf()
            S.dma("sp", CS_s, cs[:], reads=[Bcs], writes=[BCS])
            for t in range(NT):
                c0 = t * TW
                S.op("dve", lambda e, c0=c0: e.tensor_scalar(lf[:, c0:c0 + TW], cs[:, c0:c0 + TW], cs[:, c0:c0 + 1], -8.0, ALU.subtract, ALU.mult),
                     reads=[Bcs, Blf], writes=[Blf])
            S.op("dve", lambda e: e.tensor_copy(hi[:], lf[:]), reads=[Blf], writes=[Bhi])
            S.op("dve", lambda e: e.tensor_tensor(lf[:], lf[:], hi[:], ALU.subtract), reads=[Blf, Bhi], writes=[Blf])
            S.op("dve", lambda e: e.tensor_copy(mid[:], lf[:]), reads=[Blf], writes=[Bmid])
            S.op("dve", lambda e: e.tensor_tensor(lf[:], lf[:], mid[:], ALU.subtract), reads=[Blf, Bmid], writes=[Blf])
            S.op("dve", lambda e: e.tensor_copy(lo[:], lf[:]), reads=[Blf], writes=[Blo])
            qa_sb = C.sb([6, LR], BF16)
            Bqa = Buf()
            S.dma("pool", qa_sb[:], qaug.rearrange("h r t -> (h r) t"), writes=[Bqa])
            S.dma("sp", AQd_s, qa_sb[:], reads=[Bqa], writes=[BAQ])
            S.dma("sp", AQ_s[:, 0, :], hi[:], reads=[Bhi], writes=[BAQ])
            S.dma("sp", AQ_s[:, 1, :], mid[:], reads=[Bmid], writes=[BAQ])
            S.dma("sp", AQ_s[:, 2, :], lo[:], reads=[Blo], writes=[BAQ])
            S.barrier()
            S.flush()

        with contextlib.ExitStack() as st:
            C = Ctx(nc, st)
            kt = [C.sb([128, LR], BF16) for _ in range(2)]
            qt = [C.sb([128, LR], BF16) for _ in range(2)]
            vt = [C.sb([128, NB, 128], BF16) for _ in range(2)]
            Fd_sb = C.sb([128, 2 * 3 * TW], F32)
            Ff_sb = C.sb([128, 3 * TW], F32)
            dbias_sb = C.sb([128, 2 * 66], F32)
            csk = C.sb([128, 4, NB], F32)
            csT0 = C.sb([128, 4, NT], F32)
            bkt = [C.sb([128, NB], F32) for _ in range(2)]
            ones_bf = C.sb([128, 128], BF16)
            ones_b0 = C.sb([128, 128], BF16)
            ones_f = C.sb([128, 128], F32)
            lam_sb = C.sb([128, 256], F32)
            lprod = C.sb([128, 128], F32)
            lsum = C.sb([128, 2], F32)
            lexp = C.sb([128, 2], F32)
            neglam = C.sb([128, 1], F32)
            gcol = C.sb([128, 1], F32)
            pT = [C.sb([128, TW], BF16) for _ in range(4)]
            tmp = [C.sb([128, TW], F32) for _ in range(2)]
            rl = C.sb([128, TW], F32)
            a0 = C.sb([128, TW], F32)
            a1 = C.sb([128, TW], F32)
            od = C.sb([128, TW], F32)
            sq = C.sb([128, TW], F32)
            rstd = C.sb([128, TW], F32)
            ofin = [C.sb([128, TW], ODT) for _ in range(2)]
            NS = 3
            NPT = 4
            ps_s = [C.ps([128, 512]) for _ in range(NS)]
            ps_o = [C.ps([128, 512]) for _ in range(2)]
            ps_l = [C.ps([128, 512]) for _ in range(2)]
            ps_x = C.ps([128, 512])
            Bkt = [Buf(), Buf()]; Bqt = [Buf(), Buf()]; Bvt = [Buf(), Buf()]
            Bc = Buf(); Bcsk = Buf(); BcsT0 = Buf(); Bbkt = [Buf(), Buf()]
            Bps_s = [PB(), PB(), PB()]; Bps_o = [PB(), PB()]; Bps_l = [PB(), PB()]; Bps_x = PB()
            BpT = [Buf() for _ in range(4)]; Btmp = [Buf(), Buf()]
            Brl = Buf(); Ba0 = Buf(); Ba1 = Buf(); Bod = Buf(); Bsq = Buf(); Brstd = Buf(); Bofin = [Buf(), Buf()]
            Blam = Buf()

            S.dma("sp", Fd_sb[:], Fd, writes=[Bc])
            S.dma("sp", Ff_sb[:], Ff, writes=[Bc])
            S.dma("sp", dbias_sb[:], dbias, writes=[Bc])
            S.dma("sp", lam_sb[:], lamv, writes=[Blam])
            S.dma("sp", gcol[:], gsub, writes=[Blam])
            for h in range(4):
                S.dma("sp", csk[:, h, :], CS_s[h].rearrange("(b p) -> p b", p=128), reads=[BCS], writes=[Bcsk], allow_slow_non_contiguous=True)
            for h in range(4):
                src = bass.AP(CS_s.tensor, CS_s.offset + h * LR, [[0, 128], [TW, NT]])
                S.dma("sp", csT0[:, h, :], src, reads=[BCS], writes=[BcsT0], allow_slow_non_contiguous=True)
            S.op("pool", lambda e: e.memset(ones_bf[:], 1.0), writes=[Bc])
            S.op("pool", lambda e: e.memset(ones_b0[:], 1.0), writes=[Bc])
            S.op("pool", lambda e: e.memset(ones_b0[0:FPAD, :], 0.0), reads=[Bc], writes=[Bc])
            S.op("pool", lambda e: e.memset(ones_f[:], 1.0), writes=[Bc])
            for b in range(2):
                S.op("pool", lambda e, b=b: e.memset(kt[b][64:67, :], 1.0), writes=[Bkt[b]])
            S.op("dve", lambda e: e.tensor_tensor(lprod[:, 0:64], lam_sb[:, 0:64], lam_sb[:, 64:128], ALU.mult), reads=[Blam], writes=[Blam])
            S.op("dve", lambda e: e.tensor_tensor(lprod[:, 64:128], lam_sb[:, 128:192], lam_sb[:, 192:256], ALU.mult), reads=[Blam], writes=[Blam])
            S.op("dve", lambda e: e.reduce_sum(lsum[:, 0:1], lprod[:, 0:64], mybir.AxisListType.X), reads=[Blam], writes=[Blam])
            S.op("dve", lambda e: e.reduce_sum(lsum[:, 1:2], lprod[:, 64:128], mybir.AxisListType.X), reads=[Blam], writes=[Blam])
            S.op("act", lambda e: e.activation(lexp[:], lsum[:], AF.Exp), reads=[Blam], writes=[Blam])
            S.op("dve", lambda e: e.scalar_tensor_tensor(neglam[:], lexp[:, 1:2], -LAM_INIT0, lexp[:, 0:1], ALU.add, ALU.subtract),
                 reads=[Blam], writes=[Blam])
            S.op("dve", lambda e: e.tensor_scalar(gcol[:], gcol[:], 1.0 - LAM_INIT0, None, ALU.mult), reads=[Blam], writes=[Blam])

            si = 0
            pi = 0
            ti = 0
            oi = 0
            fi = 0
            for g in range(4):
                isdiff = g < 2
                units = (2 * g, 2 * g + 1)
                dv = 128 if isdiff else 64
                for ui, u in enumerate(units):
                    S.dma("sp", kt[ui][0:64, :], KT_s[u], reads=[BKT[u]], writes=[Bkt[ui]])
                    S.dma("sp", qt[ui][0:64, :], QT_s[u], reads=[BQT[u]], writes=[Bqt[ui]])
                    if isdiff:
                        S.dma("sp", qt[ui][64:67, :], AQd_s[3 * g:3 * g + 3, :], reads=[BAQ], writes=[Bqt[ui]])
                    else:
                        S.dma("sp", qt[ui][64:67, :], AQ_s[u - 4], reads=[BAQ], writes=[Bqt[ui]])
                if isdiff:
                    S.dma("sp", vt[0][:], V_s[:, g * 128:(g + 1) * 128].rearrange("(b p) c -> p b c", p=128),
                          reads=[BV], writes=[Bvt[0]])
                else:
                    for ui, u in enumerate(units):
                        hf = u - 4
                        S.dma("sp", vt[ui][:, :, 0:64], V_s[:, 256 + hf * 64:256 + (hf + 1) * 64].rearrange("(b p) c -> p b c", p=128),
                              reads=[BV], writes=[Bvt[ui]])
                items = []
                for t in range(NT):
                    for ui, u in enumerate(units):
                        for j in range(3 * t + 3):
                            items.append((t, ui, u, j))
                state = {}
                deferred = []

                def stage_a(t, ui, u, j):
                    nonlocal si, pi, ti, oi, fi
                    c0 = t * TW
                    nblk = 3 * t + 3
                    K = kt[ui]; Q = qt[ui]; BK = Bkt[ui]; BQ = Bqt[ui]
                    hf = u - 4
                    if j == 0:
                        PO = ps_o[oi % 2]; BPO = Bps_o[oi % 2]; PL = ps_l[oi % 2]; BPL = Bps_l[oi % 2]
                        oi += 1
                        BKk = None; BBK = None
                        if not isdiff:
                            BKk = bkt[fi % 2]; BBK = Bbkt[fi % 2]
                            fi += 1
                            S.op("dve", lambda e, BKk=BKk, nblk=nblk, hf=hf, t=t: e.tensor_scalar(
                                BKk[:, 0:nblk], csk[:, hf, 0:nblk], csT0[:, hf, t:t + 1], None, ALU.subtract),
                                reads=[Bcsk, BcsT0], writes=[BBK])
                        state[(t, ui)] = (PO, BPO, PL, BPL, BKk, BBK)
                    PO, BPO, PL, BPL, BKk, BBK = state[(t, ui)]
                    jj = j - 3 * t
                    PS = ps_s[si % NS]; BPS = Bps_s[si % NS]
                    si += 1
                    S.op("pe", lambda e, PS=PS, K=K, Q=Q, j=j, c0=c0: e.matmul(
                        PS[:, 0:TW], K[0:67, j * 128:(j + 1) * 128], Q[0:67, c0:c0 + TW], start=True, stop=True),
                        reads=[BK, BQ], writes=[BPS])
                    PT = pT[pi % NPT]; BPT = BpT[pi % NPT]
                    pi += 1
                    if isdiff:
                        col = g * 66 + (jj + 63)
                        bcol = dbias_sb[:, col:col + 1]
                        rb = [Bc]
                    else:
                        bcol = BKk[:, j:j + 1]
                        rb = [BBK]
                    if jj < 0:
                        S.op("act", lambda e, PT=PT, PS=PS, bcol=bcol: e.activation(PT[:], PS[:, 0:TW], AF.Exp, bias=bcol, scale=SCALE),
                             reads=[BPS] + rb, writes=[BPT])
                    else:
                        TM = tmp[ti % 2]; BTM = Btmp[ti % 2]
                        ti += 1
                        if isdiff:
                            Fap = Fd_sb[:, (g * 3 + jj) * TW:(g * 3 + jj + 1) * TW]
                        else:
                            Fap = Ff_sb[:, jj * TW:(jj + 1) * TW]
                        S.op("dve", lambda e, TM=TM, PS=PS, Fap=Fap: e.scalar_tensor_tensor(
                            TM[:], PS[:, 0:TW], SCALE, Fap, ALU.mult, ALU.add), reads=[BPS, Bc], writes=[BTM])
                        S.op("act", lambda e, PT=PT, TM=TM, bcol=bcol: e.activation(PT[:], TM[:], AF.Exp, bias=bcol, scale=1.0),
                             reads=[BTM] + rb, writes=[BPT])
                    return PT, BPT

                def stage_b(idx, t, ui, u, j, PT, BPT):
                    c0 = t * TW
                    nblk = 3 * t + 3
                    hf = u - 4
                    if isdiff:
                        VT = vt[0]; BVT = Bvt[0]
                    else:
                        VT = vt[ui]; BVT = Bvt[ui]
                    PO, BPO, PL, BPL, BKk, BBK = state[(t, ui)]
                    S.op("pe", lambda e, PO=PO, VT=VT, PT=PT, j=j, nblk=nblk, dv=dv: e.matmul(
                        PO[0:dv, 0:TW], VT[:, j, 0:dv], PT[:], start=(j == 0), stop=(j == nblk - 1)),
                        reads=[BVT, BPT], writes=[BPO], inc=False)
                    on = ones_b0 if j == 0 else ones_bf
                    S.op("pe", lambda e, PL=PL, on=on, PT=PT, j=j, nblk=nblk, dv=dv: e.matmul(
                        PL[0:dv, 0:TW], on[:, 0:dv], PT[:], start=(j == 0), stop=(j == nblk - 1)),
                        reads=[Bc, BPT], writes=[BPL], inc=True)
                    if j != nblk - 1:
                        return
                    S.op("dve", lambda e, PL=PL, dv=dv: e.tensor_scalar(rl[0:dv, :], PL[0:dv, 0:TW], 1e-30, None, ALU.add), reads=[BPL], writes=[Brl])
                    S.op("dve", lambda e, dv=dv: e.reciprocal(rl[0:dv, :], rl[0:dv, :]), reads=[Brl], writes=[Brl])
                    if not isdiff:
                        OF = ofin[(2 * t + ui) % 2]; BOF = Bofin[(2 * t + ui) % 2]
                        S.op("dve", lambda e, OF=OF, PO=PO: e.tensor_tensor(OF[0:64, :], PO[0:64, 0:TW], rl[0:64, :], ALU.mult),
                             reads=[BPO, Brl], writes=[BOF])
                        r0 = 256 + hf * 64
                        out_toks.append(S.dma("sp", T["oT_tile"](r0, r0 + 64, t), OF[0:64, :], reads=[BOF], writes=[Bo]))
                        return
                    AA = a0 if ui == 0 else a1
                    BA = Ba0 if ui == 0 else Ba1
                    S.op("dve", lambda e, AA=AA, PO=PO: e.tensor_tensor(AA[:], PO[:, 0:TW], rl[:], ALU.mult),
                         reads=[BPO, Brl], writes=[BA])
                    if ui == 0:
                        return
                    OF = ofin[t % 2]; BOF = Bofin[t % 2]
                    S.op("dve", lambda e: e.scalar_tensor_tensor(od[:], a1[:], neglam[:, 0:1], a0[:], ALU.mult, ALU.add),
                         reads=[Ba0, Ba1, Blam], writes=[Bod])
                    S.op("pool", lambda e: e.tensor_tensor(sq[:], od[:], od[:], ALU.mult), reads=[Bod], writes=[Bsq])

                    def tail(OF=OF, BOF=BOF, t=t):
                        S.op("pe", lambda e: e.matmul(ps_x[:, 0:TW], ones_f[:], sq[:], start=True, stop=True),
                             reads=[Bc, Bsq], writes=[Bps_x])
                        S.op("act", lambda e: e.activation(rstd[:], ps_x[:, 0:TW], AF.Ln, bias=RMS_EPS, scale=1.0 / 128.0),
                             reads=[Bps_x], writes=[Brstd])
                        S.op("act", lambda e: e.activation(rstd[:], rstd[:], AF.Exp, scale=-0.5), reads=[Brstd], writes=[Brstd])
                        S.op("dve", lambda e, OF=OF: e.scalar_tensor_tensor(OF[:], od[:], gcol[:, 0:1], rstd[:], ALU.mult, ALU.mult),
                             reads=[Bod, Brstd, Blam], writes=[BOF])
                        out_toks.append(S.dma("sp", T["oT_tile"](g * 128, (g + 1) * 128, t), OF[:], reads=[BOF], writes=[Bo]))
                    deferred.append((idx + 3, tail))

                LA = 2
                pend = {}
                n_it = len(items)
                for i in range(n_it + LA):
                    if i < n_it:
                        pend[i] = stage_a(*items[i])
                    k = i - LA
                    if k >= 0:
                        PT, BPT = pend.pop(k)
                        stage_b(k, *items[k], PT, BPT)
                    while deferred and deferred[0][0] <= k:
                        deferred.pop(0)[1]()
                while deferred:
                    deferred.pop(0)[1]()
            S.barrier()
            S.flush()
    return out_toks


def l1_consts(s):
    p = np.arange(128, dtype=np.float64)[:, None]
    slopes = [2.0 ** (-8.0 * (h + 1) / 4) for h in (2 * s, 2 * s + 1)]
    dbias = np.zeros((128, 2, 66), np.float32)
    Fd = np.zeros((128, 2, 3, TW), np.float32)
    qaug = np.zeros((2, 3, LR), np.float32)
    i = np.arange(TW, dtype=np.float64)[None, :]
    r = np.arange(LR)
    w = r % TW
    for hl, sl in enumerate(slopes):
        for idx in range(66):
            dbias[:, hl, idx] = (sl * (128 * (idx - 63) + p))[:, 0]
        for jj in range(3):
            rk = 128 * jj + p
            vis = (rk // 64) <= (i // 64)
            f = np.where(i >= rk, 0.0, 2 * sl * (i - rk))
            Fd[:, hl, jj, :] = np.where(vis, f, NEG)
        qaug[hl, 0] = -(8 * sl) * 256 * (w // 256)
        qaug[hl, 1] = -(8 * sl) * (w % 256)
    Ff = np.zeros((128, 3, TW), np.float32)
    for jj in range(3):
        rk = 128 * jj + p
        Ff[:, jj, :] = np.where(rk <= i, 0.0, NEG)
    return (dbias.reshape(128, 132), Fd.reshape(128, 6 * TW), Ff.reshape(128, 3 * TW), qaug)


def l1_inputs(xT_b, even_w_in, even_f_bias, diff_lambda, diff_subln_g, s):
    w = even_w_in
    dq = [w[:, h * 128:(h + 1) * 128] for h in (2 * s, 2 * s + 1)]
    dk = [w[:, 512 + h * 128:512 + (h + 1) * 128] for h in (2 * s, 2 * s + 1)]
    dvv = [w[:, 1024 + h * 128:1024 + (h + 1) * 128] for h in (2 * s, 2 * s + 1)]
    fq = w[:, 1536 + 256 * s:1536 + 256 * (s + 1)]
    fk = w[:, 2048 + 256 * s:2048 + 256 * (s + 1)]
    fv = w[:, 2560 + 256 * s:2560 + 256 * (s + 1)]
    wf = w[:, 3072 + 4 * s:3072 + 4 * (s + 1)]
    dbias, Fd, Ff, qaug = l1_consts(s)
    return {
        "xT": xT_b,
        "wq": np.ascontiguousarray(np.concatenate(dq + [fq], axis=1)),
        "wk": np.ascontiguousarray(np.concatenate(dk + [fk], axis=1)),
        "wv": np.ascontiguousarray(np.concatenate(dvv + [fv], axis=1)),
        "wf": np.ascontiguousarray(wf),
        "fbias": np.ascontiguousarray(even_f_bias[4 * s:4 * (s + 1)].reshape(4, 1)),
        "lamv": np.ascontiguousarray(np.broadcast_to(diff_lambda.reshape(1, 256), (128, 256))),
        "gsub": np.ascontiguousarray(diff_subln_g.reshape(128, 1)),
        "dbias": dbias, "Fd": Fd, "Ff": Ff, "qaug": qaug,
    }


def emit_convert(S, C, src, dst, BD, rows, cols, tag):
    stg = [C.sb([128, 2048], BF16) for _ in range(2)]
    Bs = [Buf(), Buf()]
    k = 0
    for r0 in range(0, rows, 128):
        for c0 in range(0, cols, 2048):
            w = min(2048, cols - c0)
            T = stg[k % 2]; B = Bs[k % 2]
            k += 1
            S.dma("pool", T[:, 0:w], src[r0:r0 + 128, c0:c0 + w], writes=[B])
            S.dma("sp", dst[r0:r0 + 128, c0:c0 + w], T[:, 0:w], reads=[B], writes=[BD])


def emit_ln(S, y, By, ysq, Bysq, out_f, Bof, out_b, Bob, gcols, bcols, Bp, ones_f, Bc, ps1, Bps1, ps2, Bps2,
            mean, msq, rstd, Bst, nfeat_chunks=8):
    n = nfeat_chunks
    for c in range(n):
        S.op("pe", lambda e, c=c: e.matmul(ps1[:, 0:TW], ones_f[:], y[:, c, :], start=(c == 0), stop=(c == n - 1)),
             reads=[Bc, By], writes=[Bps1], inc=(c == n - 1))
    for c in range(n):
        S.op("act", lambda e, c=c: e.activation(ysq[:, c, :], y[:, c, :], AF.Square), reads=[By], writes=[Bysq])
    for c in range(n):
        S.op("pe", lambda e, c=c: e.matmul(ps2[:, 0:TW], ones_f[:], ysq[:, c, :], start=(c == 0), stop=(c == n - 1)),
             reads=[Bc, Bysq], writes=[Bps2], inc=(c == n - 1))
    inv = 1.0 / (128.0 * n)
    S.op("dve", lambda e: e.tensor_scalar(mean[:], ps1[:, 0:TW], inv, None, ALU.mult), reads=[Bps1], writes=[Bst])
    S.op("dve", lambda e: e.tensor_tensor(msq[:], mean[:], mean[:], ALU.mult), reads=[Bst], writes=[Bst])
    S.op("dve", lambda e: e.scalar_tensor_tensor(msq[:], ps2[:, 0:TW], inv, msq[:], ALU.mult, ALU.subtract),
         reads=[Bps2, Bst], writes=[Bst])
    S.op("act", lambda e: e.activation(rstd[:], msq[:], AF.Ln, bias=LN_EPS, scale=1.0), reads=[Bst], writes=[Bst])
    S.op("act", lambda e: e.activation(rstd[:], rstd[:], AF.Exp, scale=-0.5), reads=[Bst], writes=[Bst])
    for c in range(n):
        S.op("dve", lambda e, c=c: e.tensor_tensor(ysq[:, c, :], y[:, c, :], mean[:], ALU.subtract), reads=[By, Bst], writes=[Bysq])
        S.op("dve", lambda e, c=c: e.tensor_tensor(ysq[:, c, :], ysq[:, c, :], rstd[:], ALU.mult), reads=[Bst, Bysq], writes=[Bysq])
        S.op("dve", lambda e, c=c: e.tensor_scalar(out_f[:, c, :], ysq[:, c, :], gcols[:, c:c + 1], bcols[:, c:c + 1], ALU.mult, ALU.add),
             reads=[Bysq, Bp], writes=[Bof])
        S.op("act", lambda e, c=c: e.copy(out_b[:, c, :], out_f[:, c, :]), reads=[Bof], writes=[Bob])


def emit_post(nc, S, st, ntiles, KC, src_o, src_mode, Bsrc, hres, wout_b, w1_b, w2_b, Bwb, lnp, hout, Bhout, flags=None):
    C = Ctx(nc, st)
    ot = [C.sb([128, KC, TW], BF16) for _ in range(2)]
    hr = [C.sb([128, 8, TW], F32) for _ in range(2)]
    y = C.sb([128, 8, TW], F32)
    ysq = C.sb([128, 8, TW], F32)
    h1 = C.sb([128, 8, TW], F32)
    h1b = C.sb([128, 8, TW], BF16)
    h2 = C.sb([128, 8, TW], F32)
    h2b = C.sb([128, 8, TW], BF16)
    hid = C.sb([128, 32, TW], BF16)
    rl = [C.sb([128, TW], F32) for _ in range(2)]
    WCOL = 4096 // KC
    wo = [C.sb([128, KC, WCOL], BF16) for _ in range(2)]
    w1c = [C.sb([128, 8, 512], BF16) for _ in range(2)]
    w2c = [C.sb([128, 8, 512], BF16) for _ in range(2)]
    lnp_sb = C.sb([128, 32], F32)
    ones_f = C.sb([128, 128], F32)
    if src_mode == "blend":
        cand = [C.sb([128, KC, TW], BF16) for _ in range(2)]
        fl_sb = C.sb([128, 2], F32)
        Bcand = [Buf(), Buf()]
        Bfl = Buf()
        S.dma("sp", fl_sb[:], flags, writes=[Bfl])
    mean = C.sb([128, TW], F32)
    msq = C.sb([128, TW], F32)
    rstd = C.sb([128, TW], F32)
    pa = [C.ps([128, 512]) for _ in range(2)]
    pb = [C.ps([128, 512]) for _ in range(4)]
    ps1 = C.ps([128, 512])
    ps2 = C.ps([128, 512])
    Bot = [Buf(), Buf()]; Bhr = [Buf(), Buf()]; By = Buf(); Bysq = Buf(); Bh1 = Buf(); Bh1b = Buf(); Bh2 = Buf(); Bh2b = Buf()
    Bhid = Buf(); Brl = [Buf(), Buf()]; Bwo = [Buf(), Buf()]; Bw1 = [Buf(), Buf()]; Bw2 = [Buf(), Buf()]
    Bp = Buf(); Bc = Buf(); Bst = Buf(); Bpa = [PB(), PB()]; Bpb = [PB() for _ in range(4)]; Bps1 = PB(); Bps2 = PB()
    S.dma("sp", lnp_sb[:], lnp, writes=[Bp])
    S.op("pool", lambda e: e.memset(ones_f[:], 1.0), writes=[Bc])
    wr = "(c p) o -> p c o"
    wo_v = wout_b.rearrange(wr, p=128)
    w1_v = w1_b.rearrange(wr, p=128)
    w2_v = w2_b.rearrange(wr, p=128)
    so_v = None if src_mode == "blend" else src_o.rearrange("(c p) t -> p c t", p=128)
    hr_v = hres.rearrange("(c p) t -> p c t", p=128)
    ho_v = hout.rearrange("(c p) t -> p c t", p=128)
    ai = 0
    wi = 0
    w1i = 0
    w2i = 0
    ri = 0
    toks = []
    for t in range(ntiles):
        c0 = t * TW
        OT = ot[t % 2]; BOT = Bot[t % 2]; HR = hr[t % 2]; BHR = Bhr[t % 2]
        if src_mode == "blend":
            for k in range(2):
                S.dma("sp", cand[k][:], src_o(k, t), reads=[Bsrc], writes=[Bcand[k]])
            for kc in range(KC):
                S.op("pool", lambda e, kc=kc, OT=OT: e.tensor_scalar(OT[:, kc, :], cand[0][:, kc, :], fl_sb[:, 0:1], None, ALU.mult),
                     reads=[Bcand[0], Bfl], writes=[BOT])
                S.op("dve", lambda e, kc=kc, OT=OT: e.scalar_tensor_tensor(OT[:, kc, :], cand[1][:, kc, :], fl_sb[:, 1:2], OT[:, kc, :], ALU.mult, ALU.add),
                     reads=[Bcand[1], Bfl, BOT], writes=[BOT])
        else:
            S.dma("pool" if src_mode == "f32" else "sp", OT[:], so_v[:, :, c0:c0 + TW], reads=[Bsrc], writes=[BOT])
        S.dma("sp", HR[:], hr_v[:, :, c0:c0 + TW], writes=[BHR])
        for n2 in range(1024 // WCOL):
            WO = wo[wi % 2]; BWO = Bwo[wi % 2]
            wi += 1
            S.dma("sp", WO[:], wo_v[:, :, n2 * WCOL:(n2 + 1) * WCOL], reads=[Bwb], writes=[BWO])
            for c4 in range(WCOL // 128):
                cc = n2 * (WCOL // 128) + c4
                P = pa[ai % 2]; BP = Bpa[ai % 2]
                ai += 1
                for kc in range(KC):
                    S.op("pe", lambda e, P=P, WO=WO, kc=kc, c4=c4, OT=OT: e.matmul(
                        P[:, 0:TW], WO[:, kc, c4 * 128:(c4 + 1) * 128], OT[:, kc, :], start=(kc == 0), stop=(kc == KC - 1)),
                        reads=[BWO, BOT], writes=[BP], inc=(kc == KC - 1))
                S.op("dve", lambda e, P=P, cc=cc, HR=HR: e.scalar_tensor_tensor(
                    y[:, cc, :], HR[:, cc, :], ALPHA, P[:, 0:TW], ALU.mult, ALU.add), reads=[BP, BHR], writes=[By])
        emit_ln(S, y, By, ysq, Bysq, h1, Bh1, h1b, Bh1b, lnp_sb[:, 0:8], lnp_sb[:, 8:16], Bp, ones_f, Bc,
                ps1, Bps1, ps2, Bps2, mean, msq, rstd, Bst)
        for hg in range(8):
            W1 = w1c[w1i % 2]; BW1 = Bw1[w1i % 2]
            w1i += 1
            S.dma("sp", W1[:], w1_v[:, :, hg * 512:(hg + 1) * 512], reads=[Bwb], writes=[BW1])
            for h4 in range(4):
                hc = hg * 4 + h4
                P = pa[ai % 2]; BP = Bpa[ai % 2]
                ai += 1
                for fc in range(8):
                    S.op("pe", lambda e, P=P, W1=W1, fc=fc, h4=h4: e.matmul(
                        P[:, 0:TW], W1[:, fc, h4 * 128:(h4 + 1) * 128], h1b[:, fc, :], start=(fc == 0), stop=(fc == 7)),
                        reads=[BW1, Bh1b], writes=[BP], inc=(fc == 7))
                R = rl[ri % 2]; BR = Brl[ri % 2]
                ri += 1
                S.op("act", lambda e, R=R, P=P: e.activation(R[:], P[:, 0:TW], AF.Relu), reads=[BP], writes=[BR])
                S.op("pool", lambda e, R=R, hc=hc: e.tensor_tensor(hid[:, hc, :], R[:], R[:], ALU.mult), reads=[BR], writes=[Bhid])
        for n2 in range(2):
            for kg in range(4):
                W2 = w2c[w2i % 2]; BW2 = Bw2[w2i % 2]
                w2i += 1
                S.dma("sp", W2[:], w2_v[:, kg * 8:(kg + 1) * 8, n2 * 512:(n2 + 1) * 512], reads=[Bwb], writes=[BW2])
                for c4 in range(4):
                    for k8 in range(8):
                        hc = kg * 8 + k8
                        S.op("pe", lambda e, c4=c4, W2=W2, k8=k8, hc=hc: e.matmul(
                            pb[c4][:, 0:TW], W2[:, k8, c4 * 128:(c4 + 1) * 128], hid[:, hc, :], start=(hc == 0), stop=(hc == 31)),
                            reads=[BW2, Bhid], writes=[Bpb[c4]], inc=(k8 == 7))
            for c4 in range(4):
                cc = n2 * 4 + c4
                S.op("dve", lambda e, c4=c4, cc=cc: e.scalar_tensor_tensor(
                    y[:, cc, :], h1[:, cc, :], ALPHA, pb[c4][:, 0:TW], ALU.mult, ALU.add), reads=[Bpb[c4], Bh1], writes=[By])
        emit_ln(S, y, By, ysq, Bysq, h2, Bh2, h2b, Bh2b, lnp_sb[:, 16:24], lnp_sb[:, 24:32], Bp, ones_f, Bc,
                ps1, Bps1, ps2, Bps2, mean, msq, rstd, Bst)
        toks.append(S.dma("sp", ho_v[:, :, c0:c0 + TW], h2[:], reads=[Bh2], writes=[Bhout]))
    return toks


L2_STAGE = 4
L3_STAGE = 3
L3_SUB = 3
L3_NCH = 12
RET_GAMMA = [1.0 - 2.0 ** (-5.0 - h) for h in range(4)]


def emit_state_tile(S, h2b, Bh2b, wk_sb, wv_sb, Bw, kts, Bkts, vts, Bvts, kdec_sb, Bc, Sst, BS, pk, Bpk, ctr):
    for bl in range(3):
        for half in range(2):
            P = pk[ctr[0] % 2]; BP = Bpk[ctr[0] % 2]
            ctr[0] += 1
            for fc in range(8):
                S.op("pe", lambda e, P=P, fc=fc, bl=bl, half=half: e.matmul(
                    P[:], h2b[:, fc, bl * 128:(bl + 1) * 128], wk_sb[:, fc, half * 512:(half + 1) * 512], start=(fc == 0), stop=(fc == 7)),
                    reads=[Bh2b, Bw], writes=[BP], inc=(fc == 7))
            for hh in range(2):
                h = half * 2 + hh
                S.op("dve", lambda e, P=P, bl=bl, h=h, hh=hh: e.tensor_scalar(
                    kts[:, bl, h * 256:(h + 1) * 256], P[:, hh * 256:(hh + 1) * 256], kdec_sb[:, bl * 4 + h:bl * 4 + h + 1], None, ALU.mult),
                    reads=[BP, Bc], writes=[Bkts])
        for h in range(4):
            P = pk[ctr[0] % 2]; BP = Bpk[ctr[0] % 2]
            ctr[0] += 1
            for fc in range(8):
                S.op("pe", lambda e, P=P, fc=fc, bl=bl, h=h: e.matmul(
                    P[:], h2b[:, fc, bl * 128:(bl + 1) * 128], wv_sb[:, fc, h * 512:(h + 1) * 512], start=(fc == 0), stop=(fc == 7)),
                    reads=[Bh2b, Bw], writes=[BP], inc=(fc == 7))
            S.op("act", lambda e, P=P, bl=bl, h=h: e.copy(vts[:, bl, h * 512:(h + 1) * 512], P[:]), reads=[BP], writes=[Bvts])


def emit_state_update(S, kts, Bkts, vts, Bvts, Sst, BS, Sb, BSb, pk, Bpk, ctr):
    for h in range(4):
        c384 = RET_GAMMA[h] ** TW
        for dkc in range(2):
            P = pk[ctr[0] % 2]; BP = Bpk[ctr[0] % 2]
            ctr[0] += 1
            for bl in range(3):
                S.op("pe", lambda e, P=P, bl=bl, h=h, dkc=dkc: e.matmul(
                    P[:], kts[:, bl, h * 256 + dkc * 128:h * 256 + (dkc + 1) * 128], vts[:, bl, h * 512:(h + 1) * 512],
                    start=(bl == 0), stop=(bl == 2)), reads=[Bkts, Bvts], writes=[BP], inc=(bl == 2))
            idx = h * 2 + dkc
            S.op("dve", lambda e, P=P, idx=idx, c384=c384: e.scalar_tensor_tensor(
                Sst[:, idx, :], Sst[:, idx, :], c384, P[:], ALU.mult, ALU.add), reads=[BP, BS], writes=[BS])
            if Sb is not None:
                S.op("pool", lambda e, idx=idx: e.tensor_copy(Sb[:, idx, :], Sst[:, idx, :]), reads=[BS], writes=[BSb])


def emit_prepass(nc, S, T, Bhout, BSe):
    h2T = T["h2T"]; wk = T["wk"]; wv = T["wv"]; kdec = T["kdec"]; tmask = T["tmask"]; S_end = T["S_end"]
    if True:
        with contextlib.ExitStack() as st:
            C = Ctx(nc, st)
            wk_sb = C.sb([128, 8, 1024], BF16)
            wv_sb = C.sb([128, 8, 2048], BF16)
            h2b = [C.sb([128, 8, TW], BF16) for _ in range(2)]
            kts = C.sb([128, 3, 1024], BF16)
            vts = C.sb([128, 3, 2048], BF16)
            kdec_sb = C.sb([128, 12], F32)
            tm_sb = C.sb([128, TW], F32)
            Sst = C.sb([128, 8, 512], F32)
            pk = [C.ps([128, 512]) for _ in range(2)]
            Bw = Buf(); Bh2b = [Buf(), Buf()]; Bkts = Buf(); Bvts = Buf(); Bc = Buf(); BS = Buf(); Bpk = [PB(), PB()]
            wr = "(c p) o -> p c o"
            wqueue = T.get("wqueue", "pool")
            for fc in range(8):
                S.dma(wqueue, wk_sb[:, fc, :], wk[fc * 128:(fc + 1) * 128, :], reads=[T["Bwb"]], writes=[Bw])
                S.dma(wqueue, wv_sb[:, fc, :], wv[fc * 128:(fc + 1) * 128, :], reads=[T["Bwb"]], writes=[Bw])
            S.dma("sp", kdec_sb[:], kdec, writes=[Bc])
            S.dma("sp", tm_sb[:], tmask, writes=[Bc])
            for i8 in range(8):
                S.op("dve", lambda e, i8=i8: e.memset(Sst[:, i8, :], 0.0), writes=[BS])
            hv = h2T.rearrange("(c p) t -> p c t", p=128)
            ctr = [0]
            for t in range(NTH):
                c0 = t * TW
                H = h2b[t % 2]; BH = Bh2b[t % 2]
                S.dma("pool", H[:], hv[:, :, c0:c0 + TW], reads=[Bhout], writes=[BH])
                if t == 0:
                    for fc in range(8):
                        S.op("dve", lambda e, H=H, fc=fc: e.tensor_tensor(H[:, fc, :], H[:, fc, :], tm_sb[:], ALU.mult),
                             reads=[BH, Bc], writes=[BH])
                emit_state_tile(S, H, BH, wk_sb, wv_sb, Bw, kts, Bkts, vts, Bvts, kdec_sb, Bc, Sst, BS, pk, Bpk, ctr)
                emit_state_update(S, kts, Bkts, vts, Bvts, Sst, BS, None, None, pk, Bpk, ctr)
            for k in range(4):
                S.dma("sp", S_end[k], Sst[32 * k:32 * (k + 1)], reads=[BS], writes=[BSe])
            S.barrier()
            S.flush()

def ret_consts():
    p = np.arange(128, dtype=np.float64)
    kdec = np.zeros((128, 3, 4), np.float32)
    for h in range(4):
        g = RET_GAMMA[h]
        for bl in range(3):
            kdec[:, bl, h] = g ** (TW - 1 - (bl * 128 + p)) / 16.0
    return kdec.reshape(128, 12)


def lnp_table(ln_g, ln_b, layer):
    cols = []
    for k in range(2):
        cols.append(ln_g[layer, k].reshape(8, 128).T)
        cols.append(ln_b[layer, k].reshape(8, 128).T)
    return np.ascontiguousarray(np.concatenate(cols, axis=1).astype(np.float32))


def emit_retention(nc, S, st, T, Bwb, Bh2, BSg, ByT):
    h2T = T["h2T"]; wi_b = T["wi_b"]; kdec = T["kdec"]; tmask = T["tmask"]; dmask = T["dmask"]; qdec = T["qdec"]
    yT_s = T["yT_s"]; S_src = T["S_src"]; flags = T["flags"]
    C = Ctx(nc, st)
    if True:
        if True:
            h2b = [C.sb([128, 8, TW], BF16) for _ in range(2)]
            wc = [C.sb([128, 8, 512], BF16) for _ in range(2)]
            qT = C.sb([128, 8, TW], BF16)
            qdT = C.sb([128, 8, TW], BF16)
            kT = C.sb([128, 8, TW], BF16)
            kts = C.sb([128, 3, 1024], BF16)
            vts = C.sb([128, 3, 2048], BF16)
            sg = C.sb([128, 16, TW], BF16)
            sTb = [C.sb([128, 3, TW], BF16) for _ in range(2)]
            o32 = C.sb([128, 4, TW], F32)
            osq = C.sb([128, 4, TW], F32)
            yT = [C.sb([128, 16, TW], BF16) for _ in range(2)]
            Sst = C.sb([128, 8, 512], F32)
            Sb = C.sb([128, 8, 512], BF16)
            dm_sb = C.sb([128, 12 * TW], F32)
            qd_sb = C.sb([128, 4 * TW], F32)
            kdec_sb = C.sb([128, 12], F32)
            tm_sb = C.sb([128, TW], F32)
            ones_f = C.sb([128, 128], F32)
            rstd = C.sb([128, TW], F32)
            tmpo = C.sb([128, TW], F32)
            pk = [C.ps([128, 512]) for _ in range(2)]
            psc = [C.ps([128, 512]) for _ in range(2)]
            po = [C.ps([128, 512]) for _ in range(2)]
            pss = C.ps([128, 512])
            Bh2b = [Buf(), Buf()]; Bwc = [Buf(), Buf()]; BqT = Buf(); BqdT = Buf(); BkT = Buf(); Bkts = Buf(); Bvts = Buf()
            Bsg = Buf(); BsTb = [Buf(), Buf()]; Bo32 = Buf(); Bosq = Buf(); ByTt = [Buf(), Buf()]; BS = Buf(); BSb = Buf()
            Bc = Buf(); Brstd = Buf(); Btmpo = Buf(); Bpk = [PB(), PB()]; Bpsc = [PB(), PB()]; Bpo = [PB(), PB()]; Bpss = PB()
            S.dma("sp", dm_sb[:], dmask, writes=[Bc])
            S.dma("sp", qd_sb[:], qdec, writes=[Bc])
            S.dma("sp", kdec_sb[:], kdec, writes=[Bc])
            S.dma("sp", tm_sb[:], tmask, writes=[Bc])
            S.op("pool", lambda e: e.memset(ones_f[:], 1.0), writes=[Bc])
            fl_sb = C.sb([128, 2], F32)
            S.dma("sp", fl_sb[:], flags, writes=[Bc])
            for k in range(4):
                S.dma("sp", Sst[32 * k:32 * (k + 1)], S_src[k], reads=[BSg], writes=[BS])
            for i8 in range(8):
                S.op("dve", lambda e, i8=i8: e.tensor_scalar(Sst[:, i8, :], Sst[:, i8, :], fl_sb[:, 1:2], None, ALU.mult),
                     reads=[BS, Bc], writes=[BS])
            for i8 in range(8):
                S.op("pool", lambda e, i8=i8: e.tensor_copy(Sb[:, i8, :], Sst[:, i8, :]), reads=[BS], writes=[BSb])
            hv = h2T.rearrange("(c p) t -> p c t", p=128)
            wiv = wi_b.rearrange("(c p) o -> p c o", p=128)
            yv = yT_s.rearrange("(c p) t -> p c t", p=128)
            ctr = [0]
            wci = 0
            sci = 0
            oi = 0
            for t in range(NTH if L3_STAGE >= 2 else 1):
                c0 = t * TW
                H = h2b[t % 2]; BH = Bh2b[t % 2]
                Y = yT[t % 2]; BY = ByTt[t % 2]
                S.dma("pool", H[:], hv[:, :, c0:c0 + TW], reads=[Bh2], writes=[BH])
                if t == 0:
                    for fc in range(8):
                        S.op("dve", lambda e, H=H, fc=fc: e.tensor_tensor(H[:, fc, :], H[:, fc, :], tm_sb[:], ALU.mult),
                             reads=[BH, Bc], writes=[BH])
                for ch in range(L3_NCH):
                    W = wc[wci % 2]; BW = Bwc[wci % 2]
                    wci += 1
                    S.dma("sp", W[:], wiv[:, :, ch * 512:(ch + 1) * 512], reads=[Bwb], writes=[BW])
                    if ch < 4 or ch >= 8:
                        for c4 in range(4):
                            P = pk[ctr[0] % 2]; BP = Bpk[ctr[0] % 2]
                            ctr[0] += 1
                            for fc in range(8):
                                S.op("pe", lambda e, P=P, W=W, fc=fc, c4=c4, H=H: e.matmul(
                                    P[:, 0:TW], W[:, fc, c4 * 128:(c4 + 1) * 128], H[:, fc, :], start=(fc == 0), stop=(fc == 7)),
                                    reads=[BW, BH], writes=[BP], inc=(fc == 7))
                            if ch < 2:
                                ci = ch * 4 + c4
                                h = ci // 2
                                S.op("act", lambda e, P=P, ci=ci: e.copy(qT[:, ci, :], P[:, 0:TW]), reads=[BP], writes=[BqT])
                                S.op("dve", lambda e, P=P, ci=ci, h=h: e.tensor_tensor(qdT[:, ci, :], P[:, 0:TW], qd_sb[:, h * TW:(h + 1) * TW], ALU.mult),
                                     reads=[BP, Bc], writes=[BqdT])
                            elif ch < 4:
                                ci = (ch - 2) * 4 + c4
                                S.op("act", lambda e, P=P, ci=ci: e.copy(kT[:, ci, :], P[:, 0:TW]), reads=[BP], writes=[BkT])
                            else:
                                gi = (ch - 8) * 4 + c4
                                S.op("act", lambda e, P=P, gi=gi: e.activation(sg[:, gi, :], P[:, 0:TW], AF.Silu), reads=[BP], writes=[Bsg])
                    if 2 <= ch < 4:
                        half = ch - 2
                        for bl in range(3):
                            P = pk[ctr[0] % 2]; BP = Bpk[ctr[0] % 2]
                            ctr[0] += 1
                            for fc in range(8):
                                S.op("pe", lambda e, P=P, W=W, fc=fc, bl=bl, H=H: e.matmul(
                                    P[:], H[:, fc, bl * 128:(bl + 1) * 128], W[:, fc, :], start=(fc == 0), stop=(fc == 7)),
                                    reads=[BH, BW], writes=[BP], inc=(fc == 7))
                            for hh in range(2):
                                h = half * 2 + hh
                                S.op("dve", lambda e, P=P, bl=bl, h=h, hh=hh: e.tensor_scalar(
                                    kts[:, bl, h * 256:(h + 1) * 256], P[:, hh * 256:(hh + 1) * 256], kdec_sb[:, bl * 4 + h:bl * 4 + h + 1], None, ALU.mult),
                                    reads=[BP, Bc], writes=[Bkts])
                    if 4 <= ch < 8:
                        h = ch - 4
                        for bl in range(3):
                            P = pk[ctr[0] % 2]; BP = Bpk[ctr[0] % 2]
                            ctr[0] += 1
                            for fc in range(8):
                                S.op("pe", lambda e, P=P, W=W, fc=fc, bl=bl, H=H: e.matmul(
                                    P[:], H[:, fc, bl * 128:(bl + 1) * 128], W[:, fc, :], start=(fc == 0), stop=(fc == 7)),
                                    reads=[BH, BW], writes=[BP], inc=(fc == 7))
                            S.op("act", lambda e, P=P, bl=bl, h=h: e.copy(vts[:, bl, h * 512:(h + 1) * 512], P[:]), reads=[BP], writes=[Bvts])
                for h in range(4 if L3_SUB >= 2 else 0):
                    ST = sTb[sci % 2]; BST = BsTb[sci % 2]
                    sci += 1
                    for jb in range(3):
                        P = psc[(sci + jb) % 2]; BP = Bpsc[(sci + jb) % 2]
                        for dc in range(2):
                            S.op("pe", lambda e, P=P, h=h, dc=dc, jb=jb: e.matmul(
                                P[:, 0:TW], kT[:, h * 2 + dc, jb * 128:(jb + 1) * 128], qT[:, h * 2 + dc, :], start=(dc == 0), stop=(dc == 1)),
                                reads=[BkT, BqT], writes=[BP], inc=(dc == 1))
                        S.op("dve", lambda e, P=P, ST=ST, jb=jb, h=h: e.tensor_tensor(
                            ST[:, jb, :], P[:, 0:TW], dm_sb[:, (h * 3 + jb) * TW:(h * 3 + jb + 1) * TW], ALU.mult),
                            reads=[BP, Bc], writes=[BST])
                    for ec in range(4):
                        P = po[oi % 2]; BP = Bpo[oi % 2]
                        oi += 1
                        for jb in range(3):
                            S.op("pe", lambda e, P=P, h=h, ec=ec, jb=jb, ST=ST: e.matmul(
                                P[:, 0:TW], vts[:, jb, h * 512 + ec * 128:h * 512 + (ec + 1) * 128], ST[:, jb, :], start=(jb == 0), stop=False),
                                reads=[Bvts, BST], writes=[BP], inc=False)
                        for dc in range(2):
                            S.op("pe", lambda e, P=P, h=h, ec=ec, dc=dc: e.matmul(
                                P[:, 0:TW], Sb[:, h * 2 + dc, ec * 128:(ec + 1) * 128], qdT[:, h * 2 + dc, :], start=False, stop=(dc == 1)),
                                reads=[BSb, BqdT], writes=[BP], inc=(dc == 1))
                        S.op("act", lambda e, P=P, ec=ec: e.copy(o32[:, ec, :], P[:, 0:TW]), reads=[BP], writes=[Bo32])
                        S.op("act", lambda e, P=P, ec=ec: e.activation(osq[:, ec, :], P[:, 0:TW], AF.Square), reads=[BP], writes=[Bosq])
                    for ec in range(4):
                        S.op("pe", lambda e, ec=ec: e.matmul(pss[:, 0:TW], ones_f[:], osq[:, ec, :], start=(ec == 0), stop=(ec == 3)),
                             reads=[Bc, Bosq], writes=[Bpss], inc=(ec == 3))
                    S.op("act", lambda e: e.activation(rstd[:], pss[:, 0:TW], AF.Ln, bias=RMS_EPS, scale=1.0 / 512.0), reads=[Bpss], writes=[Brstd])
                    S.op("act", lambda e: e.activation(rstd[:], rstd[:], AF.Exp, scale=-0.5), reads=[Brstd], writes=[Brstd])
                    for ec in range(4):
                        S.op("dve", lambda e, ec=ec: e.tensor_tensor(tmpo[:], o32[:, ec, :], rstd[:], ALU.mult), reads=[Bo32, Brstd], writes=[Btmpo])
                        S.op("dve", lambda e, ec=ec, h=h, Y=Y: e.tensor_tensor(Y[:, h * 4 + ec, :], tmpo[:], sg[:, h * 4 + ec, :], ALU.mult),
                             reads=[Btmpo, Bsg], writes=[BY])
                if L3_SUB >= 3:
                    emit_state_update(S, kts, Bkts, vts, Bvts, Sst, BS, Sb, BSb, pk, Bpk, ctr)
                if L3_SUB >= 2:
                    S.dma("sp", yv[:, :, c0:c0 + TW], Y[:], reads=[BY], writes=[ByT])
            S.barrier()
            S.flush()

def l3_consts():
    p = np.arange(128, dtype=np.float64)[:, None]
    i = np.arange(TW, dtype=np.float64)[None, :]
    dm = np.zeros((128, 4, 3, TW), np.float32)
    qd = np.zeros((128, 4, TW), np.float32)
    for h in range(4):
        g = RET_GAMMA[h]
        for jb in range(3):
            j = jb * 128 + p
            same = (j // 64) == (i // 64)
            before = (j // 64) < (i // 64)
            val = np.where(same, g ** np.abs(i - j), np.where(before, g ** np.maximum(i - j, 0), 0.0)) / 16.0
            dm[:, h, jb, :] = val
        qd[:, h, :] = g ** (i + 1.0)
    return dm.reshape(128, 12 * TW), qd.reshape(128, 4 * TW)


PAIRS = [[0, 1], [2, 3], [4, 5], [6, 7]]


def build_fused():
    nc = bass.Bass("TRN2", target_bir_lowering=False)

    def din(name, shape):
        return nc.dram_tensor(name, list(shape), F32, kind="ExternalInput").ap()

    T1 = {"xT": din("xT", [D, LR]), "wq": din("wq", [D, 512]), "wk": din("wk", [D, 512]), "wv": din("wv", [D, 512]),
          "wf": din("wf", [D, 4]), "fbias": din("fbias", [4, 1]), "lamv": din("lamv", [128, 256]), "gsub": din("gsub", [128, 1]),
          "dbias": din("dbias", [128, 2 * 66]), "Fd": din("Fd", [128, 2 * 3 * TW]), "Ff": din("Ff", [128, 3 * TW]),
          "qaug": din("qaug", [2, 3, LR]), "odt": BF16}
    hres = din("hres", [D, HALF])
    flags = din("flags", [128, 2])
    w_out0 = din("w_out0", [D, D])
    w1_0 = din("w1_0", [D, 4 * D])
    w2_0 = din("w2_0", [4 * D, D])
    lnp0 = din("lnp0", [128, 32])
    w_in1 = din("w_in1", [D, 6144])
    w_out1 = din("w_out1", [2048, D])
    w1_1 = din("w1_1", [D, 4 * D])
    w2_1 = din("w2_1", [4 * D, D])
    lnp1 = din("lnp1", [128, 32])
    kdec = din("kdec", [128, 12])
    tmask = din("tmask", [128, TW])
    dmask = din("dmask", [128, 12 * TW])
    qdec = din("qdec", [128, 4 * TW])
    outT = nc.dram_tensor("outT", [D, HALF], F32, kind="ExternalOutput").ap()
    wo0_b = nc.dram_tensor("wo0_b", [D, D], BF16).ap()
    w10_b = nc.dram_tensor("w10_b", [D, 4 * D], BF16).ap()
    w20_b = nc.dram_tensor("w20_b", [4 * D, D], BF16).ap()
    wi_b = nc.dram_tensor("wi_b", [D, 6144], BF16).ap()
    wo1_b = nc.dram_tensor("wo1_b", [2048, D], BF16).ap()
    w11_b = nc.dram_tensor("w11_b", [D, 4 * D], BF16).ap()
    w21_b = nc.dram_tensor("w21_b", [4 * D, D], BF16).ap()
    NCH = NT // 2
    oT_c = [nc.dram_tensor("oT_c%d" % k, [512, 2 * TW], BF16) for k in range(NCH)]
    G_c = [nc.dram_tensor("G_c%d" % k, [1024, 2 * TW], BF16) for k in range(NCH)]
    h2T_i = nc.dram_tensor("h2T_i", [D, HALF], F32).ap()
    Se_c = [nc.dram_tensor("Se_c%d" % k, [256, 512], F32) for k in range(4)]
    Sg_c = [nc.dram_tensor("Sg_c%d" % k, [512, 512], F32) for k in range(4)]
    yT_s = nc.dram_tensor("yT_s", [2048, HALF], BF16).ap()
    T1["oT_tile"] = lambda r0, r1, t: oT_c[t // 2].ap()[r0:r1, (t % 2) * TW:(t % 2 + 1) * TW]

    def g_tile(k, t):
        gt = k * NTH + t
        return G_c[gt // 2].ap().rearrange("(c p) t -> p c t", p=128)[:, :, (gt % 2) * TW:(gt % 2 + 1) * TW]
    with contextlib.ExitStack() as outer:
        S = Sched(nc, outer)
        Bwb = Buf(); Bo = Buf(); BG = Buf(); Bh2 = Buf(); BSe = Buf(); BSg = Buf(); ByT = Buf(); Bout = Buf()
        with contextlib.ExitStack() as st:
            C = Ctx(nc, st)
            emit_convert(S, C, w_out0, wo0_b, Bwb, D, D, "wo0")
            emit_convert(S, C, w1_0, w10_b, Bwb, D, 4 * D, "w10")
            emit_convert(S, C, w2_0, w20_b, Bwb, 4 * D, D, "w20")
            emit_convert(S, C, w_in1, wi_b, Bwb, D, 6144, "wi")
            emit_convert(S, C, w_out1, wo1_b, Bwb, 2048, D, "wo1")
            emit_convert(S, C, w1_1, w11_b, Bwb, D, 4 * D, "w11")
            emit_convert(S, C, w2_1, w21_b, Bwb, 4 * D, D, "w21")
            S.barrier()
            S.flush()
        emit_l1(nc, S, T1, Bo)
        for k in range(NCH):
            S.op("pool", lambda e, k=k: e.collective_compute("AllGather", ALU.bypass, replica_groups=PAIRS,
                                                             ins=[oT_c[k].ap().opt()], outs=[G_c[k].ap().opt()]), reads=[Bo], writes=[BG])
        with contextlib.ExitStack() as st:
            emit_post(nc, S, st, NTH, 8, g_tile, "blend", BG, hres, wo0_b, w10_b, w20_b, Bwb, lnp0, h2T_i, Bh2, flags=flags)
            S.barrier()
            S.flush()
        Tp = {"h2T": h2T_i, "wk": wi_b[:, 1024:2048], "wv": wi_b[:, 2048:4096], "kdec": kdec, "tmask": tmask,
              "S_end": [Se_c[k].ap().rearrange("(p i) f -> p i f", i=8) for k in range(4)], "wqueue": "sp", "Bwb": Bwb}
        emit_prepass(nc, S, Tp, Bh2, BSe)
        for k in range(4):
            S.op("pool", lambda e, k=k: e.collective_compute("AllGather", ALU.bypass, replica_groups=PAIRS,
                                                             ins=[Se_c[k].ap().opt()], outs=[Sg_c[k].ap().opt()]), reads=[BSe], writes=[BSg])
        Tr = {"h2T": h2T_i, "wi_b": wi_b, "kdec": kdec, "tmask": tmask, "dmask": dmask, "qdec": qdec, "yT_s": yT_s,
              "S_src": [Sg_c[k].ap()[0:256, :].rearrange("(p i) f -> p i f", i=8) for k in range(4)], "flags": flags}
        with contextlib.ExitStack() as st:
            emit_retention(nc, S, st, Tr, Bwb, Bh2, BSg, ByT)
        with contextlib.ExitStack() as st:
            emit_post(nc, S, st, NTH, 16, yT_s, "bf16", ByT, h2T_i, wo1_b, w11_b, w21_b, Bwb, lnp1, outT, Bout)
            S.barrier()
            S.flush()
    return nc


def _token_mask(u):
    tm = np.ones((128, TW), np.float32)
    if u == 0:
        tm[:, 0:FPAD] = 0.0
    return tm


def kernel(x, meta_tokens, even_w_in, even_f_bias, diff_lambda, diff_subln_g, even_w_out,
           ret_w_in, ret_w_out, ln_g, ln_b, ffn_w1, ffn_w2):
    x = np.asarray(x, np.float32)
    f32 = lambda a: np.ascontiguousarray(np.asarray(a, np.float32))
    meta = f32(meta_tokens)
    B = x.shape[0]
    cores = list(range(8))
    kdec = ret_consts()
    dm, qd = l3_consts()
    wo = f32(even_w_out[0])
    wo_perm = np.ascontiguousarray(np.concatenate([wo[0:256], wo[512:768], wo[256:512], wo[768:1024]], axis=0))
    shared = {"w_out0": wo_perm, "w1_0": f32(ffn_w1[0]), "w2_0": f32(ffn_w2[0]), "lnp0": lnp_table(f32(ln_g), f32(ln_b), 0),
              "w_in1": f32(ret_w_in[0]), "w_out1": f32(ret_w_out[0]), "w1_1": f32(ffn_w1[1]), "w2_1": f32(ffn_w2[1]),
              "lnp1": lnp_table(f32(ln_g), f32(ln_b), 1), "kdec": kdec, "dmask": dm, "qdec": qd}
    in_maps = []
    for c in cores:
        b, u = c // 2, c % 2
        hp = np.zeros((LR, D), np.float32)
        hp[FPAD:FPAD + NMETA] = meta
        hp[FPAD + NMETA:FPAD + NMETA + SEQ] = x[b]
        xT = np.ascontiguousarray(hp.T)
        m = l1_inputs(xT, f32(even_w_in[0]), f32(even_f_bias[0]), f32(diff_lambda[0]), f32(diff_subln_g[0]), u)
        m["hres"] = np.ascontiguousarray(xT[:, u * HALF:(u + 1) * HALF])
        fl = np.zeros((128, 2), np.float32)
        fl[:, u] = 1.0
        m["flags"] = fl
        m["tmask"] = _token_mask(u)
        m.update(shared)
        in_maps.append(m)
    res = run_bass_kernel_spmd(build_fused(), in_maps, core_ids=cores).results
    out = np.empty((B, SEQ, D), np.float32)
    for b in range(B):
        hT = np.concatenate([res[2 * b]["outT"], res[2 * b + 1]["outT"]], axis=1)
        out[b] = hT[:, FPAD + NMETA:FPAD + NMETA + SEQ].T
    return out
```

```python
import contextlib
import math
import numpy as np
import concourse.bass as bass
import concourse.mybir as mybir
from concourse.bass_utils import run_bass_kernel_spmd

F32 = mybir.dt.float32
BF16 = mybir.dt.bfloat16
AF = mybir.ActivationFunctionType
ALU = mybir.AluOpType

D = 1024
SEQ = 8192
NMETA = 16
FPAD = 48
LR = 8448
TW = 384
NT = LR // TW
NB = LR // 128
HALF = LR // 2
NTH = HALF // TW
ALPHA = 4 ** 0.25
LN_EPS = 1e-5
RMS_EPS = 1e-6
LAM_INIT0 = 0.8 - 0.6 * math.exp(-0.3 * 0)
NEG = -30000.0
REAL_END = FPAD + NMETA + SEQ

ENGS = ("pe", "act", "dve", "pool", "sp")


class Buf:
    __slots__ = ("name", "w", "r", "ex")

    def __init__(self, name="", ex=False):
        self.name = name
        self.w = None
        self.r = []
        self.ex = ex


def PB():
    return Buf(ex=True)


class Sched:
    def __init__(self, nc, stack, n_dma_sems=16):
        self.nc = nc
        self.streams = {e: [] for e in ENGS}
        self.cnt = {e: 0 for e in ENGS}
        self.seen = {e: {} for e in ENGS}
        self.n_dma = n_dma_sems
        self.dma_k = 0
        self.sems = {e: stack.enter_context(nc.semaphore("s_" + e)) for e in ENGS}
        self.dsems = [stack.enter_context(nc.semaphore("d_%d" % i)) for i in range(n_dma_sems)]
        self.dlast = [0] * n_dma_sems

    def _need(self, eng, tok, waits):
        if tok is None:
            return
        kind, key, val = tok
        if kind == "e" and key == eng and eng in ("pe", "sp"):
            return
        k = (kind, key)
        if self.seen[eng].get(k, 0) >= val:
            return
        if val > waits.get(k, 0):
            waits[k] = val

    def _emit_waits(self, eng, waits):
        for k, val in waits.items():
            self.seen[eng][k] = val
            self.streams[eng].append(("wait", k, val))

    def _deps(self, eng, reads, writes):
        waits = {}
        for b in reads:
            self._need(eng, b.w, waits)
        for b in writes:
            self._need(eng, b.w, waits)
            for t in b.r:
                self._need(eng, t, waits)
        return waits

    def _mark(self, tok, reads, writes):
        for b in reads:
            b.r.append(tok)
            if len(b.r) > 24:
                b.r = b.r[-24:]
        for b in writes:
            b.w = tok
            b.r = []

    def op(self, eng, fn, reads=(), writes=(), inc=True):
        if any(b.ex for b in reads):
            writes = list(writes) + [b for b in reads if b.ex]
            reads = [b for b in reads if not b.ex]
        waits = self._deps(eng, reads, writes)
        self._emit_waits(eng, waits)
        if inc:
            self.cnt[eng] += 1
            tok = ("e", eng, self.cnt[eng])
        else:
            tok = ("e", eng, self.cnt[eng] + 1)
        self.streams[eng].append(("op", fn, inc))
        self._mark(tok, reads, writes)
        return tok

    def dma(self, q, out_ap, in_ap, reads=(), writes=(), **kw):
        waits = self._deps(q, reads, writes)
        s = self.dma_k % self.n_dma
        v = self.dlast[s] + 16
        self.dma_k += 1
        if v > 16:
            self._need(q, ("d", s, v - 16), waits)
        self._emit_waits(q, waits)
        self.dlast[s] = v
        tok = ("d", s, v)
        self.streams[q].append(("dma", out_ap, in_ap, s, kw))
        self._mark(tok, reads, writes)
        return tok

    def barrier(self):
        toks = [("e", e, self.cnt[e]) for e in ENGS if self.cnt[e] > 0]
        toks += [("d", s, self.dlast[s]) for s in range(self.n_dma) if self.dlast[s] > 0]
        for e in ENGS:
            waits = {}
            for t in toks:
                if t[0] == "e" and t[1] == e:
                    continue
                self._need(e, t, waits)
            self._emit_waits(e, waits)

    def flush(self):
        nc = self.nc
        with nc.Block() as block:
            def run(e, engobj):
                for item in self.streams[e]:
                    if item[0] == "wait":
                        (kind, key), val = item[1], item[2]
                        sem = self.sems[key] if kind == "e" else self.dsems[key]
                        engobj.wait_ge(sem, val)
                    elif item[0] == "op":
                        ins = item[1](engobj)
                        if item[2]:
                            ins.then_inc(self.sems[e], 1)
                    else:
                        _, o, i, s, kw = item
                        engobj.dma_start(out=o, in_=i, **kw).then_inc(self.dsems[s], 16)

            @block.tensor
            def _(eng):
                run("pe", eng)

            @block.scalar
            def _(eng):
                run("act", eng)

            @block.vector
            def _(eng):
                run("dve", eng)

            @block.gpsimd
            def _(eng):
                run("pool", eng)

            @block.sync
            def _(eng):
                run("sp", eng)
        self.streams = {e: [] for e in ENGS}


class Ctx:
    K = [0]

    def __init__(self, nc, stack):
        self.nc = nc
        self.st = stack

    def sb(self, shape, dt, name=None):
        Ctx.K[0] += 1
        return self.st.enter_context(self.nc.sbuf_tensor(name or ("t%d" % Ctx.K[0]), list(shape), dt))

    def ps(self, shape, dt=F32, name=None):
        Ctx.K[0] += 1
        return self.st.enter_context(self.nc.psum_tensor(name or ("p%d" % Ctx.K[0]), list(shape), dt))


SCALE = 0.125
DEBUG_A = False


def emit_l1(nc, S, T, Bo):
    xT = T["xT"]; wq = T["wq"]; wk = T["wk"]; wv = T["wv"]; wf = T["wf"]; fbias = T["fbias"]
    lamv = T["lamv"]; gsub = T["gsub"]; dbias = T["dbias"]; Fd = T["Fd"]; Ff = T["Ff"]; qaug = T["qaug"]
    ODT = T["odt"]
    QT_s = nc.dram_tensor("QT_s", [8, 64, LR], BF16).ap()
    KT_s = nc.dram_tensor("KT_s", [8, 64, LR], BF16).ap()
    V_s = nc.dram_tensor("V_s", [LR, 512], BF16).ap()
    AQ_s = nc.dram_tensor("AQ_s", [4, 3, LR], BF16).ap()
    AQd_s = nc.dram_tensor("AQd_s", [6, LR], BF16).ap()
    out_toks = []
    if True:
        BQT = [Buf() for _ in range(8)]
        BKT = [Buf() for _ in range(8)]
        BV = Buf()
        BAQ = Buf()
        with contextlib.ExitStack() as st:
            C = Ctx(nc, st)
            wq_sb = C.sb([128, 8, 512], BF16)
            wk_sb = C.sb([128, 8, 512], BF16)
            wv_sb = C.sb([128, 8, 512], BF16)
            wf_sb = C.sb([128, 8, 4], BF16)
            fb_sb = C.sb([4, 1], F32)
            nfb_sb = C.sb([4, 1], F32)
            ident = C.sb([128, 128], F32)
            xt = [C.sb([128, 8, TW], BF16) for _ in range(2)]
            stq = [C.sb([128, TW], BF16) for _ in range(4)]
            stv = [C.sb([128, 512], BF16) for _ in range(2)]
            lf = C.sb([4, LR], F32)
            cs = C.sb([4, LR], F32)
            ones4 = C.sb([4, LR // 4], F32)
            e_t = C.sb([4, TW], F32)
            hi = C.sb([4, LR], BF16)
            mid = C.sb([4, LR], BF16)
            lo = C.sb([4, LR], BF16)
            pq = [C.ps([128, 512]) for _ in range(4)]
            pv = [C.ps([128, 512]) for _ in range(2)]
            pf = C.ps([128, 512])
            Bw = Buf(); Bxt = [Buf(), Buf()]; Bstq = [Buf() for _ in range(4)]; Bstv = [Buf(), Buf()]
            Bpq = [PB() for _ in range(4)]; Bpv = [PB(), PB()]; Bpf = PB(); Blf = Buf(); Bcs = Buf()
            Bet = Buf(); Bfb = Buf(); Bo4 = Buf(); Bhi = Buf(); Bmid = Buf(); Blo = Buf()

            wr = "(c p) o -> p c o"
            S.dma("pool", wq_sb[:], wq.rearrange(wr, p=128), writes=[Bw])
            for fc in range(8):
                S.dma("pool", wk_sb[:, fc, :], wk[fc * 128:(fc + 1) * 128, :], writes=[Bw])
                S.dma("pool", wv_sb[:, fc, :], wv[fc * 128:(fc + 1) * 128, :], writes=[Bw])
            S.dma("pool", wf_sb[:], wf.rearrange(wr, p=128), writes=[Bw])
            S.dma("sp", fb_sb[:], fbias, writes=[Bfb])
            S.op("dve", lambda e: e.tensor_scalar(nfb_sb[:], fb_sb[:], -1.0, None, ALU.mult), reads=[Bfb], writes=[Bfb])
            S.op("dve", lambda e: e.memset(ones4[:], 1.0), writes=[Bo4])
            xTr = xT.rearrange("(c p) t -> p c t", p=128)
            conv_it = T["hookA"](C) if T.get("hookA") else None
            qi = 0
            vi = 0
            for t in range(NT):
                c0 = t * TW
                X = xt[t % 2]; BX = Bxt[t % 2]
                S.dma("pool", X[:], xTr[:, :, c0:c0 + TW], writes=[BX])
                for which, (w_sb, dst, BD) in enumerate(((wq_sb, QT_s, BQT), (wk_sb, KT_s, BKT))):
                    for g in range(4):
                        P = pq[qi % 4]; BP = Bpq[qi % 4]; ST = stq[qi % 4]; BS = Bstq[qi % 4]
                        qi += 1
                        for c in range(8):
                            S.op("pe", lambda e, P=P, w_sb=w_sb, c=c, g=g, X=X: e.matmul(
                                P[:, 0:TW], w_sb[:, c, g * 128:(g + 1) * 128], X[:, c, :], start=(c == 0), stop=(c == 7)),
                                reads=[Bw, BX], writes=[BP], inc=(c == 7))
                        eng = "act" if (g % 2 == 0) else "dve"
                        if eng == "act":
                            S.op("act", lambda e, ST=ST, P=P: e.copy(ST[:], P[:, 0:TW]), reads=[BP], writes=[BS])
                        else:
                            S.op("dve", lambda e, ST=ST, P=P: e.tensor_copy(ST[:], P[:, 0:TW]), reads=[BP], writes=[BS])
                        S.dma("sp", dst[2 * g:2 * g + 2].rearrange("u r t -> (u r) t")[:, c0:c0 + TW], ST[:],
                              reads=[BS], writes=[BD[2 * g], BD[2 * g + 1]])
                for bl in range(3):
                    P = pv[vi % 2]; BP = Bpv[vi % 2]; ST = stv[vi % 2]; BS = Bstv[vi % 2]
                    vi += 1
                    for c in range(8):
                        S.op("pe", lambda e, P=P, c=c, bl=bl, X=X: e.matmul(
                            P[:], X[:, c, bl * 128:(bl + 1) * 128], wv_sb[:, c, :], start=(c == 0), stop=(c == 7)),
                            reads=[Bw, BX], writes=[BP], inc=(c == 7))
                    S.op("dve", lambda e, ST=ST, P=P: e.tensor_copy(ST[:], P[:]), reads=[BP], writes=[BS])
                    r0 = c0 + bl * 128
                    S.dma("sp", V_s[r0:r0 + 128, :], ST[:], reads=[BS], writes=[BV])
                for c in range(8):
                    S.op("pe", lambda e, c=c, X=X: e.matmul(pf[0:4, 0:TW], wf_sb[:, c, :], X[:, c, :], start=(c == 0), stop=(c == 7)),
                         reads=[Bw, BX], writes=[Bpf], inc=(c == 7))
                S.op("act", lambda e: e.activation(e_t[:], pf[0:4, 0:TW], AF.Exp, bias=nfb_sb[:, 0:1], scale=-1.0),
                     reads=[Bpf, Bfb], writes=[Bet])
                S.op("act", lambda e, c0=c0: e.activation(lf[:, c0:c0 + TW], e_t[:], AF.Ln, bias=1.0, scale=1.0),
                     reads=[Bet], writes=[Blf])
                if conv_it is not None:
                    for _ in range(3):
                        next(conv_it, None)
            S.op("dve", lambda e: e.memset(lf[:, 0:FPAD], 0.0), reads=[Blf], writes=[Blf])
            S.op("dve", lambda e: e.memset(lf[:, REAL_END:LR], 0.0), reads=[Blf], writes=[Blf])
            CH = LR // 4
            for k in range(4):
                a = k * CH
                if k == 0:
                    S.op("dve", lambda e, a=a: e.tensor_tensor_scan(cs[:, a:a + CH], ones4[:], lf[:, a:a + CH], 0.0, ALU.mult, ALU.add),
                         reads=[Blf, Bo4], writes=[Bcs])
                else:
                    S.op("dve", lambda e, a=a: e.tensor_tensor_scan(cs[:, a:a + CH], ones4[:], lf[:, a:a + CH], cs[:, a - 1:a], ALU.mult, ALU.add),
                         reads=[Blf, Bo4, Bcs], writes=[Bcs])
            CS_s = nc.dram_tensor("CS_s", [4, LR], F32).ap()
            BCS = Buf()
            S.dma("sp", CS_s, cs[:], reads=[Bcs], writes=[BCS])
            for t in range(NT):
                c0 = t * TW
                S.op("dve", lambda e, c0=c0: e.tensor_scalar(lf[:, c0:c0 + TW], cs[:, c0:c0 + TW], cs[:, c0:c0 + 1], -8.0, ALU.subtract, ALU.mult),
                     reads=[Bcs, Blf], writes=[Blf])
            S.op("dve", lambda e: e.tensor_copy(hi[:], lf[:]), reads=[Blf], writes=[Bhi])
            S.op("dve", lambda e: e.tensor_tensor(lf[:], lf[:], hi[:], ALU.subtract), reads=[Blf, Bhi], writes=[Blf])
            S.op("dve", lambda e: e.tensor_copy(mid[:], lf[:]), reads=[Blf], writes=[Bmid])
            S.op("dve", lambda e: e.tensor_tensor(lf[:], lf[:], mid[:], ALU.subtract), reads=[Blf, Bmid], writes=[Blf])
            S.op("dve", lambda e: e.tensor_copy(lo[:], lf[:]), reads=[Blf], writes=[Blo])
            qa_sb = C.sb([6, LR], BF16)
            Bqa = Buf()
            S.dma("pool", qa_sb[:], qaug.rearrange("h r t -> (h r) t"), writes=[Bqa])
            S.dma("sp", AQd_s, qa_sb[:], reads=[Bqa], writes=[BAQ])
            S.dma("sp", AQ_s[:, 0, :], hi[:], reads=[Bhi], writes=[BAQ])
            S.dma("sp", AQ_s[:, 1, :], mid[:], reads=[Bmid], writes=[BAQ])
            S.dma("sp", AQ_s[:, 2, :], lo[:], reads=[Blo], writes=[BAQ])
            if conv_it is not None:
                for _ in conv_it:
                    pass
            S.barrier()
            S.flush()

        with contextlib.ExitStack() as st:
            C = Ctx(nc, st)
            kt = [C.sb([128, LR], BF16) for _ in range(2)]
            qt = [C.sb([128, LR], BF16) for _ in range(2)]
            vt = [C.sb([128, NB, 128], BF16) for _ in range(2)]
            Fd_sb = C.sb([128, 2 * 3 * TW], F32)
            Ff_sb = C.sb([128, 3 * TW], F32)
            dbias_sb = C.sb([128, 2 * 66], F32)
            csk = C.sb([128, 4, NB], F32)
            csT0 = C.sb([128, 4, NT], F32)
            bkt = [C.sb([128, NB], F32) for _ in range(2)]
            ones_bf = C.sb([128, 128], BF16)
            ones_b0 = C.sb([128, 128], BF16)
            ones_f = C.sb([128, 128], F32)
            lam_sb = C.sb([128, 256], F32)
            lprod = C.sb([128, 128], F32)
            lsum = C.sb([128, 2], F32)
            lexp = C.sb([128, 2], F32)
            neglam = C.sb([128, 1], F32)
            gcol = C.sb([128, 1], F32)
            pT = [C.sb([128, TW], BF16) for _ in range(4)]
            tmp = [C.sb([128, TW], F32) for _ in range(2)]
            rl = C.sb([128, TW], F32)
            lsb = [C.sb([128, TW], F32) for _ in range(2)]
            l0b = [C.sb([128, TW], F32) for _ in range(2)]
            Blsb = [Buf(), Buf()]; Bl0b = [Buf(), Buf()]
            a0 = C.sb([128, TW], F32)
            a1 = C.sb([128, TW], F32)
            od = C.sb([128, TW], F32)
            sq = C.sb([128, TW], F32)
            rstd = C.sb([128, TW], F32)
            ofin = [C.sb([128, TW], ODT) for _ in range(2)]
            NS = 3
            NPT = 4
            ps_s = [C.ps([128, 512]) for _ in range(NS)]
            ps_o = [C.ps([128, 512]) for _ in range(2)]
            ps_l = [C.ps([128, 512]) for _ in range(2)]
            ps_x = C.ps([128, 512])
            Bkt = [Buf(), Buf()]; Bqt = [Buf(), Buf()]; Bvt = [Buf(), Buf()]
            Bc = Buf(); Bcsk = Buf(); BcsT0 = Buf(); Bbkt = [Buf(), Buf()]
            Bps_s = [PB(), PB(), PB()]; Bps_o = [PB(), PB()]; Bps_l = [PB(), PB()]; Bps_x = PB()
            BpT = [Buf() for _ in range(4)]; Btmp = [Buf(), Buf()]
            Brl = Buf(); Ba0 = Buf(); Ba1 = Buf(); Bod = Buf(); Bsq = Buf(); Brstd = Buf(); Bofin = [Buf(), Buf()]
            Blam = Buf()

            S.dma("sp", Fd_sb[:], Fd, writes=[Bc])
            S.dma("sp", Ff_sb[:], Ff, writes=[Bc])
            S.dma("sp", dbias_sb[:], dbias, writes=[Bc])
            S.dma("sp", lam_sb[:], lamv, writes=[Blam])
            S.dma("sp", gcol[:], gsub, writes=[Blam])
            for h in range(4):
                S.dma("sp", csk[:, h, :], CS_s[h].rearrange("(b p) -> p b", p=128), reads=[BCS], writes=[Bcsk], allow_slow_non_contiguous=True)
            for h in range(4):
                src = bass.AP(CS_s.tensor, CS_s.offset + h * LR, [[0, 128], [TW, NT]])
                S.dma("sp", csT0[:, h, :], src, reads=[BCS], writes=[BcsT0], allow_slow_non_contiguous=True)
            S.op("pool", lambda e: e.memset(ones_bf[:], 1.0), writes=[Bc])
            S.op("pool", lambda e: e.memset(ones_b0[:], 1.0), writes=[Bc])
            S.op("pool", lambda e: e.memset(ones_b0[0:FPAD, :], 0.0), reads=[Bc], writes=[Bc])
            S.op("pool", lambda e: e.memset(ones_f[:], 1.0), writes=[Bc])
            for b in range(2):
                S.op("pool", lambda e, b=b: e.memset(kt[b][64:67, :], 1.0), writes=[Bkt[b]])
            S.op("dve", lambda e: e.tensor_tensor(lprod[:, 0:64], lam_sb[:, 0:64], lam_sb[:, 64:128], ALU.mult), reads=[Blam], writes=[Blam])
            S.op("dve", lambda e: e.tensor_tensor(lprod[:, 64:128], lam_sb[:, 128:192], lam_sb[:, 192:256], ALU.mult), reads=[Blam], writes=[Blam])
            S.op("dve", lambda e: e.reduce_sum(lsum[:, 0:1], lprod[:, 0:64], mybir.AxisListType.X), reads=[Blam], writes=[Blam])
            S.op("dve", lambda e: e.reduce_sum(lsum[:, 1:2], lprod[:, 64:128], mybir.AxisListType.X), reads=[Blam], writes=[Blam])
            S.op("act", lambda e: e.activation(lexp[:], lsum[:], AF.Exp), reads=[Blam], writes=[Blam])
            S.op("dve", lambda e: e.scalar_tensor_tensor(neglam[:], lexp[:, 1:2], -LAM_INIT0, lexp[:, 0:1], ALU.add, ALU.subtract),
                 reads=[Blam], writes=[Blam])
            S.op("dve", lambda e: e.tensor_scalar(gcol[:], gcol[:], 1.0 - LAM_INIT0, None, ALU.mult), reads=[Blam], writes=[Blam])

            conv_itB = T["hookB"](C) if T.get("hookB") else None
            si = 0
            pi = 0
            ti = 0
            oi = 0
            fi = 0
            for g in range(4):
                isdiff = g < 2
                units = (2 * g, 2 * g + 1)
                dv = 128 if isdiff else 64
                for ui, u in enumerate(units):
                    S.dma("sp", kt[ui][0:64, :], KT_s[u], reads=[BKT[u]], writes=[Bkt[ui]])
                    S.dma("sp", qt[ui][0:64, :], QT_s[u], reads=[BQT[u]], writes=[Bqt[ui]])
                    if isdiff:
                        S.dma("sp", qt[ui][64:67, :], AQd_s[3 * g:3 * g + 3, :], reads=[BAQ], writes=[Bqt[ui]])
                    else:
                        S.dma("sp", qt[ui][64:67, :], AQ_s[u - 4], reads=[BAQ], writes=[Bqt[ui]])
                if isdiff:
                    S.dma("sp", vt[0][:], V_s[:, g * 128:(g + 1) * 128].rearrange("(b p) c -> p b c", p=128),
                          reads=[BV], writes=[Bvt[0]])
                else:
                    for ui, u in enumerate(units):
                        hf = u - 4
                        S.dma("sp", vt[ui][:, :, 0:64], V_s[:, 256 + hf * 64:256 + (hf + 1) * 64].rearrange("(b p) c -> p b c", p=128),
                              reads=[BV], writes=[Bvt[ui]])
                        S.op("pool", lambda e, ui=ui: e.memset(vt[ui][:, :, 64:128], 1.0), writes=[Bvt[ui]])
                        S.op("pool", lambda e, ui=ui: e.memset(vt[ui][0:FPAD, 0, 64:128], 0.0), reads=[Bvt[ui]], writes=[Bvt[ui]])
                items = []
                for t in range(NT):
                    for ui, u in enumerate(units):
                        for j in range(3 * t + 3):
                            items.append((t, ui, u, j))
                state = {}
                deferred = []

                def stage_a(t, ui, u, j):
                    nonlocal si, pi, ti, oi, fi
                    c0 = t * TW
                    nblk = 3 * t + 3
                    K = kt[ui]; Q = qt[ui]; BK = Bkt[ui]; BQ = Bqt[ui]
                    hf = u - 4
                    if j == 0:
                        PO = ps_o[oi % 2]; BPO = Bps_o[oi % 2]; PL = ps_l[oi % 2]; BPL = Bps_l[oi % 2]
                        oi += 1
                        BKk = None; BBK = None
                        if not isdiff:
                            BKk = bkt[fi % 2]; BBK = Bbkt[fi % 2]
                            fi += 1
                            S.op("dve", lambda e, BKk=BKk, nblk=nblk, hf=hf, t=t: e.tensor_scalar(
                                BKk[:, 0:nblk], csk[:, hf, 0:nblk], csT0[:, hf, t:t + 1], None, ALU.subtract),
                                reads=[Bcsk, BcsT0], writes=[BBK])
                        state[(t, ui)] = (PO, BPO, PL, BPL, BKk, BBK)
                    PO, BPO, PL, BPL, BKk, BBK = state[(t, ui)]
                    jj = j - 3 * t
                    PS = ps_s[si % NS]; BPS = Bps_s[si % NS]
                    si += 1
                    S.op("pe", lambda e, PS=PS, K=K, Q=Q, j=j, c0=c0: e.matmul(
                        PS[:, 0:TW], K[0:67, j * 128:(j + 1) * 128], Q[0:67, c0:c0 + TW], start=True, stop=True),
                        reads=[BK, BQ], writes=[BPS])
                    PT = pT[pi % NPT]; BPT = BpT[pi % NPT]
                    pi += 1
                    if isdiff:
                        col = g * 66 + (jj + 63)
                        bcol = dbias_sb[:, col:col + 1]
                        rb = [Bc]
                    else:
                        bcol = BKk[:, j:j + 1]
                        rb = [BBK]
                    if jj < 0:
                        S.op("act", lambda e, PT=PT, PS=PS, bcol=bcol: e.activation(PT[:], PS[:, 0:TW], AF.Exp, bias=bcol, scale=SCALE),
                             reads=[BPS] + rb, writes=[BPT])
                    else:
                        TM = tmp[ti % 2]; BTM = Btmp[ti % 2]
                        ti += 1
                        if isdiff:
                            Fap = Fd_sb[:, (g * 3 + jj) * TW:(g * 3 + jj + 1) * TW]
                        else:
                            Fap = Ff_sb[:, jj * TW:(jj + 1) * TW]
                        S.op("dve", lambda e, TM=TM, PS=PS, Fap=Fap: e.scalar_tensor_tensor(
                            TM[:], PS[:, 0:TW], SCALE, Fap, ALU.mult, ALU.add), reads=[BPS, Bc], writes=[BTM])
                        S.op("act", lambda e, PT=PT, TM=TM, bcol=bcol: e.activation(PT[:], TM[:], AF.Exp, bias=bcol, scale=1.0),
                             reads=[BTM] + rb, writes=[BPT])
                    return PT, BPT

                def stage_b(idx, t, ui, u, j, PT, BPT):
                    c0 = t * TW
                    nblk = 3 * t + 3
                    hf = u - 4
                    if isdiff:
                        VT = vt[0]; BVT = Bvt[0]
                    else:
                        VT = vt[ui]; BVT = Bvt[ui]
                    PO, BPO, PL, BPL, BKk, BBK = state[(t, ui)]
                    if not isdiff:
                        S.op("pe", lambda e, PO=PO, VT=VT, PT=PT, j=j, nblk=nblk: e.matmul(
                            PO[:, 0:TW], VT[:, j, :], PT[:], start=(j == 0), stop=(j == nblk - 1)),
                            reads=[BVT, BPT], writes=[BPO], inc=True)
                        if j != nblk - 1:
                            return
                        LS = lsb[(2 * t + ui) % 2]; BLS = Blsb[(2 * t + ui) % 2]
                        L0 = l0b[(2 * t + ui) % 2]; BL0 = Bl0b[(2 * t + ui) % 2]
                        S.op("dve", lambda e, LS=LS, PO=PO: e.tensor_scalar(LS[64:128, :], PO[64:128, 0:TW], 1e-30, None, ALU.add),
                             reads=[BPO], writes=[BLS])
                        S.dma("sp", L0[0:64, :], LS[64:128, :], reads=[BLS], writes=[BL0])
                        S.op("dve", lambda e, L0=L0: e.reciprocal(rl[0:64, :], L0[0:64, :]), reads=[BL0], writes=[Brl])
                        OF = ofin[(2 * t + ui) % 2]; BOF = Bofin[(2 * t + ui) % 2]
                        S.op("dve", lambda e, OF=OF, PO=PO: e.tensor_tensor(OF[0:64, :], PO[0:64, 0:TW], rl[0:64, :], ALU.mult),
                             reads=[BPO, Brl], writes=[BOF])
                        r0 = 256 + hf * 64
                        out_toks.append(S.dma("sp", T["oT_tile"](r0, r0 + 64, t), OF[0:64, :], reads=[BOF], writes=[Bo]))
                        return
                    S.op("pe", lambda e, PO=PO, VT=VT, PT=PT, j=j, nblk=nblk, dv=dv: e.matmul(
                        PO[0:dv, 0:TW], VT[:, j, 0:dv], PT[:], start=(j == 0), stop=(j == nblk - 1)),
                        reads=[BVT, BPT], writes=[BPO], inc=False)
                    on = ones_b0 if j == 0 else ones_bf
                    S.op("pe", lambda e, PL=PL, on=on, PT=PT, j=j, nblk=nblk, dv=dv: e.matmul(
                        PL[0:dv, 0:TW], on[:, 0:dv], PT[:], start=(j == 0), stop=(j == nblk - 1)),
                        reads=[Bc, BPT], writes=[BPL], inc=True)
                    if j != nblk - 1:
                        return
                    S.op("dve", lambda e, PL=PL, dv=dv: e.tensor_scalar(rl[0:dv, :], PL[0:dv, 0:TW], 1e-30, None, ALU.add), reads=[BPL], writes=[Brl])
                    S.op("dve", lambda e, dv=dv: e.reciprocal(rl[0:dv, :], rl[0:dv, :]), reads=[Brl], writes=[Brl])
                    if not isdiff:
                        OF = ofin[(2 * t + ui) % 2]; BOF = Bofin[(2 * t + ui) % 2]
                        S.op("dve", lambda e, OF=OF, PO=PO: e.tensor_tensor(OF[0:64, :], PO[0:64, 0:TW], rl[0:64, :], ALU.mult),
                             reads=[BPO, Brl], writes=[BOF])
                        r0 = 256 + hf * 64
                        out_toks.append(S.dma("sp", T["oT_tile"](r0, r0 + 64, t), OF[0:64, :], reads=[BOF], writes=[Bo]))
                        return
                    AA = a0 if ui == 0 else a1
                    BA = Ba0 if ui == 0 else Ba1
                    S.op("dve", lambda e, AA=AA, PO=PO: e.tensor_tensor(AA[:], PO[:, 0:TW], rl[:], ALU.mult),
                         reads=[BPO, Brl], writes=[BA])
                    if ui == 0:
                        return
                    OF = ofin[t % 2]; BOF = Bofin[t % 2]
                    S.op("dve", lambda e: e.scalar_tensor_tensor(od[:], a1[:], neglam[:, 0:1], a0[:], ALU.mult, ALU.add),
                         reads=[Ba0, Ba1, Blam], writes=[Bod])
                    S.op("pool", lambda e: e.tensor_tensor(sq[:], od[:], od[:], ALU.mult), reads=[Bod], writes=[Bsq])

                    def tail(OF=OF, BOF=BOF, t=t):
                        S.op("pe", lambda e: e.matmul(ps_x[:, 0:TW], ones_f[:], sq[:], start=True, stop=True),
                             reads=[Bc, Bsq], writes=[Bps_x])
                        S.op("act", lambda e: e.activation(rstd[:], ps_x[:, 0:TW], AF.Ln, bias=RMS_EPS, scale=1.0 / 128.0),
                             reads=[Bps_x], writes=[Brstd])
                        S.op("act", lambda e: e.activation(rstd[:], rstd[:], AF.Exp, scale=-0.5), reads=[Brstd], writes=[Brstd])
                        S.op("dve", lambda e, OF=OF: e.scalar_tensor_tensor(OF[:], od[:], gcol[:, 0:1], rstd[:], ALU.mult, ALU.mult),
                             reads=[Bod, Brstd, Blam], writes=[BOF])
                        out_toks.append(S.dma("sp", T["oT_tile"](g * 128, (g + 1) * 128, t), OF[:], reads=[BOF], writes=[Bo]))
                    deferred.append((idx + 3, tail))

                LA = 2
                pend = {}
                n_it = len(items)
                for i in range(n_it + LA):
                    if conv_itB is not None and i % 40 == 39:
                        next(conv_itB, None)
                    if i < n_it:
                        pend[i] = stage_a(*items[i])
                    k = i - LA
                    if k >= 0:
                        PT, BPT = pend.pop(k)
                        stage_b(k, *items[k], PT, BPT)
                    while deferred and deferred[0][0] <= k:
                        deferred.pop(0)[1]()
                while deferred:
                    deferred.pop(0)[1]()
            if conv_itB is not None:
                for _ in conv_itB:
                    pass
            S.barrier()
            S.flush()
    return out_toks


def l1_consts(s):
    p = np.arange(128, dtype=np.float64)[:, None]
    slopes = [2.0 ** (-8.0 * (h + 1) / 4) for h in (2 * s, 2 * s + 1)]
    dbias = np.zeros((128, 2, 66), np.float32)
    Fd = np.zeros((128, 2, 3, TW), np.float32)
    qaug = np.zeros((2, 3, LR), np.float32)
    i = np.arange(TW, dtype=np.float64)[None, :]
    r = np.arange(LR)
    w = r % TW
    for hl, sl in enumerate(slopes):
        for idx in range(66):
            dbias[:, hl, idx] = (sl * (128 * (idx - 63) + p))[:, 0]
        for jj in range(3):
            rk = 128 * jj + p
            vis = (rk // 64) <= (i // 64)
            f = np.where(i >= rk, 0.0, 2 * sl * (i - rk))
            Fd[:, hl, jj, :] = np.where(vis, f, NEG)
        qaug[hl, 0] = -(8 * sl) * 256 * (w // 256)
        qaug[hl, 1] = -(8 * sl) * (w % 256)
    Ff = np.zeros((128, 3, TW), np.float32)
    for jj in range(3):
        rk = 128 * jj + p
        Ff[:, jj, :] = np.where(rk <= i, 0.0, NEG)
    return (dbias.reshape(128, 132), Fd.reshape(128, 6 * TW), Ff.reshape(128, 3 * TW), qaug)


def l1_inputs(xT_b, even_w_in, even_f_bias, diff_lambda, diff_subln_g, s):
    w = even_w_in
    dq = [w[:, h * 128:(h + 1) * 128] for h in (2 * s, 2 * s + 1)]
    dk = [w[:, 512 + h * 128:512 + (h + 1) * 128] for h in (2 * s, 2 * s + 1)]
    dvv = [w[:, 1024 + h * 128:1024 + (h + 1) * 128] for h in (2 * s, 2 * s + 1)]
    fq = w[:, 1536 + 256 * s:1536 + 256 * (s + 1)]
    fk = w[:, 2048 + 256 * s:2048 + 256 * (s + 1)]
    fv = w[:, 2560 + 256 * s:2560 + 256 * (s + 1)]
    wf = w[:, 3072 + 4 * s:3072 + 4 * (s + 1)]
    dbias, Fd, Ff, qaug = l1_consts(s)
    return {
        "xT": xT_b,
        "wq": np.ascontiguousarray(np.concatenate(dq + [fq], axis=1)),
        "wk": np.ascontiguousarray(np.concatenate(dk + [fk], axis=1)),
        "wv": np.ascontiguousarray(np.concatenate(dvv + [fv], axis=1)),
        "wf": np.ascontiguousarray(wf),
        "fbias": np.ascontiguousarray(even_f_bias[4 * s:4 * (s + 1)].reshape(4, 1)),
        "lamv": np.ascontiguousarray(np.broadcast_to(diff_lambda.reshape(1, 256), (128, 256))),
        "gsub": np.ascontiguousarray(diff_subln_g.reshape(128, 1)),
        "dbias": dbias, "Fd": Fd, "Ff": Ff, "qaug": qaug,
    }


def emit_convert(S, C, src, dst, BD, rows, cols, tag, shared=None):
    if shared is None:
        stg = [C.sb([128, 2048], BF16) for _ in range(2)]
        Bs = [Buf(), Buf()]
    else:
        stg, Bs = shared
    for _ in iter_convert(S, stg, Bs, src, dst, BD, rows, cols):
        pass


def iter_convert(S, stg, Bs, src, dst, BD, rows, cols, ctr=[0]):
    for r0 in range(0, rows, 128):
        for c0 in range(0, cols, 2048):
            w = min(2048, cols - c0)
            T = stg[ctr[0] % 2]; B = Bs[ctr[0] % 2]
            ctr[0] += 1
            S.dma("pool", T[:, 0:w], src[r0:r0 + 128, c0:c0 + w], writes=[B])
            S.dma("sp", dst[r0:r0 + 128, c0:c0 + w], T[:, 0:w], reads=[B], writes=[BD])
            yield


def emit_ln(S, y, By, yb, Byb, sqb, Bsqb, tmpn, Btmpn, out_f, Bof, out_b, Bob, gcols, bcols, Bp, ones_b, Bc, ps1, Bps1, ps2, Bps2,
            mean, msq, rstd, Bst, nfeat_chunks=8):
    n = nfeat_chunks
    for c in range(n):
        S.op("act", lambda e, c=c: e.copy(yb[:, c, :], y[:, c, :]), reads=[By[c]], writes=[Byb[c]])
        S.op("act", lambda e, c=c: e.activation(sqb[:, c, :], y[:, c, :], AF.Square), reads=[By[c]], writes=[Bsqb[c]])
    for c in range(n):
        S.op("pe", lambda e, c=c: e.matmul(ps1[:, 0:TW], ones_b[:], yb[:, c, :], start=(c == 0), stop=(c == n - 1)),
             reads=[Bc, Byb[c]], writes=[Bps1], inc=(c == n - 1))
    for c in range(n):
        S.op("pe", lambda e, c=c: e.matmul(ps2[:, 0:TW], ones_b[:], sqb[:, c, :], start=(c == 0), stop=(c == n - 1)),
             reads=[Bc, Bsqb[c]], writes=[Bps2], inc=(c == n - 1))
    inv = 1.0 / (128.0 * n)
    S.op("dve", lambda e: e.tensor_scalar(mean[:], ps1[:, 0:TW], inv, None, ALU.mult), reads=[Bps1], writes=[Bst])
    S.op("dve", lambda e: e.tensor_tensor(msq[:], mean[:], mean[:], ALU.mult), reads=[Bst], writes=[Bst])
    S.op("dve", lambda e: e.scalar_tensor_tensor(msq[:], ps2[:, 0:TW], inv, msq[:], ALU.mult, ALU.subtract),
         reads=[Bps2, Bst], writes=[Bst])
    S.op("act", lambda e: e.activation(rstd[:], msq[:], AF.Ln, bias=LN_EPS, scale=1.0), reads=[Bst], writes=[Bst])
    S.op("act", lambda e: e.activation(rstd[:], rstd[:], AF.Exp, scale=-0.5), reads=[Bst], writes=[Bst])
    for c in range(n):
        eng = "dve" if c % 2 == 0 else "pool"
        S.op(eng, lambda e, c=c: e.tensor_tensor(tmpn[:, c, :], y[:, c, :], mean[:], ALU.subtract), reads=[By[c], Bst], writes=[Btmpn[c]])
        S.op(eng, lambda e, c=c: e.tensor_tensor(tmpn[:, c, :], tmpn[:, c, :], rstd[:], ALU.mult), reads=[Bst, Btmpn[c]], writes=[Btmpn[c]])
        S.op("dve", lambda e, c=c: e.tensor_scalar(out_f[:, c, :], tmpn[:, c, :], gcols[:, c:c + 1], bcols[:, c:c + 1], ALU.mult, ALU.add),
             reads=[Btmpn[c], Bp], writes=[Bof[c]])
        if out_b is not None:
            S.op("act", lambda e, c=c: e.copy(out_b[:, c, :], out_f[:, c, :]), reads=[Bof[c]], writes=[Bob[c]])


def emit_post(nc, S, st, ntiles, KC, src_o, src_mode, Bsrc, hres, wout_b, w1_b, w2_b, Bwb, lnp, hout, Bhout, flags=None):
    C = Ctx(nc, st)
    ot = [C.sb([128, KC, TW], BF16) for _ in range(2)]
    hr = [C.sb([128, 8, TW], F32) for _ in range(2)]
    y = C.sb([128, 8, TW], F32)
    ysq = C.sb([128, 8, TW], F32)
    h1 = C.sb([128, 8, TW], F32)
    h1b = C.sb([128, 8, TW], BF16)
    h2 = C.sb([128, 8, TW], F32)
    yb = C.sb([128, 8, TW], BF16)
    sqb = C.sb([128, 8, TW], BF16)
    hid = C.sb([128, 32, TW], BF16)
    rl = [C.sb([128, TW], F32) for _ in range(2)]
    WCOL = 4096 // KC
    wo = [C.sb([128, KC, WCOL], BF16) for _ in range(2)]
    w1c = [C.sb([128, 8, 512], BF16) for _ in range(2)]
    w2c = [C.sb([128, 8, 512], BF16) for _ in range(2)]
    lnp_sb = C.sb([128, 32], F32)
    ones_f = C.sb([128, 128], BF16)
    if src_mode == "blend":
        cand = [C.sb([128, KC, TW], BF16) for _ in range(2)]
        fl_sb = C.sb([128, 2], F32)
        Bcand = [Buf(), Buf()]
        Bfl = Buf()
        S.dma("sp", fl_sb[:], flags, writes=[Bfl])
    mean = C.sb([128, TW], F32)
    msq = C.sb([128, TW], F32)
    rstd = C.sb([128, TW], F32)
    pa = [C.ps([128, 512]) for _ in range(2)]
    pb = [C.ps([128, 512]) for _ in range(4)]
    ps1 = C.ps([128, 512])
    ps2 = C.ps([128, 512])
    Bot = [Buf(), Buf()]; Bhr = [Buf(), Buf()]
    L8 = lambda: [Buf() for _ in range(8)]
    By = L8(); Bysq = L8(); Bh1 = L8(); Bh1b = L8(); Bh2 = L8(); Byb = L8(); Bsqb = L8()
    Bhid = [Buf() for _ in range(32)]; Brl = [Buf(), Buf()]; Bwo = [Buf(), Buf()]; Bw1 = [Buf(), Buf()]; Bw2 = [Buf(), Buf()]
    Bp = Buf(); Bc = Buf(); Bst = Buf(); Bpa = [PB(), PB()]; Bpb = [PB() for _ in range(4)]; Bps1 = PB(); Bps2 = PB()
    S.dma("sp", lnp_sb[:], lnp, writes=[Bp])
    S.op("pool", lambda e: e.memset(ones_f[:], 1.0), writes=[Bc])
    wr = "(c p) o -> p c o"
    wo_v = wout_b.rearrange(wr, p=128)
    w1_v = w1_b.rearrange(wr, p=128)
    w2_v = w2_b.rearrange(wr, p=128)
    so_v = None if src_mode == "blend" else src_o.rearrange("(c p) t -> p c t", p=128)
    hr_v = hres.rearrange("(c p) t -> p c t", p=128)
    ho_v = hout.rearrange("(c p) t -> p c t", p=128)
    ai = 0
    wi = 0
    w1i = 0
    w2i = 0
    ri = 0
    toks = []
    for t in range(ntiles):
        c0 = t * TW
        OT = ot[t % 2]; BOT = Bot[t % 2]; HR = hr[t % 2]; BHR = Bhr[t % 2]
        if src_mode == "blend":
            for k in range(2):
                S.dma("sp", cand[k][:], src_o(k, t), reads=[Bsrc], writes=[Bcand[k]])
            for kc in range(KC):
                S.op("pool", lambda e, kc=kc, OT=OT: e.tensor_scalar(OT[:, kc, :], cand[0][:, kc, :], fl_sb[:, 0:1], None, ALU.mult),
                     reads=[Bcand[0], Bfl], writes=[BOT])
                S.op("dve", lambda e, kc=kc, OT=OT: e.scalar_tensor_tensor(OT[:, kc, :], cand[1][:, kc, :], fl_sb[:, 1:2], OT[:, kc, :], ALU.mult, ALU.add),
                     reads=[Bcand[1], Bfl, BOT], writes=[BOT])
        else:
            S.dma("pool" if src_mode == "f32" else "sp", OT[:], so_v[:, :, c0:c0 + TW], reads=[Bsrc], writes=[BOT])
        S.dma("sp", HR[:], hr_v[:, :, c0:c0 + TW], writes=[BHR])
        for n2 in range(1024 // WCOL):
            WO = wo[wi % 2]; BWO = Bwo[wi % 2]
            wi += 1
            S.dma("sp", WO[:], wo_v[:, :, n2 * WCOL:(n2 + 1) * WCOL], reads=[Bwb], writes=[BWO])
            for c4 in range(WCOL // 128):
                cc = n2 * (WCOL // 128) + c4
                P = pa[ai % 2]; BP = Bpa[ai % 2]
                ai += 1
                for kc in range(KC):
                    S.op("pe", lambda e, P=P, WO=WO, kc=kc, c4=c4, OT=OT: e.matmul(
                        P[:, 0:TW], WO[:, kc, c4 * 128:(c4 + 1) * 128], OT[:, kc, :], start=(kc == 0), stop=(kc == KC - 1)),
                        reads=[BWO, BOT], writes=[BP], inc=(kc == KC - 1))
                S.op("dve", lambda e, P=P, cc=cc, HR=HR: e.scalar_tensor_tensor(
                    y[:, cc, :], HR[:, cc, :], ALPHA, P[:, 0:TW], ALU.mult, ALU.add), reads=[BP, BHR], writes=[By[cc]])
        emit_ln(S, y, By, yb, Byb, sqb, Bsqb, ysq, Bysq, h1, Bh1, h1b, Bh1b, lnp_sb[:, 0:8], lnp_sb[:, 8:16], Bp, ones_f, Bc,
                ps1, Bps1, ps2, Bps2, mean, msq, rstd, Bst)
        for hg in range(8):
            W1 = w1c[w1i % 2]; BW1 = Bw1[w1i % 2]
            w1i += 1
            S.dma("sp", W1[:], w1_v[:, :, hg * 512:(hg + 1) * 512], reads=[Bwb], writes=[BW1])
            for h4 in range(4):
                hc = hg * 4 + h4
                P = pa[ai % 2]; BP = Bpa[ai % 2]
                ai += 1
                for fc in range(8):
                    S.op("pe", lambda e, P=P, W1=W1, fc=fc, h4=h4: e.matmul(
                        P[:, 0:TW], W1[:, fc, h4 * 128:(h4 + 1) * 128], h1b[:, fc, :], start=(fc == 0), stop=(fc == 7)),
                        reads=[BW1, Bh1b[fc]], writes=[BP], inc=(fc == 7))
                R = rl[ri % 2]; BR = Brl[ri % 2]
                ri += 1
                S.op("act", lambda e, R=R, P=P: e.activation(R[:], P[:, 0:TW], AF.Relu), reads=[BP], writes=[BR])
                S.op("pool", lambda e, R=R, hc=hc: e.tensor_tensor(hid[:, hc, :], R[:], R[:], ALU.mult), reads=[BR], writes=[Bhid[hc]])
        for n2 in range(2):
            for kg in range(4):
                W2 = w2c[w2i % 2]; BW2 = Bw2[w2i % 2]
                w2i += 1
                S.dma("sp", W2[:], w2_v[:, kg * 8:(kg + 1) * 8, n2 * 512:(n2 + 1) * 512], reads=[Bwb], writes=[BW2])
                for c4 in range(4):
                    for k8 in range(8):
                        hc = kg * 8 + k8
                        S.op("pe", lambda e, c4=c4, W2=W2, k8=k8, hc=hc: e.matmul(
                            pb[c4][:, 0:TW], W2[:, k8, c4 * 128:(c4 + 1) * 128], hid[:, hc, :], start=(hc == 0), stop=(hc == 31)),
                            reads=[BW2, Bhid[hc]], writes=[Bpb[c4]], inc=(k8 == 7))
            for c4 in range(4):
                cc = n2 * 4 + c4
                S.op("dve", lambda e, c4=c4, cc=cc: e.scalar_tensor_tensor(
                    y[:, cc, :], h1[:, cc, :], ALPHA, pb[c4][:, 0:TW], ALU.mult, ALU.add), reads=[Bpb[c4], Bh1[cc]], writes=[By[cc]])
        emit_ln(S, y, By, yb, Byb, sqb, Bsqb, ysq, Bysq, h2, Bh2, None, None, lnp_sb[:, 16:24], lnp_sb[:, 24:32], Bp, ones_f, Bc,
                ps1, Bps1, ps2, Bps2, mean, msq, rstd, Bst)
        toks.append(S.dma("sp", ho_v[:, :, c0:c0 + TW], h2[:], reads=Bh2, writes=[Bhout]))
    return toks


L2_STAGE = 4
L3_STAGE = 3
L3_SUB = 3
L3_NCH = 12
RET_GAMMA = [1.0 - 2.0 ** (-5.0 - h) for h in range(4)]


def emit_state_tile(S, h2b, Bh2b, wk_sb, wv_sb, Bw, kts, Bkts, vts, Bvts, kdec_sb, Bc, Sst, BS, pk, Bpk, ctr):
    for bl in range(3):
        for half in range(2):
            P = pk[ctr[0] % 2]; BP = Bpk[ctr[0] % 2]
            ctr[0] += 1
            for fc in range(8):
                S.op("pe", lambda e, P=P, fc=fc, bl=bl, half=half: e.matmul(
                    P[:], h2b[:, fc, bl * 128:(bl + 1) * 128], wk_sb[:, fc, half * 512:(half + 1) * 512], start=(fc == 0), stop=(fc == 7)),
                    reads=[Bh2b, Bw], writes=[BP], inc=(fc == 7))
            for hh in range(2):
                h = half * 2 + hh
                S.op("dve", lambda e, P=P, bl=bl, h=h, hh=hh: e.tensor_scalar(
                    kts[:, bl, h * 256:(h + 1) * 256], P[:, hh * 256:(hh + 1) * 256], kdec_sb[:, bl * 4 + h:bl * 4 + h + 1], None, ALU.mult),
                    reads=[BP, Bc], writes=[Bkts])
        for h in range(4):
            P = pk[ctr[0] % 2]; BP = Bpk[ctr[0] % 2]
            ctr[0] += 1
            for fc in range(8):
                S.op("pe", lambda e, P=P, fc=fc, bl=bl, h=h: e.matmul(
                    P[:], h2b[:, fc, bl * 128:(bl + 1) * 128], wv_sb[:, fc, h * 512:(h + 1) * 512], start=(fc == 0), stop=(fc == 7)),
                    reads=[Bh2b, Bw], writes=[BP], inc=(fc == 7))
            S.op("act", lambda e, P=P, bl=bl, h=h: e.copy(vts[:, bl, h * 512:(h + 1) * 512], P[:]), reads=[BP], writes=[Bvts])


def emit_state_update(S, kts, Bkts, vts, Bvts, Sst, BS, Sb, BSb, pk, Bpk, ctr):
    for h in range(4):
        c384 = RET_GAMMA[h] ** TW
        for dkc in range(2):
            P = pk[ctr[0] % 2]; BP = Bpk[ctr[0] % 2]
            ctr[0] += 1
            for bl in range(3):
                S.op("pe", lambda e, P=P, bl=bl, h=h, dkc=dkc: e.matmul(
                    P[:], kts[:, bl, h * 256 + dkc * 128:h * 256 + (dkc + 1) * 128], vts[:, bl, h * 512:(h + 1) * 512],
                    start=(bl == 0), stop=(bl == 2)), reads=[Bkts, Bvts], writes=[BP], inc=(bl == 2))
            idx = h * 2 + dkc
            S.op("dve", lambda e, P=P, idx=idx, c384=c384: e.scalar_tensor_tensor(
                Sst[:, idx, :], Sst[:, idx, :], c384, P[:], ALU.mult, ALU.add), reads=[BP, BS], writes=[BS])
            if Sb is not None:
                S.op("pool", lambda e, idx=idx: e.tensor_copy(Sb[:, idx, :], Sst[:, idx, :]), reads=[BS], writes=[BSb])


def emit_prepass(nc, S, T, Bhout, BSe):
    h2T = T["h2T"]; wk = T["wk"]; wv = T["wv"]; kdec = T["kdec"]; tmask = T["tmask"]; S_end = T["S_end"]
    if True:
        with contextlib.ExitStack() as st:
            C = Ctx(nc, st)
            wk_sb = C.sb([128, 8, 1024], BF16)
            wv_sb = C.sb([128, 8, 2048], BF16)
            h2b = [C.sb([128, 8, TW], BF16) for _ in range(2)]
            kts = C.sb([128, 3, 1024], BF16)
            vts = C.sb([128, 3, 2048], BF16)
            kdec_sb = C.sb([128, 12], F32)
            tm_sb = C.sb([128, TW], F32)
            Sst = C.sb([128, 8, 512], F32)
            pk = [C.ps([128, 512]) for _ in range(2)]
            Bw = Buf(); Bh2b = [Buf(), Buf()]; Bkts = Buf(); Bvts = Buf(); Bc = Buf(); BS = Buf(); Bpk = [PB(), PB()]
            wr = "(c p) o -> p c o"
            wqueue = T.get("wqueue", "pool")
            for fc in range(8):
                S.dma(wqueue, wk_sb[:, fc, :], wk[fc * 128:(fc + 1) * 128, :], reads=[T["Bwb"]], writes=[Bw])
                S.dma(wqueue, wv_sb[:, fc, :], wv[fc * 128:(fc + 1) * 128, :], reads=[T["Bwb"]], writes=[Bw])
            S.dma("sp", kdec_sb[:], kdec, writes=[Bc])
            S.dma("sp", tm_sb[:], tmask, writes=[Bc])
            for i8 in range(8):
                S.op("dve", lambda e, i8=i8: e.memset(Sst[:, i8, :], 0.0), writes=[BS])
            hv = h2T.rearrange("(c p) t -> p c t", p=128)
            ctr = [0]
            for t in range(NTH):
                c0 = t * TW
                H = h2b[t % 2]; BH = Bh2b[t % 2]
                S.dma("pool", H[:], hv[:, :, c0:c0 + TW], reads=[Bhout], writes=[BH])
                if t == 0:
                    for fc in range(8):
                        S.op("dve", lambda e, H=H, fc=fc: e.tensor_tensor(H[:, fc, :], H[:, fc, :], tm_sb[:], ALU.mult),
                             reads=[BH, Bc], writes=[BH])
                emit_state_tile(S, H, BH, wk_sb, wv_sb, Bw, kts, Bkts, vts, Bvts, kdec_sb, Bc, Sst, BS, pk, Bpk, ctr)
                emit_state_update(S, kts, Bkts, vts, Bvts, Sst, BS, None, None, pk, Bpk, ctr)
            for k in range(4):
                S.dma("sp", S_end[k], Sst[32 * k:32 * (k + 1)], reads=[BS], writes=[BSe])
            S.barrier()
            S.flush()

def ret_consts():
    p = np.arange(128, dtype=np.float64)
    kdec = np.zeros((128, 3, 4), np.float32)
    for h in range(4):
        g = RET_GAMMA[h]
        for bl in range(3):
            kdec[:, bl, h] = g ** (TW - 1 - (bl * 128 + p)) / 16.0
    return kdec.reshape(128, 12)


def lnp_table(ln_g, ln_b, layer):
    cols = []
    for k in range(2):
        cols.append(ln_g[layer, k].reshape(8, 128).T)
        cols.append(ln_b[layer, k].reshape(8, 128).T)
    return np.ascontiguousarray(np.concatenate(cols, axis=1).astype(np.float32))


def emit_retention(nc, S, st, T, Bwb, Bh2, BSg, ByT):
    h2T = T["h2T"]; wi_b = T["wi_b"]; kdec = T["kdec"]; tmask = T["tmask"]; dmask = T["dmask"]; qdec = T["qdec"]
    yT_s = T["yT_s"]; S_src = T["S_src"]; flags = T["flags"]
    C = Ctx(nc, st)
    if True:
        if True:
            h2b = [C.sb([128, 8, TW], BF16) for _ in range(2)]
            wc = [C.sb([128, 8, 512], BF16) for _ in range(2)]
            qT = C.sb([128, 8, TW], BF16)
            qdT = C.sb([128, 8, TW], BF16)
            kT = C.sb([128, 8, TW], BF16)
            kts = C.sb([128, 3, 1024], BF16)
            vts = C.sb([128, 3, 2048], BF16)
            sg = C.sb([128, 16, TW], BF16)
            sTb = [C.sb([128, 3, TW], BF16) for _ in range(2)]
            o32 = C.sb([128, 4, TW], F32)
            osq = C.sb([128, 4, TW], F32)
            yT = [C.sb([128, 16, TW], BF16) for _ in range(2)]
            Sst = C.sb([128, 8, 512], F32)
            Sb = C.sb([128, 8, 512], BF16)
            dm_sb = C.sb([128, 12 * TW], F32)
            qd_sb = C.sb([128, 4 * TW], F32)
            kdec_sb = C.sb([128, 12], F32)
            tm_sb = C.sb([128, TW], F32)
            ones_f = C.sb([128, 128], F32)
            rstd = C.sb([128, TW], F32)
            tmpo = C.sb([128, TW], F32)
            pk = [C.ps([128, 512]) for _ in range(2)]
            psc = [C.ps([128, 512]) for _ in range(2)]
            po = [C.ps([128, 512]) for _ in range(2)]
            pss = C.ps([128, 512])
            Bh2b = [Buf(), Buf()]; Bwc = [Buf(), Buf()]; BqT = Buf(); BqdT = Buf(); BkT = Buf(); Bkts = Buf(); Bvts = Buf()
            Bsg = Buf(); BsTb = [Buf(), Buf()]; Bo32 = Buf(); Bosq = Buf(); ByTt = [Buf(), Buf()]; BS = Buf(); BSb = Buf()
            Bc = Buf(); Brstd = Buf(); Btmpo = Buf(); Bpk = [PB(), PB()]; Bpsc = [PB(), PB()]; Bpo = [PB(), PB()]; Bpss = PB()
            S.dma("sp", dm_sb[:], dmask, writes=[Bc])
            S.dma("sp", qd_sb[:], qdec, writes=[Bc])
            S.dma("sp", kdec_sb[:], kdec, writes=[Bc])
            S.dma("sp", tm_sb[:], tmask, writes=[Bc])
            S.op("pool", lambda e: e.memset(ones_f[:], 1.0), writes=[Bc])
            fl_sb = C.sb([128, 2], F32)
            S.dma("sp", fl_sb[:], flags, writes=[Bc])
            for k in range(4):
                S.dma("sp", Sst[32 * k:32 * (k + 1)], S_src[k], reads=[BSg], writes=[BS])
            for i8 in range(8):
                S.op("dve", lambda e, i8=i8: e.tensor_scalar(Sst[:, i8, :], Sst[:, i8, :], fl_sb[:, 1:2], None, ALU.mult),
                     reads=[BS, Bc], writes=[BS])
            for i8 in range(8):
                S.op("pool", lambda e, i8=i8: e.tensor_copy(Sb[:, i8, :], Sst[:, i8, :]), reads=[BS], writes=[BSb])
            hv = h2T.rearrange("(c p) t -> p c t", p=128)
            wiv = wi_b.rearrange("(c p) o -> p c o", p=128)
            yv = yT_s.rearrange("(c p) t -> p c t", p=128)
            ctr = [0]
            wci = 0
            sci = 0
            oi = 0
            for t in range(NTH if L3_STAGE >= 2 else 1):
                c0 = t * TW
                H = h2b[t % 2]; BH = Bh2b[t % 2]
                Y = yT[t % 2]; BY = ByTt[t % 2]
                S.dma("pool", H[:], hv[:, :, c0:c0 + TW], reads=[Bh2], writes=[BH])
                if t == 0:
                    for fc in range(8):
                        S.op("dve", lambda e, H=H, fc=fc: e.tensor_tensor(H[:, fc, :], H[:, fc, :], tm_sb[:], ALU.mult),
                             reads=[BH, Bc], writes=[BH])
                for ch in range(L3_NCH):
                    W = wc[wci % 2]; BW = Bwc[wci % 2]
                    wci += 1
                    S.dma("sp", W[:], wiv[:, :, ch * 512:(ch + 1) * 512], reads=[Bwb], writes=[BW])
                    if ch < 4 or ch >= 8:
                        for c4 in range(4):
                            P = pk[ctr[0] % 2]; BP = Bpk[ctr[0] % 2]
                            ctr[0] += 1
                            for fc in range(8):
                                S.op("pe", lambda e, P=P, W=W, fc=fc, c4=c4, H=H: e.matmul(
                                    P[:, 0:TW], W[:, fc, c4 * 128:(c4 + 1) * 128], H[:, fc, :], start=(fc == 0), stop=(fc == 7)),
                                    reads=[BW, BH], writes=[BP], inc=(fc == 7))
                            if ch < 2:
                                ci = ch * 4 + c4
                                h = ci // 2
                                S.op("act", lambda e, P=P, ci=ci: e.copy(qT[:, ci, :], P[:, 0:TW]), reads=[BP], writes=[BqT])
                                S.op("dve", lambda e, P=P, ci=ci, h=h: e.tensor_tensor(qdT[:, ci, :], P[:, 0:TW], qd_sb[:, h * TW:(h + 1) * TW], ALU.mult),
                                     reads=[BP, Bc], writes=[BqdT])
                            elif ch < 4:
                                ci = (ch - 2) * 4 + c4
                                S.op("act", lambda e, P=P, ci=ci: e.copy(kT[:, ci, :], P[:, 0:TW]), reads=[BP], writes=[BkT])
                            else:
                                gi = (ch - 8) * 4 + c4
                                S.op("act", lambda e, P=P, gi=gi: e.activation(sg[:, gi, :], P[:, 0:TW], AF.Silu), reads=[BP], writes=[Bsg])
                    if 2 <= ch < 4:
                        half = ch - 2
                        for bl in range(3):
                            P = pk[ctr[0] % 2]; BP = Bpk[ctr[0] % 2]
                            ctr[0] += 1
                            for fc in range(8):
                                S.op("pe", lambda e, P=P, W=W, fc=fc, bl=bl, H=H: e.matmul(
                                    P[:], H[:, fc, bl * 128:(bl + 1) * 128], W[:, fc, :], start=(fc == 0), stop=(fc == 7)),
                                    reads=[BH, BW], writes=[BP], inc=(fc == 7))
                            for hh in range(2):
                                h = half * 2 + hh
                                S.op("dve", lambda e, P=P, bl=bl, h=h, hh=hh: e.tensor_scalar(
                                    kts[:, bl, h * 256:(h + 1) * 256], P[:, hh * 256:(hh + 1) * 256], kdec_sb[:, bl * 4 + h:bl * 4 + h + 1], None, ALU.mult),
                                    reads=[BP, Bc], writes=[Bkts])
                    if 4 <= ch < 8:
                        h = ch - 4
                        for bl in range(3):
                            P = pk[ctr[0] % 2]; BP = Bpk[ctr[0] % 2]
                            ctr[0] += 1
                            for fc in range(8):
                                S.op("pe", lambda e, P=P, W=W, fc=fc, bl=bl, H=H: e.matmul(
                                    P[:], H[:, fc, bl * 128:(bl + 1) * 128], W[:, fc, :], start=(fc == 0), stop=(fc == 7)),
                                    reads=[BH, BW], writes=[BP], inc=(fc == 7))
                            S.op("act", lambda e, P=P, bl=bl, h=h: e.copy(vts[:, bl, h * 512:(h + 1) * 512], P[:]), reads=[BP], writes=[Bvts])
                for h in range(4 if L3_SUB >= 2 else 0):
                    ST = sTb[sci % 2]; BST = BsTb[sci % 2]
                    sci += 1
                    for jb in range(3):
                        P = psc[(sci + jb) % 2]; BP = Bpsc[(sci + jb) % 2]
                        for dc in range(2):
                            S.op("pe", lambda e, P=P, h=h, dc=dc, jb=jb: e.matmul(
                                P[:, 0:TW], kT[:, h * 2 + dc, jb * 128:(jb + 1) * 128], qT[:, h * 2 + dc, :], start=(dc == 0), stop=(dc == 1)),
                                reads=[BkT, BqT], writes=[BP], inc=(dc == 1))
                        S.op("dve", lambda e, P=P, ST=ST, jb=jb, h=h: e.tensor_tensor(
                            ST[:, jb, :], P[:, 0:TW], dm_sb[:, (h * 3 + jb) * TW:(h * 3 + jb + 1) * TW], ALU.mult),
                            reads=[BP, Bc], writes=[BST])
                    for ec in range(4):
                        P = po[oi % 2]; BP = Bpo[oi % 2]
                        oi += 1
                        for jb in range(3):
                            S.op("pe", lambda e, P=P, h=h, ec=ec, jb=jb, ST=ST: e.matmul(
                                P[:, 0:TW], vts[:, jb, h * 512 + ec * 128:h * 512 + (ec + 1) * 128], ST[:, jb, :], start=(jb == 0), stop=False),
                                reads=[Bvts, BST], writes=[BP], inc=False)
                        for dc in range(2):
                            S.op("pe", lambda e, P=P, h=h, ec=ec, dc=dc: e.matmul(
                                P[:, 0:TW], Sb[:, h * 2 + dc, ec * 128:(ec + 1) * 128], qdT[:, h * 2 + dc, :], start=False, stop=(dc == 1)),
                                reads=[BSb, BqdT], writes=[BP], inc=(dc == 1))
                        S.op("act", lambda e, P=P, ec=ec: e.copy(o32[:, ec, :], P[:, 0:TW]), reads=[BP], writes=[Bo32])
                        S.op("act", lambda e, P=P, ec=ec: e.activation(osq[:, ec, :], P[:, 0:TW], AF.Square), reads=[BP], writes=[Bosq])
                    for ec in range(4):
                        S.op("pe", lambda e, ec=ec: e.matmul(pss[:, 0:TW], ones_f[:], osq[:, ec, :], start=(ec == 0), stop=(ec == 3)),
                             reads=[Bc, Bosq], writes=[Bpss], inc=(ec == 3))
                    S.op("act", lambda e: e.activation(rstd[:], pss[:, 0:TW], AF.Ln, bias=RMS_EPS, scale=1.0 / 512.0), reads=[Bpss], writes=[Brstd])
                    S.op("act", lambda e: e.activation(rstd[:], rstd[:], AF.Exp, scale=-0.5), reads=[Brstd], writes=[Brstd])
                    for ec in range(4):
                        S.op("dve", lambda e, ec=ec: e.tensor_tensor(tmpo[:], o32[:, ec, :], rstd[:], ALU.mult), reads=[Bo32, Brstd], writes=[Btmpo])
                        S.op("dve", lambda e, ec=ec, h=h, Y=Y: e.tensor_tensor(Y[:, h * 4 + ec, :], tmpo[:], sg[:, h * 4 + ec, :], ALU.mult),
                             reads=[Btmpo, Bsg], writes=[BY])
                if L3_SUB >= 3:
                    emit_state_update(S, kts, Bkts, vts, Bvts, Sst, BS, Sb, BSb, pk, Bpk, ctr)
                if L3_SUB >= 2:
                    S.dma("sp", yv[:, :, c0:c0 + TW], Y[:], reads=[BY], writes=[ByT])
            S.barrier()
            S.flush()

def l3_consts():
    p = np.arange(128, dtype=np.float64)[:, None]
    i = np.arange(TW, dtype=np.float64)[None, :]
    dm = np.zeros((128, 4, 3, TW), np.float32)
    qd = np.zeros((128, 4, TW), np.float32)
    for h in range(4):
        g = RET_GAMMA[h]
        for jb in range(3):
            j = jb * 128 + p
            same = (j // 64) == (i // 64)
            before = (j // 64) < (i // 64)
            val = np.where(same, g ** np.abs(i - j), np.where(before, g ** np.maximum(i - j, 0), 0.0)) / 16.0
            dm[:, h, jb, :] = val
        qd[:, h, :] = g ** (i + 1.0)
    return dm.reshape(128, 12 * TW), qd.reshape(128, 4 * TW)


PAIRS = [[0, 1], [2, 3], [4, 5], [6, 7]]


def build_fused():
    nc = bass.Bass("TRN2", target_bir_lowering=False)

    def din(name, shape):
        return nc.dram_tensor(name, list(shape), F32, kind="ExternalInput").ap()

    T1 = {"xT": din("xT", [D, LR]), "wq": din("wq", [D, 512]), "wk": din("wk", [D, 512]), "wv": din("wv", [D, 512]),
          "wf": din("wf", [D, 4]), "fbias": din("fbias", [4, 1]), "lamv": din("lamv", [128, 256]), "gsub": din("gsub", [128, 1]),
          "dbias": din("dbias", [128, 2 * 66]), "Fd": din("Fd", [128, 2 * 3 * TW]), "Ff": din("Ff", [128, 3 * TW]),
          "qaug": din("qaug", [2, 3, LR]), "odt": BF16}
    hres = din("hres", [D, HALF])
    flags = din("flags", [128, 2])
    w_out0 = din("w_out0", [D, D])
    w1_0 = din("w1_0", [D, 4 * D])
    w2_0 = din("w2_0", [4 * D, D])
    lnp0 = din("lnp0", [128, 32])
    w_in1 = din("w_in1", [D, 6144])
    w_out1 = din("w_out1", [2048, D])
    w1_1 = din("w1_1", [D, 4 * D])
    w2_1 = din("w2_1", [4 * D, D])
    lnp1 = din("lnp1", [128, 32])
    kdec = din("kdec", [128, 12])
    tmask = din("tmask", [128, TW])
    dmask = din("dmask", [128, 12 * TW])
    qdec = din("qdec", [128, 4 * TW])
    outT = nc.dram_tensor("outT", [D, HALF], F32, kind="ExternalOutput").ap()
    wo0_b = nc.dram_tensor("wo0_b", [D, D], BF16).ap()
    w10_b = nc.dram_tensor("w10_b", [D, 4 * D], BF16).ap()
    w20_b = nc.dram_tensor("w20_b", [4 * D, D], BF16).ap()
    wi_b = nc.dram_tensor("wi_b", [D, 6144], BF16).ap()
    wo1_b = nc.dram_tensor("wo1_b", [2048, D], BF16).ap()
    w11_b = nc.dram_tensor("w11_b", [D, 4 * D], BF16).ap()
    w21_b = nc.dram_tensor("w21_b", [4 * D, D], BF16).ap()
    NCH = NT // 2
    oT_c = [nc.dram_tensor("oT_c%d" % k, [512, 2 * TW], BF16) for k in range(NCH)]
    G_c = [nc.dram_tensor("G_c%d" % k, [1024, 2 * TW], BF16) for k in range(NCH)]
    h2T_i = nc.dram_tensor("h2T_i", [D, HALF], F32).ap()
    Se_c = [nc.dram_tensor("Se_c%d" % k, [256, 512], F32) for k in range(4)]
    Sg_c = [nc.dram_tensor("Sg_c%d" % k, [512, 512], F32) for k in range(4)]
    yT_s = nc.dram_tensor("yT_s", [2048, HALF], BF16).ap()
    T1["oT_tile"] = lambda r0, r1, t: oT_c[t // 2].ap()[r0:r1, (t % 2) * TW:(t % 2 + 1) * TW]

    def g_tile(k, t):
        gt = k * NTH + t
        return G_c[gt // 2].ap().rearrange("(c p) t -> p c t", p=128)[:, :, (gt % 2) * TW:(gt % 2 + 1) * TW]
    with contextlib.ExitStack() as outer:
        S = Sched(nc, outer)
        Bwb = Buf(); Bo = Buf(); BG = Buf(); Bh2 = Buf(); BSe = Buf(); BSg = Buf(); ByT = Buf(); Bout = Buf()
        def hookA(C):
            stg = [C.sb([128, 2048], BF16) for _ in range(2)]
            Bs = [Buf(), Buf()]
            for (s_, d_, r_, c_) in ((w_out0, wo0_b, D, D), (w1_0, w10_b, D, 4 * D), (w2_0, w20_b, 4 * D, D)):
                yield from iter_convert(S, stg, Bs, s_, d_, Bwb, r_, c_)

        def hookB(C):
            stg = [C.sb([128, 2048], BF16) for _ in range(2)]
            Bs = [Buf(), Buf()]
            for (s_, d_, r_, c_) in ((w_in1, wi_b, D, 6144), (w_out1, wo1_b, 2048, D), (w1_1, w11_b, D, 4 * D), (w2_1, w21_b, 4 * D, D)):
                yield from iter_convert(S, stg, Bs, s_, d_, Bwb, r_, c_)
        T1["hookA"] = hookA
        T1["hookB"] = hookB
        emit_l1(nc, S, T1, Bo)
        for k in range(NCH):
            S.op("pool", lambda e, k=k: e.collective_compute("AllGather", ALU.bypass, replica_groups=PAIRS,
                                                             ins=[oT_c[k].ap().opt()], outs=[G_c[k].ap().opt()]), reads=[Bo], writes=[BG])
        with contextlib.ExitStack() as st:
            emit_post(nc, S, st, NTH, 8, g_tile, "blend", BG, hres, wo0_b, w10_b, w20_b, Bwb, lnp0, h2T_i, Bh2, flags=flags)
            S.barrier()
            S.flush()
        Tp = {"h2T": h2T_i, "wk": wi_b[:, 1024:2048], "wv": wi_b[:, 2048:4096], "kdec": kdec, "tmask": tmask,
              "S_end": [Se_c[k].ap().rearrange("(p i) f -> p i f", i=8) for k in range(4)], "wqueue": "sp", "Bwb": Bwb}
        emit_prepass(nc, S, Tp, Bh2, BSe)
        for k in range(4):
            S.op("pool", lambda e, k=k: e.collective_compute("AllGather", ALU.bypass, replica_groups=PAIRS,
                                                             ins=[Se_c[k].ap().opt()], outs=[Sg_c[k].ap().opt()]), reads=[BSe], writes=[BSg])
        Tr = {"h2T": h2T_i, "wi_b": wi_b, "kdec": kdec, "tmask": tmask, "dmask": dmask, "qdec": qdec, "yT_s": yT_s,
              "S_src": [Sg_c[k].ap()[0:256, :].rearrange("(p i) f -> p i f", i=8) for k in range(4)], "flags": flags}
        with contextlib.ExitStack() as st:
            emit_retention(nc, S, st, Tr, Bwb, Bh2, BSg, ByT)
        with contextlib.ExitStack() as st:
            emit_post(nc, S, st, NTH, 16, yT_s, "bf16", ByT, h2T_i, wo1_b, w11_b, w21_b, Bwb, lnp1, outT, Bout)
            S.barrier()
            S.flush()
    return nc


def _token_mask(u):
    tm = np.ones((128, TW), np.float32)
    if u == 0:
        tm[:, 0:FPAD] = 0.0
    return tm


def kernel(x, meta_tokens, even_w_in, even_f_bias, diff_lambda, diff_subln_g, even_w_out,
           ret_w_in, ret_w_out, ln_g, ln_b, ffn_w1, ffn_w2):
    x = np.asarray(x, np.float32)
    f32 = lambda a: np.ascontiguousarray(np.asarray(a, np.float32))
    meta = f32(meta_tokens)
    B = x.shape[0]
    cores = list(range(8))
    kdec = ret_consts()
    dm, qd = l3_consts()
    wo = f32(even_w_out[0])
    wo_perm = np.ascontiguousarray(np.concatenate([wo[0:256], wo[512:768], wo[256:512], wo[768:1024]], axis=0))
    shared = {"w_out0": wo_perm, "w1_0": f32(ffn_w1[0]), "w2_0": f32(ffn_w2[0]), "lnp0": lnp_table(f32(ln_g), f32(ln_b), 0),
              "w_in1": f32(ret_w_in[0]), "w_out1": f32(ret_w_out[0]), "w1_1": f32(ffn_w1[1]), "w2_1": f32(ffn_w2[1]),
              "lnp1": lnp_table(f32(ln_g), f32(ln_b), 1), "kdec": kdec, "dmask": dm, "qdec": qd}
    in_maps = []
    for c in cores:
        b, u = c // 2, c % 2
        hp = np.zeros((LR, D), np.float32)
        hp[FPAD:FPAD + NMETA] = meta
        hp[FPAD + NMETA:FPAD + NMETA + SEQ] = x[b]
        xT = np.ascontiguousarray(hp.T)
        m = l1_inputs(xT, f32(even_w_in[0]), f32(even_f_bias[0]), f32(diff_lambda[0]), f32(diff_subln_g[0]), u)
        m["hres"] = np.ascontiguousarray(xT[:, u * HALF:(u + 1) * HALF])
        fl = np.zeros((128, 2), np.float32)
        fl[:, u] = 1.0
        m["flags"] = fl
        m["tmask"] = _token_mask(u)
        m.update(shared)
        in_maps.append(m)
    res = run_bass_kernel_spmd(build_fused(), in_maps, core_ids=cores).results
    out = np.empty((B, SEQ, D), np.float32)
    for b in range(B):
        hT = np.concatenate([res[2 * b]["outT"], res[2 * b + 1]["outT"]], axis=1)
        out[b] = hT[:, FPAD + NMETA:FPAD + NMETA + SEQ].T
    return out
```

```python
import contextlib
import math
import numpy as np
import concourse.bass as bass
import concourse.mybir as mybir
from concourse.bass_utils import run_bass_kernel_spmd

F32 = mybir.dt.float32
BF16 = mybir.dt.bfloat16
AF = mybir.ActivationFunctionType
ALU = mybir.AluOpType

D = 1024
SEQ = 8192
NMETA = 16
FPAD = 48
LR = 8448
TW = 384
NT = LR // TW
NB = LR // 128
HALF = LR // 2
NTH = HALF // TW
ALPHA = 4 ** 0.25
LN_EPS = 1e-5
RMS_EPS = 1e-6
LAM_INIT0 = 0.8 - 0.6 * math.exp(-0.3 * 0)
NEG = -30000.0
REAL_END = FPAD + NMETA + SEQ

ENGS = ("pe", "act", "dve", "pool", "sp")


class Buf:
    __slots__ = ("name", "w", "r", "ex")

    def __init__(self, name="", ex=False):
        self.name = name
        self.w = None
        self.r = []
        self.ex = ex


def PB():
    return Buf(ex=True)


class Sched:
    def __init__(self, nc, stack, n_dma_sems=16):
        self.nc = nc
        self.streams = {e: [] for e in ENGS}
        self.cnt = {e: 0 for e in ENGS}
        self.seen = {e: {} for e in ENGS}
        self.n_dma = n_dma_sems
        self.dma_k = 0
        self.sems = {e: stack.enter_context(nc.semaphore("s_" + e)) for e in ENGS}
        self.dsems = [stack.enter_context(nc.semaphore("d_%d" % i)) for i in range(n_dma_sems)]
        self.dlast = [0] * n_dma_sems

    def _need(self, eng, tok, waits):
        if tok is None:
            return
        kind, key, val = tok
        if kind == "e" and key == eng and eng in ("pe", "sp"):
            return
        k = (kind, key)
        if self.seen[eng].get(k, 0) >= val:
            return
        if val > waits.get(k, 0):
            waits[k] = val

    def _emit_waits(self, eng, waits):
        for k, val in waits.items():
            self.seen[eng][k] = val
            self.streams[eng].append(("wait", k, val))

    def _deps(self, eng, reads, writes):
        waits = {}
        for b in reads:
            self._need(eng, b.w, waits)
        for b in writes:
            self._need(eng, b.w, waits)
            for t in b.r:
                self._need(eng, t, waits)
        return waits

    def _mark(self, tok, reads, writes):
        for b in reads:
            b.r.append(tok)
            if len(b.r) > 24:
                b.r = b.r[-24:]
        for b in writes:
            b.w = tok
            b.r = []

    def op(self, eng, fn, reads=(), writes=(), inc=True):
        if any(b.ex for b in reads):
            writes = list(writes) + [b for b in reads if b.ex]
            reads = [b for b in reads if not b.ex]
        waits = self._deps(eng, reads, writes)
        self._emit_waits(eng, waits)
        if inc:
            self.cnt[eng] += 1
            tok = ("e", eng, self.cnt[eng])
        else:
            tok = ("e", eng, self.cnt[eng] + 1)
        self.streams[eng].append(("op", fn, inc))
        self._mark(tok, reads, writes)
        return tok

    def dma(self, q, out_ap, in_ap, reads=(), writes=(), **kw):
        waits = self._deps(q, reads, writes)
        s = self.dma_k % self.n_dma
        v = self.dlast[s] + 16
        self.dma_k += 1
        if v > 16:
            self._need(q, ("d", s, v - 16), waits)
        self._emit_waits(q, waits)
        self.dlast[s] = v
        tok = ("d", s, v)
        self.streams[q].append(("dma", out_ap, in_ap, s, kw))
        self._mark(tok, reads, writes)
        return tok

    def barrier(self):
        toks = [("e", e, self.cnt[e]) for e in ENGS if self.cnt[e] > 0]
        toks += [("d", s, self.dlast[s]) for s in range(self.n_dma) if self.dlast[s] > 0]
        for e in ENGS:
            waits = {}
            for t in toks:
                if t[0] == "e" and t[1] == e:
                    continue
                self._need(e, t, waits)
            self._emit_waits(e, waits)

    def flush(self):
        nc = self.nc
        with nc.Block() as block:
            def run(e, engobj):
                for item in self.streams[e]:
                    if item[0] == "wait":
                        (kind, key), val = item[1], item[2]
                        sem = self.sems[key] if kind == "e" else self.dsems[key]
                        engobj.wait_ge(sem, val)
                    elif item[0] == "op":
                        ins = item[1](engobj)
                        if item[2]:
                            ins.then_inc(self.sems[e], 1)
                    else:
                        _, o, i, s, kw = item
                        engobj.dma_start(out=o, in_=i, **kw).then_inc(self.dsems[s], 16)

            @block.tensor
            def _(eng):
                run("pe", eng)

            @block.scalar
            def _(eng):
                run("act", eng)

            @block.vector
            def _(eng):
                run("dve", eng)

            @block.gpsimd
            def _(eng):
                run("pool", eng)

            @block.sync
            def _(eng):
                run("sp", eng)
        self.streams = {e: [] for e in ENGS}


class Ctx:
    K = [0]

    def __init__(self, nc, stack):
        self.nc = nc
        self.st = stack

    def sb(self, shape, dt, name=None):
        Ctx.K[0] += 1
        return self.st.enter_context(self.nc.sbuf_tensor(name or ("t%d" % Ctx.K[0]), list(shape), dt))

    def ps(self, shape, dt=F32, name=None):
        Ctx.K[0] += 1
        return self.st.enter_context(self.nc.psum_tensor(name or ("p%d" % Ctx.K[0]), list(shape), dt))


SCALE = 0.125
DEBUG_A = False


def emit_l1(nc, S, T, Bo):
    xT = T["xT"]; wq = T["wq"]; wk = T["wk"]; wv = T["wv"]; wf = T["wf"]; fbias = T["fbias"]
    lamv = T["lamv"]; gsub = T["gsub"]; dbias = T["dbias"]; Fd = T["Fd"]; Ff = T["Ff"]; qaug = T["qaug"]
    ODT = T["odt"]
    QT_s = nc.dram_tensor("QT_s", [8, 64, LR], BF16).ap()
    KT_s = nc.dram_tensor("KT_s", [8, 64, LR], BF16).ap()
    V_s = nc.dram_tensor("V_s", [LR, 512], BF16).ap()
    AQ_s = nc.dram_tensor("AQ_s", [4, 3, LR], BF16).ap()
    AQd_s = nc.dram_tensor("AQd_s", [6, LR], BF16).ap()
    out_toks = []
    if True:
        BQT = [Buf() for _ in range(8)]
        BKT = [Buf() for _ in range(8)]
        BV = Buf()
        BAQ = Buf()
        with contextlib.ExitStack() as st:
            C = Ctx(nc, st)
            wq_sb = C.sb([128, 8, 512], BF16)
            wk_sb = C.sb([128, 8, 512], BF16)
            wv_sb = C.sb([128, 8, 512], BF16)
            wf_sb = C.sb([128, 8, 4], BF16)
            fb_sb = C.sb([4, 1], F32)
            nfb_sb = C.sb([4, 1], F32)
            ident = C.sb([128, 128], F32)
            xt = [C.sb([128, 8, TW], BF16) for _ in range(2)]
            stq = [C.sb([128, TW], BF16) for _ in range(4)]
            stv = [C.sb([128, 512], BF16) for _ in range(2)]
            lf = C.sb([4, LR], F32)
            cs = C.sb([4, LR], F32)
            ones4 = C.sb([4, LR // 4], F32)
            e_t = C.sb([4, TW], F32)
            hi = C.sb([4, LR], BF16)
            mid = C.sb([4, LR], BF16)
            lo = C.sb([4, LR], BF16)
            pq = [C.ps([128, 512]) for _ in range(4)]
            pv = [C.ps([128, 512]) for _ in range(2)]
            pf = C.ps([128, 512])
            Bw = Buf(); Bxt = [Buf(), Buf()]; Bstq = [Buf() for _ in range(4)]; Bstv = [Buf(), Buf()]
            Bpq = [PB() for _ in range(4)]; Bpv = [PB(), PB()]; Bpf = PB(); Blf = Buf(); Bcs = Buf()
            Bet = Buf(); Bfb = Buf(); Bo4 = Buf(); Bhi = Buf(); Bmid = Buf(); Blo = Buf()

            wr = "(c p) o -> p c o"
            S.dma("pool", wq_sb[:], wq.rearrange(wr, p=128), writes=[Bw])
            for fc in range(8):
                S.dma("pool", wk_sb[:, fc, :], wk[fc * 128:(fc + 1) * 128, :], writes=[Bw])
                S.dma("pool", wv_sb[:, fc, :], wv[fc * 128:(fc + 1) * 128, :], writes=[Bw])
            S.dma("pool", wf_sb[:], wf.rearrange(wr, p=128), writes=[Bw])
            S.dma("sp", fb_sb[:], fbias, writes=[Bfb])
            S.op("dve", lambda e: e.tensor_scalar(nfb_sb[:], fb_sb[:], -1.0, None, ALU.mult), reads=[Bfb], writes=[Bfb])
            S.op("dve", lambda e: e.memset(ones4[:], 1.0), writes=[Bo4])
            xTr = xT.rearrange("(c p) t -> p c t", p=128)
            conv_it = T["hookA"](C) if T.get("hookA") else None
            qi = 0
            vi = 0
            for t in range(NT):
                c0 = t * TW
                X = xt[t % 2]; BX = Bxt[t % 2]
                S.dma("pool", X[:], xTr[:, :, c0:c0 + TW], writes=[BX])
                for which, (w_sb, dst, BD) in enumerate(((wq_sb, QT_s, BQT), (wk_sb, KT_s, BKT))):
                    for g in range(4):
                        P = pq[qi % 4]; BP = Bpq[qi % 4]; ST = stq[qi % 4]; BS = Bstq[qi % 4]
                        qi += 1
                        for c in range(8):
                            S.op("pe", lambda e, P=P, w_sb=w_sb, c=c, g=g, X=X: e.matmul(
                                P[:, 0:TW], w_sb[:, c, g * 128:(g + 1) * 128], X[:, c, :], start=(c == 0), stop=(c == 7)),
                                reads=[Bw, BX], writes=[BP], inc=(c == 7))
                        eng = "act" if (g % 2 == 0) else "dve"
                        if eng == "act":
                            S.op("act", lambda e, ST=ST, P=P: e.copy(ST[:], P[:, 0:TW]), reads=[BP], writes=[BS])
                        else:
                            S.op("dve", lambda e, ST=ST, P=P: e.tensor_copy(ST[:], P[:, 0:TW]), reads=[BP], writes=[BS])
                        S.dma("sp", dst[2 * g:2 * g + 2].rearrange("u r t -> (u r) t")[:, c0:c0 + TW], ST[:],
                              reads=[BS], writes=[BD[2 * g], BD[2 * g + 1]])
                for bl in range(3):
                    P = pv[vi % 2]; BP = Bpv[vi % 2]; ST = stv[vi % 2]; BS = Bstv[vi % 2]
                    vi += 1
                    for c in range(8):
                        S.op("pe", lambda e, P=P, c=c, bl=bl, X=X: e.matmul(
                            P[:], X[:, c, bl * 128:(bl + 1) * 128], wv_sb[:, c, :], start=(c == 0), stop=(c == 7)),
                            reads=[Bw, BX], writes=[BP], inc=(c == 7))
                    S.op("dve", lambda e, ST=ST, P=P: e.tensor_copy(ST[:], P[:]), reads=[BP], writes=[BS])
                    r0 = c0 + bl * 128
                    S.dma("sp", V_s[r0:r0 + 128, :], ST[:], reads=[BS], writes=[BV])
                for c in range(8):
                    S.op("pe", lambda e, c=c, X=X: e.matmul(pf[0:4, 0:TW], wf_sb[:, c, :], X[:, c, :], start=(c == 0), stop=(c == 7)),
                         reads=[Bw, BX], writes=[Bpf], inc=(c == 7))
                S.op("act", lambda e: e.activation(e_t[:], pf[0:4, 0:TW], AF.Exp, bias=nfb_sb[:, 0:1], scale=-1.0),
                     reads=[Bpf, Bfb], writes=[Bet])
                S.op("act", lambda e, c0=c0: e.activation(lf[:, c0:c0 + TW], e_t[:], AF.Ln, bias=1.0, scale=1.0),
                     reads=[Bet], writes=[Blf])
                if conv_it is not None:
                    for _ in range(3):
                        next(conv_it, None)
            S.op("dve", lambda e: e.memset(lf[:, 0:FPAD], 0.0), reads=[Blf], writes=[Blf])
            S.op("dve", lambda e: e.memset(lf[:, REAL_END:LR], 0.0), reads=[Blf], writes=[Blf])
            CH = LR // 4
            for k in range(4):
                a = k * CH
                if k == 0:
                    S.op("dve", lambda e, a=a: e.tensor_tensor_scan(cs[:, a:a + CH], ones4[:], lf[:, a:a + CH], 0.0, ALU.mult, ALU.add),
                         reads=[Blf, Bo4], writes=[Bcs])
                else:
                    S.op("dve", lambda e, a=a: e.tensor_tensor_scan(cs[:, a:a + CH], ones4[:], lf[:, a:a + CH], cs[:, a - 1:a], ALU.mult, ALU.add),
                         reads=[Blf, Bo4, Bcs], writes=[Bcs])
            CS_s = nc.dram_tensor("CS_s", [4, LR], F32).ap()
            BCS = Buf()
            S.dma("sp", CS_s, cs[:], reads=[Bcs], writes=[BCS])
            for t in range(NT):
                c0 = t * TW
                S.op("dve", lambda e, c0=c0: e.tensor_scalar(lf[:, c0:c0 + TW], cs[:, c0:c0 + TW], cs[:, c0:c0 + 1], -8.0, ALU.subtract, ALU.mult),
                     reads=[Bcs, Blf], writes=[Blf])
            S.op("dve", lambda e: e.tensor_copy(hi[:], lf[:]), reads=[Blf], writes=[Bhi])
            S.op("dve", lambda e: e.tensor_tensor(lf[:], lf[:], hi[:], ALU.subtract), reads=[Blf, Bhi], writes=[Blf])
            S.op("dve", lambda e: e.tensor_copy(mid[:], lf[:]), reads=[Blf], writes=[Bmid])
            S.op("dve", lambda e: e.tensor_tensor(lf[:], lf[:], mid[:], ALU.subtract), reads=[Blf, Bmid], writes=[Blf])
            S.op("dve", lambda e: e.tensor_copy(lo[:], lf[:]), reads=[Blf], writes=[Blo])
            qa_sb = C.sb([6, LR], BF16)
            Bqa = Buf()
            S.dma("pool", qa_sb[:], qaug.rearrange("h r t -> (h r) t"), writes=[Bqa])
            S.dma("sp", AQd_s, qa_sb[:], reads=[Bqa], writes=[BAQ])
            S.dma("sp", AQ_s[:, 0, :], hi[:], reads=[Bhi], writes=[BAQ])
            S.dma("sp", AQ_s[:, 1, :], mid[:], reads=[Bmid], writes=[BAQ])
            S.dma("sp", AQ_s[:, 2, :], lo[:], reads=[Blo], writes=[BAQ])
            if conv_it is not None:
                for _ in conv_it:
                    pass
            S.barrier()
            S.flush()

        with contextlib.ExitStack() as st:
            C = Ctx(nc, st)
            kt = [C.sb([128, LR], BF16) for _ in range(2)]
            qt = [C.sb([128, LR], BF16) for _ in range(2)]
            vt = [C.sb([128, NB, 128], BF16) for _ in range(2)]
            Fd_sb = C.sb([128, 2 * 3 * TW], F32)
            Ff_sb = C.sb([128, 3 * TW], F32)
            dbias_sb = C.sb([128, 2 * 66], F32)
            csk = C.sb([128, 4, NB], F32)
            csT0 = C.sb([128, 4, NT], F32)
            bkt = [C.sb([128, NB], F32) for _ in range(2)]
            ones_bf = C.sb([128, 128], BF16)
            ones_b0 = C.sb([128, 128], BF16)
            ones_f = C.sb([128, 128], F32)
            lam_sb = C.sb([128, 256], F32)
            lprod = C.sb([128, 128], F32)
            lsum = C.sb([128, 2], F32)
            lexp = C.sb([128, 2], F32)
            neglam = C.sb([128, 1], F32)
            gcol = C.sb([128, 1], F32)
            pT = [C.sb([128, TW], BF16) for _ in range(4)]
            tmp = [C.sb([128, TW], F32) for _ in range(2)]
            rl = C.sb([128, TW], F32)
            lsb = [C.sb([128, TW], F32) for _ in range(2)]
            l0b = [C.sb([128, TW], F32) for _ in range(2)]
            Blsb = [Buf(), Buf()]; Bl0b = [Buf(), Buf()]
            a0 = C.sb([128, TW], F32)
            a1 = C.sb([128, TW], F32)
            od = C.sb([128, TW], F32)
            sq = C.sb([128, TW], F32)
            rstd = C.sb([128, TW], F32)
            ofin = [C.sb([128, TW], ODT) for _ in range(2)]
            NS = 3
            NPT = 4
            ps_s = [C.ps([128, 512]) for _ in range(NS)]
            ps_o = [C.ps([128, 512]) for _ in range(2)]
            ps_l = [C.ps([128, 512]) for _ in range(2)]
            ps_x = C.ps([128, 512])
            Bkt = [Buf(), Buf()]; Bqt = [Buf(), Buf()]; Bvt = [Buf(), Buf()]
            Bc = Buf(); Bcsk = Buf(); BcsT0 = Buf(); Bbkt = [Buf(), Buf()]
            Bps_s = [PB(), PB(), PB()]; Bps_o = [PB(), PB()]; Bps_l = [PB(), PB()]; Bps_x = PB()
            BpT = [Buf() for _ in range(4)]; Btmp = [Buf(), Buf()]
            Brl = Buf(); Ba0 = Buf(); Ba1 = Buf(); Bod = Buf(); Bsq = Buf(); Brstd = Buf(); Bofin = [Buf(), Buf()]
            Blam = Buf()

            S.dma("sp", Fd_sb[:], Fd, writes=[Bc])
            S.dma("sp", Ff_sb[:], Ff, writes=[Bc])
            S.dma("sp", dbias_sb[:], dbias, writes=[Bc])
            S.dma("sp", lam_sb[:], lamv, writes=[Blam])
            S.dma("sp", gcol[:], gsub, writes=[Blam])
            for h in range(4):
                S.dma("sp", csk[:, h, :], CS_s[h].rearrange("(b p) -> p b", p=128), reads=[BCS], writes=[Bcsk], allow_slow_non_contiguous=True)
            for h in range(4):
                src = bass.AP(CS_s.tensor, CS_s.offset + h * LR, [[0, 128], [TW, NT]])
                S.dma("sp", csT0[:, h, :], src, reads=[BCS], writes=[BcsT0], allow_slow_non_contiguous=True)
            S.op("pool", lambda e: e.memset(ones_bf[:], 1.0), writes=[Bc])
            S.op("pool", lambda e: e.memset(ones_b0[:], 1.0), writes=[Bc])
            S.op("pool", lambda e: e.memset(ones_b0[0:FPAD, :], 0.0), reads=[Bc], writes=[Bc])
            S.op("pool", lambda e: e.memset(ones_f[:], 1.0), writes=[Bc])
            for b in range(2):
                S.op("pool", lambda e, b=b: e.memset(kt[b][64:67, :], 1.0), writes=[Bkt[b]])
            S.op("dve", lambda e: e.tensor_tensor(lprod[:, 0:64], lam_sb[:, 0:64], lam_sb[:, 64:128], ALU.mult), reads=[Blam], writes=[Blam])
            S.op("dve", lambda e: e.tensor_tensor(lprod[:, 64:128], lam_sb[:, 128:192], lam_sb[:, 192:256], ALU.mult), reads=[Blam], writes=[Blam])
            S.op("dve", lambda e: e.reduce_sum(lsum[:, 0:1], lprod[:, 0:64], mybir.AxisListType.X), reads=[Blam], writes=[Blam])
            S.op("dve", lambda e: e.reduce_sum(lsum[:, 1:2], lprod[:, 64:128], mybir.AxisListType.X), reads=[Blam], writes=[Blam])
            S.op("act", lambda e: e.activation(lexp[:], lsum[:], AF.Exp), reads=[Blam], writes=[Blam])
            S.op("dve", lambda e: e.scalar_tensor_tensor(neglam[:], lexp[:, 1:2], -LAM_INIT0, lexp[:, 0:1], ALU.add, ALU.subtract),
                 reads=[Blam], writes=[Blam])
            S.op("dve", lambda e: e.tensor_scalar(gcol[:], gcol[:], 1.0 - LAM_INIT0, None, ALU.mult), reads=[Blam], writes=[Blam])

            conv_itB = T["hookB"](C) if T.get("hookB") else None
            si = 0
            pi = 0
            ti = 0
            oi = 0
            fi = 0
            for g in range(4):
                isdiff = g < 2
                units = (2 * g, 2 * g + 1)
                dv = 128 if isdiff else 64
                for ui, u in enumerate(units):
                    S.dma("sp", kt[ui][0:64, :], KT_s[u], reads=[BKT[u]], writes=[Bkt[ui]])
                    S.dma("sp", qt[ui][0:64, :], QT_s[u], reads=[BQT[u]], writes=[Bqt[ui]])
                    if isdiff:
                        S.dma("sp", qt[ui][64:67, :], AQd_s[3 * g:3 * g + 3, :], reads=[BAQ], writes=[Bqt[ui]])
                    else:
                        S.dma("sp", qt[ui][64:67, :], AQ_s[u - 4], reads=[BAQ], writes=[Bqt[ui]])
                if isdiff:
                    S.dma("sp", vt[0][:], V_s[:, g * 128:(g + 1) * 128].rearrange("(b p) c -> p b c", p=128),
                          reads=[BV], writes=[Bvt[0]])
                else:
                    for ui, u in enumerate(units):
                        hf = u - 4
                        S.dma("sp", vt[ui][:, :, 0:64], V_s[:, 256 + hf * 64:256 + (hf + 1) * 64].rearrange("(b p) c -> p b c", p=128),
                              reads=[BV], writes=[Bvt[ui]])
                        S.op("pool", lambda e, ui=ui: e.memset(vt[ui][:, :, 64:128], 1.0), writes=[Bvt[ui]])
                        S.op("pool", lambda e, ui=ui: e.memset(vt[ui][0:FPAD, 0, 64:128], 0.0), reads=[Bvt[ui]], writes=[Bvt[ui]])
                items = []
                for t in range(NT):
                    for ui, u in enumerate(units):
                        for j in range(3 * t + 3):
                            items.append((t, ui, u, j))
                state = {}
                deferred = []

                def stage_a(t, ui, u, j):
                    nonlocal si, pi, ti, oi, fi
                    c0 = t * TW
                    nblk = 3 * t + 3
                    K = kt[ui]; Q = qt[ui]; BK = Bkt[ui]; BQ = Bqt[ui]
                    hf = u - 4
                    if j == 0:
                        PO = ps_o[oi % 2]; BPO = Bps_o[oi % 2]; PL = ps_l[oi % 2]; BPL = Bps_l[oi % 2]
                        oi += 1
                        BKk = None; BBK = None
                        if not isdiff:
                            BKk = bkt[fi % 2]; BBK = Bbkt[fi % 2]
                            fi += 1
                            S.op("dve", lambda e, BKk=BKk, nblk=nblk, hf=hf, t=t: e.tensor_scalar(
                                BKk[:, 0:nblk], csk[:, hf, 0:nblk], csT0[:, hf, t:t + 1], None, ALU.subtract),
                                reads=[Bcsk, BcsT0], writes=[BBK])
                        state[(t, ui)] = (PO, BPO, PL, BPL, BKk, BBK)
                    PO, BPO, PL, BPL, BKk, BBK = state[(t, ui)]
                    jj = j - 3 * t
                    PS = ps_s[si % NS]; BPS = Bps_s[si % NS]
                    si += 1
                    S.op("pe", lambda e, PS=PS, K=K, Q=Q, j=j, c0=c0: e.matmul(
                        PS[:, 0:TW], K[0:67, j * 128:(j + 1) * 128], Q[0:67, c0:c0 + TW], start=True, stop=True),
                        reads=[BK, BQ], writes=[BPS])
                    PT = pT[pi % NPT]; BPT = BpT[pi % NPT]
                    pi += 1
                    if isdiff:
                        col = g * 66 + (jj + 63)
                        bcol = dbias_sb[:, col:col + 1]
                        rb = [Bc]
                    else:
                        bcol = BKk[:, j:j + 1]
                        rb = [BBK]
                    if jj < 0:
                        S.op("act", lambda e, PT=PT, PS=PS, bcol=bcol: e.activation(PT[:], PS[:, 0:TW], AF.Exp, bias=bcol, scale=SCALE),
                             reads=[BPS] + rb, writes=[BPT])
                    else:
                        TM = tmp[ti % 2]; BTM = Btmp[ti % 2]
                        ti += 1
                        if isdiff:
                            Fap = Fd_sb[:, (g * 3 + jj) * TW:(g * 3 + jj + 1) * TW]
                        else:
                            Fap = Ff_sb[:, jj * TW:(jj + 1) * TW]
                        S.op("dve", lambda e, TM=TM, PS=PS, Fap=Fap: e.scalar_tensor_tensor(
                            TM[:], PS[:, 0:TW], SCALE, Fap, ALU.mult, ALU.add), reads=[BPS, Bc], writes=[BTM])
                        S.op("act", lambda e, PT=PT, TM=TM, bcol=bcol: e.activation(PT[:], TM[:], AF.Exp, bias=bcol, scale=1.0),
                             reads=[BTM] + rb, writes=[BPT])
                    return PT, BPT

                def stage_b(idx, t, ui, u, j, PT, BPT):
                    c0 = t * TW
                    nblk = 3 * t + 3
                    hf = u - 4
                    if isdiff:
                        VT = vt[0]; BVT = Bvt[0]
                    else:
                        VT = vt[ui]; BVT = Bvt[ui]
                    PO, BPO, PL, BPL, BKk, BBK = state[(t, ui)]
                    if not isdiff:
                        S.op("pe", lambda e, PO=PO, VT=VT, PT=PT, j=j, nblk=nblk: e.matmul(
                            PO[:, 0:TW], VT[:, j, :], PT[:], start=(j == 0), stop=(j == nblk - 1)),
                            reads=[BVT, BPT], writes=[BPO], inc=True)
                        if j != nblk - 1:
                            return
                        LS = lsb[(2 * t + ui) % 2]; BLS = Blsb[(2 * t + ui) % 2]
                        L0 = l0b[(2 * t + ui) % 2]; BL0 = Bl0b[(2 * t + ui) % 2]
                        S.op("dve", lambda e, LS=LS, PO=PO: e.tensor_scalar(LS[64:128, :], PO[64:128, 0:TW], 1e-30, None, ALU.add),
                             reads=[BPO], writes=[BLS])
                        S.dma("sp", L0[0:64, :], LS[64:128, :], reads=[BLS], writes=[BL0])
                        S.op("dve", lambda e, L0=L0: e.reciprocal(rl[0:64, :], L0[0:64, :]), reads=[BL0], writes=[Brl])
                        OF = ofin[(2 * t + ui) % 2]; BOF = Bofin[(2 * t + ui) % 2]
                        S.op("dve", lambda e, OF=OF, PO=PO: e.tensor_tensor(OF[0:64, :], PO[0:64, 0:TW], rl[0:64, :], ALU.mult),
                             reads=[BPO, Brl], writes=[BOF])
                        r0 = 256 + hf * 64
                        out_toks.append(S.dma("sp", T["oT_tile"](r0, r0 + 64, t), OF[0:64, :], reads=[BOF], writes=[Bo]))
                        return
                    S.op("pe", lambda e, PO=PO, VT=VT, PT=PT, j=j, nblk=nblk, dv=dv: e.matmul(
                        PO[0:dv, 0:TW], VT[:, j, 0:dv], PT[:], start=(j == 0), stop=(j == nblk - 1)),
                        reads=[BVT, BPT], writes=[BPO], inc=False)
                    on = ones_b0 if j == 0 else ones_bf
                    S.op("pe", lambda e, PL=PL, on=on, PT=PT, j=j, nblk=nblk, dv=dv: e.matmul(
                        PL[0:dv, 0:TW], on[:, 0:dv], PT[:], start=(j == 0), stop=(j == nblk - 1)),
                        reads=[Bc, BPT], writes=[BPL], inc=True)
                    if j != nblk - 1:
                        return
                    S.op("dve", lambda e, PL=PL, dv=dv: e.tensor_scalar(rl[0:dv, :], PL[0:dv, 0:TW], 1e-30, None, ALU.add), reads=[BPL], writes=[Brl])
                    S.op("dve", lambda e, dv=dv: e.reciprocal(rl[0:dv, :], rl[0:dv, :]), reads=[Brl], writes=[Brl])
                    if not isdiff:
                        OF = ofin[(2 * t + ui) % 2]; BOF = Bofin[(2 * t + ui) % 2]
                        S.op("dve", lambda e, OF=OF, PO=PO: e.tensor_tensor(OF[0:64, :], PO[0:64, 0:TW], rl[0:64, :], ALU.mult),
                             reads=[BPO, Brl], writes=[BOF])
                        r0 = 256 + hf * 64
                        out_toks.append(S.dma("sp", T["oT_tile"](r0, r0 + 64, t), OF[0:64, :], reads=[BOF], writes=[Bo]))
                        return
                    AA = a0 if ui == 0 else a1
                    BA = Ba0 if ui == 0 else Ba1
                    S.op("dve", lambda e, AA=AA, PO=PO: e.tensor_tensor(AA[:], PO[:, 0:TW], rl[:], ALU.mult),
                         reads=[BPO, Brl], writes=[BA])
                    if ui == 0:
                        return
                    OF = ofin[t % 2]; BOF = Bofin[t % 2]
                    S.op("dve", lambda e: e.scalar_tensor_tensor(od[:], a1[:], neglam[:, 0:1], a0[:], ALU.mult, ALU.add),
                         reads=[Ba0, Ba1, Blam], writes=[Bod])
                    S.op("pool", lambda e: e.tensor_tensor(sq[:], od[:], od[:], ALU.mult), reads=[Bod], writes=[Bsq])

                    def tail(OF=OF, BOF=BOF, t=t):
                        S.op("pe", lambda e: e.matmul(ps_x[:, 0:TW], ones_f[:], sq[:], start=True, stop=True),
                             reads=[Bc, Bsq], writes=[Bps_x])
                        S.op("act", lambda e: e.activation(rstd[:], ps_x[:, 0:TW], AF.Ln, bias=RMS_EPS, scale=1.0 / 128.0),
                             reads=[Bps_x], writes=[Brstd])
                        S.op("act", lambda e: e.activation(rstd[:], rstd[:], AF.Exp, scale=-0.5), reads=[Brstd], writes=[Brstd])
                        S.op("dve", lambda e, OF=OF: e.scalar_tensor_tensor(OF[:], od[:], gcol[:, 0:1], rstd[:], ALU.mult, ALU.mult),
                             reads=[Bod, Brstd, Blam], writes=[BOF])
                        out_toks.append(S.dma("sp", T["oT_tile"](g * 128, (g + 1) * 128, t), OF[:], reads=[BOF], writes=[Bo]))
                    deferred.append((idx + 3, tail))

                LA = 2
                pend = {}
                n_it = len(items)
                for i in range(n_it + LA):
                    if conv_itB is not None and i % 40 == 39:
                        next(conv_itB, None)
                    if i < n_it:
                        pend[i] = stage_a(*items[i])
                    k = i - LA
                    if k >= 0:
                        PT, BPT = pend.pop(k)
                        stage_b(k, *items[k], PT, BPT)
                    while deferred and deferred[0][0] <= k:
                        deferred.pop(0)[1]()
                while deferred:
                    deferred.pop(0)[1]()
            if conv_itB is not None:
                for _ in conv_itB:
                    pass
            S.barrier()
            S.flush()
    return out_toks


def l1_consts(s):
    p = np.arange(128, dtype=np.float64)[:, None]
    slopes = [2.0 ** (-8.0 * (h + 1) / 4) for h in (2 * s, 2 * s + 1)]
    dbias = np.zeros((128, 2, 66), np.float32)
    Fd = np.zeros((128, 2, 3, TW), np.float32)
    qaug = np.zeros((2, 3, LR), np.float32)
    i = np.arange(TW, dtype=np.float64)[None, :]
    r = np.arange(LR)
    w = r % TW
    for hl, sl in enumerate(slopes):
        for idx in range(66):
            dbias[:, hl, idx] = (sl * (128 * (idx - 63) + p))[:, 0]
        for jj in range(3):
            rk = 128 * jj + p
            vis = (rk // 64) <= (i // 64)
            f = np.where(i >= rk, 0.0, 2 * sl * (i - rk))
            Fd[:, hl, jj, :] = np.where(vis, f, NEG)
        qaug[hl, 0] = -(8 * sl) * 256 * (w // 256)
        qaug[hl, 1] = -(8 * sl) * (w % 256)
    Ff = np.zeros((128, 3, TW), np.float32)
    for jj in range(3):
        rk = 128 * jj + p
        Ff[:, jj, :] = np.where(rk <= i, 0.0, NEG)
    return (dbias.reshape(128, 132), Fd.reshape(128, 6 * TW), Ff.reshape(128, 3 * TW), qaug)


def l1_inputs(xT_b, even_w_in, even_f_bias, diff_lambda, diff_subln_g, s):
    w = even_w_in
    dq = [w[:, h * 128:(h + 1) * 128] for h in (2 * s, 2 * s + 1)]
    dk = [w[:, 512 + h * 128:512 + (h + 1) * 128] for h in (2 * s, 2 * s + 1)]
    dvv = [w[:, 1024 + h * 128:1024 + (h + 1) * 128] for h in (2 * s, 2 * s + 1)]
    fq = w[:, 1536 + 256 * s:1536 + 256 * (s + 1)]
    fk = w[:, 2048 + 256 * s:2048 + 256 * (s + 1)]
    fv = w[:, 2560 + 256 * s:2560 + 256 * (s + 1)]
    wf = w[:, 3072 + 4 * s:3072 + 4 * (s + 1)]
    dbias, Fd, Ff, qaug = l1_consts(s)
    return {
        "xT": xT_b,
        "wq": np.ascontiguousarray(np.concatenate(dq + [fq], axis=1)),
        "wk": np.ascontiguousarray(np.concatenate(dk + [fk], axis=1)),
        "wv": np.ascontiguousarray(np.concatenate(dvv + [fv], axis=1)),
        "wf": np.ascontiguousarray(wf),
        "fbias": np.ascontiguousarray(even_f_bias[4 * s:4 * (s + 1)].reshape(4, 1)),
        "lamv": np.ascontiguousarray(np.broadcast_to(diff_lambda.reshape(1, 256), (128, 256))),
        "gsub": np.ascontiguousarray(diff_subln_g.reshape(128, 1)),
        "dbias": dbias, "Fd": Fd, "Ff": Ff, "qaug": qaug,
    }


def emit_convert(S, C, src, dst, BD, rows, cols, tag, shared=None):
    if shared is None:
        stg = [C.sb([128, 2048], BF16) for _ in range(2)]
        Bs = [Buf(), Buf()]
    else:
        stg, Bs = shared
    for _ in iter_convert(S, stg, Bs, src, dst, BD, rows, cols):
        pass


def iter_convert(S, stg, Bs, src, dst, BD, rows, cols, ctr=[0]):
    for r0 in range(0, rows, 128):
        for c0 in range(0, cols, 2048):
            w = min(2048, cols - c0)
            T = stg[ctr[0] % 2]; B = Bs[ctr[0] % 2]
            ctr[0] += 1
            S.dma("pool", T[:, 0:w], src[r0:r0 + 128, c0:c0 + w], writes=[B])
            S.dma("sp", dst[r0:r0 + 128, c0:c0 + w], T[:, 0:w], reads=[B], writes=[BD])
            yield


def emit_ln(S, y, By, yb, Byb, sqb, Bsqb, tmpn, Btmpn, out_f, Bof, out_b, Bob, gcols, bcols, Bp, ones_b, Bc, ps1, Bps1, ps2, Bps2,
            mean, msq, rstd, Bst, nfeat_chunks=8):
    n = nfeat_chunks
    for c in range(n):
        S.op("act", lambda e, c=c: e.copy(yb[:, c, :], y[:, c, :]), reads=[By[c]], writes=[Byb[c]])
        S.op("act", lambda e, c=c: e.activation(sqb[:, c, :], y[:, c, :], AF.Square), reads=[By[c]], writes=[Bsqb[c]])
    for c in range(n):
        S.op("pe", lambda e, c=c: e.matmul(ps1[:, 0:TW], ones_b[:], yb[:, c, :], start=(c == 0), stop=(c == n - 1)),
             reads=[Bc, Byb[c]], writes=[Bps1], inc=(c == n - 1))
    for c in range(n):
        S.op("pe", lambda e, c=c: e.matmul(ps2[:, 0:TW], ones_b[:], sqb[:, c, :], start=(c == 0), stop=(c == n - 1)),
             reads=[Bc, Bsqb[c]], writes=[Bps2], inc=(c == n - 1))
    inv = 1.0 / (128.0 * n)
    S.op("dve", lambda e: e.tensor_scalar(mean[:], ps1[:, 0:TW], inv, None, ALU.mult), reads=[Bps1], writes=[Bst])
    S.op("dve", lambda e: e.tensor_tensor(msq[:], mean[:], mean[:], ALU.mult), reads=[Bst], writes=[Bst])
    S.op("dve", lambda e: e.scalar_tensor_tensor(msq[:], ps2[:, 0:TW], inv, msq[:], ALU.mult, ALU.subtract),
         reads=[Bps2, Bst], writes=[Bst])
    S.op("act", lambda e: e.activation(rstd[:], msq[:], AF.Ln, bias=LN_EPS, scale=1.0), reads=[Bst], writes=[Bst])
    S.op("act", lambda e: e.activation(rstd[:], rstd[:], AF.Exp, scale=-0.5), reads=[Bst], writes=[Bst])
    for c in range(n):
        eng = "dve" if c % 2 == 0 else "pool"
        S.op(eng, lambda e, c=c: e.tensor_tensor(tmpn[:, c, :], y[:, c, :], mean[:], ALU.subtract), reads=[By[c], Bst], writes=[Btmpn[c]])
        S.op(eng, lambda e, c=c: e.tensor_tensor(tmpn[:, c, :], tmpn[:, c, :], rstd[:], ALU.mult), reads=[Bst, Btmpn[c]], writes=[Btmpn[c]])
        S.op("dve", lambda e, c=c: e.tensor_scalar(out_f[:, c, :], tmpn[:, c, :], gcols[:, c:c + 1], bcols[:, c:c + 1], ALU.mult, ALU.add),
             reads=[Btmpn[c], Bp], writes=[Bof[c]])
        if out_b is not None:
            S.op("act", lambda e, c=c: e.copy(out_b[:, c, :], out_f[:, c, :]), reads=[Bof[c]], writes=[Bob[c]])


def emit_post(nc, S, st, ntiles, KC, src_o, src_mode, Bsrc, hres, wout_b, w1_b, w2_b, Bwb, lnp, hout, Bhout, flags=None):
    C = Ctx(nc, st)
    ot = [C.sb([128, KC, TW], BF16) for _ in range(2)]
    hr = [C.sb([128, 8, TW], F32) for _ in range(2)]
    y = C.sb([128, 8, TW], F32)
    ysq = C.sb([128, 8, TW], F32)
    h1 = C.sb([128, 8, TW], F32)
    h1b = C.sb([128, 8, TW], BF16)
    h2 = C.sb([128, 8, TW], F32)
    yb = C.sb([128, 8, TW], BF16)
    sqb = C.sb([128, 8, TW], BF16)
    hid = C.sb([128, 32, TW], BF16)
    rl = [C.sb([128, TW], F32) for _ in range(2)]
    WCOL = 4096 // KC
    wo = [C.sb([128, KC, WCOL], BF16) for _ in range(2)]
    w1c = [C.sb([128, 8, 512], BF16) for _ in range(2)]
    w2c = [C.sb([128, 8, 512], BF16) for _ in range(2)]
    lnp_sb = C.sb([128, 32], F32)
    ones_f = C.sb([128, 128], BF16)
    if src_mode == "blend":
        cand = [C.sb([128, KC, TW], BF16) for _ in range(2)]
        fl_sb = C.sb([128, 2], F32)
        Bcand = [Buf(), Buf()]
        Bfl = Buf()
        S.dma("sp", fl_sb[:], flags, writes=[Bfl])
    mean = C.sb([128, TW], F32)
    msq = C.sb([128, TW], F32)
    rstd = C.sb([128, TW], F32)
    pa = [C.ps([128, 512]) for _ in range(2)]
    pb = [C.ps([128, 512]) for _ in range(4)]
    ps1 = C.ps([128, 512])
    ps2 = C.ps([128, 512])
    Bot = [Buf(), Buf()]; Bhr = [Buf(), Buf()]
    L8 = lambda: [Buf() for _ in range(8)]
    By = L8(); Bysq = L8(); Bh1 = L8(); Bh1b = L8(); Bh2 = L8(); Byb = L8(); Bsqb = L8()
    Bhid = [Buf() for _ in range(32)]; Brl = [Buf(), Buf()]; Bwo = [Buf(), Buf()]; Bw1 = [Buf(), Buf()]; Bw2 = [Buf(), Buf()]
    Bp = Buf(); Bc = Buf(); Bst = Buf(); Bpa = [PB(), PB()]; Bpb = [PB() for _ in range(4)]; Bps1 = PB(); Bps2 = PB()
    S.dma("sp", lnp_sb[:], lnp, writes=[Bp])
    S.op("pool", lambda e: e.memset(ones_f[:], 1.0), writes=[Bc])
    wr = "(c p) o -> p c o"
    wo_v = wout_b.rearrange(wr, p=128)
    w1_v = w1_b.rearrange(wr, p=128)
    w2_v = w2_b.rearrange(wr, p=128)
    so_v = None if src_mode == "blend" else src_o.rearrange("(c p) t -> p c t", p=128)
    hr_v = hres.rearrange("(c p) t -> p c t", p=128)
    ho_v = hout.rearrange("(c p) t -> p c t", p=128)
    ai = 0
    wi = 0
    w1i = 0
    w2i = 0
    ri = 0
    toks = []
    for t in range(ntiles):
        c0 = t * TW
        OT = ot[t % 2]; BOT = Bot[t % 2]; HR = hr[t % 2]; BHR = Bhr[t % 2]
        if src_mode == "blend":
            for k in range(2):
                S.dma("sp", cand[k][:], src_o(k, t), reads=[Bsrc], writes=[Bcand[k]])
            for kc in range(KC):
                S.op("dve", lambda e, kc=kc, OT=OT: e.tensor_scalar(OT[:, kc, :], cand[0][:, kc, :], fl_sb[:, 0:1], None, ALU.mult),
                     reads=[Bcand[0], Bfl], writes=[BOT])
                S.op("dve", lambda e, kc=kc, OT=OT: e.scalar_tensor_tensor(OT[:, kc, :], cand[1][:, kc, :], fl_sb[:, 1:2], OT[:, kc, :], ALU.mult, ALU.add),
                     reads=[Bcand[1], Bfl, BOT], writes=[BOT])
        else:
            S.dma("pool" if src_mode == "f32" else "sp", OT[:], so_v[:, :, c0:c0 + TW], reads=[Bsrc], writes=[BOT])
        S.dma("sp", HR[:], hr_v[:, :, c0:c0 + TW], writes=[BHR])
        for n2 in range(1024 // WCOL):
            WO = wo[wi % 2]; BWO = Bwo[wi % 2]
            wi += 1
            S.dma("sp", WO[:], wo_v[:, :, n2 * WCOL:(n2 + 1) * WCOL], reads=[Bwb], writes=[BWO])
            for c4 in range(WCOL // 128):
                cc = n2 * (WCOL // 128) + c4
                P = pa[ai % 2]; BP = Bpa[ai % 2]
                ai += 1
                for kc in range(KC):
                    S.op("pe", lambda e, P=P, WO=WO, kc=kc, c4=c4, OT=OT: e.matmul(
                        P[:, 0:TW], WO[:, kc, c4 * 128:(c4 + 1) * 128], OT[:, kc, :], start=(kc == 0), stop=(kc == KC - 1)),
                        reads=[BWO, BOT], writes=[BP], inc=(kc == KC - 1))
                S.op("dve", lambda e, P=P, cc=cc, HR=HR: e.scalar_tensor_tensor(
                    y[:, cc, :], HR[:, cc, :], ALPHA, P[:, 0:TW], ALU.mult, ALU.add), reads=[BP, BHR], writes=[By[cc]])
        emit_ln(S, y, By, yb, Byb, sqb, Bsqb, ysq, Bysq, h1, Bh1, h1b, Bh1b, lnp_sb[:, 0:8], lnp_sb[:, 8:16], Bp, ones_f, Bc,
                ps1, Bps1, ps2, Bps2, mean, msq, rstd, Bst)
        for hg in range(8):
            W1 = w1c[w1i % 2]; BW1 = Bw1[w1i % 2]
            w1i += 1
            S.dma("sp", W1[:], w1_v[:, :, hg * 512:(hg + 1) * 512], reads=[Bwb], writes=[BW1])
            for h4 in range(4):
                hc = hg * 4 + h4
                P = pa[ai % 2]; BP = Bpa[ai % 2]
                ai += 1
                for fc in range(8):
                    S.op("pe", lambda e, P=P, W1=W1, fc=fc, h4=h4: e.matmul(
                        P[:, 0:TW], W1[:, fc, h4 * 128:(h4 + 1) * 128], h1b[:, fc, :], start=(fc == 0), stop=(fc == 7)),
                        reads=[BW1, Bh1b[fc]], writes=[BP], inc=(fc == 7))
                R = rl[ri % 2]; BR = Brl[ri % 2]
                ri += 1
                S.op("act", lambda e, R=R, P=P: e.activation(R[:], P[:, 0:TW], AF.Relu), reads=[BP], writes=[BR])
                S.op("pool", lambda e, R=R, hc=hc: e.tensor_tensor(hid[:, hc, :], R[:], R[:], ALU.mult), reads=[BR], writes=[Bhid[hc]])
        for n2 in range(2):
            for kg in range(4):
                W2 = w2c[w2i % 2]; BW2 = Bw2[w2i % 2]
                w2i += 1
                S.dma("sp", W2[:], w2_v[:, kg * 8:(kg + 1) * 8, n2 * 512:(n2 + 1) * 512], reads=[Bwb], writes=[BW2])
                for c4 in range(4):
                    for k8 in range(8):
                        hc = kg * 8 + k8
                        S.op("pe", lambda e, c4=c4, W2=W2, k8=k8, hc=hc: e.matmul(
                            pb[c4][:, 0:TW], W2[:, k8, c4 * 128:(c4 + 1) * 128], hid[:, hc, :], start=(hc == 0), stop=(hc == 31)),
                            reads=[BW2, Bhid[hc]], writes=[Bpb[c4]], inc=(k8 == 7))
            for c4 in range(4):
                cc = n2 * 4 + c4
                S.op("dve", lambda e, c4=c4, cc=cc: e.scalar_tensor_tensor(
                    y[:, cc, :], h1[:, cc, :], ALPHA, pb[c4][:, 0:TW], ALU.mult, ALU.add), reads=[Bpb[c4], Bh1[cc]], writes=[By[cc]])
        emit_ln(S, y, By, yb, Byb, sqb, Bsqb, ysq, Bysq, h2, Bh2, None, None, lnp_sb[:, 16:24], lnp_sb[:, 24:32], Bp, ones_f, Bc,
                ps1, Bps1, ps2, Bps2, mean, msq, rstd, Bst)
        toks.append(S.dma("sp", ho_v[:, :, c0:c0 + TW], h2[:], reads=Bh2, writes=[Bhout]))
    return toks


L2_STAGE = 4
L3_STAGE = 3
L3_SUB = 3
L3_NCH = 12
RET_GAMMA = [1.0 - 2.0 ** (-5.0 - h) for h in range(4)]


def emit_state_tile(S, h2b, Bh2b, wk_sb, wv_sb, Bw, kts, Bkts, vts, Bvts, kdec_sb, Bc, Sst, BS, pk, Bpk, ctr):
    for bl in range(3):
        for half in range(2):
            P = pk[ctr[0] % 2]; BP = Bpk[ctr[0] % 2]
            ctr[0] += 1
            for fc in range(8):
                S.op("pe", lambda e, P=P, fc=fc, bl=bl, half=half: e.matmul(
                    P[:], h2b[:, fc, bl * 128:(bl + 1) * 128], wk_sb[:, fc, half * 512:(half + 1) * 512], start=(fc == 0), stop=(fc == 7)),
                    reads=[Bh2b, Bw], writes=[BP], inc=(fc == 7))
            for hh in range(2):
                h = half * 2 + hh
                S.op("dve", lambda e, P=P, bl=bl, h=h, hh=hh: e.tensor_scalar(
                    kts[:, bl, h * 256:(h + 1) * 256], P[:, hh * 256:(hh + 1) * 256], kdec_sb[:, bl * 4 + h:bl * 4 + h + 1], None, ALU.mult),
                    reads=[BP, Bc], writes=[Bkts])
        for h in range(4):
            P = pk[ctr[0] % 2]; BP = Bpk[ctr[0] % 2]
            ctr[0] += 1
            for fc in range(8):
                S.op("pe", lambda e, P=P, fc=fc, bl=bl, h=h: e.matmul(
                    P[:], h2b[:, fc, bl * 128:(bl + 1) * 128], wv_sb[:, fc, h * 512:(h + 1) * 512], start=(fc == 0), stop=(fc == 7)),
                    reads=[Bh2b, Bw], writes=[BP], inc=(fc == 7))
            S.op("act", lambda e, P=P, bl=bl, h=h: e.copy(vts[:, bl, h * 512:(h + 1) * 512], P[:]), reads=[BP], writes=[Bvts])


def emit_state_update(S, kts, Bkts, vts, Bvts, Sst, BS, Sb, BSb, pk, Bpk, ctr):
    for h in range(4):
        c384 = RET_GAMMA[h] ** TW
        for dkc in range(2):
            P = pk[ctr[0] % 2]; BP = Bpk[ctr[0] % 2]
            ctr[0] += 1
            for bl in range(3):
                S.op("pe", lambda e, P=P, bl=bl, h=h, dkc=dkc: e.matmul(
                    P[:], kts[:, bl, h * 256 + dkc * 128:h * 256 + (dkc + 1) * 128], vts[:, bl, h * 512:(h + 1) * 512],
                    start=(bl == 0), stop=(bl == 2)), reads=[Bkts, Bvts], writes=[BP], inc=(bl == 2))
            idx = h * 2 + dkc
            S.op("dve", lambda e, P=P, idx=idx, c384=c384: e.scalar_tensor_tensor(
                Sst[:, idx, :], Sst[:, idx, :], c384, P[:], ALU.mult, ALU.add), reads=[BP, BS], writes=[BS])
            if Sb is not None:
                S.op("pool", lambda e, idx=idx: e.tensor_copy(Sb[:, idx, :], Sst[:, idx, :]), reads=[BS], writes=[BSb])


def emit_prepass(nc, S, T, Bhout, BSe):
    h2T = T["h2T"]; wk = T["wk"]; wv = T["wv"]; kdec = T["kdec"]; tmask = T["tmask"]; S_end = T["S_end"]
    if True:
        with contextlib.ExitStack() as st:
            C = Ctx(nc, st)
            wk_sb = C.sb([128, 8, 1024], BF16)
            wv_sb = C.sb([128, 8, 2048], BF16)
            h2b = [C.sb([128, 8, TW], BF16) for _ in range(2)]
            kts = C.sb([128, 3, 1024], BF16)
            vts = C.sb([128, 3, 2048], BF16)
            kdec_sb = C.sb([128, 12], F32)
            tm_sb = C.sb([128, TW], F32)
            Sst = C.sb([128, 8, 512], F32)
            pk = [C.ps([128, 512]) for _ in range(2)]
            Bw = Buf(); Bh2b = [Buf(), Buf()]; Bkts = Buf(); Bvts = Buf(); Bc = Buf(); BS = Buf(); Bpk = [PB(), PB()]
            wr = "(c p) o -> p c o"
            wqueue = T.get("wqueue", "pool")
            for fc in range(8):
                S.dma(wqueue, wk_sb[:, fc, :], wk[fc * 128:(fc + 1) * 128, :], reads=[T["Bwb"]], writes=[Bw])
                S.dma(wqueue, wv_sb[:, fc, :], wv[fc * 128:(fc + 1) * 128, :], reads=[T["Bwb"]], writes=[Bw])
            S.dma("sp", kdec_sb[:], kdec, writes=[Bc])
            S.dma("sp", tm_sb[:], tmask, writes=[Bc])
            for i8 in range(8):
                S.op("dve", lambda e, i8=i8: e.memset(Sst[:, i8, :], 0.0), writes=[BS])
            hv = h2T.rearrange("(c p) t -> p c t", p=128)
            ctr = [0]
            for t in range(NTH):
                c0 = t * TW
                H = h2b[t % 2]; BH = Bh2b[t % 2]
                S.dma("pool", H[:], hv[:, :, c0:c0 + TW], reads=[Bhout], writes=[BH])
                if t == 0:
                    for fc in range(8):
                        S.op("dve", lambda e, H=H, fc=fc: e.tensor_tensor(H[:, fc, :], H[:, fc, :], tm_sb[:], ALU.mult),
                             reads=[BH, Bc], writes=[BH])
                emit_state_tile(S, H, BH, wk_sb, wv_sb, Bw, kts, Bkts, vts, Bvts, kdec_sb, Bc, Sst, BS, pk, Bpk, ctr)
                emit_state_update(S, kts, Bkts, vts, Bvts, Sst, BS, None, None, pk, Bpk, ctr)
            for k in range(4):
                S.dma("sp", S_end[k], Sst[32 * k:32 * (k + 1)], reads=[BS], writes=[BSe])
            S.barrier()
            S.flush()

def ret_consts():
    p = np.arange(128, dtype=np.float64)
    kdec = np.zeros((128, 3, 4), np.float32)
    for h in range(4):
        g = RET_GAMMA[h]
        for bl in range(3):
            kdec[:, bl, h] = g ** (TW - 1 - (bl * 128 + p)) / 16.0
    return kdec.reshape(128, 12)


def lnp_table(ln_g, ln_b, layer):
    cols = []
    for k in range(2):
        cols.append(ln_g[layer, k].reshape(8, 128).T)
        cols.append(ln_b[layer, k].reshape(8, 128).T)
    return np.ascontiguousarray(np.concatenate(cols, axis=1).astype(np.float32))


def emit_retention(nc, S, st, T, Bwb, Bh2, BSg, ByT):
    h2T = T["h2T"]; wi_b = T["wi_b"]; kdec = T["kdec"]; tmask = T["tmask"]; dmask = T["dmask"]; qdec = T["qdec"]
    yT_s = T["yT_s"]; S_src = T["S_src"]; flags = T["flags"]
    C = Ctx(nc, st)
    if True:
        if True:
            h2b = [C.sb([128, 8, TW], BF16) for _ in range(2)]
            wc = [C.sb([128, 8, 512], BF16) for _ in range(2)]
            qT = C.sb([128, 8, TW], BF16)
            qdT = C.sb([128, 8, TW], BF16)
            kT = C.sb([128, 8, TW], BF16)
            kts = C.sb([128, 3, 1024], BF16)
            vts = C.sb([128, 3, 2048], BF16)
            sg = C.sb([128, 16, TW], BF16)
            sTb = [C.sb([128, 3, TW], BF16) for _ in range(4)]
            o32 = [C.sb([128, 4, TW], F32) for _ in range(2)]
            osq = [C.sb([128, 4, TW], BF16) for _ in range(2)]
            yT = [C.sb([128, 16, TW], BF16) for _ in range(2)]
            Sst = C.sb([128, 8, 512], F32)
            Sb = C.sb([128, 8, 512], BF16)
            dm_sb = C.sb([128, 12 * TW], F32)
            qd_sb = C.sb([128, 4 * TW], F32)
            kdec_sb = C.sb([128, 12], F32)
            tm_sb = C.sb([128, TW], F32)
            ones_f = C.sb([128, 128], BF16)
            rstd = [C.sb([128, TW], F32) for _ in range(2)]
            tmpo = C.sb([128, TW], F32)
            pk = [C.ps([128, 512]) for _ in range(2)]
            psc = [C.ps([128, 512]) for _ in range(2)]
            po = [C.ps([128, 512]) for _ in range(2)]
            pss = C.ps([128, 512])
            Bh2b = [Buf(), Buf()]; Bwc = [Buf(), Buf()]; BqT = Buf(); BqdT = Buf(); BkT = Buf(); Bkts = Buf(); Bvts = Buf()
            Bsg = Buf(); BsTb = [Buf() for _ in range(4)]; Bo32 = [Buf(), Buf()]; Bosq = [Buf(), Buf()]; ByTt = [Buf(), Buf()]; BS = Buf(); BSb = Buf()
            Bc = Buf(); Brstd = [Buf(), Buf()]; Btmpo = Buf(); Bpk = [PB(), PB()]; Bpsc = [PB(), PB()]; Bpo = [PB(), PB()]; Bpss = PB()
            S.dma("sp", dm_sb[:], dmask, writes=[Bc])
            S.dma("sp", qd_sb[:], qdec, writes=[Bc])
            S.dma("sp", kdec_sb[:], kdec, writes=[Bc])
            S.dma("sp", tm_sb[:], tmask, writes=[Bc])
            S.op("pool", lambda e: e.memset(ones_f[:], 1.0), writes=[Bc])
            fl_sb = C.sb([128, 2], F32)
            S.dma("sp", fl_sb[:], flags, writes=[Bc])
            for k in range(4):
                S.dma("sp", Sst[32 * k:32 * (k + 1)], S_src[k], reads=[BSg], writes=[BS])
            for i8 in range(8):
                S.op("dve", lambda e, i8=i8: e.tensor_scalar(Sst[:, i8, :], Sst[:, i8, :], fl_sb[:, 1:2], None, ALU.mult),
                     reads=[BS, Bc], writes=[BS])
            for i8 in range(8):
                S.op("pool", lambda e, i8=i8: e.tensor_copy(Sb[:, i8, :], Sst[:, i8, :]), reads=[BS], writes=[BSb])
            hv = h2T.rearrange("(c p) t -> p c t", p=128)
            wiv = wi_b.rearrange("(c p) o -> p c o", p=128)
            yv = yT_s.rearrange("(c p) t -> p c t", p=128)
            ctr = [0]
            wci = 0
            sci = 0
            oi = 0
            for t in range(NTH if L3_STAGE >= 2 else 1):
                c0 = t * TW
                H = h2b[t % 2]; BH = Bh2b[t % 2]
                Y = yT[t % 2]; BY = ByTt[t % 2]
                S.dma("pool", H[:], hv[:, :, c0:c0 + TW], reads=[Bh2], writes=[BH])
                if t == 0:
                    for fc in range(8):
                        S.op("dve", lambda e, H=H, fc=fc: e.tensor_tensor(H[:, fc, :], H[:, fc, :], tm_sb[:], ALU.mult),
                             reads=[BH, Bc], writes=[BH])
                for ch in range(L3_NCH):
                    W = wc[wci % 2]; BW = Bwc[wci % 2]
                    wci += 1
                    S.dma("sp", W[:], wiv[:, :, ch * 512:(ch + 1) * 512], reads=[Bwb], writes=[BW])
                    if ch < 4 or ch >= 8:
                        for c4 in range(4):
                            P = pk[ctr[0] % 2]; BP = Bpk[ctr[0] % 2]
                            ctr[0] += 1
                            for fc in range(8):
                                S.op("pe", lambda e, P=P, W=W, fc=fc, c4=c4, H=H: e.matmul(
                                    P[:, 0:TW], W[:, fc, c4 * 128:(c4 + 1) * 128], H[:, fc, :], start=(fc == 0), stop=(fc == 7)),
                                    reads=[BW, BH], writes=[BP], inc=(fc == 7))
                            if ch < 2:
                                ci = ch * 4 + c4
                                h = ci // 2
                                S.op("act", lambda e, P=P, ci=ci: e.copy(qT[:, ci, :], P[:, 0:TW]), reads=[BP], writes=[BqT])
                                S.op("dve", lambda e, P=P, ci=ci, h=h: e.tensor_tensor(qdT[:, ci, :], P[:, 0:TW], qd_sb[:, h * TW:(h + 1) * TW], ALU.mult),
                                     reads=[BP, Bc], writes=[BqdT])
                            elif ch < 4:
                                ci = (ch - 2) * 4 + c4
                                S.op("act", lambda e, P=P, ci=ci: e.copy(kT[:, ci, :], P[:, 0:TW]), reads=[BP], writes=[BkT])
                            else:
                                gi = (ch - 8) * 4 + c4
                                S.op("act", lambda e, P=P, gi=gi: e.activation(sg[:, gi, :], P[:, 0:TW], AF.Silu), reads=[BP], writes=[Bsg])
                    if 2 <= ch < 4:
                        half = ch - 2
                        for bl in range(3):
                            P = pk[ctr[0] % 2]; BP = Bpk[ctr[0] % 2]
                            ctr[0] += 1
                            for fc in range(8):
                                S.op("pe", lambda e, P=P, W=W, fc=fc, bl=bl, H=H: e.matmul(
                                    P[:], H[:, fc, bl * 128:(bl + 1) * 128], W[:, fc, :], start=(fc == 0), stop=(fc == 7)),
                                    reads=[BH, BW], writes=[BP], inc=(fc == 7))
                            for hh in range(2):
                                h = half * 2 + hh
                                S.op("dve", lambda e, P=P, bl=bl, h=h, hh=hh: e.tensor_scalar(
                                    kts[:, bl, h * 256:(h + 1) * 256], P[:, hh * 256:(hh + 1) * 256], kdec_sb[:, bl * 4 + h:bl * 4 + h + 1], None, ALU.mult),
                                    reads=[BP, Bc], writes=[Bkts])
                    if 4 <= ch < 8:
                        h = ch - 4
                        for bl in range(3):
                            P = pk[ctr[0] % 2]; BP = Bpk[ctr[0] % 2]
                            ctr[0] += 1
                            for fc in range(8):
                                S.op("pe", lambda e, P=P, W=W, fc=fc, bl=bl, H=H: e.matmul(
                                    P[:], H[:, fc, bl * 128:(bl + 1) * 128], W[:, fc, :], start=(fc == 0), stop=(fc == 7)),
                                    reads=[BH, BW], writes=[BP], inc=(fc == 7))
                            S.op("act", lambda e, P=P, bl=bl, h=h: e.copy(vts[:, bl, h * 512:(h + 1) * 512], P[:]), reads=[BP], writes=[Bvts])
                def st1(h):
                    nonlocal sci
                    ST = sTb[h]; BST = BsTb[h]
                    for jb in range(3):
                        P = psc[sci % 2]; BP = Bpsc[sci % 2]
                        sci += 1
                        for dc in range(2):
                            S.op("pe", lambda e, P=P, h=h, dc=dc, jb=jb: e.matmul(
                                P[:, 0:TW], kT[:, h * 2 + dc, jb * 128:(jb + 1) * 128], qT[:, h * 2 + dc, :], start=(dc == 0), stop=(dc == 1)),
                                reads=[BkT, BqT], writes=[BP], inc=(dc == 1))
                        S.op("dve", lambda e, P=P, ST=ST, jb=jb, h=h: e.tensor_tensor(
                            ST[:, jb, :], P[:, 0:TW], dm_sb[:, (h * 3 + jb) * TW:(h * 3 + jb + 1) * TW], ALU.mult),
                            reads=[BP, Bc], writes=[BST])

                def st2(h):
                    nonlocal oi
                    ST = sTb[h]; BST = BsTb[h]
                    O32 = o32[h % 2]; OSQ = osq[h % 2]
                    for ec in range(4):
                        P = po[oi % 2]; BP = Bpo[oi % 2]
                        oi += 1
                        for jb in range(3):
                            S.op("pe", lambda e, P=P, h=h, ec=ec, jb=jb, ST=ST: e.matmul(
                                P[:, 0:TW], vts[:, jb, h * 512 + ec * 128:h * 512 + (ec + 1) * 128], ST[:, jb, :], start=(jb == 0), stop=False),
                                reads=[Bvts, BST], writes=[BP], inc=False)
                        for dc in range(2):
                            S.op("pe", lambda e, P=P, h=h, ec=ec, dc=dc: e.matmul(
                                P[:, 0:TW], Sb[:, h * 2 + dc, ec * 128:(ec + 1) * 128], qdT[:, h * 2 + dc, :], start=False, stop=(dc == 1)),
                                reads=[BSb, BqdT], writes=[BP], inc=(dc == 1))
                        S.op("act", lambda e, P=P, ec=ec, O32=O32: e.copy(O32[:, ec, :], P[:, 0:TW]), reads=[BP], writes=[Bo32[h % 2]])
                        S.op("act", lambda e, P=P, ec=ec, OSQ=OSQ: e.activation(OSQ[:, ec, :], P[:, 0:TW], AF.Square), reads=[BP], writes=[Bosq[h % 2]])

                def st3(h, Y=Y, BY=BY):
                    O32 = o32[h % 2]; OSQ = osq[h % 2]; R = rstd[h % 2]; BR = Brstd[h % 2]
                    for ec in range(4):
                        S.op("pe", lambda e, ec=ec, OSQ=OSQ: e.matmul(pss[:, 0:TW], ones_f[:], OSQ[:, ec, :], start=(ec == 0), stop=(ec == 3)),
                             reads=[Bc, Bosq[h % 2]], writes=[Bpss], inc=(ec == 3))
                    S.op("act", lambda e, R=R: e.activation(R[:], pss[:, 0:TW], AF.Ln, bias=RMS_EPS, scale=1.0 / 512.0), reads=[Bpss], writes=[BR])
                    S.op("act", lambda e, R=R: e.activation(R[:], R[:], AF.Exp, scale=-0.5), reads=[BR], writes=[BR])
                    for ec in range(4):
                        S.op("dve", lambda e, ec=ec, O32=O32, R=R: e.tensor_tensor(tmpo[:], O32[:, ec, :], R[:], ALU.mult), reads=[Bo32[h % 2], BR], writes=[Btmpo])
                        S.op("dve", lambda e, ec=ec, h=h, Y=Y: e.tensor_tensor(Y[:, h * 4 + ec, :], tmpo[:], sg[:, h * 4 + ec, :], ALU.mult),
                             reads=[Btmpo, Bsg], writes=[BY])

                if L3_SUB >= 2:
                    for stg_fn, hh in ((st1, 0), (st1, 1), (st2, 0), (st1, 2), (st2, 1), (st3, 0), (st1, 3), (st2, 2), (st3, 1),
                                       (st2, 3), (st3, 2), (st3, 3)):
                        stg_fn(hh)
                if L3_SUB >= 3:
                    emit_state_update(S, kts, Bkts, vts, Bvts, Sst, BS, Sb, BSb, pk, Bpk, ctr)
                if L3_SUB >= 2:
                    S.dma("sp", yv[:, :, c0:c0 + TW], Y[:], reads=[BY], writes=[ByT])
            S.barrier()
            S.flush()

def l3_consts():
    p = np.arange(128, dtype=np.float64)[:, None]
    i = np.arange(TW, dtype=np.float64)[None, :]
    dm = np.zeros((128, 4, 3, TW), np.float32)
    qd = np.zeros((128, 4, TW), np.float32)
    for h in range(4):
        g = RET_GAMMA[h]
        for jb in range(3):
            j = jb * 128 + p
            same = (j // 64) == (i // 64)
            before = (j // 64) < (i // 64)
            val = np.where(same, g ** np.abs(i - j), np.where(before, g ** np.maximum(i - j, 0), 0.0)) / 16.0
            dm[:, h, jb, :] = val
        qd[:, h, :] = g ** (i + 1.0)
    return dm.reshape(128, 12 * TW), qd.reshape(128, 4 * TW)


PAIRS = [[0, 1], [2, 3], [4, 5], [6, 7]]


def build_fused():
    nc = bass.Bass("TRN2", target_bir_lowering=False)

    def din(name, shape):
        return nc.dram_tensor(name, list(shape), F32, kind="ExternalInput").ap()

    T1 = {"xT": din("xT", [D, LR]), "wq": din("wq", [D, 512]), "wk": din("wk", [D, 512]), "wv": din("wv", [D, 512]),
          "wf": din("wf", [D, 4]), "fbias": din("fbias", [4, 1]), "lamv": din("lamv", [128, 256]), "gsub": din("gsub", [128, 1]),
          "dbias": din("dbias", [128, 2 * 66]), "Fd": din("Fd", [128, 2 * 3 * TW]), "Ff": din("Ff", [128, 3 * TW]),
          "qaug": din("qaug", [2, 3, LR]), "odt": BF16}
    hres = din("hres", [D, HALF])
    flags = din("flags", [128, 2])
    w_out0 = din("w_out0", [D, D])
    w1_0 = din("w1_0", [D, 4 * D])
    w2_0 = din("w2_0", [4 * D, D])
    lnp0 = din("lnp0", [128, 32])
    w_in1 = din("w_in1", [D, 6144])
    w_out1 = din("w_out1", [2048, D])
    w1_1 = din("w1_1", [D, 4 * D])
    w2_1 = din("w2_1", [4 * D, D])
    lnp1 = din("lnp1", [128, 32])
    kdec = din("kdec", [128, 12])
    tmask = din("tmask", [128, TW])
    dmask = din("dmask", [128, 12 * TW])
    qdec = din("qdec", [128, 4 * TW])
    outT = nc.dram_tensor("outT", [D, HALF], F32, kind="ExternalOutput").ap()
    wo0_b = nc.dram_tensor("wo0_b", [D, D], BF16).ap()
    w10_b = nc.dram_tensor("w10_b", [D, 4 * D], BF16).ap()
    w20_b = nc.dram_tensor("w20_b", [4 * D, D], BF16).ap()
    wi_b = nc.dram_tensor("wi_b", [D, 6144], BF16).ap()
    wo1_b = nc.dram_tensor("wo1_b", [2048, D], BF16).ap()
    w11_b = nc.dram_tensor("w11_b", [D, 4 * D], BF16).ap()
    w21_b = nc.dram_tensor("w21_b", [4 * D, D], BF16).ap()
    NCH = NT // 2
    oT_c = [nc.dram_tensor("oT_c%d" % k, [512, 2 * TW], BF16) for k in range(NCH)]
    G_c = [nc.dram_tensor("G_c%d" % k, [1024, 2 * TW], BF16) for k in range(NCH)]
    h2T_i = nc.dram_tensor("h2T_i", [D, HALF], F32).ap()
    Se_c = [nc.dram_tensor("Se_c%d" % k, [256, 512], F32) for k in range(4)]
    Sg_c = [nc.dram_tensor("Sg_c%d" % k, [512, 512], F32) for k in range(4)]
    yT_s = nc.dram_tensor("yT_s", [2048, HALF], BF16).ap()
    T1["oT_tile"] = lambda r0, r1, t: oT_c[t // 2].ap()[r0:r1, (t % 2) * TW:(t % 2 + 1) * TW]

    def g_tile(k, t):
        gt = k * NTH + t
        return G_c[gt // 2].ap().rearrange("(c p) t -> p c t", p=128)[:, :, (gt % 2) * TW:(gt % 2 + 1) * TW]
    with contextlib.ExitStack() as outer:
        S = Sched(nc, outer)
        Bwb = Buf(); Bo = Buf(); BG = Buf(); Bh2 = Buf(); BSe = Buf(); BSg = Buf(); ByT = Buf(); Bout = Buf()
        def hookA(C):
            stg = [C.sb([128, 2048], BF16) for _ in range(2)]
            Bs = [Buf(), Buf()]
            for (s_, d_, r_, c_) in ((w_out0, wo0_b, D, D), (w1_0, w10_b, D, 4 * D), (w2_0, w20_b, 4 * D, D)):
                yield from iter_convert(S, stg, Bs, s_, d_, Bwb, r_, c_)

        def hookB(C):
            stg = [C.sb([128, 2048], BF16) for _ in range(2)]
            Bs = [Buf(), Buf()]
            for (s_, d_, r_, c_) in ((w_in1, wi_b, D, 6144), (w_out1, wo1_b, 2048, D), (w1_1, w11_b, D, 4 * D), (w2_1, w21_b, 4 * D, D)):
                yield from iter_convert(S, stg, Bs, s_, d_, Bwb, r_, c_)
        T1["hookA"] = hookA
        T1["hookB"] = hookB
        emit_l1(nc, S, T1, Bo)
        for k in range(NCH):
            S.op("pool", lambda e, k=k: e.collective_compute("AllGather", ALU.bypass, replica_groups=PAIRS,
                                                             ins=[oT_c[k].ap().opt()], outs=[G_c[k].ap().opt()]), reads=[Bo], writes=[BG])
        with contextlib.ExitStack() as st:
            emit_post(nc, S, st, NTH, 8, g_tile, "blend", BG, hres, wo0_b, w10_b, w20_b, Bwb, lnp0, h2T_i, Bh2, flags=flags)
            S.barrier()
            S.flush()
        Tp = {"h2T": h2T_i, "wk": wi_b[:, 1024:2048], "wv": wi_b[:, 2048:4096], "kdec": kdec, "tmask": tmask,
              "S_end": [Se_c[k].ap().rearrange("(p i) f -> p i f", i=8) for k in range(4)], "wqueue": "sp", "Bwb": Bwb}
        emit_prepass(nc, S, Tp, Bh2, BSe)
        for k in range(4):
            S.op("pool", lambda e, k=k: e.collective_compute("AllGather", ALU.bypass, replica_groups=PAIRS,
                                                             ins=[Se_c[k].ap().opt()], outs=[Sg_c[k].ap().opt()]), reads=[BSe], writes=[BSg])
        Tr = {"h2T": h2T_i, "wi_b": wi_b, "kdec": kdec, "tmask": tmask, "dmask": dmask, "qdec": qdec, "yT_s": yT_s,
              "S_src": [Sg_c[k].ap()[0:256, :].rearrange("(p i) f -> p i f", i=8) for k in range(4)], "flags": flags}
        with contextlib.ExitStack() as st:
            emit_retention(nc, S, st, Tr, Bwb, Bh2, BSg, ByT)
        with contextlib.ExitStack() as st:
            emit_post(nc, S, st, NTH, 16, yT_s, "bf16", ByT, h2T_i, wo1_b, w11_b, w21_b, Bwb, lnp1, outT, Bout)
            S.barrier()
            S.flush()
    return nc


def _token_mask(u):
    tm = np.ones((128, TW), np.float32)
    if u == 0:
        tm[:, 0:FPAD] = 0.0
    return tm


def kernel(x, meta_tokens, even_w_in, even_f_bias, diff_lambda, diff_subln_g, even_w_out,
           ret_w_in, ret_w_out, ln_g, ln_b, ffn_w1, ffn_w2):
    x = np.asarray(x, np.float32)
    f32 = lambda a: np.ascontiguousarray(np.asarray(a, np.float32))
    meta = f32(meta_tokens)
    B = x.shape[0]
    cores = list(range(8))
    kdec = ret_consts()
    dm, qd = l3_consts()
    wo = f32(even_w_out[0])
    wo_perm = np.ascontiguousarray(np.concatenate([wo[0:256], wo[512:768], wo[256:512], wo[768:1024]], axis=0))
    shared = {"w_out0": wo_perm, "w1_0": f32(ffn_w1[0]), "w2_0": f32(ffn_w2[0]), "lnp0": lnp_table(f32(ln_g), f32(ln_b), 0),
              "w_in1": f32(ret_w_in[0]), "w_out1": f32(ret_w_out[0]), "w1_1": f32(ffn_w1[1]), "w2_1": f32(ffn_w2[1]),
              "lnp1": lnp_table(f32(ln_g), f32(ln_b), 1), "kdec": kdec, "dmask": dm, "qdec": qd}
    in_maps = []
    for c in cores:
        b, u = c // 2, c % 2
        hp = np.zeros((LR, D), np.float32)
        hp[FPAD:FPAD + NMETA] = meta
        hp[FPAD + NMETA:FPAD + NMETA + SEQ] = x[b]
        xT = np.ascontiguousarray(hp.T)
        m = l1_inputs(xT, f32(even_w_in[0]), f32(even_f_bias[0]), f32(diff_lambda[0]), f32(diff_subln_g[0]), u)
        m["hres"] = np.ascontiguousarray(xT[:, u * HALF:(u + 1) * HALF])
        fl = np.zeros((128, 2), np.float32)
        fl[:, u] = 1.0
        m["flags"] = fl
        m["tmask"] = _token_mask(u)
        m.update(shared)
        in_maps.append(m)
    res = run_bass_kernel_spmd(build_fused(), in_maps, core_ids=cores).results
    out = np.empty((B, SEQ, D), np.float32)
    for b in range(B):
        hT = np.concatenate([res[2 * b]["outT"], res[2 * b + 1]["outT"]], axis=1)
        out[b] = hT[:, FPAD + NMETA:FPAD + NMETA + SEQ].T
    return out
```

```python
import contextlib
import math
import numpy as np
import concourse.bass as bass
import concourse.mybir as mybir
from concourse.bass_utils import run_bass_kernel_spmd

F32 = mybir.dt.float32
BF16 = mybir.dt.bfloat16
AF = mybir.ActivationFunctionType
ALU = mybir.AluOpType

D = 1024
SEQ = 8192
NMETA = 16
FPAD = 48
LR = 8448
TW = 384
NT = LR // TW
NB = LR // 128
HALF = LR // 2
NTH = HALF // TW
ALPHA = 4 ** 0.25
LN_EPS = 1e-5
RMS_EPS = 1e-6
LAM_INIT0 = 0.8 - 0.6 * math.exp(-0.3 * 0)
NEG = -30000.0
REAL_END = FPAD + NMETA + SEQ

ENGS = ("pe", "act", "dve", "pool", "sp")


class Buf:
    __slots__ = ("name", "w", "r", "ex")

    def __init__(self, name="", ex=False):
        self.name = name
        self.w = None
        self.r = []
        self.ex = ex


def PB():
    return Buf(ex=True)


class Sched:
    def __init__(self, nc, stack, n_dma_sems=16):
        self.nc = nc
        self.streams = {e: [] for e in ENGS}
        self.cnt = {e: 0 for e in ENGS}
        self.seen = {e: {} for e in ENGS}
        self.n_dma = n_dma_sems
        self.dma_k = 0
        self.sems = {e: stack.enter_context(nc.semaphore("s_" + e)) for e in ENGS}
        self.dsems = [stack.enter_context(nc.semaphore("d_%d" % i)) for i in range(n_dma_sems)]
        self.dlast = [0] * n_dma_sems

    def _need(self, eng, tok, waits):
        if tok is None:
            return
        kind, key, val = tok
        if kind == "e" and key == eng and eng in ("pe", "sp"):
            return
        k = (kind, key)
        if self.seen[eng].get(k, 0) >= val:
            return
        if val > waits.get(k, 0):
            waits[k] = val

    def _emit_waits(self, eng, waits):
        for k, val in waits.items():
            self.seen[eng][k] = val
            self.streams[eng].append(("wait", k, val))

    def _deps(self, eng, reads, writes):
        waits = {}
        for b in reads:
            self._need(eng, b.w, waits)
        for b in writes:
            self._need(eng, b.w, waits)
            for t in b.r:
                self._need(eng, t, waits)
        return waits

    def _mark(self, tok, reads, writes):
        for b in reads:
            b.r.append(tok)
            if len(b.r) > 24:
                b.r = b.r[-24:]
        for b in writes:
            b.w = tok
            b.r = []

    def op(self, eng, fn, reads=(), writes=(), inc=True):
        if any(b.ex for b in reads):
            writes = list(writes) + [b for b in reads if b.ex]
            reads = [b for b in reads if not b.ex]
        waits = self._deps(eng, reads, writes)
        self._emit_waits(eng, waits)
        if inc:
            self.cnt[eng] += 1
            tok = ("e", eng, self.cnt[eng])
        else:
            tok = ("e", eng, self.cnt[eng] + 1)
        self.streams[eng].append(("op", fn, inc))
        self._mark(tok, reads, writes)
        return tok

    def dma(self, q, out_ap, in_ap, reads=(), writes=(), **kw):
        waits = self._deps(q, reads, writes)
        s = self.dma_k % self.n_dma
        v = self.dlast[s] + 16
        self.dma_k += 1
        if v > 16:
            self._need(q, ("d", s, v - 16), waits)
        self._emit_waits(q, waits)
        self.dlast[s] = v
        tok = ("d", s, v)
        self.streams[q].append(("dma", out_ap, in_ap, s, kw))
        self._mark(tok, reads, writes)
        return tok

    def barrier(self):
        toks = [("e", e, self.cnt[e]) for e in ENGS if self.cnt[e] > 0]
        toks += [("d", s, self.dlast[s]) for s in range(self.n_dma) if self.dlast[s] > 0]
        for e in ENGS:
            waits = {}
            for t in toks:
                if t[0] == "e" and t[1] == e:
                    continue
                self._need(e, t, waits)
            self._emit_waits(e, waits)

    def flush(self):
        nc = self.nc
        with nc.Block() as block:
            def run(e, engobj):
                for item in self.streams[e]:
                    if item[0] == "wait":
                        (kind, key), val = item[1], item[2]
                        sem = self.sems[key] if kind == "e" else self.dsems[key]
                        engobj.wait_ge(sem, val)
                    elif item[0] == "op":
                        ins = item[1](engobj)
                        if item[2]:
                            ins.then_inc(self.sems[e], 1)
                    else:
                        _, o, i, s, kw = item
                        engobj.dma_start(out=o, in_=i, **kw).then_inc(self.dsems[s], 16)

            @block.tensor
            def _(eng):
                run("pe", eng)

            @block.scalar
            def _(eng):
                run("act", eng)

            @block.vector
            def _(eng):
                run("dve", eng)

            @block.gpsimd
            def _(eng):
                run("pool", eng)

            @block.sync
            def _(eng):
                run("sp", eng)
        self.streams = {e: [] for e in ENGS}


class Ctx:
    K = [0]

    def __init__(self, nc, stack):
        self.nc = nc
        self.st = stack

    def sb(self, shape, dt, name=None):
        Ctx.K[0] += 1
        return self.st.enter_context(self.nc.sbuf_tensor(name or ("t%d" % Ctx.K[0]), list(shape), dt))

    def ps(self, shape, dt=F32, name=None):
        Ctx.K[0] += 1
        return self.st.enter_context(self.nc.psum_tensor(name or ("p%d" % Ctx.K[0]), list(shape), dt))


SCALE = 0.125
DEBUG_A = False


def emit_l1(nc, S, T, Bo):
    xT = T["xT"]; wq = T["wq"]; wk = T["wk"]; wv = T["wv"]; wf = T["wf"]; fbias = T["fbias"]
    lamv = T["lamv"]; gsub = T["gsub"]; dbias = T["dbias"]; Fd = T["Fd"]; Ff = T["Ff"]; qaug = T["qaug"]
    ODT = T["odt"]
    QT_s = nc.dram_tensor("QT_s", [8, 64, LR], BF16).ap()
    KT_s = nc.dram_tensor("KT_s", [8, 64, LR], BF16).ap()
    V_s = nc.dram_tensor("V_s", [LR, 512], BF16).ap()
    AQ_s = nc.dram_tensor("AQ_s", [4, 3, LR], BF16).ap()
    AQd_s = nc.dram_tensor("AQd_s", [6, LR], BF16).ap()
    out_toks = []
    if True:
        BQT = [Buf() for _ in range(8)]
        BKT = [Buf() for _ in range(8)]
        BV = Buf()
        BAQ = Buf()
        with contextlib.ExitStack() as st:
            C = Ctx(nc, st)
            wq_sb = C.sb([128, 8, 512], BF16)
            wk_sb = C.sb([128, 8, 512], BF16)
            wv_sb = C.sb([128, 8, 512], BF16)
            wf_sb = C.sb([128, 8, 4], BF16)
            fb_sb = C.sb([4, 1], F32)
            nfb_sb = C.sb([4, 1], F32)
            ident = C.sb([128, 128], F32)
            xt = [C.sb([128, 8, TW], BF16) for _ in range(2)]
            stq = [C.sb([128, TW], BF16) for _ in range(4)]
            stv = [C.sb([128, 512], BF16) for _ in range(2)]
            lf = C.sb([4, LR], F32)
            cs = C.sb([4, LR], F32)
            ones4 = C.sb([4, LR // 4], F32)
            e_t = C.sb([4, TW], F32)
            hi = C.sb([4, LR], BF16)
            mid = C.sb([4, LR], BF16)
            lo = C.sb([4, LR], BF16)
            pq = [C.ps([128, 512]) for _ in range(4)]
            pv = [C.ps([128, 512]) for _ in range(2)]
            pf = C.ps([128, 512])
            Bw = Buf(); Bxt = [Buf(), Buf()]; Bstq = [Buf() for _ in range(4)]; Bstv = [Buf(), Buf()]
            Bpq = [PB() for _ in range(4)]; Bpv = [PB(), PB()]; Bpf = PB(); Blf = Buf(); Bcs = Buf()
            Bet = Buf(); Bfb = Buf(); Bo4 = Buf(); Bhi = Buf(); Bmid = Buf(); Blo = Buf()

            wr = "(c p) o -> p c o"
            S.dma("pool", wq_sb[:], wq.rearrange(wr, p=128), writes=[Bw])
            for fc in range(8):
                S.dma("pool", wk_sb[:, fc, :], wk[fc * 128:(fc + 1) * 128, :], writes=[Bw])
                S.dma("pool", wv_sb[:, fc, :], wv[fc * 128:(fc + 1) * 128, :], writes=[Bw])
            S.dma("pool", wf_sb[:], wf.rearrange(wr, p=128), writes=[Bw])
            S.dma("sp", fb_sb[:], fbias, writes=[Bfb])
            S.op("dve", lambda e: e.tensor_scalar(nfb_sb[:], fb_sb[:], -1.0, None, ALU.mult), reads=[Bfb], writes=[Bfb])
            S.op("dve", lambda e: e.memset(ones4[:], 1.0), writes=[Bo4])
            xTr = xT.rearrange("(c p) t -> p c t", p=128)
            conv_it = T["hookA"](C) if T.get("hookA") else None
            qi = 0
            vi = 0
            for t in range(NT):
                c0 = t * TW
                X = xt[t % 2]; BX = Bxt[t % 2]
                S.dma("pool", X[:], xTr[:, :, c0:c0 + TW], writes=[BX])
                for which, (w_sb, dst, BD) in enumerate(((wq_sb, QT_s, BQT), (wk_sb, KT_s, BKT))):
                    for g in range(4):
                        P = pq[qi % 4]; BP = Bpq[qi % 4]; ST = stq[qi % 4]; BS = Bstq[qi % 4]
                        qi += 1
                        for c in range(8):
                            S.op("pe", lambda e, P=P, w_sb=w_sb, c=c, g=g, X=X: e.matmul(
                                P[:, 0:TW], w_sb[:, c, g * 128:(g + 1) * 128], X[:, c, :], start=(c == 0), stop=(c == 7)),
                                reads=[Bw, BX], writes=[BP], inc=(c == 7))
                        eng = "act" if (g % 2 == 0) else "dve"
                        if eng == "act":
                            S.op("act", lambda e, ST=ST, P=P: e.copy(ST[:], P[:, 0:TW]), reads=[BP], writes=[BS])
                        else:
                            S.op("dve", lambda e, ST=ST, P=P: e.tensor_copy(ST[:], P[:, 0:TW]), reads=[BP], writes=[BS])
                        S.dma("sp", dst[2 * g:2 * g + 2].rearrange("u r t -> (u r) t")[:, c0:c0 + TW], ST[:],
                              reads=[BS], writes=[BD[2 * g], BD[2 * g + 1]])
                for bl in range(3):
                    P = pv[vi % 2]; BP = Bpv[vi % 2]; ST = stv[vi % 2]; BS = Bstv[vi % 2]
                    vi += 1
                    for c in range(8):
                        S.op("pe", lambda e, P=P, c=c, bl=bl, X=X: e.matmul(
                            P[:], X[:, c, bl * 128:(bl + 1) * 128], wv_sb[:, c, :], start=(c == 0), stop=(c == 7)),
                            reads=[Bw, BX], writes=[BP], inc=(c == 7))
                    S.op("dve", lambda e, ST=ST, P=P: e.tensor_copy(ST[:], P[:]), reads=[BP], writes=[BS])
                    r0 = c0 + bl * 128
                    S.dma("sp", V_s[r0:r0 + 128, :], ST[:], reads=[BS], writes=[BV])
                for c in range(8):
                    S.op("pe", lambda e, c=c, X=X: e.matmul(pf[0:4, 0:TW], wf_sb[:, c, :], X[:, c, :], start=(c == 0), stop=(c == 7)),
                         reads=[Bw, BX], writes=[Bpf], inc=(c == 7))
                S.op("act", lambda e: e.activation(e_t[:], pf[0:4, 0:TW], AF.Exp, bias=nfb_sb[:, 0:1], scale=-1.0),
                     reads=[Bpf, Bfb], writes=[Bet])
                S.op("act", lambda e, c0=c0: e.activation(lf[:, c0:c0 + TW], e_t[:], AF.Ln, bias=1.0, scale=1.0),
                     reads=[Bet], writes=[Blf])
                if conv_it is not None:
                    for _ in range(3):
                        next(conv_it, None)
            S.op("dve", lambda e: e.memset(lf[:, 0:FPAD], 0.0), reads=[Blf], writes=[Blf])
            S.op("dve", lambda e: e.memset(lf[:, REAL_END:LR], 0.0), reads=[Blf], writes=[Blf])
            CH = LR // 4
            for k in range(4):
                a = k * CH
                if k == 0:
                    S.op("dve", lambda e, a=a: e.tensor_tensor_scan(cs[:, a:a + CH], ones4[:], lf[:, a:a + CH], 0.0, ALU.mult, ALU.add),
                         reads=[Blf, Bo4], writes=[Bcs])
                else:
                    S.op("dve", lambda e, a=a: e.tensor_tensor_scan(cs[:, a:a + CH], ones4[:], lf[:, a:a + CH], cs[:, a - 1:a], ALU.mult, ALU.add),
                         reads=[Blf, Bo4, Bcs], writes=[Bcs])
            CS_s = nc.dram_tensor("CS_s", [4, LR], F32).ap()
            BCS = Buf()
            S.dma("sp", CS_s, cs[:], reads=[Bcs], writes=[BCS])
            for t in range(NT):
                c0 = t * TW
                S.op("dve", lambda e, c0=c0: e.tensor_scalar(lf[:, c0:c0 + TW], cs[:, c0:c0 + TW], cs[:, c0:c0 + 1], -8.0, ALU.subtract, ALU.mult),
                     reads=[Bcs, Blf], writes=[Blf])
            S.op("dve", lambda e: e.tensor_copy(hi[:], lf[:]), reads=[Blf], writes=[Bhi])
            S.op("dve", lambda e: e.tensor_tensor(lf[:], lf[:], hi[:], ALU.subtract), reads=[Blf, Bhi], writes=[Blf])
            S.op("dve", lambda e: e.tensor_copy(mid[:], lf[:]), reads=[Blf], writes=[Bmid])
            S.op("dve", lambda e: e.tensor_tensor(lf[:], lf[:], mid[:], ALU.subtract), reads=[Blf, Bmid], writes=[Blf])
            S.op("dve", lambda e: e.tensor_copy(lo[:], lf[:]), reads=[Blf], writes=[Blo])
            qa_sb = C.sb([6, LR], BF16)
            Bqa = Buf()
            S.dma("pool", qa_sb[:], qaug.rearrange("h r t -> (h r) t"), writes=[Bqa])
            S.dma("sp", AQd_s, qa_sb[:], reads=[Bqa], writes=[BAQ])
            S.dma("sp", AQ_s[:, 0, :], hi[:], reads=[Bhi], writes=[BAQ])
            S.dma("sp", AQ_s[:, 1, :], mid[:], reads=[Bmid], writes=[BAQ])
            S.dma("sp", AQ_s[:, 2, :], lo[:], reads=[Blo], writes=[BAQ])
            if conv_it is not None:
                for _ in conv_it:
                    pass
            S.barrier()
            S.flush()

        with contextlib.ExitStack() as st:
            C = Ctx(nc, st)
            kt = [C.sb([128, LR], BF16) for _ in range(2)]
            qt = [C.sb([128, LR], BF16) for _ in range(2)]
            vt = [C.sb([128, NB, 128], BF16) for _ in range(2)]
            Fd_sb = C.sb([128, 2 * 3 * TW], F32)
            Ff_sb = C.sb([128, 3 * TW], F32)
            dbias_sb = C.sb([128, 2 * 66], F32)
            csk = C.sb([128, 4, NB], F32)
            csT0 = C.sb([128, 4, NT], F32)
            bkt = [C.sb([128, NB], F32) for _ in range(2)]
            ones_bf = C.sb([128, 128], BF16)
            ones_b0 = C.sb([128, 128], BF16)
            ones_f = C.sb([128, 128], F32)
            lam_sb = C.sb([128, 256], F32)
            lprod = C.sb([128, 128], F32)
            lsum = C.sb([128, 2], F32)
            lexp = C.sb([128, 2], F32)
            neglam = C.sb([128, 1], F32)
            gcol = C.sb([128, 1], F32)
            pT = [C.sb([128, TW], BF16) for _ in range(4)]
            tmp = [C.sb([128, TW], F32) for _ in range(2)]
            rl = C.sb([128, TW], F32)
            lsb = [C.sb([128, TW], F32) for _ in range(2)]
            l0b = [C.sb([128, TW], F32) for _ in range(2)]
            Blsb = [Buf(), Buf()]; Bl0b = [Buf(), Buf()]
            a0 = C.sb([128, TW], F32)
            a1 = C.sb([128, TW], F32)
            od = C.sb([128, TW], F32)
            sq = C.sb([128, TW], F32)
            rstd = C.sb([128, TW], F32)
            ofin = [C.sb([128, TW], ODT) for _ in range(2)]
            NS = 3
            NPT = 4
            ps_s = [C.ps([128, 512]) for _ in range(NS)]
            ps_o = [C.ps([128, 512]) for _ in range(2)]
            ps_l = [C.ps([128, 512]) for _ in range(2)]
            ps_x = C.ps([128, 512])
            Bkt = [Buf(), Buf()]; Bqt = [Buf(), Buf()]; Bvt = [Buf(), Buf()]
            Bc = Buf(); Bcsk = Buf(); BcsT0 = Buf(); Bbkt = [Buf(), Buf()]
            Bps_s = [PB(), PB(), PB()]; Bps_o = [PB(), PB()]; Bps_l = [PB(), PB()]; Bps_x = PB()
            BpT = [Buf() for _ in range(4)]; Btmp = [Buf(), Buf()]
            Brl = Buf(); Ba0 = Buf(); Ba1 = Buf(); Bod = Buf(); Bsq = Buf(); Brstd = Buf(); Bofin = [Buf(), Buf()]
            Blam = Buf()

            S.dma("sp", Fd_sb[:], Fd, writes=[Bc])
            S.dma("sp", Ff_sb[:], Ff, writes=[Bc])
            S.dma("sp", dbias_sb[:], dbias, writes=[Bc])
            S.dma("sp", lam_sb[:], lamv, writes=[Blam])
            S.dma("sp", gcol[:], gsub, writes=[Blam])
            for h in range(4):
                S.dma("sp", csk[:, h, :], CS_s[h].rearrange("(b p) -> p b", p=128), reads=[BCS], writes=[Bcsk], allow_slow_non_contiguous=True)
            for h in range(4):
                src = bass.AP(CS_s.tensor, CS_s.offset + h * LR, [[0, 128], [TW, NT]])
                S.dma("sp", csT0[:, h, :], src, reads=[BCS], writes=[BcsT0], allow_slow_non_contiguous=True)
            S.op("pool", lambda e: e.memset(ones_bf[:], 1.0), writes=[Bc])
            S.op("pool", lambda e: e.memset(ones_b0[:], 1.0), writes=[Bc])
            S.op("pool", lambda e: e.memset(ones_b0[0:FPAD, :], 0.0), reads=[Bc], writes=[Bc])
            S.op("pool", lambda e: e.memset(ones_f[:], 1.0), writes=[Bc])
            for b in range(2):
                S.op("pool", lambda e, b=b: e.memset(kt[b][64:67, :], 1.0), writes=[Bkt[b]])
            S.op("dve", lambda e: e.tensor_tensor(lprod[:, 0:64], lam_sb[:, 0:64], lam_sb[:, 64:128], ALU.mult), reads=[Blam], writes=[Blam])
            S.op("dve", lambda e: e.tensor_tensor(lprod[:, 64:128], lam_sb[:, 128:192], lam_sb[:, 192:256], ALU.mult), reads=[Blam], writes=[Blam])
            S.op("dve", lambda e: e.reduce_sum(lsum[:, 0:1], lprod[:, 0:64], mybir.AxisListType.X), reads=[Blam], writes=[Blam])
            S.op("dve", lambda e: e.reduce_sum(lsum[:, 1:2], lprod[:, 64:128], mybir.AxisListType.X), reads=[Blam], writes=[Blam])
            S.op("act", lambda e: e.activation(lexp[:], lsum[:], AF.Exp), reads=[Blam], writes=[Blam])
            S.op("dve", lambda e: e.scalar_tensor_tensor(neglam[:], lexp[:, 1:2], -LAM_INIT0, lexp[:, 0:1], ALU.add, ALU.subtract),
                 reads=[Blam], writes=[Blam])
            S.op("dve", lambda e: e.tensor_scalar(gcol[:], gcol[:], 1.0 - LAM_INIT0, None, ALU.mult), reads=[Blam], writes=[Blam])

            conv_itB = T["hookB"](C) if T.get("hookB") else None
            si = 0
            pi = 0
            ti = 0
            oi = 0
            fi = 0
            for g in range(4):
                isdiff = g < 2
                units = (2 * g, 2 * g + 1)
                dv = 128 if isdiff else 64
                for ui, u in enumerate(units):
                    S.dma("sp", kt[ui][0:64, :], KT_s[u], reads=[BKT[u]], writes=[Bkt[ui]])
                    S.dma("sp", qt[ui][0:64, :], QT_s[u], reads=[BQT[u]], writes=[Bqt[ui]])
                    if isdiff:
                        S.dma("sp", qt[ui][64:67, :], AQd_s[3 * g:3 * g + 3, :], reads=[BAQ], writes=[Bqt[ui]])
                    else:
                        S.dma("sp", qt[ui][64:67, :], AQ_s[u - 4], reads=[BAQ], writes=[Bqt[ui]])
                if isdiff:
                    S.dma("sp", vt[0][:], V_s[:, g * 128:(g + 1) * 128].rearrange("(b p) c -> p b c", p=128),
                          reads=[BV], writes=[Bvt[0]])
                else:
                    for ui, u in enumerate(units):
                        hf = u - 4
                        S.dma("sp", vt[ui][:, :, 0:64], V_s[:, 256 + hf * 64:256 + (hf + 1) * 64].rearrange("(b p) c -> p b c", p=128),
                              reads=[BV], writes=[Bvt[ui]])
                        S.op("pool", lambda e, ui=ui: e.memset(vt[ui][:, :, 64:128], 1.0), writes=[Bvt[ui]])
                        S.op("pool", lambda e, ui=ui: e.memset(vt[ui][0:FPAD, 0, 64:128], 0.0), reads=[Bvt[ui]], writes=[Bvt[ui]])
                items = []
                for t in range(NT):
                    for ui, u in enumerate(units):
                        for j in range(3 * t + 3):
                            items.append((t, ui, u, j))
                state = {}
                deferred = []

                def stage_a(t, ui, u, j):
                    nonlocal si, pi, ti, oi, fi
                    c0 = t * TW
                    nblk = 3 * t + 3
                    K = kt[ui]; Q = qt[ui]; BK = Bkt[ui]; BQ = Bqt[ui]
                    hf = u - 4
                    if j == 0:
                        PO = ps_o[oi % 2]; BPO = Bps_o[oi % 2]; PL = ps_l[oi % 2]; BPL = Bps_l[oi % 2]
                        oi += 1
                        BKk = None; BBK = None
                        if not isdiff:
                            BKk = bkt[fi % 2]; BBK = Bbkt[fi % 2]
                            fi += 1
                            S.op("dve", lambda e, BKk=BKk, nblk=nblk, hf=hf, t=t: e.tensor_scalar(
                                BKk[:, 0:nblk], csk[:, hf, 0:nblk], csT0[:, hf, t:t + 1], None, ALU.subtract),
                                reads=[Bcsk, BcsT0], writes=[BBK])
                        state[(t, ui)] = (PO, BPO, PL, BPL, BKk, BBK)
                    PO, BPO, PL, BPL, BKk, BBK = state[(t, ui)]
                    jj = j - 3 * t
                    PS = ps_s[si % NS]; BPS = Bps_s[si % NS]
                    si += 1
                    S.op("pe", lambda e, PS=PS, K=K, Q=Q, j=j, c0=c0: e.matmul(
                        PS[:, 0:TW], K[0:67, j * 128:(j + 1) * 128], Q[0:67, c0:c0 + TW], start=True, stop=True),
                        reads=[BK, BQ], writes=[BPS])
                    PT = pT[pi % NPT]; BPT = BpT[pi % NPT]
                    pi += 1
                    if isdiff:
                        col = g * 66 + (jj + 63)
                        bcol = dbias_sb[:, col:col + 1]
                        rb = [Bc]
                    else:
                        bcol = BKk[:, j:j + 1]
                        rb = [BBK]
                    if jj < 0:
                        S.op("act", lambda e, PT=PT, PS=PS, bcol=bcol: e.activation(PT[:], PS[:, 0:TW], AF.Exp, bias=bcol, scale=SCALE),
                             reads=[BPS] + rb, writes=[BPT])
                    else:
                        TM = tmp[ti % 2]; BTM = Btmp[ti % 2]
                        ti += 1
                        if isdiff:
                            Fap = Fd_sb[:, (g * 3 + jj) * TW:(g * 3 + jj + 1) * TW]
                        else:
                            Fap = Ff_sb[:, jj * TW:(jj + 1) * TW]
                        S.op("dve", lambda e, TM=TM, PS=PS, Fap=Fap: e.scalar_tensor_tensor(
                            TM[:], PS[:, 0:TW], SCALE, Fap, ALU.mult, ALU.add), reads=[BPS, Bc], writes=[BTM])
                        S.op("act", lambda e, PT=PT, TM=TM, bcol=bcol: e.activation(PT[:], TM[:], AF.Exp, bias=bcol, scale=1.0),
                             reads=[BTM] + rb, writes=[BPT])
                    return PT, BPT

                def stage_b(idx, t, ui, u, j, PT, BPT):
                    c0 = t * TW
                    nblk = 3 * t + 3
                    hf = u - 4
                    if isdiff:
                        VT = vt[0]; BVT = Bvt[0]
                    else:
                        VT = vt[ui]; BVT = Bvt[ui]
                    PO, BPO, PL, BPL, BKk, BBK = state[(t, ui)]
                    if not isdiff:
                        S.op("pe", lambda e, PO=PO, VT=VT, PT=PT, j=j, nblk=nblk: e.matmul(
                            PO[:, 0:TW], VT[:, j, :], PT[:], start=(j == 0), stop=(j == nblk - 1)),
                            reads=[BVT, BPT], writes=[BPO], inc=True)
                        if j != nblk - 1:
                            return
                        LS = lsb[(2 * t + ui) % 2]; BLS = Blsb[(2 * t + ui) % 2]
                        L0 = l0b[(2 * t + ui) % 2]; BL0 = Bl0b[(2 * t + ui) % 2]
                        S.op("dve", lambda e, LS=LS, PO=PO: e.tensor_scalar(LS[64:128, :], PO[64:128, 0:TW], 1e-30, None, ALU.add),
                             reads=[BPO], writes=[BLS])
                        S.dma("sp", L0[0:64, :], LS[64:128, :], reads=[BLS], writes=[BL0])
                        S.op("dve", lambda e, L0=L0: e.reciprocal(rl[0:64, :], L0[0:64, :]), reads=[BL0], writes=[Brl])
                        OF = ofin[(2 * t + ui) % 2]; BOF = Bofin[(2 * t + ui) % 2]
                        S.op("dve", lambda e, OF=OF, PO=PO: e.tensor_tensor(OF[0:64, :], PO[0:64, 0:TW], rl[0:64, :], ALU.mult),
                             reads=[BPO, Brl], writes=[BOF])
                        r0 = 256 + hf * 64
                        out_toks.append(S.dma("sp", T["oT_tile"](r0, r0 + 64, t), OF[0:64, :], reads=[BOF], writes=[Bo]))
                        return
                    S.op("pe", lambda e, PO=PO, VT=VT, PT=PT, j=j, nblk=nblk, dv=dv: e.matmul(
                        PO[0:dv, 0:TW], VT[:, j, 0:dv], PT[:], start=(j == 0), stop=(j == nblk - 1)),
                        reads=[BVT, BPT], writes=[BPO], inc=False)
                    on = ones_b0 if j == 0 else ones_bf
                    S.op("pe", lambda e, PL=PL, on=on, PT=PT, j=j, nblk=nblk, dv=dv: e.matmul(
                        PL[0:dv, 0:TW], on[:, 0:dv], PT[:], start=(j == 0), stop=(j == nblk - 1)),
                        reads=[Bc, BPT], writes=[BPL], inc=True)
                    if j != nblk - 1:
                        return
                    S.op("dve", lambda e, PL=PL, dv=dv: e.tensor_scalar(rl[0:dv, :], PL[0:dv, 0:TW], 1e-30, None, ALU.add), reads=[BPL], writes=[Brl])
                    S.op("dve", lambda e, dv=dv: e.reciprocal(rl[0:dv, :], rl[0:dv, :]), reads=[Brl], writes=[Brl])
                    if not isdiff:
                        OF = ofin[(2 * t + ui) % 2]; BOF = Bofin[(2 * t + ui) % 2]
                        S.op("dve", lambda e, OF=OF, PO=PO: e.tensor_tensor(OF[0:64, :], PO[0:64, 0:TW], rl[0:64, :], ALU.mult),
                             reads=[BPO, Brl], writes=[BOF])
                        r0 = 256 + hf * 64
                        out_toks.append(S.dma("sp", T["oT_tile"](r0, r0 + 64, t), OF[0:64, :], reads=[BOF], writes=[Bo]))
                        return
                    AA = a0 if ui == 0 else a1
                    BA = Ba0 if ui == 0 else Ba1
                    S.op("dve", lambda e, AA=AA, PO=PO: e.tensor_tensor(AA[:], PO[:, 0:TW], rl[:], ALU.mult),
                         reads=[BPO, Brl], writes=[BA])
                    if ui == 0:
                        return
                    OF = ofin[t % 2]; BOF = Bofin[t % 2]
                    S.op("dve", lambda e: e.scalar_tensor_tensor(od[:], a1[:], neglam[:, 0:1], a0[:], ALU.mult, ALU.add),
                         reads=[Ba0, Ba1, Blam], writes=[Bod])
                    S.op("pool", lambda e: e.tensor_tensor(sq[:], od[:], od[:], ALU.mult), reads=[Bod], writes=[Bsq])

                    def tail(OF=OF, BOF=BOF, t=t):
                        S.op("pe", lambda e: e.matmul(ps_x[:, 0:TW], ones_f[:], sq[:], start=True, stop=True),
                             reads=[Bc, Bsq], writes=[Bps_x])
                        S.op("act", lambda e: e.activation(rstd[:], ps_x[:, 0:TW], AF.Ln, bias=RMS_EPS, scale=1.0 / 128.0),
                             reads=[Bps_x], writes=[Brstd])
                        S.op("act", lambda e: e.activation(rstd[:], rstd[:], AF.Exp, scale=-0.5), reads=[Brstd], writes=[Brstd])
                        S.op("dve", lambda e, OF=OF: e.scalar_tensor_tensor(OF[:], od[:], gcol[:, 0:1], rstd[:], ALU.mult, ALU.mult),
                             reads=[Bod, Brstd, Blam], writes=[BOF])
                        out_toks.append(S.dma("sp", T["oT_tile"](g * 128, (g + 1) * 128, t), OF[:], reads=[BOF], writes=[Bo]))
                    deferred.append((idx + 3, tail))

                LA = 2
                pend = {}
                n_it = len(items)
                for i in range(n_it + LA):
                    if conv_itB is not None and i % 8 == 7:
                        next(conv_itB, None)
                    if i < n_it:
                        pend[i] = stage_a(*items[i])
                    k = i - LA
                    if k >= 0:
                        PT, BPT = pend.pop(k)
                        stage_b(k, *items[k], PT, BPT)
                    while deferred and deferred[0][0] <= k:
                        deferred.pop(0)[1]()
                while deferred:
                    deferred.pop(0)[1]()
            if conv_itB is not None:
                for _ in conv_itB:
                    pass
            S.barrier()
            S.flush()
    return out_toks


def l1_consts(s):
    p = np.arange(128, dtype=np.float64)[:, None]
    slopes = [2.0 ** (-8.0 * (h + 1) / 4) for h in (2 * s, 2 * s + 1)]
    dbias = np.zeros((128, 2, 66), np.float32)
    Fd = np.zeros((128, 2, 3, TW), np.float32)
    qaug = np.zeros((2, 3, LR), np.float32)
    i = np.arange(TW, dtype=np.float64)[None, :]
    r = np.arange(LR)
    w = r % TW
    for hl, sl in enumerate(slopes):
        for idx in range(66):
            dbias[:, hl, idx] = (sl * (128 * (idx - 63) + p))[:, 0]
        for jj in range(3):
            rk = 128 * jj + p
            vis = (rk // 64) <= (i // 64)
            f = np.where(i >= rk, 0.0, 2 * sl * (i - rk))
            Fd[:, hl, jj, :] = np.where(vis, f, NEG)
        qaug[hl, 0] = -(8 * sl) * 256 * (w // 256)
        qaug[hl, 1] = -(8 * sl) * (w % 256)
    Ff = np.zeros((128, 3, TW), np.float32)
    for jj in range(3):
        rk = 128 * jj + p
        Ff[:, jj, :] = np.where(rk <= i, 0.0, NEG)
    return (dbias.reshape(128, 132), Fd.reshape(128, 6 * TW), Ff.reshape(128, 3 * TW), qaug)


def l1_inputs(xT_b, even_w_in, even_f_bias, diff_lambda, diff_subln_g, s):
    w = even_w_in
    dq = [w[:, h * 128:(h + 1) * 128] for h in (2 * s, 2 * s + 1)]
    dk = [w[:, 512 + h * 128:512 + (h + 1) * 128] for h in (2 * s, 2 * s + 1)]
    dvv = [w[:, 1024 + h * 128:1024 + (h + 1) * 128] for h in (2 * s, 2 * s + 1)]
    fq = w[:, 1536 + 256 * s:1536 + 256 * (s + 1)]
    fk = w[:, 2048 + 256 * s:2048 + 256 * (s + 1)]
    fv = w[:, 2560 + 256 * s:2560 + 256 * (s + 1)]
    wf = w[:, 3072 + 4 * s:3072 + 4 * (s + 1)]
    dbias, Fd, Ff, qaug = l1_consts(s)
    return {
        "xT": xT_b,
        "wq": np.ascontiguousarray(np.concatenate(dq + [fq], axis=1)),
        "wk": np.ascontiguousarray(np.concatenate(dk + [fk], axis=1)),
        "wv": np.ascontiguousarray(np.concatenate(dvv + [fv], axis=1)),
        "wf": np.ascontiguousarray(wf),
        "fbias": np.ascontiguousarray(even_f_bias[4 * s:4 * (s + 1)].reshape(4, 1)),
        "lamv": np.ascontiguousarray(np.broadcast_to(diff_lambda.reshape(1, 256), (128, 256))),
        "gsub": np.ascontiguousarray(diff_subln_g.reshape(128, 1)),
        "dbias": dbias, "Fd": Fd, "Ff": Ff, "qaug": qaug,
    }


def emit_convert(S, C, src, dst, BD, rows, cols, tag, shared=None):
    if shared is None:
        stg = [C.sb([128, 2048], BF16) for _ in range(2)]
        Bs = [Buf(), Buf()]
    else:
        stg, Bs = shared
    for _ in iter_convert(S, stg, Bs, src, dst, BD, rows, cols):
        pass


def iter_convert(S, stg, Bs, src, dst, BD, rows, cols, ctr=[0]):
    for r0 in range(0, rows, 128):
        for c0 in range(0, cols, 2048):
            w = min(2048, cols - c0)
            T = stg[ctr[0] % 2]; B = Bs[ctr[0] % 2]
            ctr[0] += 1
            S.dma("pool", T[:, 0:w], src[r0:r0 + 128, c0:c0 + w], writes=[B])
            S.dma("sp", dst[r0:r0 + 128, c0:c0 + w], T[:, 0:w], reads=[B], writes=[BD])
            yield


def emit_ln(S, y, By, yb, Byb, sqb, Bsqb, tmpn, Btmpn, out_f, Bof, out_b, Bob, gcols, bcols, Bp, ones_b, Bc, ps1, Bps1, ps2, Bps2,
            mean, msq, rstd, Bst, nfeat_chunks=8):
    n = nfeat_chunks
    for c in range(n):
        S.op("act", lambda e, c=c: e.copy(yb[:, c, :], y[:, c, :]), reads=[By[c]], writes=[Byb[c]])
        S.op("act", lambda e, c=c: e.activation(sqb[:, c, :], y[:, c, :], AF.Square), reads=[By[c]], writes=[Bsqb[c]])
    for c in range(n):
        S.op("pe", lambda e, c=c: e.matmul(ps1[:, 0:TW], ones_b[:], yb[:, c, :], start=(c == 0), stop=(c == n - 1)),
             reads=[Bc, Byb[c]], writes=[Bps1], inc=(c == n - 1))
    for c in range(n):
        S.op("pe", lambda e, c=c: e.matmul(ps2[:, 0:TW], ones_b[:], sqb[:, c, :], start=(c == 0), stop=(c == n - 1)),
             reads=[Bc, Bsqb[c]], writes=[Bps2], inc=(c == n - 1))
    inv = 1.0 / (128.0 * n)
    S.op("dve", lambda e: e.tensor_scalar(mean[:], ps1[:, 0:TW], inv, None, ALU.mult), reads=[Bps1], writes=[Bst])
    S.op("dve", lambda e: e.tensor_tensor(msq[:], mean[:], mean[:], ALU.mult), reads=[Bst], writes=[Bst])
    S.op("dve", lambda e: e.scalar_tensor_tensor(msq[:], ps2[:, 0:TW], inv, msq[:], ALU.mult, ALU.subtract),
         reads=[Bps2, Bst], writes=[Bst])
    S.op("act", lambda e: e.activation(rstd[:], msq[:], AF.Ln, bias=LN_EPS, scale=1.0), reads=[Bst], writes=[Bst])
    S.op("act", lambda e: e.activation(rstd[:], rstd[:], AF.Exp, scale=-0.5), reads=[Bst], writes=[Bst])
    for c in range(n):
        eng = "dve" if c % 2 == 0 else "pool"
        S.op(eng, lambda e, c=c: e.tensor_tensor(tmpn[:, c, :], y[:, c, :], mean[:], ALU.subtract), reads=[By[c], Bst], writes=[Btmpn[c]])
        S.op(eng, lambda e, c=c: e.tensor_tensor(tmpn[:, c, :], tmpn[:, c, :], rstd[:], ALU.mult), reads=[Bst, Btmpn[c]], writes=[Btmpn[c]])
        S.op("dve", lambda e, c=c: e.tensor_scalar(out_f[:, c, :], tmpn[:, c, :], gcols[:, c:c + 1], bcols[:, c:c + 1], ALU.mult, ALU.add),
             reads=[Btmpn[c], Bp], writes=[Bof[c]])
        if out_b is not None:
            S.op("act", lambda e, c=c: e.copy(out_b[:, c, :], out_f[:, c, :]), reads=[Bof[c]], writes=[Bob[c]])


def emit_post(nc, S, st, ntiles, KC, src_o, src_mode, Bsrc, hres, wout_b, w1_b, w2_b, Bwb, lnp, hout, Bhout, flags=None):
    C = Ctx(nc, st)
    ot = [C.sb([128, KC, TW], BF16) for _ in range(2)]
    hr = [C.sb([128, 8, TW], F32) for _ in range(2)]
    y = C.sb([128, 8, TW], F32)
    ysq = C.sb([128, 8, TW], F32)
    h1 = C.sb([128, 8, TW], F32)
    h1b = C.sb([128, 8, TW], BF16)
    h2 = C.sb([128, 8, TW], F32)
    yb = C.sb([128, 8, TW], BF16)
    sqb = C.sb([128, 8, TW], BF16)
    hid = C.sb([128, 32, TW], BF16)
    rl = [C.sb([128, TW], F32) for _ in range(2)]
    WCOL = 4096 // KC
    wo = [C.sb([128, KC, WCOL], BF16) for _ in range(2)]
    w1c = [C.sb([128, 8, 512], BF16) for _ in range(2)]
    w2c = [C.sb([128, 8, 512], BF16) for _ in range(2)]
    lnp_sb = C.sb([128, 32], F32)
    ones_f = C.sb([128, 128], BF16)
    if src_mode == "blend":
        cand = [C.sb([128, KC, TW], BF16) for _ in range(2)]
        fl_sb = C.sb([128, 2], F32)
        Bcand = [Buf(), Buf()]
        Bfl = Buf()
        S.dma("sp", fl_sb[:], flags, writes=[Bfl])
    mean = C.sb([128, TW], F32)
    msq = C.sb([128, TW], F32)
    rstd = C.sb([128, TW], F32)
    pa = [C.ps([128, 512]) for _ in range(2)]
    pb = [C.ps([128, 512]) for _ in range(4)]
    ps1 = C.ps([128, 512])
    ps2 = C.ps([128, 512])
    Bot = [Buf(), Buf()]; Bhr = [Buf(), Buf()]
    L8 = lambda: [Buf() for _ in range(8)]
    By = L8(); Bysq = L8(); Bh1 = L8(); Bh1b = L8(); Bh2 = L8(); Byb = L8(); Bsqb = L8()
    Bhid = [Buf() for _ in range(32)]; Brl = [Buf(), Buf()]; Bwo = [Buf(), Buf()]; Bw1 = [Buf(), Buf()]; Bw2 = [Buf(), Buf()]
    Bp = Buf(); Bc = Buf(); Bst = Buf(); Bpa = [PB(), PB()]; Bpb = [PB() for _ in range(4)]; Bps1 = PB(); Bps2 = PB()
    S.dma("sp", lnp_sb[:], lnp, writes=[Bp])
    S.op("pool", lambda e: e.memset(ones_f[:], 1.0), writes=[Bc])
    wr = "(c p) o -> p c o"
    wo_v = wout_b.rearrange(wr, p=128)
    w1_v = w1_b.rearrange(wr, p=128)
    w2_v = w2_b.rearrange(wr, p=128)
    so_v = None if src_mode == "blend" else src_o.rearrange("(c p) t -> p c t", p=128)
    hr_v = hres.rearrange("(c p) t -> p c t", p=128)
    ho_v = hout.rearrange("(c p) t -> p c t", p=128)
    ai = 0
    wi = 0
    w1i = 0
    w2i = 0
    ri = 0
    toks = []
    def stA(t):
        nonlocal ai, wi
        c0 = t * TW
        OT = ot[t % 2]; BOT = Bot[t % 2]; HR = hr[t % 2]; BHR = Bhr[t % 2]
        if src_mode == "blend":
            for k in range(2):
                S.dma("sp", cand[k][:], src_o(k, t), reads=[Bsrc], writes=[Bcand[k]])
            for kc in range(KC):
                S.op("dve", lambda e, kc=kc, OT=OT: e.tensor_scalar(OT[:, kc, :], cand[0][:, kc, :], fl_sb[:, 0:1], None, ALU.mult),
                     reads=[Bcand[0], Bfl], writes=[BOT])
                S.op("dve", lambda e, kc=kc, OT=OT: e.scalar_tensor_tensor(OT[:, kc, :], cand[1][:, kc, :], fl_sb[:, 1:2], OT[:, kc, :], ALU.mult, ALU.add),
                     reads=[Bcand[1], Bfl, BOT], writes=[BOT])
        else:
            S.dma("pool" if src_mode == "f32" else "sp", OT[:], so_v[:, :, c0:c0 + TW], reads=[Bsrc], writes=[BOT])
        S.dma("sp", HR[:], hr_v[:, :, c0:c0 + TW], writes=[BHR])
        for n2 in range(1024 // WCOL):
            WO = wo[wi % 2]; BWO = Bwo[wi % 2]
            wi += 1
            S.dma("sp", WO[:], wo_v[:, :, n2 * WCOL:(n2 + 1) * WCOL], reads=[Bwb], writes=[BWO])
            for c4 in range(WCOL // 128):
                cc = n2 * (WCOL // 128) + c4
                P = pa[ai % 2]; BP = Bpa[ai % 2]
                ai += 1
                for kc in range(KC):
                    S.op("pe", lambda e, P=P, WO=WO, kc=kc, c4=c4, OT=OT: e.matmul(
                        P[:, 0:TW], WO[:, kc, c4 * 128:(c4 + 1) * 128], OT[:, kc, :], start=(kc == 0), stop=(kc == KC - 1)),
                        reads=[BWO, BOT], writes=[BP], inc=(kc == KC - 1))
                S.op("dve", lambda e, P=P, cc=cc, HR=HR: e.scalar_tensor_tensor(
                    y[:, cc, :], HR[:, cc, :], ALPHA, P[:, 0:TW], ALU.mult, ALU.add), reads=[BP, BHR], writes=[By[cc]])

    def stB(t):
        emit_ln(S, y, By, yb, Byb, sqb, Bsqb, ysq, Bysq, h1, Bh1, h1b, Bh1b, lnp_sb[:, 0:8], lnp_sb[:, 8:16], Bp, ones_f, Bc,
                ps1, Bps1, ps2, Bps2, mean, msq, rstd, Bst)

    def stC(t):
        nonlocal ai, w1i, ri
        for hg in range(8):
            W1 = w1c[w1i % 2]; BW1 = Bw1[w1i % 2]
            w1i += 1
            S.dma("sp", W1[:], w1_v[:, :, hg * 512:(hg + 1) * 512], reads=[Bwb], writes=[BW1])
            for h4 in range(4):
                hc = hg * 4 + h4
                P = pa[ai % 2]; BP = Bpa[ai % 2]
                ai += 1
                for fc in range(8):
                    S.op("pe", lambda e, P=P, W1=W1, fc=fc, h4=h4: e.matmul(
                        P[:, 0:TW], W1[:, fc, h4 * 128:(h4 + 1) * 128], h1b[:, fc, :], start=(fc == 0), stop=(fc == 7)),
                        reads=[BW1, Bh1b[fc]], writes=[BP], inc=(fc == 7))
                R = rl[ri % 2]; BR = Brl[ri % 2]
                ri += 1
                S.op("act", lambda e, R=R, P=P: e.activation(R[:], P[:, 0:TW], AF.Relu), reads=[BP], writes=[BR])
                S.op("pool", lambda e, R=R, hc=hc: e.tensor_tensor(hid[:, hc, :], R[:], R[:], ALU.mult), reads=[BR], writes=[Bhid[hc]])

    def stD(t):
        nonlocal w2i
        for n2 in range(2):
            for kg in range(4):
                W2 = w2c[w2i % 2]; BW2 = Bw2[w2i % 2]
                w2i += 1
                S.dma("sp", W2[:], w2_v[:, kg * 8:(kg + 1) * 8, n2 * 512:(n2 + 1) * 512], reads=[Bwb], writes=[BW2])
                for c4 in range(4):
                    for k8 in range(8):
                        hc = kg * 8 + k8
                        S.op("pe", lambda e, c4=c4, W2=W2, k8=k8, hc=hc: e.matmul(
                            pb[c4][:, 0:TW], W2[:, k8, c4 * 128:(c4 + 1) * 128], hid[:, hc, :], start=(hc == 0), stop=(hc == 31)),
                            reads=[BW2, Bhid[hc]], writes=[Bpb[c4]], inc=(k8 == 7))
            for c4 in range(4):
                cc = n2 * 4 + c4
                S.op("dve", lambda e, c4=c4, cc=cc: e.scalar_tensor_tensor(
                    h2[:, cc, :], h1[:, cc, :], ALPHA, pb[c4][:, 0:TW], ALU.mult, ALU.add), reads=[Bpb[c4], Bh1[cc]], writes=[Bh2[cc]])

    def stE(t):
        c0 = t * TW
        emit_ln(S, h2, Bh2, yb, Byb, sqb, Bsqb, ysq, Bysq, h2, Bh2, None, None, lnp_sb[:, 16:24], lnp_sb[:, 24:32], Bp, ones_f, Bc,
                ps1, Bps1, ps2, Bps2, mean, msq, rstd, Bst)
        toks.append(S.dma("sp", ho_v[:, :, c0:c0 + TW], h2[:], reads=Bh2, writes=[Bhout]))

    stA(0)
    for t in range(ntiles):
        stB(t)
        stC(t)
        stD(t)
        if t + 1 < ntiles:
            stA(t + 1)
        stE(t)
    return toks


L2_STAGE = 4
L3_STAGE = 3
L3_SUB = 3
L3_NCH = 12
RET_GAMMA = [1.0 - 2.0 ** (-5.0 - h) for h in range(4)]


def emit_state_tile(S, h2b, Bh2b, wk_sb, wv_sb, Bw, kts, Bkts, vts, Bvts, kdec_sb, Bc, Sst, BS, pk, Bpk, ctr):
    for bl in range(3):
        for half in range(2):
            P = pk[ctr[0] % 2]; BP = Bpk[ctr[0] % 2]
            ctr[0] += 1
            for fc in range(8):
                S.op("pe", lambda e, P=P, fc=fc, bl=bl, half=half: e.matmul(
                    P[:], h2b[:, fc, bl * 128:(bl + 1) * 128], wk_sb[:, fc, half * 512:(half + 1) * 512], start=(fc == 0), stop=(fc == 7)),
                    reads=[Bh2b, Bw], writes=[BP], inc=(fc == 7))
            for hh in range(2):
                h = half * 2 + hh
                S.op("dve", lambda e, P=P, bl=bl, h=h, hh=hh: e.tensor_scalar(
                    kts[:, bl, h * 256:(h + 1) * 256], P[:, hh * 256:(hh + 1) * 256], kdec_sb[:, bl * 4 + h:bl * 4 + h + 1], None, ALU.mult),
                    reads=[BP, Bc], writes=[Bkts])
        for h in range(4):
            P = pk[ctr[0] % 2]; BP = Bpk[ctr[0] % 2]
            ctr[0] += 1
            for fc in range(8):
                S.op("pe", lambda e, P=P, fc=fc, bl=bl, h=h: e.matmul(
                    P[:], h2b[:, fc, bl * 128:(bl + 1) * 128], wv_sb[:, fc, h * 512:(h + 1) * 512], start=(fc == 0), stop=(fc == 7)),
                    reads=[Bh2b, Bw], writes=[BP], inc=(fc == 7))
            S.op("act", lambda e, P=P, bl=bl, h=h: e.copy(vts[:, bl, h * 512:(h + 1) * 512], P[:]), reads=[BP], writes=[Bvts])


def emit_state_update(S, kts, Bkts, vts, Bvts, Sst, BS, Sb, BSb, pk, Bpk, ctr):
    for h in range(4):
        c384 = RET_GAMMA[h] ** TW
        for dkc in range(2):
            P = pk[ctr[0] % 2]; BP = Bpk[ctr[0] % 2]
            ctr[0] += 1
            for bl in range(3):
                S.op("pe", lambda e, P=P, bl=bl, h=h, dkc=dkc: e.matmul(
                    P[:], kts[:, bl, h * 256 + dkc * 128:h * 256 + (dkc + 1) * 128], vts[:, bl, h * 512:(h + 1) * 512],
                    start=(bl == 0), stop=(bl == 2)), reads=[Bkts, Bvts], writes=[BP], inc=(bl == 2))
            idx = h * 2 + dkc
            S.op("dve", lambda e, P=P, idx=idx, c384=c384: e.scalar_tensor_tensor(
                Sst[:, idx, :], Sst[:, idx, :], c384, P[:], ALU.mult, ALU.add), reads=[BP, BS], writes=[BS])
            if Sb is not None:
                S.op("pool", lambda e, idx=idx: e.tensor_copy(Sb[:, idx, :], Sst[:, idx, :]), reads=[BS], writes=[BSb])


def emit_prepass(nc, S, T, Bhout, BSe):
    h2T = T["h2T"]; wk = T["wk"]; wv = T["wv"]; kdec = T["kdec"]; tmask = T["tmask"]; S_end = T["S_end"]
    if True:
        with contextlib.ExitStack() as st:
            C = Ctx(nc, st)
            wk_sb = C.sb([128, 8, 1024], BF16)
            wv_sb = C.sb([128, 8, 2048], BF16)
            h2b = [C.sb([128, 8, TW], BF16) for _ in range(2)]
            kts = C.sb([128, 3, 1024], BF16)
            vts = C.sb([128, 3, 2048], BF16)
            kdec_sb = C.sb([128, 12], F32)
            tm_sb = C.sb([128, TW], F32)
            Sst = C.sb([128, 8, 512], F32)
            pk = [C.ps([128, 512]) for _ in range(2)]
            Bw = Buf(); Bh2b = [Buf(), Buf()]; Bkts = Buf(); Bvts = Buf(); Bc = Buf(); BS = Buf(); Bpk = [PB(), PB()]
            wr = "(c p) o -> p c o"
            wqueue = T.get("wqueue", "pool")
            for fc in range(8):
                S.dma(wqueue, wk_sb[:, fc, :], wk[fc * 128:(fc + 1) * 128, :], reads=[T["Bwb"]], writes=[Bw])
                S.dma(wqueue, wv_sb[:, fc, :], wv[fc * 128:(fc + 1) * 128, :], reads=[T["Bwb"]], writes=[Bw])
            S.dma("sp", kdec_sb[:], kdec, writes=[Bc])
            S.dma("sp", tm_sb[:], tmask, writes=[Bc])
            for i8 in range(8):
                S.op("dve", lambda e, i8=i8: e.memset(Sst[:, i8, :], 0.0), writes=[BS])
            hv = h2T.rearrange("(c p) t -> p c t", p=128)
            ctr = [0]
            for t in range(NTH):
                c0 = t * TW
                H = h2b[t % 2]; BH = Bh2b[t % 2]
                S.dma("pool", H[:], hv[:, :, c0:c0 + TW], reads=[Bhout], writes=[BH])
                if t == 0:
                    for fc in range(8):
                        S.op("dve", lambda e, H=H, fc=fc: e.tensor_tensor(H[:, fc, :], H[:, fc, :], tm_sb[:], ALU.mult),
                             reads=[BH, Bc], writes=[BH])
                emit_state_tile(S, H, BH, wk_sb, wv_sb, Bw, kts, Bkts, vts, Bvts, kdec_sb, Bc, Sst, BS, pk, Bpk, ctr)
                emit_state_update(S, kts, Bkts, vts, Bvts, Sst, BS, None, None, pk, Bpk, ctr)
            for k in range(4):
                S.dma("sp", S_end[k], Sst[32 * k:32 * (k + 1)], reads=[BS], writes=[BSe])
            S.barrier()
            S.flush()

def ret_consts():
    p = np.arange(128, dtype=np.float64)
    kdec = np.zeros((128, 3, 4), np.float32)
    for h in range(4):
        g = RET_GAMMA[h]
        for bl in range(3):
            kdec[:, bl, h] = g ** (TW - 1 - (bl * 128 + p)) / 16.0
    return kdec.reshape(128, 12)


def lnp_table(ln_g, ln_b, layer):
    cols = []
    for k in range(2):
        cols.append(ln_g[layer, k].reshape(8, 128).T)
        cols.append(ln_b[layer, k].reshape(8, 128).T)
    return np.ascontiguousarray(np.concatenate(cols, axis=1).astype(np.float32))


def emit_retention(nc, S, st, T, Bwb, Bh2, BSg, ByT):
    h2T = T["h2T"]; wi_b = T["wi_b"]; kdec = T["kdec"]; tmask = T["tmask"]; dmask = T["dmask"]; qdec = T["qdec"]
    yT_s = T["yT_s"]; S_src = T["S_src"]; flags = T["flags"]
    C = Ctx(nc, st)
    if True:
        if True:
            h2b = [C.sb([128, 8, TW], BF16) for _ in range(2)]
            wc = [C.sb([128, 8, 512], BF16) for _ in range(2)]
            qT = C.sb([128, 8, TW], BF16)
            qdT = C.sb([128, 8, TW], BF16)
            kT = C.sb([128, 8, TW], BF16)
            kts = C.sb([128, 3, 1024], BF16)
            vts = C.sb([128, 3, 2048], BF16)
            sg = C.sb([128, 16, TW], BF16)
            sTb = [C.sb([128, 3, TW], BF16) for _ in range(4)]
            o32 = [C.sb([128, 4, TW], F32) for _ in range(2)]
            osq = [C.sb([128, 4, TW], BF16) for _ in range(2)]
            yT = [C.sb([128, 16, TW], BF16) for _ in range(2)]
            Sst = C.sb([128, 8, 512], F32)
            Sb = C.sb([128, 8, 512], BF16)
            dm_sb = C.sb([128, 12 * TW], F32)
            qd_sb = C.sb([128, 4 * TW], F32)
            kdec_sb = C.sb([128, 12], F32)
            tm_sb = C.sb([128, TW], F32)
            ones_f = C.sb([128, 128], BF16)
            rstd = [C.sb([128, TW], F32) for _ in range(2)]
            tmpo = C.sb([128, TW], F32)
            pk = [C.ps([128, 512]) for _ in range(2)]
            psc = [C.ps([128, 512]) for _ in range(2)]
            po = [C.ps([128, 512]) for _ in range(2)]
            pss = C.ps([128, 512])
            Bh2b = [Buf(), Buf()]; Bwc = [Buf(), Buf()]; BqT = Buf(); BqdT = Buf(); BkT = Buf(); Bkts = Buf(); Bvts = Buf()
            Bsg = Buf(); BsTb = [Buf() for _ in range(4)]; Bo32 = [Buf(), Buf()]; Bosq = [Buf(), Buf()]; ByTt = [Buf(), Buf()]; BS = Buf(); BSb = Buf()
            Bc = Buf(); Brstd = [Buf(), Buf()]; Btmpo = Buf(); Bpk = [PB(), PB()]; Bpsc = [PB(), PB()]; Bpo = [PB(), PB()]; Bpss = PB()
            S.dma("sp", dm_sb[:], dmask, writes=[Bc])
            S.dma("sp", qd_sb[:], qdec, writes=[Bc])
            S.dma("sp", kdec_sb[:], kdec, writes=[Bc])
            S.dma("sp", tm_sb[:], tmask, writes=[Bc])
            S.op("pool", lambda e: e.memset(ones_f[:], 1.0), writes=[Bc])
            fl_sb = C.sb([128, 2], F32)
            S.dma("sp", fl_sb[:], flags, writes=[Bc])
            for k in range(4):
                S.dma("sp", Sst[32 * k:32 * (k + 1)], S_src[k], reads=[BSg], writes=[BS])
            for i8 in range(8):
                S.op("dve", lambda e, i8=i8: e.tensor_scalar(Sst[:, i8, :], Sst[:, i8, :], fl_sb[:, 1:2], None, ALU.mult),
                     reads=[BS, Bc], writes=[BS])
            for i8 in range(8):
                S.op("pool", lambda e, i8=i8: e.tensor_copy(Sb[:, i8, :], Sst[:, i8, :]), reads=[BS], writes=[BSb])
            hv = h2T.rearrange("(c p) t -> p c t", p=128)
            wiv = wi_b.rearrange("(c p) o -> p c o", p=128)
            yv = yT_s.rearrange("(c p) t -> p c t", p=128)
            ctr = [0]
            wci = 0
            sci = 0
            oi = 0
            for t in range(NTH if L3_STAGE >= 2 else 1):
                c0 = t * TW
                H = h2b[t % 2]; BH = Bh2b[t % 2]
                Y = yT[t % 2]; BY = ByTt[t % 2]
                S.dma("pool", H[:], hv[:, :, c0:c0 + TW], reads=[Bh2], writes=[BH])
                if t == 0:
                    for fc in range(8):
                        S.op("dve", lambda e, H=H, fc=fc: e.tensor_tensor(H[:, fc, :], H[:, fc, :], tm_sb[:], ALU.mult),
                             reads=[BH, Bc], writes=[BH])
                for ch in range(L3_NCH):
                    W = wc[wci % 2]; BW = Bwc[wci % 2]
                    wci += 1
                    S.dma("sp", W[:], wiv[:, :, ch * 512:(ch + 1) * 512], reads=[Bwb], writes=[BW])
                    if ch < 4 or ch >= 8:
                        for c4 in range(4):
                            P = pk[ctr[0] % 2]; BP = Bpk[ctr[0] % 2]
                            ctr[0] += 1
                            for fc in range(8):
                                S.op("pe", lambda e, P=P, W=W, fc=fc, c4=c4, H=H: e.matmul(
                                    P[:, 0:TW], W[:, fc, c4 * 128:(c4 + 1) * 128], H[:, fc, :], start=(fc == 0), stop=(fc == 7)),
                                    reads=[BW, BH], writes=[BP], inc=(fc == 7))
                            if ch < 2:
                                ci = ch * 4 + c4
                                h = ci // 2
                                S.op("act", lambda e, P=P, ci=ci: e.copy(qT[:, ci, :], P[:, 0:TW]), reads=[BP], writes=[BqT])
                                S.op("dve", lambda e, P=P, ci=ci, h=h: e.tensor_tensor(qdT[:, ci, :], P[:, 0:TW], qd_sb[:, h * TW:(h + 1) * TW], ALU.mult),
                                     reads=[BP, Bc], writes=[BqdT])
                            elif ch < 4:
                                ci = (ch - 2) * 4 + c4
                                S.op("act", lambda e, P=P, ci=ci: e.copy(kT[:, ci, :], P[:, 0:TW]), reads=[BP], writes=[BkT])
                            else:
                                gi = (ch - 8) * 4 + c4
                                S.op("act", lambda e, P=P, gi=gi: e.activation(sg[:, gi, :], P[:, 0:TW], AF.Silu), reads=[BP], writes=[Bsg])
                    if 2 <= ch < 4:
                        half = ch - 2
                        for bl in range(3):
                            P = pk[ctr[0] % 2]; BP = Bpk[ctr[0] % 2]
                            ctr[0] += 1
                            for fc in range(8):
                                S.op("pe", lambda e, P=P, W=W, fc=fc, bl=bl, H=H: e.matmul(
                                    P[:], H[:, fc, bl * 128:(bl + 1) * 128], W[:, fc, :], start=(fc == 0), stop=(fc == 7)),
                                    reads=[BH, BW], writes=[BP], inc=(fc == 7))
                            for hh in range(2):
                                h = half * 2 + hh
                                S.op("dve", lambda e, P=P, bl=bl, h=h, hh=hh: e.tensor_scalar(
                                    kts[:, bl, h * 256:(h + 1) * 256], P[:, hh * 256:(hh + 1) * 256], kdec_sb[:, bl * 4 + h:bl * 4 + h + 1], None, ALU.mult),
                                    reads=[BP, Bc], writes=[Bkts])
                    if 4 <= ch < 8:
                        h = ch - 4
                        for bl in range(3):
                            P = pk[ctr[0] % 2]; BP = Bpk[ctr[0] % 2]
                            ctr[0] += 1
                            for fc in range(8):
                                S.op("pe", lambda e, P=P, W=W, fc=fc, bl=bl, H=H: e.matmul(
                                    P[:], H[:, fc, bl * 128:(bl + 1) * 128], W[:, fc, :], start=(fc == 0), stop=(fc == 7)),
                                    reads=[BH, BW], writes=[BP], inc=(fc == 7))
                            S.op("act", lambda e, P=P, bl=bl, h=h: e.copy(vts[:, bl, h * 512:(h + 1) * 512], P[:]), reads=[BP], writes=[Bvts])
                def st1(h):
                    nonlocal sci
                    ST = sTb[h]; BST = BsTb[h]
                    for jb in range(3):
                        P = psc[sci % 2]; BP = Bpsc[sci % 2]
                        sci += 1
                        for dc in range(2):
                            S.op("pe", lambda e, P=P, h=h, dc=dc, jb=jb: e.matmul(
                                P[:, 0:TW], kT[:, h * 2 + dc, jb * 128:(jb + 1) * 128], qT[:, h * 2 + dc, :], start=(dc == 0), stop=(dc == 1)),
                                reads=[BkT, BqT], writes=[BP], inc=(dc == 1))
                        S.op("dve", lambda e, P=P, ST=ST, jb=jb, h=h: e.tensor_tensor(
                            ST[:, jb, :], P[:, 0:TW], dm_sb[:, (h * 3 + jb) * TW:(h * 3 + jb + 1) * TW], ALU.mult),
                            reads=[BP, Bc], writes=[BST])

                def st2(h):
                    nonlocal oi
                    ST = sTb[h]; BST = BsTb[h]
                    O32 = o32[h % 2]; OSQ = osq[h % 2]
                    for ec in range(4):
                        P = po[oi % 2]; BP = Bpo[oi % 2]
                        oi += 1
                        for jb in range(3):
                            S.op("pe", lambda e, P=P, h=h, ec=ec, jb=jb, ST=ST: e.matmul(
                                P[:, 0:TW], vts[:, jb, h * 512 + ec * 128:h * 512 + (ec + 1) * 128], ST[:, jb, :], start=(jb == 0), stop=False),
                                reads=[Bvts, BST], writes=[BP], inc=False)
                        for dc in range(2):
                            S.op("pe", lambda e, P=P, h=h, ec=ec, dc=dc: e.matmul(
                                P[:, 0:TW], Sb[:, h * 2 + dc, ec * 128:(ec + 1) * 128], qdT[:, h * 2 + dc, :], start=False, stop=(dc == 1)),
                                reads=[BSb, BqdT], writes=[BP], inc=(dc == 1))
                        S.op("act", lambda e, P=P, ec=ec, O32=O32: e.copy(O32[:, ec, :], P[:, 0:TW]), reads=[BP], writes=[Bo32[h % 2]])
                        S.op("act", lambda e, P=P, ec=ec, OSQ=OSQ: e.activation(OSQ[:, ec, :], P[:, 0:TW], AF.Square), reads=[BP], writes=[Bosq[h % 2]])

                def st3(h, Y=Y, BY=BY):
                    O32 = o32[h % 2]; OSQ = osq[h % 2]; R = rstd[h % 2]; BR = Brstd[h % 2]
                    for ec in range(4):
                        S.op("pe", lambda e, ec=ec, OSQ=OSQ: e.matmul(pss[:, 0:TW], ones_f[:], OSQ[:, ec, :], start=(ec == 0), stop=(ec == 3)),
                             reads=[Bc, Bosq[h % 2]], writes=[Bpss], inc=(ec == 3))
                    S.op("act", lambda e, R=R: e.activation(R[:], pss[:, 0:TW], AF.Ln, bias=RMS_EPS, scale=1.0 / 512.0), reads=[Bpss], writes=[BR])
                    S.op("act", lambda e, R=R: e.activation(R[:], R[:], AF.Exp, scale=-0.5), reads=[BR], writes=[BR])
                    for ec in range(4):
                        S.op("dve", lambda e, ec=ec, O32=O32, R=R: e.tensor_tensor(tmpo[:], O32[:, ec, :], R[:], ALU.mult), reads=[Bo32[h % 2], BR], writes=[Btmpo])
                        S.op("dve", lambda e, ec=ec, h=h, Y=Y: e.tensor_tensor(Y[:, h * 4 + ec, :], tmpo[:], sg[:, h * 4 + ec, :], ALU.mult),
                             reads=[Btmpo, Bsg], writes=[BY])

                if L3_SUB >= 2:
                    for stg_fn, hh in ((st1, 0), (st1, 1), (st2, 0), (st1, 2), (st2, 1), (st3, 0), (st1, 3), (st2, 2), (st3, 1),
                                       (st2, 3), (st3, 2), (st3, 3)):
                        stg_fn(hh)
                if L3_SUB >= 3:
                    emit_state_update(S, kts, Bkts, vts, Bvts, Sst, BS, Sb, BSb, pk, Bpk, ctr)
                if L3_SUB >= 2:
                    S.dma("sp", yv[:, :, c0:c0 + TW], Y[:], reads=[BY], writes=[ByT])
            S.barrier()
            S.flush()

def l3_consts():
    p = np.arange(128, dtype=np.float64)[:, None]
    i = np.arange(TW, dtype=np.float64)[None, :]
    dm = np.zeros((128, 4, 3, TW), np.float32)
    qd = np.zeros((128, 4, TW), np.float32)
    for h in range(4):
        g = RET_GAMMA[h]
        for jb in range(3):
            j = jb * 128 + p
            same = (j // 64) == (i // 64)
            before = (j // 64) < (i // 64)
            val = np.where(same, g ** np.abs(i - j), np.where(before, g ** np.maximum(i - j, 0), 0.0)) / 16.0
            dm[:, h, jb, :] = val
        qd[:, h, :] = g ** (i + 1.0)
    return dm.reshape(128, 12 * TW), qd.reshape(128, 4 * TW)


PAIRS = [[0, 1], [2, 3], [4, 5], [6, 7]]


def build_fused():
    nc = bass.Bass("TRN2", target_bir_lowering=False)

    def din(name, shape):
        return nc.dram_tensor(name, list(shape), F32, kind="ExternalInput").ap()

    T1 = {"xT": din("xT", [D, LR]), "wq": din("wq", [D, 512]), "wk": din("wk", [D, 512]), "wv": din("wv", [D, 512]),
          "wf": din("wf", [D, 4]), "fbias": din("fbias", [4, 1]), "lamv": din("lamv", [128, 256]), "gsub": din("gsub", [128, 1]),
          "dbias": din("dbias", [128, 2 * 66]), "Fd": din("Fd", [128, 2 * 3 * TW]), "Ff": din("Ff", [128, 3 * TW]),
          "qaug": din("qaug", [2, 3, LR]), "odt": BF16}
    hres = din("hres", [D, HALF])
    flags = din("flags", [128, 2])
    w_out0 = din("w_out0", [D, D])
    w1_0 = din("w1_0", [D, 4 * D])
    w2_0 = din("w2_0", [4 * D, D])
    lnp0 = din("lnp0", [128, 32])
    w_in1 = din("w_in1", [D, 6144])
    w_out1 = din("w_out1", [2048, D])
    w1_1 = din("w1_1", [D, 4 * D])
    w2_1 = din("w2_1", [4 * D, D])
    lnp1 = din("lnp1", [128, 32])
    kdec = din("kdec", [128, 12])
    tmask = din("tmask", [128, TW])
    dmask = din("dmask", [128, 12 * TW])
    qdec = din("qdec", [128, 4 * TW])
    outT = nc.dram_tensor("outT", [D, HALF], F32, kind="ExternalOutput").ap()
    wo0_b = nc.dram_tensor("wo0_b", [D, D], BF16).ap()
    w10_b = nc.dram_tensor("w10_b", [D, 4 * D], BF16).ap()
    w20_b = nc.dram_tensor("w20_b", [4 * D, D], BF16).ap()
    wi_b = nc.dram_tensor("wi_b", [D, 6144], BF16).ap()
    wo1_b = nc.dram_tensor("wo1_b", [2048, D], BF16).ap()
    w11_b = nc.dram_tensor("w11_b", [D, 4 * D], BF16).ap()
    w21_b = nc.dram_tensor("w21_b", [4 * D, D], BF16).ap()
    NCH = NT // 2
    oT_c = [nc.dram_tensor("oT_c%d" % k, [512, 2 * TW], BF16) for k in range(NCH)]
    G_c = [nc.dram_tensor("G_c%d" % k, [1024, 2 * TW], BF16) for k in range(NCH)]
    h2T_i = nc.dram_tensor("h2T_i", [D, HALF], F32).ap()
    Se_c = [nc.dram_tensor("Se_c%d" % k, [256, 512], F32) for k in range(4)]
    Sg_c = [nc.dram_tensor("Sg_c%d" % k, [512, 512], F32) for k in range(4)]
    yT_s = nc.dram_tensor("yT_s", [2048, HALF], BF16).ap()
    T1["oT_tile"] = lambda r0, r1, t: oT_c[t // 2].ap()[r0:r1, (t % 2) * TW:(t % 2 + 1) * TW]

    def g_tile(k, t):
        gt = k * NTH + t
        return G_c[gt // 2].ap().rearrange("(c p) t -> p c t", p=128)[:, :, (gt % 2) * TW:(gt % 2 + 1) * TW]
    with contextlib.ExitStack() as outer:
        S = Sched(nc, outer)
        Bwb = Buf(); Bo = Buf(); BG = Buf(); Bh2 = Buf(); BSe = Buf(); BSg = Buf(); ByT = Buf(); Bout = Buf()
        def hookA(C):
            stg = [C.sb([128, 2048], BF16) for _ in range(2)]
            Bs = [Buf(), Buf()]
            for (s_, d_, r_, c_) in ():
                yield from iter_convert(S, stg, Bs, s_, d_, Bwb, r_, c_)

        def hookB(C):
            stg = [C.sb([128, 2048], BF16) for _ in range(2)]
            Bs = [Buf(), Buf()]
            for (s_, d_, r_, c_) in ((w_out0, wo0_b, D, D), (w1_0, w10_b, D, 4 * D), (w2_0, w20_b, 4 * D, D),
                                     (w_in1, wi_b, D, 6144), (w_out1, wo1_b, 2048, D), (w1_1, w11_b, D, 4 * D), (w2_1, w21_b, 4 * D, D)):
                yield from iter_convert(S, stg, Bs, s_, d_, Bwb, r_, c_)
        T1["hookA"] = hookA
        T1["hookB"] = hookB
        emit_l1(nc, S, T1, Bo)
        for k in range(NCH):
            S.op("pool", lambda e, k=k: e.collective_compute("AllGather", ALU.bypass, replica_groups=PAIRS,
                                                             ins=[oT_c[k].ap().opt()], outs=[G_c[k].ap().opt()]), reads=[Bo], writes=[BG])
        with contextlib.ExitStack() as st:
            emit_post(nc, S, st, NTH, 8, g_tile, "blend", BG, hres, wo0_b, w10_b, w20_b, Bwb, lnp0, h2T_i, Bh2, flags=flags)
            S.barrier()
            S.flush()
        Tp = {"h2T": h2T_i, "wk": wi_b[:, 1024:2048], "wv": wi_b[:, 2048:4096], "kdec": kdec, "tmask": tmask,
              "S_end": [Se_c[k].ap().rearrange("(p i) f -> p i f", i=8) for k in range(4)], "wqueue": "sp", "Bwb": Bwb}
        emit_prepass(nc, S, Tp, Bh2, BSe)
        for k in range(4):
            S.op("pool", lambda e, k=k: e.collective_compute("AllGather", ALU.bypass, replica_groups=PAIRS,
                                                             ins=[Se_c[k].ap().opt()], outs=[Sg_c[k].ap().opt()]), reads=[BSe], writes=[BSg])
        Tr = {"h2T": h2T_i, "wi_b": wi_b, "kdec": kdec, "tmask": tmask, "dmask": dmask, "qdec": qdec, "yT_s": yT_s,
              "S_src": [Sg_c[k].ap()[0:256, :].rearrange("(p i) f -> p i f", i=8) for k in range(4)], "flags": flags}
        with contextlib.ExitStack() as st:
            emit_retention(nc, S, st, Tr, Bwb, Bh2, BSg, ByT)
        with contextlib.ExitStack() as st:
            emit_post(nc, S, st, NTH, 16, yT_s, "bf16", ByT, h2T_i, wo1_b, w11_b, w21_b, Bwb, lnp1, outT, Bout)
            S.barrier()
            S.flush()
    return nc


def _token_mask(u):
    tm = np.ones((128, TW), np.float32)
    if u == 0:
        tm[:, 0:FPAD] = 0.0
    return tm


def kernel(x, meta_tokens, even_w_in, even_f_bias, diff_lambda, diff_subln_g, even_w_out,
           ret_w_in, ret_w_out, ln_g, ln_b, ffn_w1, ffn_w2):
    x = np.asarray(x, np.float32)
    f32 = lambda a: np.ascontiguousarray(np.asarray(a, np.float32))
    meta = f32(meta_tokens)
    B = x.shape[0]
    cores = list(range(8))
    kdec = ret_consts()
    dm, qd = l3_consts()
    wo = f32(even_w_out[0])
    wo_perm = np.ascontiguousarray(np.concatenate([wo[0:256], wo[512:768], wo[256:512], wo[768:1024]], axis=0))
    shared = {"w_out0": wo_perm, "w1_0": f32(ffn_w1[0]), "w2_0": f32(ffn_w2[0]), "lnp0": lnp_table(f32(ln_g), f32(ln_b), 0),
              "w_in1": f32(ret_w_in[0]), "w_out1": f32(ret_w_out[0]), "w1_1": f32(ffn_w1[1]), "w2_1": f32(ffn_w2[1]),
              "lnp1": lnp_table(f32(ln_g), f32(ln_b), 1), "kdec": kdec, "dmask": dm, "qdec": qd}
    in_maps = []
    for c in cores:
        b, u = c // 2, c % 2
        hp = np.zeros((LR, D), np.float32)
        hp[FPAD:FPAD + NMETA] = meta
        hp[FPAD + NMETA:FPAD + NMETA + SEQ] = x[b]
        xT = np.ascontiguousarray(hp.T)
        m = l1_inputs(xT, f32(even_w_in[0]), f32(even_f_bias[0]), f32(diff_lambda[0]), f32(diff_subln_g[0]), u)
        m["hres"] = np.ascontiguousarray(xT[:, u * HALF:(u + 1) * HALF])
        fl = np.zeros((128, 2), np.float32)
        fl[:, u] = 1.0
        m["flags"] = fl
        m["tmask"] = _token_mask(u)
        m.update(shared)
        in_maps.append(m)
    res = run_bass_kernel_spmd(build_fused(), in_maps, core_ids=cores).results
    out = np.empty((B, SEQ, D), np.float32)
    for b in range(B):
        hT = np.concatenate([res[2 * b]["outT"], res[2 * b + 1]["outT"]], axis=1)
        out[b] = hT[:, FPAD + NMETA:FPAD + NMETA + SEQ].T
    return out
```

```python
import contextlib
import math
import numpy as np
import concourse.bass as bass
import concourse.mybir as mybir
from concourse.bass_utils import run_bass_kernel_spmd

F32 = mybir.dt.float32
BF16 = mybir.dt.bfloat16
AF = mybir.ActivationFunctionType
ALU = mybir.AluOpType

D = 1024
SEQ = 8192
NMETA = 16
FPAD = 48
LR = 8448
TW = 384
NT = LR // TW
NB = LR // 128
HALF = LR // 2
NTH = HALF // TW
ALPHA = 4 ** 0.25
LN_EPS = 1e-5
RMS_EPS = 1e-6
LAM_INIT0 = 0.8 - 0.6 * math.exp(-0.3 * 0)
NEG = -30000.0
REAL_END = FPAD + NMETA + SEQ

ENGS = ("pe", "act", "dve", "pool", "sp")


class Buf:
    __slots__ = ("name", "w", "r", "ex")

    def __init__(self, name="", ex=False):
        self.name = name
        self.w = None
        self.r = []
        self.ex = ex


def PB():
    return Buf(ex=True)


class Sched:
    def __init__(self, nc, stack, n_dma_sems=16):
        self.nc = nc
        self.streams = {e: [] for e in ENGS}
        self.cnt = {e: 0 for e in ENGS}
        self.seen = {e: {} for e in ENGS}
        self.n_dma = n_dma_sems
        self.dma_k = 0
        self.sems = {e: stack.enter_context(nc.semaphore("s_" + e)) for e in ENGS}
        self.dsems = [stack.enter_context(nc.semaphore("d_%d" % i)) for i in range(n_dma_sems)]
        self.dlast = [0] * n_dma_sems

    def _need(self, eng, tok, waits):
        if tok is None:
            return
        kind, key, val = tok
        if kind == "e" and key == eng and eng in ("pe", "sp"):
            return
        k = (kind, key)
        if self.seen[eng].get(k, 0) >= val:
            return
        if val > waits.get(k, 0):
            waits[k] = val

    def _emit_waits(self, eng, waits):
        for k, val in waits.items():
            self.seen[eng][k] = val
            self.streams[eng].append(("wait", k, val))

    def _deps(self, eng, reads, writes):
        waits = {}
        for b in reads:
            self._need(eng, b.w, waits)
        for b in writes:
            self._need(eng, b.w, waits)
            for t in b.r:
                self._need(eng, t, waits)
        return waits

    def _mark(self, tok, reads, writes):
        for b in reads:
            b.r.append(tok)
            if len(b.r) > 24:
                b.r = b.r[-24:]
        for b in writes:
            b.w = tok
            b.r = []

    def op(self, eng, fn, reads=(), writes=(), inc=True):
        if any(b.ex for b in reads):
            writes = list(writes) + [b for b in reads if b.ex]
            reads = [b for b in reads if not b.ex]
        waits = self._deps(eng, reads, writes)
        self._emit_waits(eng, waits)
        if inc:
            self.cnt[eng] += 1
            tok = ("e", eng, self.cnt[eng])
        else:
            tok = ("e", eng, self.cnt[eng] + 1)
        self.streams[eng].append(("op", fn, inc))
        self._mark(tok, reads, writes)
        return tok

    def dma(self, q, out_ap, in_ap, reads=(), writes=(), **kw):
        waits = self._deps(q, reads, writes)
        s = self.dma_k % self.n_dma
        v = self.dlast[s] + 16
        self.dma_k += 1
        if v > 16:
            self._need(q, ("d", s, v - 16), waits)
        self._emit_waits(q, waits)
        self.dlast[s] = v
        tok = ("d", s, v)
        self.streams[q].append(("dma", out_ap, in_ap, s, kw))
        self._mark(tok, reads, writes)
        return tok

    def barrier(self):
        toks = [("e", e, self.cnt[e]) for e in ENGS if self.cnt[e] > 0]
        toks += [("d", s, self.dlast[s]) for s in range(self.n_dma) if self.dlast[s] > 0]
        for e in ENGS:
            waits = {}
            for t in toks:
                if t[0] == "e" and t[1] == e:
                    continue
                self._need(e, t, waits)
            self._emit_waits(e, waits)

    def flush(self):
        nc = self.nc
        with nc.Block() as block:
            def run(e, engobj):
                for item in self.streams[e]:
                    if item[0] == "wait":
                        (kind, key), val = item[1], item[2]
                        sem = self.sems[key] if kind == "e" else self.dsems[key]
                        engobj.wait_ge(sem, val)
                    elif item[0] == "op":
                        ins = item[1](engobj)
                        if item[2]:
                            ins.then_inc(self.sems[e], 1)
                    else:
                        _, o, i, s, kw = item
                        engobj.dma_start(out=o, in_=i, **kw).then_inc(self.dsems[s], 16)

            @block.tensor
            def _(eng):
                run("pe", eng)

            @block.scalar
            def _(eng):
                run("act", eng)

            @block.vector
            def _(eng):
                run("dve", eng)

            @block.gpsimd
            def _(eng):
                run("pool", eng)

            @block.sync
            def _(eng):
                run("sp", eng)
        self.streams = {e: [] for e in ENGS}


class Ctx:
    K = [0]

    def __init__(self, nc, stack):
        self.nc = nc
        self.st = stack

    def sb(self, shape, dt, name=None):
        Ctx.K[0] += 1
        return self.st.enter_context(self.nc.sbuf_tensor(name or ("t%d" % Ctx.K[0]), list(shape), dt))

    def ps(self, shape, dt=F32, name=None):
        Ctx.K[0] += 1
        return self.st.enter_context(self.nc.psum_tensor(name or ("p%d" % Ctx.K[0]), list(shape), dt))


SCALE = 0.125
DEBUG_A = False


def emit_l1(nc, S, T, Bo):
    xT = T["xT"]; wq = T["wq"]; wk = T["wk"]; wv = T["wv"]; wf = T["wf"]; fbias = T["fbias"]
    lamv = T["lamv"]; gsub = T["gsub"]; dbias = T["dbias"]; Fd = T["Fd"]; Ff = T["Ff"]; qaug = T["qaug"]
    ODT = T["odt"]
    QT_s = nc.dram_tensor("QT_s", [8, 64, LR], BF16).ap()
    KT_s = nc.dram_tensor("KT_s", [8, 64, LR], BF16).ap()
    V_s = nc.dram_tensor("V_s", [LR, 512], BF16).ap()
    AQ_s = nc.dram_tensor("AQ_s", [4, 3, LR], BF16).ap()
    AQd_s = nc.dram_tensor("AQd_s", [6, LR], BF16).ap()
    out_toks = []
    if True:
        BQT = [Buf() for _ in range(8)]
        BKT = [Buf() for _ in range(8)]
        BV = Buf()
        BAQ = Buf()
        with contextlib.ExitStack() as st:
            C = Ctx(nc, st)
            wq_sb = C.sb([128, 8, 512], BF16)
            wk_sb = C.sb([128, 8, 512], BF16)
            wv_sb = C.sb([128, 8, 512], BF16)
            wf_sb = C.sb([128, 8, 4], BF16)
            fb_sb = C.sb([4, 1], F32)
            nfb_sb = C.sb([4, 1], F32)
            ident = C.sb([128, 128], F32)
            xt = [C.sb([128, 8, TW], BF16) for _ in range(2)]
            stq = [C.sb([128, TW], BF16) for _ in range(4)]
            stv = [C.sb([128, 512], BF16) for _ in range(2)]
            lf = C.sb([4, LR], F32)
            cs = C.sb([4, LR], F32)
            ones4 = C.sb([4, LR // 4], F32)
            e_t = C.sb([4, TW], F32)
            hi = C.sb([4, LR], BF16)
            mid = C.sb([4, LR], BF16)
            lo = C.sb([4, LR], BF16)
            pq = [C.ps([128, 512]) for _ in range(4)]
            pv = [C.ps([128, 512]) for _ in range(2)]
            pf = C.ps([128, 512])
            Bw = Buf(); Bxt = [Buf(), Buf()]; Bstq = [Buf() for _ in range(4)]; Bstv = [Buf(), Buf()]
            Bpq = [PB() for _ in range(4)]; Bpv = [PB(), PB()]; Bpf = PB(); Blf = Buf(); Bcs = Buf()
            Bet = Buf(); Bfb = Buf(); Bo4 = Buf(); Bhi = Buf(); Bmid = Buf(); Blo = Buf()

            wr = "(c p) o -> p c o"
            S.dma("pool", wq_sb[:], wq.rearrange(wr, p=128), writes=[Bw])
            for fc in range(8):
                S.dma("pool", wk_sb[:, fc, :], wk[fc * 128:(fc + 1) * 128, :], writes=[Bw])
                S.dma("pool", wv_sb[:, fc, :], wv[fc * 128:(fc + 1) * 128, :], writes=[Bw])
            S.dma("pool", wf_sb[:], wf.rearrange(wr, p=128), writes=[Bw])
            S.dma("sp", fb_sb[:], fbias, writes=[Bfb])
            S.op("dve", lambda e: e.tensor_scalar(nfb_sb[:], fb_sb[:], -1.0, None, ALU.mult), reads=[Bfb], writes=[Bfb])
            S.op("dve", lambda e: e.memset(ones4[:], 1.0), writes=[Bo4])
            xTr = xT.rearrange("(c p) t -> p c t", p=128)
            conv_it = T["hookA"](C) if T.get("hookA") else None
            qi = 0
            vi = 0
            for t in range(NT):
                c0 = t * TW
                X = xt[t % 2]; BX = Bxt[t % 2]
                S.dma("pool", X[:], xTr[:, :, c0:c0 + TW], writes=[BX])
                for which, (w_sb, dst, BD) in enumerate(((wq_sb, QT_s, BQT), (wk_sb, KT_s, BKT))):
                    for g in range(4):
                        P = pq[qi % 4]; BP = Bpq[qi % 4]; ST = stq[qi % 4]; BS = Bstq[qi % 4]
                        qi += 1
                        for c in range(8):
                            S.op("pe", lambda e, P=P, w_sb=w_sb, c=c, g=g, X=X: e.matmul(
                                P[:, 0:TW], w_sb[:, c, g * 128:(g + 1) * 128], X[:, c, :], start=(c == 0), stop=(c == 7)),
                                reads=[Bw, BX], writes=[BP], inc=(c == 7))
                        eng = "act" if (g % 2 == 0) else "dve"
                        if eng == "act":
                            S.op("act", lambda e, ST=ST, P=P: e.copy(ST[:], P[:, 0:TW]), reads=[BP], writes=[BS])
                        else:
                            S.op("dve", lambda e, ST=ST, P=P: e.tensor_copy(ST[:], P[:, 0:TW]), reads=[BP], writes=[BS])
                        S.dma("sp", dst[2 * g:2 * g + 2].rearrange("u r t -> (u r) t")[:, c0:c0 + TW], ST[:],
                              reads=[BS], writes=[BD[2 * g], BD[2 * g + 1]])
                for bl in range(3):
                    P = pv[vi % 2]; BP = Bpv[vi % 2]; ST = stv[vi % 2]; BS = Bstv[vi % 2]
                    vi += 1
                    for c in range(8):
                        S.op("pe", lambda e, P=P, c=c, bl=bl, X=X: e.matmul(
                            P[:], X[:, c, bl * 128:(bl + 1) * 128], wv_sb[:, c, :], start=(c == 0), stop=(c == 7)),
                            reads=[Bw, BX], writes=[BP], inc=(c == 7))
                    S.op("dve", lambda e, ST=ST, P=P: e.tensor_copy(ST[:], P[:]), reads=[BP], writes=[BS])
                    r0 = c0 + bl * 128
                    S.dma("sp", V_s[r0:r0 + 128, :], ST[:], reads=[BS], writes=[BV])
                for c in range(8):
                    S.op("pe", lambda e, c=c, X=X: e.matmul(pf[0:4, 0:TW], wf_sb[:, c, :], X[:, c, :], start=(c == 0), stop=(c == 7)),
                         reads=[Bw, BX], writes=[Bpf], inc=(c == 7))
                S.op("act", lambda e: e.activation(e_t[:], pf[0:4, 0:TW], AF.Exp, bias=nfb_sb[:, 0:1], scale=-1.0),
                     reads=[Bpf, Bfb], writes=[Bet])
                S.op("act", lambda e, c0=c0: e.activation(lf[:, c0:c0 + TW], e_t[:], AF.Ln, bias=1.0, scale=1.0),
                     reads=[Bet], writes=[Blf])
                if conv_it is not None:
                    for _ in range(3):
                        next(conv_it, None)
            S.op("dve", lambda e: e.memset(lf[:, 0:FPAD], 0.0), reads=[Blf], writes=[Blf])
            S.op("dve", lambda e: e.memset(lf[:, REAL_END:LR], 0.0), reads=[Blf], writes=[Blf])
            CH = LR // 4
            for k in range(4):
                a = k * CH
                if k == 0:
                    S.op("dve", lambda e, a=a: e.tensor_tensor_scan(cs[:, a:a + CH], ones4[:], lf[:, a:a + CH], 0.0, ALU.mult, ALU.add),
                         reads=[Blf, Bo4], writes=[Bcs])
                else:
                    S.op("dve", lambda e, a=a: e.tensor_tensor_scan(cs[:, a:a + CH], ones4[:], lf[:, a:a + CH], cs[:, a - 1:a], ALU.mult, ALU.add),
                         reads=[Blf, Bo4, Bcs], writes=[Bcs])
            CS_s = nc.dram_tensor("CS_s", [4, LR], F32).ap()
            BCS = Buf()
            S.dma("sp", CS_s, cs[:], reads=[Bcs], writes=[BCS])
            for t in range(NT):
                c0 = t * TW
                S.op("dve", lambda e, c0=c0: e.tensor_scalar(lf[:, c0:c0 + TW], cs[:, c0:c0 + TW], cs[:, c0:c0 + 1], -8.0, ALU.subtract, ALU.mult),
                     reads=[Bcs, Blf], writes=[Blf])
            S.op("dve", lambda e: e.tensor_copy(hi[:], lf[:]), reads=[Blf], writes=[Bhi])
            S.op("dve", lambda e: e.tensor_tensor(lf[:], lf[:], hi[:], ALU.subtract), reads=[Blf, Bhi], writes=[Blf])
            S.op("dve", lambda e: e.tensor_copy(mid[:], lf[:]), reads=[Blf], writes=[Bmid])
            S.op("dve", lambda e: e.tensor_tensor(lf[:], lf[:], mid[:], ALU.subtract), reads=[Blf, Bmid], writes=[Blf])
            S.op("dve", lambda e: e.tensor_copy(lo[:], lf[:]), reads=[Blf], writes=[Blo])
            qa_sb = C.sb([6, LR], BF16)
            Bqa = Buf()
            S.dma("pool", qa_sb[:], qaug.rearrange("h r t -> (h r) t"), writes=[Bqa])
            S.dma("sp", AQd_s, qa_sb[:], reads=[Bqa], writes=[BAQ])
            S.dma("sp", AQ_s[:, 0, :], hi[:], reads=[Bhi], writes=[BAQ])
            S.dma("sp", AQ_s[:, 1, :], mid[:], reads=[Bmid], writes=[BAQ])
            S.dma("sp", AQ_s[:, 2, :], lo[:], reads=[Blo], writes=[BAQ])
            if conv_it is not None:
                for _ in conv_it:
                    pass
            S.barrier()
            S.flush()

        with contextlib.ExitStack() as st:
            C = Ctx(nc, st)
            kt = [C.sb([128, LR], BF16) for _ in range(2)]
            qt = [C.sb([128, LR], BF16) for _ in range(2)]
            vt = [C.sb([128, NB, 128], BF16) for _ in range(2)]
            Fd_sb = C.sb([128, 2 * 3 * TW], F32)
            Ff_sb = C.sb([128, 3 * TW], F32)
            dbias_sb = C.sb([128, 2 * 66], F32)
            csk = C.sb([128, 4, NB], F32)
            csT0 = C.sb([128, 4, NT], F32)
            bkt = [C.sb([128, NB], F32) for _ in range(2)]
            ones_bf = C.sb([128, 128], BF16)
            ones_b0 = C.sb([128, 128], BF16)
            ones_f = C.sb([128, 128], F32)
            lam_sb = C.sb([128, 256], F32)
            lprod = C.sb([128, 128], F32)
            lsum = C.sb([128, 2], F32)
            lexp = C.sb([128, 2], F32)
            neglam = C.sb([128, 1], F32)
            gcol = C.sb([128, 1], F32)
            pT = [C.sb([128, TW], BF16) for _ in range(4)]
            tmp = [C.sb([128, TW], F32) for _ in range(2)]
            rl = C.sb([128, TW], F32)
            lsb = [C.sb([128, TW], F32) for _ in range(2)]
            l0b = [C.sb([128, TW], F32) for _ in range(2)]
            Blsb = [Buf(), Buf()]; Bl0b = [Buf(), Buf()]
            a0 = C.sb([128, TW], F32)
            a1 = C.sb([128, TW], F32)
            od = C.sb([128, TW], F32)
            sq = C.sb([128, TW], F32)
            rstd = C.sb([128, TW], F32)
            ofin = [C.sb([128, TW], ODT) for _ in range(2)]
            NS = 3
            NPT = 4
            ps_s = [C.ps([128, 512]) for _ in range(NS)]
            ps_o = [C.ps([128, 512]) for _ in range(2)]
            ps_l = [C.ps([128, 512]) for _ in range(2)]
            ps_x = C.ps([128, 512])
            Bkt = [Buf(), Buf()]; Bqt = [Buf(), Buf()]; Bvt = [Buf(), Buf()]
            Bc = Buf(); Bcsk = Buf(); BcsT0 = Buf(); Bbkt = [Buf(), Buf()]
            Bps_s = [PB(), PB(), PB()]; Bps_o = [PB(), PB()]; Bps_l = [PB(), PB()]; Bps_x = PB()
            BpT = [Buf() for _ in range(4)]; Btmp = [Buf(), Buf()]
            Brl = Buf(); Ba0 = Buf(); Ba1 = Buf(); Bod = Buf(); Bsq = Buf(); Brstd = Buf(); Bofin = [Buf(), Buf()]
            Blam = Buf()

            S.dma("sp", Fd_sb[:], Fd, writes=[Bc])
            S.dma("sp", Ff_sb[:], Ff, writes=[Bc])
            S.dma("sp", dbias_sb[:], dbias, writes=[Bc])
            S.dma("sp", lam_sb[:], lamv, writes=[Blam])
            S.dma("sp", gcol[:], gsub, writes=[Blam])
            for h in range(4):
                S.dma("sp", csk[:, h, :], CS_s[h].rearrange("(b p) -> p b", p=128), reads=[BCS], writes=[Bcsk], allow_slow_non_contiguous=True)
            for h in range(4):
                src = bass.AP(CS_s.tensor, CS_s.offset + h * LR, [[0, 128], [TW, NT]])
                S.dma("sp", csT0[:, h, :], src, reads=[BCS], writes=[BcsT0], allow_slow_non_contiguous=True)
            S.op("pool", lambda e: e.memset(ones_bf[:], 1.0), writes=[Bc])
            S.op("pool", lambda e: e.memset(ones_b0[:], 1.0), writes=[Bc])
            S.op("pool", lambda e: e.memset(ones_b0[0:FPAD, :], 0.0), reads=[Bc], writes=[Bc])
            S.op("pool", lambda e: e.memset(ones_f[:], 1.0), writes=[Bc])
            for b in range(2):
                S.op("pool", lambda e, b=b: e.memset(kt[b][64:67, :], 1.0), writes=[Bkt[b]])
            S.op("dve", lambda e: e.tensor_tensor(lprod[:, 0:64], lam_sb[:, 0:64], lam_sb[:, 64:128], ALU.mult), reads=[Blam], writes=[Blam])
            S.op("dve", lambda e: e.tensor_tensor(lprod[:, 64:128], lam_sb[:, 128:192], lam_sb[:, 192:256], ALU.mult), reads=[Blam], writes=[Blam])
            S.op("dve", lambda e: e.reduce_sum(lsum[:, 0:1], lprod[:, 0:64], mybir.AxisListType.X), reads=[Blam], writes=[Blam])
            S.op("dve", lambda e: e.reduce_sum(lsum[:, 1:2], lprod[:, 64:128], mybir.AxisListType.X), reads=[Blam], writes=[Blam])
            S.op("act", lambda e: e.activation(lexp[:], lsum[:], AF.Exp), reads=[Blam], writes=[Blam])
            S.op("dve", lambda e: e.scalar_tensor_tensor(neglam[:], lexp[:, 1:2], -LAM_INIT0, lexp[:, 0:1], ALU.add, ALU.subtract),
                 reads=[Blam], writes=[Blam])
            S.op("dve", lambda e: e.tensor_scalar(gcol[:], gcol[:], 1.0 - LAM_INIT0, None, ALU.mult), reads=[Blam], writes=[Blam])

            conv_itB = T["hookB"](C) if T.get("hookB") else None
            si = 0
            pi = 0
            ti = 0
            oi = 0
            fi = 0
            for g in range(4):
                isdiff = g < 2
                units = (2 * g, 2 * g + 1)
                dv = 128 if isdiff else 64
                for ui, u in enumerate(units):
                    S.dma("sp", kt[ui][0:64, :], KT_s[u], reads=[BKT[u]], writes=[Bkt[ui]])
                    S.dma("sp", qt[ui][0:64, :], QT_s[u], reads=[BQT[u]], writes=[Bqt[ui]])
                    if isdiff:
                        S.dma("sp", qt[ui][64:67, :], AQd_s[3 * g:3 * g + 3, :], reads=[BAQ], writes=[Bqt[ui]])
                    else:
                        S.dma("sp", qt[ui][64:67, :], AQ_s[u - 4], reads=[BAQ], writes=[Bqt[ui]])
                if isdiff:
                    S.dma("sp", vt[0][:], V_s[:, g * 128:(g + 1) * 128].rearrange("(b p) c -> p b c", p=128),
                          reads=[BV], writes=[Bvt[0]])
                else:
                    for ui, u in enumerate(units):
                        hf = u - 4
                        S.dma("sp", vt[ui][:, :, 0:64], V_s[:, 256 + hf * 64:256 + (hf + 1) * 64].rearrange("(b p) c -> p b c", p=128),
                              reads=[BV], writes=[Bvt[ui]])
                        S.op("pool", lambda e, ui=ui: e.memset(vt[ui][:, :, 64:128], 1.0), writes=[Bvt[ui]])
                        S.op("pool", lambda e, ui=ui: e.memset(vt[ui][0:FPAD, 0, 64:128], 0.0), reads=[Bvt[ui]], writes=[Bvt[ui]])
                items = []
                for t in range(NT):
                    for ui, u in enumerate(units):
                        for j in range(3 * t + 3):
                            items.append((t, ui, u, j))
                state = {}
                deferred = []

                def stage_a(t, ui, u, j):
                    nonlocal si, pi, ti, oi, fi
                    c0 = t * TW
                    nblk = 3 * t + 3
                    K = kt[ui]; Q = qt[ui]; BK = Bkt[ui]; BQ = Bqt[ui]
                    hf = u - 4
                    if j == 0:
                        PO = ps_o[oi % 2]; BPO = Bps_o[oi % 2]; PL = ps_l[oi % 2]; BPL = Bps_l[oi % 2]
                        oi += 1
                        BKk = None; BBK = None
                        if not isdiff:
                            BKk = bkt[fi % 2]; BBK = Bbkt[fi % 2]
                            fi += 1
                            S.op("dve", lambda e, BKk=BKk, nblk=nblk, hf=hf, t=t: e.tensor_scalar(
                                BKk[:, 0:nblk], csk[:, hf, 0:nblk], csT0[:, hf, t:t + 1], None, ALU.subtract),
                                reads=[Bcsk, BcsT0], writes=[BBK])
                        state[(t, ui)] = (PO, BPO, PL, BPL, BKk, BBK)
                    PO, BPO, PL, BPL, BKk, BBK = state[(t, ui)]
                    jj = j - 3 * t
                    PS = ps_s[si % NS]; BPS = Bps_s[si % NS]
                    si += 1
                    S.op("pe", lambda e, PS=PS, K=K, Q=Q, j=j, c0=c0: e.matmul(
                        PS[:, 0:TW], K[0:67, j * 128:(j + 1) * 128], Q[0:67, c0:c0 + TW], start=True, stop=True),
                        reads=[BK, BQ], writes=[BPS])
                    PT = pT[pi % NPT]; BPT = BpT[pi % NPT]
                    pi += 1
                    if isdiff:
                        col = g * 66 + (jj + 63)
                        bcol = dbias_sb[:, col:col + 1]
                        rb = [Bc]
                    else:
                        bcol = BKk[:, j:j + 1]
                        rb = [BBK]
                    if jj < 0:
                        S.op("act", lambda e, PT=PT, PS=PS, bcol=bcol: e.activation(PT[:], PS[:, 0:TW], AF.Exp, bias=bcol, scale=SCALE),
                             reads=[BPS] + rb, writes=[BPT])
                    else:
                        TM = tmp[ti % 2]; BTM = Btmp[ti % 2]
                        ti += 1
                        if isdiff:
                            Fap = Fd_sb[:, (g * 3 + jj) * TW:(g * 3 + jj + 1) * TW]
                        else:
                            Fap = Ff_sb[:, jj * TW:(jj + 1) * TW]
                        S.op("dve", lambda e, TM=TM, PS=PS, Fap=Fap: e.scalar_tensor_tensor(
                            TM[:], PS[:, 0:TW], SCALE, Fap, ALU.mult, ALU.add), reads=[BPS, Bc], writes=[BTM])
                        S.op("act", lambda e, PT=PT, TM=TM, bcol=bcol: e.activation(PT[:], TM[:], AF.Exp, bias=bcol, scale=1.0),
                             reads=[BTM] + rb, writes=[BPT])
                    return PT, BPT

                def stage_b(idx, t, ui, u, j, PT, BPT):
                    c0 = t * TW
                    nblk = 3 * t + 3
                    hf = u - 4
                    if isdiff:
                        VT = vt[0]; BVT = Bvt[0]
                    else:
                        VT = vt[ui]; BVT = Bvt[ui]
                    PO, BPO, PL, BPL, BKk, BBK = state[(t, ui)]
                    if not isdiff:
                        S.op("pe", lambda e, PO=PO, VT=VT, PT=PT, j=j, nblk=nblk: e.matmul(
                            PO[:, 0:TW], VT[:, j, :], PT[:], start=(j == 0), stop=(j == nblk - 1)),
                            reads=[BVT, BPT], writes=[BPO], inc=True)
                        if j != nblk - 1:
                            return
                        LS = lsb[(2 * t + ui) % 2]; BLS = Blsb[(2 * t + ui) % 2]
                        L0 = l0b[(2 * t + ui) % 2]; BL0 = Bl0b[(2 * t + ui) % 2]
                        S.op("dve", lambda e, LS=LS, PO=PO: e.tensor_scalar(LS[64:128, :], PO[64:128, 0:TW], 1e-30, None, ALU.add),
                             reads=[BPO], writes=[BLS])
                        S.dma("sp", L0[0:64, :], LS[64:128, :], reads=[BLS], writes=[BL0])
                        S.op("dve", lambda e, L0=L0: e.reciprocal(rl[0:64, :], L0[0:64, :]), reads=[BL0], writes=[Brl])
                        OF = ofin[(2 * t + ui) % 2]; BOF = Bofin[(2 * t + ui) % 2]
                        S.op("dve", lambda e, OF=OF, PO=PO: e.tensor_tensor(OF[0:64, :], PO[0:64, 0:TW], rl[0:64, :], ALU.mult),
                             reads=[BPO, Brl], writes=[BOF])
                        r0 = 256 + hf * 64
                        out_toks.append(S.dma("sp", T["oT_tile"](r0, r0 + 64, t), OF[0:64, :], reads=[BOF], writes=[Bo]))
                        return
                    S.op("pe", lambda e, PO=PO, VT=VT, PT=PT, j=j, nblk=nblk, dv=dv: e.matmul(
                        PO[0:dv, 0:TW], VT[:, j, 0:dv], PT[:], start=(j == 0), stop=(j == nblk - 1)),
                        reads=[BVT, BPT], writes=[BPO], inc=False)
                    on = ones_b0 if j == 0 else ones_bf
                    S.op("pe", lambda e, PL=PL, on=on, PT=PT, j=j, nblk=nblk, dv=dv: e.matmul(
                        PL[0:dv, 0:TW], on[:, 0:dv], PT[:], start=(j == 0), stop=(j == nblk - 1)),
                        reads=[Bc, BPT], writes=[BPL], inc=True)
                    if j != nblk - 1:
                        return
                    S.op("dve", lambda e, PL=PL, dv=dv: e.tensor_scalar(rl[0:dv, :], PL[0:dv, 0:TW], 1e-30, None, ALU.add), reads=[BPL], writes=[Brl])
                    S.op("dve", lambda e, dv=dv: e.reciprocal(rl[0:dv, :], rl[0:dv, :]), reads=[Brl], writes=[Brl])
                    if not isdiff:
                        OF = ofin[(2 * t + ui) % 2]; BOF = Bofin[(2 * t + ui) % 2]
                        S.op("dve", lambda e, OF=OF, PO=PO: e.tensor_tensor(OF[0:64, :], PO[0:64, 0:TW], rl[0:64, :], ALU.mult),
                             reads=[BPO, Brl], writes=[BOF])
                        r0 = 256 + hf * 64
                        out_toks.append(S.dma("sp", T["oT_tile"](r0, r0 + 64, t), OF[0:64, :], reads=[BOF], writes=[Bo]))
                        return
                    AA = a0 if ui == 0 else a1
                    BA = Ba0 if ui == 0 else Ba1
                    S.op("dve", lambda e, AA=AA, PO=PO: e.tensor_tensor(AA[:], PO[:, 0:TW], rl[:], ALU.mult),
                         reads=[BPO, Brl], writes=[BA])
                    if ui == 0:
                        return
                    OF = ofin[t % 2]; BOF = Bofin[t % 2]
                    S.op("dve", lambda e: e.scalar_tensor_tensor(od[:], a1[:], neglam[:, 0:1], a0[:], ALU.mult, ALU.add),
                         reads=[Ba0, Ba1, Blam], writes=[Bod])
                    S.op("pool", lambda e: e.tensor_tensor(sq[:], od[:], od[:], ALU.mult), reads=[Bod], writes=[Bsq])

                    def tail(OF=OF, BOF=BOF, t=t):
                        S.op("pe", lambda e: e.matmul(ps_x[:, 0:TW], ones_f[:], sq[:], start=True, stop=True),
                             reads=[Bc, Bsq], writes=[Bps_x])
                        S.op("act", lambda e: e.activation(rstd[:], ps_x[:, 0:TW], AF.Ln, bias=RMS_EPS, scale=1.0 / 128.0),
                             reads=[Bps_x], writes=[Brstd])
                        S.op("act", lambda e: e.activation(rstd[:], rstd[:], AF.Exp, scale=-0.5), reads=[Brstd], writes=[Brstd])
                        S.op("dve", lambda e, OF=OF: e.scalar_tensor_tensor(OF[:], od[:], gcol[:, 0:1], rstd[:], ALU.mult, ALU.mult),
                             reads=[Bod, Brstd, Blam], writes=[BOF])
                        out_toks.append(S.dma("sp", T["oT_tile"](g * 128, (g + 1) * 128, t), OF[:], reads=[BOF], writes=[Bo]))
                    deferred.append((idx + 3, tail))

                LA = 2
                pend = {}
                n_it = len(items)
                for i in range(n_it + LA):
                    if conv_itB is not None and i % 8 == 7:
                        next(conv_itB, None)
                    if i < n_it:
                        pend[i] = stage_a(*items[i])
                    k = i - LA
                    if k >= 0:
                        PT, BPT = pend.pop(k)
                        stage_b(k, *items[k], PT, BPT)
                    while deferred and deferred[0][0] <= k:
                        deferred.pop(0)[1]()
                while deferred:
                    deferred.pop(0)[1]()
            if conv_itB is not None:
                for _ in conv_itB:
                    pass
            S.barrier()
            S.flush()
    return out_toks


def l1_consts(s):
    p = np.arange(128, dtype=np.float64)[:, None]
    slopes = [2.0 ** (-8.0 * (h + 1) / 4) for h in (2 * s, 2 * s + 1)]
    dbias = np.zeros((128, 2, 66), np.float32)
    Fd = np.zeros((128, 2, 3, TW), np.float32)
    qaug = np.zeros((2, 3, LR), np.float32)
    i = np.arange(TW, dtype=np.float64)[None, :]
    r = np.arange(LR)
    w = r % TW
    for hl, sl in enumerate(slopes):
        for idx in range(66):
            dbias[:, hl, idx] = (sl * (128 * (idx - 63) + p))[:, 0]
        for jj in range(3):
            rk = 128 * jj + p
            vis = (rk // 64) <= (i // 64)
            f = np.where(i >= rk, 0.0, 2 * sl * (i - rk))
            Fd[:, hl, jj, :] = np.where(vis, f, NEG)
        qaug[hl, 0] = -(8 * sl) * 256 * (w // 256)
        qaug[hl, 1] = -(8 * sl) * (w % 256)
    Ff = np.zeros((128, 3, TW), np.float32)
    for jj in range(3):
        rk = 128 * jj + p
        Ff[:, jj, :] = np.where(rk <= i, 0.0, NEG)
    return (dbias.reshape(128, 132), Fd.reshape(128, 6 * TW), Ff.reshape(128, 3 * TW), qaug)


def l1_inputs(xT_b, even_w_in, even_f_bias, diff_lambda, diff_subln_g, s):
    w = even_w_in
    dq = [w[:, h * 128:(h + 1) * 128] for h in (2 * s, 2 * s + 1)]
    dk = [w[:, 512 + h * 128:512 + (h + 1) * 128] for h in (2 * s, 2 * s + 1)]
    dvv = [w[:, 1024 + h * 128:1024 + (h + 1) * 128] for h in (2 * s, 2 * s + 1)]
    fq = w[:, 1536 + 256 * s:1536 + 256 * (s + 1)]
    fk = w[:, 2048 + 256 * s:2048 + 256 * (s + 1)]
    fv = w[:, 2560 + 256 * s:2560 + 256 * (s + 1)]
    wf = w[:, 3072 + 4 * s:3072 + 4 * (s + 1)]
    dbias, Fd, Ff, qaug = l1_consts(s)
    return {
        "xT": xT_b,
        "wq": np.ascontiguousarray(np.concatenate(dq + [fq], axis=1)),
        "wk": np.ascontiguousarray(np.concatenate(dk + [fk], axis=1)),
        "wv": np.ascontiguousarray(np.concatenate(dvv + [fv], axis=1)),
        "wf": np.ascontiguousarray(wf),
        "fbias": np.ascontiguousarray(even_f_bias[4 * s:4 * (s + 1)].reshape(4, 1)),
        "lamv": np.ascontiguousarray(np.broadcast_to(diff_lambda.reshape(1, 256), (128, 256))),
        "gsub": np.ascontiguousarray(diff_subln_g.reshape(128, 1)),
        "dbias": dbias, "Fd": Fd, "Ff": Ff, "qaug": qaug,
    }


def emit_convert(S, C, src, dst, BD, rows, cols, tag, shared=None):
    if shared is None:
        stg = [C.sb([128, 2048], BF16) for _ in range(2)]
        Bs = [Buf(), Buf()]
    else:
        stg, Bs = shared
    for _ in iter_convert(S, stg, Bs, src, dst, BD, rows, cols):
        pass


def iter_convert(S, stg, Bs, src, dst, BD, rows, cols, ctr=[0]):
    for r0 in range(0, rows, 128):
        for c0 in range(0, cols, 2048):
            w = min(2048, cols - c0)
            T = stg[ctr[0] % 2]; B = Bs[ctr[0] % 2]
            ctr[0] += 1
            S.dma("pool", T[:, 0:w], src[r0:r0 + 128, c0:c0 + w], writes=[B])
            S.dma("sp", dst[r0:r0 + 128, c0:c0 + w], T[:, 0:w], reads=[B], writes=[BD])
            yield


def emit_ln(S, y, By, yb, Byb, sqb, Bsqb, tmpn, Btmpn, out_f, Bof, out_b, Bob, gcols, bcols, Bp, ones_b, Bc, ps1, Bps1, ps2, Bps2,
            mean, msq, rstd, Bst, nfeat_chunks=8):
    n = nfeat_chunks
    for c in range(n):
        S.op("act", lambda e, c=c: e.copy(yb[:, c, :], y[:, c, :]), reads=[By[c]], writes=[Byb[c]])
        S.op("act", lambda e, c=c: e.activation(sqb[:, c, :], y[:, c, :], AF.Square), reads=[By[c]], writes=[Bsqb[c]])
    for c in range(n):
        S.op("pe", lambda e, c=c: e.matmul(ps1[:, 0:TW], ones_b[:], yb[:, c, :], start=(c == 0), stop=(c == n - 1)),
             reads=[Bc, Byb[c]], writes=[Bps1], inc=(c == n - 1))
    for c in range(n):
        S.op("pe", lambda e, c=c: e.matmul(ps2[:, 0:TW], ones_b[:], sqb[:, c, :], start=(c == 0), stop=(c == n - 1)),
             reads=[Bc, Bsqb[c]], writes=[Bps2], inc=(c == n - 1))
    inv = 1.0 / (128.0 * n)
    S.op("dve", lambda e: e.tensor_scalar(mean[:], ps1[:, 0:TW], inv, None, ALU.mult), reads=[Bps1], writes=[Bst])
    S.op("dve", lambda e: e.tensor_tensor(msq[:], mean[:], mean[:], ALU.mult), reads=[Bst], writes=[Bst])
    S.op("dve", lambda e: e.scalar_tensor_tensor(msq[:], ps2[:, 0:TW], inv, msq[:], ALU.mult, ALU.subtract),
         reads=[Bps2, Bst], writes=[Bst])
    S.op("act", lambda e: e.activation(rstd[:], msq[:], AF.Ln, bias=LN_EPS, scale=1.0), reads=[Bst], writes=[Bst])
    S.op("act", lambda e: e.activation(rstd[:], rstd[:], AF.Exp, scale=-0.5), reads=[Bst], writes=[Bst])
    for c in range(n):
        eng = "dve" if c % 2 == 0 else "pool"
        S.op(eng, lambda e, c=c: e.tensor_tensor(tmpn[:, c, :], y[:, c, :], mean[:], ALU.subtract), reads=[By[c], Bst], writes=[Btmpn[c]])
        S.op(eng, lambda e, c=c: e.tensor_tensor(tmpn[:, c, :], tmpn[:, c, :], rstd[:], ALU.mult), reads=[Bst, Btmpn[c]], writes=[Btmpn[c]])
        S.op("dve", lambda e, c=c: e.tensor_scalar(out_f[:, c, :], tmpn[:, c, :], gcols[:, c:c + 1], bcols[:, c:c + 1], ALU.mult, ALU.add),
             reads=[Btmpn[c], Bp], writes=[Bof[c]])
        if out_b is not None:
            S.op("act", lambda e, c=c: e.copy(out_b[:, c, :], out_f[:, c, :]), reads=[Bof[c]], writes=[Bob[c]])


def emit_post(nc, S, st, ntiles, KC, src_o, src_mode, Bsrc, hres, wout_b, w1_b, w2_b, Bwb, lnp, hout, Bhout, flags=None):
    C = Ctx(nc, st)
    ot = [C.sb([128, KC, TW], BF16) for _ in range(2)]
    hr = [C.sb([128, 8, TW], F32) for _ in range(2)]
    y = C.sb([128, 8, TW], F32)
    ysq = C.sb([128, 8, TW], F32)
    h1 = [C.sb([128, 8, TW], F32) for _ in range(2)]
    h1b = C.sb([128, 8, TW], BF16)
    h2 = C.sb([128, 8, TW], F32)
    yb = C.sb([128, 8, TW], BF16)
    sqb = C.sb([128, 8, TW], BF16)
    hid = C.sb([128, 32, TW], BF16)
    rl = [C.sb([128, TW], F32) for _ in range(2)]
    WCOL = 4096 // KC
    wo = [C.sb([128, KC, WCOL], BF16) for _ in range(2)]
    w1c = [C.sb([128, 8, 512], BF16) for _ in range(2)]
    w2c = [C.sb([128, 8, 512], BF16) for _ in range(2)]
    lnp_sb = C.sb([128, 32], F32)
    ones_f = C.sb([128, 128], BF16)
    if src_mode == "blend":
        cand = [C.sb([128, KC, TW], BF16) for _ in range(2)]
        fl_sb = C.sb([128, 2], F32)
        Bcand = [Buf(), Buf()]
        Bfl = Buf()
        S.dma("sp", fl_sb[:], flags, writes=[Bfl])
    mean = C.sb([128, TW], F32)
    msq = C.sb([128, TW], F32)
    rstd = C.sb([128, TW], F32)
    pa = [C.ps([128, 512]) for _ in range(2)]
    pb = [C.ps([128, 512]) for _ in range(4)]
    ps1 = C.ps([128, 512])
    ps2 = C.ps([128, 512])
    Bot = [Buf(), Buf()]; Bhr = [Buf(), Buf()]
    L8 = lambda: [Buf() for _ in range(8)]
    By = L8(); Bysq = L8(); Bh1 = [L8(), L8()]; Bh1b = L8(); Bh2 = L8(); Byb = L8(); Bsqb = L8()
    Bhid = [Buf() for _ in range(32)]; Brl = [Buf(), Buf()]; Bwo = [Buf(), Buf()]; Bw1 = [Buf(), Buf()]; Bw2 = [Buf(), Buf()]
    Bp = Buf(); Bc = Buf(); Bst = Buf(); Bpa = [PB(), PB()]; Bpb = [PB() for _ in range(4)]; Bps1 = PB(); Bps2 = PB()
    S.dma("sp", lnp_sb[:], lnp, writes=[Bp])
    S.op("pool", lambda e: e.memset(ones_f[:], 1.0), writes=[Bc])
    wr = "(c p) o -> p c o"
    wo_v = wout_b.rearrange(wr, p=128)
    w1_v = w1_b.rearrange(wr, p=128)
    w2_v = w2_b.rearrange(wr, p=128)
    so_v = None if src_mode == "blend" else src_o.rearrange("(c p) t -> p c t", p=128)
    hr_v = hres.rearrange("(c p) t -> p c t", p=128)
    ho_v = hout.rearrange("(c p) t -> p c t", p=128)
    ai = 0
    wi = 0
    w1i = 0
    w2i = 0
    ri = 0
    toks = []
    def stA(t):
        nonlocal ai, wi
        c0 = t * TW
        OT = ot[t % 2]; BOT = Bot[t % 2]; HR = hr[t % 2]; BHR = Bhr[t % 2]
        if src_mode == "blend":
            for k in range(2):
                S.dma("sp", cand[k][:], src_o(k, t), reads=[Bsrc], writes=[Bcand[k]])
            for kc in range(KC):
                S.op("dve", lambda e, kc=kc, OT=OT: e.tensor_scalar(OT[:, kc, :], cand[0][:, kc, :], fl_sb[:, 0:1], None, ALU.mult),
                     reads=[Bcand[0], Bfl], writes=[BOT])
                S.op("dve", lambda e, kc=kc, OT=OT: e.scalar_tensor_tensor(OT[:, kc, :], cand[1][:, kc, :], fl_sb[:, 1:2], OT[:, kc, :], ALU.mult, ALU.add),
                     reads=[Bcand[1], Bfl, BOT], writes=[BOT])
        else:
            S.dma("pool" if src_mode == "f32" else "sp", OT[:], so_v[:, :, c0:c0 + TW], reads=[Bsrc], writes=[BOT])
        S.dma("sp", HR[:], hr_v[:, :, c0:c0 + TW], writes=[BHR])
        for n2 in range(1024 // WCOL):
            WO = wo[wi % 2]; BWO = Bwo[wi % 2]
            wi += 1
            S.dma("sp", WO[:], wo_v[:, :, n2 * WCOL:(n2 + 1) * WCOL], reads=[Bwb], writes=[BWO])
            for c4 in range(WCOL // 128):
                cc = n2 * (WCOL // 128) + c4
                P = pa[ai % 2]; BP = Bpa[ai % 2]
                ai += 1
                for kc in range(KC):
                    S.op("pe", lambda e, P=P, WO=WO, kc=kc, c4=c4, OT=OT: e.matmul(
                        P[:, 0:TW], WO[:, kc, c4 * 128:(c4 + 1) * 128], OT[:, kc, :], start=(kc == 0), stop=(kc == KC - 1)),
                        reads=[BWO, BOT], writes=[BP], inc=(kc == KC - 1))
                S.op("dve", lambda e, P=P, cc=cc, HR=HR: e.scalar_tensor_tensor(
                    y[:, cc, :], HR[:, cc, :], ALPHA, P[:, 0:TW], ALU.mult, ALU.add), reads=[BP, BHR], writes=[By[cc]])

    def stB(t):
        emit_ln(S, y, By, yb, Byb, sqb, Bsqb, ysq, Bysq, h1[t % 2], Bh1[t % 2], h1b, Bh1b, lnp_sb[:, 0:8], lnp_sb[:, 8:16], Bp, ones_f, Bc,
                ps1, Bps1, ps2, Bps2, mean, msq, rstd, Bst)

    def stC(t):
        nonlocal ai, w1i, ri
        for hg in range(8):
            W1 = w1c[w1i % 2]; BW1 = Bw1[w1i % 2]
            w1i += 1
            S.dma("sp", W1[:], w1_v[:, :, hg * 512:(hg + 1) * 512], reads=[Bwb], writes=[BW1])
            for h4 in range(4):
                hc = hg * 4 + h4
                P = pa[ai % 2]; BP = Bpa[ai % 2]
                ai += 1
                for fc in range(8):
                    S.op("pe", lambda e, P=P, W1=W1, fc=fc, h4=h4: e.matmul(
                        P[:, 0:TW], W1[:, fc, h4 * 128:(h4 + 1) * 128], h1b[:, fc, :], start=(fc == 0), stop=(fc == 7)),
                        reads=[BW1, Bh1b[fc]], writes=[BP], inc=(fc == 7))
                R = rl[ri % 2]; BR = Brl[ri % 2]
                ri += 1
                S.op("act", lambda e, R=R, P=P: e.activation(R[:], P[:, 0:TW], AF.Relu), reads=[BP], writes=[BR])
                S.op("pool", lambda e, R=R, hc=hc: e.tensor_tensor(hid[:, hc, :], R[:], R[:], ALU.mult), reads=[BR], writes=[Bhid[hc]])

    def stD(t, n2s):
        nonlocal w2i
        for n2 in n2s:
            for kg in range(4):
                W2 = w2c[w2i % 2]; BW2 = Bw2[w2i % 2]
                w2i += 1
                S.dma("sp", W2[:], w2_v[:, kg * 8:(kg + 1) * 8, n2 * 512:(n2 + 1) * 512], reads=[Bwb], writes=[BW2])
                for c4 in range(4):
                    for k8 in range(8):
                        hc = kg * 8 + k8
                        S.op("pe", lambda e, c4=c4, W2=W2, k8=k8, hc=hc: e.matmul(
                            pb[c4][:, 0:TW], W2[:, k8, c4 * 128:(c4 + 1) * 128], hid[:, hc, :], start=(hc == 0), stop=(hc == 31)),
                            reads=[BW2, Bhid[hc]], writes=[Bpb[c4]], inc=(k8 == 7))
            for c4 in range(4):
                cc = n2 * 4 + c4
                S.op("dve", lambda e, c4=c4, cc=cc, t=t: e.scalar_tensor_tensor(
                    h2[:, cc, :], h1[t % 2][:, cc, :], ALPHA, pb[c4][:, 0:TW], ALU.mult, ALU.add), reads=[Bpb[c4], Bh1[t % 2][cc]], writes=[Bh2[cc]])

    def stE(t):
        c0 = t * TW
        emit_ln(S, h2, Bh2, yb, Byb, sqb, Bsqb, ysq, Bysq, h2, Bh2, None, None, lnp_sb[:, 16:24], lnp_sb[:, 24:32], Bp, ones_f, Bc,
                ps1, Bps1, ps2, Bps2, mean, msq, rstd, Bst)
        toks.append(S.dma("sp", ho_v[:, :, c0:c0 + TW], h2[:], reads=Bh2, writes=[Bhout]))

    stA(0)
    stB(0)
    for t in range(ntiles):
        stC(t)
        if t + 1 < ntiles:
            stA(t + 1)
        stD(t, (0,))
        if t + 1 < ntiles:
            stB(t + 1)
        stD(t, (1,))
        stE(t)
    return toks


L2_STAGE = 4
L3_STAGE = 3
L3_SUB = 3
L3_NCH = 12
RET_GAMMA = [1.0 - 2.0 ** (-5.0 - h) for h in range(4)]


def emit_state_tile(S, h2b, Bh2b, wk_sb, wv_sb, Bw, kts, Bkts, vts, Bvts, kdec_sb, Bc, Sst, BS, pk, Bpk, ctr):
    for bl in range(3):
        for half in range(2):
            P = pk[ctr[0] % 2]; BP = Bpk[ctr[0] % 2]
            ctr[0] += 1
            for fc in range(8):
                S.op("pe", lambda e, P=P, fc=fc, bl=bl, half=half: e.matmul(
                    P[:], h2b[:, fc, bl * 128:(bl + 1) * 128], wk_sb[:, fc, half * 512:(half + 1) * 512], start=(fc == 0), stop=(fc == 7)),
                    reads=[Bh2b, Bw], writes=[BP], inc=(fc == 7))
            for hh in range(2):
                h = half * 2 + hh
                S.op("dve", lambda e, P=P, bl=bl, h=h, hh=hh: e.tensor_scalar(
                    kts[:, bl, h * 256:(h + 1) * 256], P[:, hh * 256:(hh + 1) * 256], kdec_sb[:, bl * 4 + h:bl * 4 + h + 1], None, ALU.mult),
                    reads=[BP, Bc], writes=[Bkts])
        for h in range(4):
            P = pk[ctr[0] % 2]; BP = Bpk[ctr[0] % 2]
            ctr[0] += 1
            for fc in range(8):
                S.op("pe", lambda e, P=P, fc=fc, bl=bl, h=h: e.matmul(
                    P[:], h2b[:, fc, bl * 128:(bl + 1) * 128], wv_sb[:, fc, h * 512:(h + 1) * 512], start=(fc == 0), stop=(fc == 7)),
                    reads=[Bh2b, Bw], writes=[BP], inc=(fc == 7))
            S.op("act", lambda e, P=P, bl=bl, h=h: e.copy(vts[:, bl, h * 512:(h + 1) * 512], P[:]), reads=[BP], writes=[Bvts])


def emit_state_update(S, kts, Bkts, vts, Bvts, Sst, BS, Sb, BSb, pk, Bpk, ctr):
    for h in range(4):
        c384 = RET_GAMMA[h] ** TW
        for dkc in range(2):
            P = pk[ctr[0] % 2]; BP = Bpk[ctr[0] % 2]
            ctr[0] += 1
            for bl in range(3):
                S.op("pe", lambda e, P=P, bl=bl, h=h, dkc=dkc: e.matmul(
                    P[:], kts[:, bl, h * 256 + dkc * 128:h * 256 + (dkc + 1) * 128], vts[:, bl, h * 512:(h + 1) * 512],
                    start=(bl == 0), stop=(bl == 2)), reads=[Bkts, Bvts], writes=[BP], inc=(bl == 2))
            idx = h * 2 + dkc
            S.op("dve", lambda e, P=P, idx=idx, c384=c384: e.scalar_tensor_tensor(
                Sst[:, idx, :], Sst[:, idx, :], c384, P[:], ALU.mult, ALU.add), reads=[BP, BS], writes=[BS])
            if Sb is not None:
                S.op("pool", lambda e, idx=idx: e.tensor_copy(Sb[:, idx, :], Sst[:, idx, :]), reads=[BS], writes=[BSb])


def emit_prepass(nc, S, T, Bhout, BSe):
    h2T = T["h2T"]; wk = T["wk"]; wv = T["wv"]; kdec = T["kdec"]; tmask = T["tmask"]; S_end = T["S_end"]
    if True:
        with contextlib.ExitStack() as st:
            C = Ctx(nc, st)
            wk_sb = C.sb([128, 8, 1024], BF16)
            wv_sb = C.sb([128, 8, 2048], BF16)
            h2b = [C.sb([128, 8, TW], BF16) for _ in range(2)]
            kts = C.sb([128, 3, 1024], BF16)
            vts = C.sb([128, 3, 2048], BF16)
            kdec_sb = C.sb([128, 12], F32)
            tm_sb = C.sb([128, TW], F32)
            Sst = C.sb([128, 8, 512], F32)
            pk = [C.ps([128, 512]) for _ in range(2)]
            Bw = Buf(); Bh2b = [Buf(), Buf()]; Bkts = Buf(); Bvts = Buf(); Bc = Buf(); BS = Buf(); Bpk = [PB(), PB()]
            wr = "(c p) o -> p c o"
            wqueue = T.get("wqueue", "pool")
            for fc in range(8):
                S.dma(wqueue, wk_sb[:, fc, :], wk[fc * 128:(fc + 1) * 128, :], reads=[T["Bwb"]], writes=[Bw])
                S.dma(wqueue, wv_sb[:, fc, :], wv[fc * 128:(fc + 1) * 128, :], reads=[T["Bwb"]], writes=[Bw])
            S.dma("sp", kdec_sb[:], kdec, writes=[Bc])
            S.dma("sp", tm_sb[:], tmask, writes=[Bc])
            for i8 in range(8):
                S.op("dve", lambda e, i8=i8: e.memset(Sst[:, i8, :], 0.0), writes=[BS])
            hv = h2T.rearrange("(c p) t -> p c t", p=128)
            ctr = [0]
            for t in range(NTH):
                c0 = t * TW
                H = h2b[t % 2]; BH = Bh2b[t % 2]
                S.dma("pool", H[:], hv[:, :, c0:c0 + TW], reads=[Bhout], writes=[BH])
                if t == 0:
                    for fc in range(8):
                        S.op("dve", lambda e, H=H, fc=fc: e.tensor_tensor(H[:, fc, :], H[:, fc, :], tm_sb[:], ALU.mult),
                             reads=[BH, Bc], writes=[BH])
                emit_state_tile(S, H, BH, wk_sb, wv_sb, Bw, kts, Bkts, vts, Bvts, kdec_sb, Bc, Sst, BS, pk, Bpk, ctr)
                emit_state_update(S, kts, Bkts, vts, Bvts, Sst, BS, None, None, pk, Bpk, ctr)
            for k in range(4):
                S.dma("sp", S_end[k], Sst[32 * k:32 * (k + 1)], reads=[BS], writes=[BSe])
            S.barrier()
            S.flush()

def ret_consts():
    p = np.arange(128, dtype=np.float64)
    kdec = np.zeros((128, 3, 4), np.float32)
    for h in range(4):
        g = RET_GAMMA[h]
        for bl in range(3):
            kdec[:, bl, h] = g ** (TW - 1 - (bl * 128 + p)) / 16.0
    return kdec.reshape(128, 12)


def lnp_table(ln_g, ln_b, layer):
    cols = []
    for k in range(2):
        cols.append(ln_g[layer, k].reshape(8, 128).T)
        cols.append(ln_b[layer, k].reshape(8, 128).T)
    return np.ascontiguousarray(np.concatenate(cols, axis=1).astype(np.float32))


def emit_retention(nc, S, st, T, Bwb, Bh2, BSg, ByT):
    h2T = T["h2T"]; wi_b = T["wi_b"]; kdec = T["kdec"]; tmask = T["tmask"]; dmask = T["dmask"]; qdec = T["qdec"]
    yT_s = T["yT_s"]; S_src = T["S_src"]; flags = T["flags"]
    C = Ctx(nc, st)
    if True:
        if True:
            h2b = [C.sb([128, 8, TW], BF16) for _ in range(2)]
            wc = [C.sb([128, 8, 512], BF16) for _ in range(2)]
            qT = C.sb([128, 8, TW], BF16)
            qdT = C.sb([128, 8, TW], BF16)
            kT = C.sb([128, 8, TW], BF16)
            kts = C.sb([128, 3, 1024], BF16)
            vts = C.sb([128, 3, 2048], BF16)
            sg = C.sb([128, 16, TW], BF16)
            sTb = [C.sb([128, 3, TW], BF16) for _ in range(4)]
            o32 = [C.sb([128, 4, TW], F32) for _ in range(2)]
            osq = [C.sb([128, 4, TW], BF16) for _ in range(2)]
            yT = [C.sb([128, 16, TW], BF16) for _ in range(2)]
            Sst = C.sb([128, 8, 512], F32)
            Sb = C.sb([128, 8, 512], BF16)
            dm_sb = C.sb([128, 12 * TW], F32)
            qd_sb = C.sb([128, 4 * TW], F32)
            kdec_sb = C.sb([128, 12], F32)
            tm_sb = C.sb([128, TW], F32)
            ones_f = C.sb([128, 128], BF16)
            rstd = [C.sb([128, TW], F32) for _ in range(2)]
            tmpo = C.sb([128, TW], F32)
            pk = [C.ps([128, 512]) for _ in range(2)]
            psc = [C.ps([128, 512]) for _ in range(2)]
            po = [C.ps([128, 512]) for _ in range(2)]
            pss = C.ps([128, 512])
            Bh2b = [Buf(), Buf()]; Bwc = [Buf(), Buf()]; BqT = Buf(); BqdT = Buf(); BkT = Buf(); Bkts = Buf(); Bvts = Buf()
            Bsg = Buf(); BsTb = [Buf() for _ in range(4)]; Bo32 = [Buf(), Buf()]; Bosq = [Buf(), Buf()]; ByTt = [Buf(), Buf()]; BS = Buf(); BSb = Buf()
            Bc = Buf(); Brstd = [Buf(), Buf()]; Btmpo = Buf(); Bpk = [PB(), PB()]; Bpsc = [PB(), PB()]; Bpo = [PB(), PB()]; Bpss = PB()
            S.dma("sp", dm_sb[:], dmask, writes=[Bc])
            S.dma("sp", qd_sb[:], qdec, writes=[Bc])
            S.dma("sp", kdec_sb[:], kdec, writes=[Bc])
            S.dma("sp", tm_sb[:], tmask, writes=[Bc])
            S.op("pool", lambda e: e.memset(ones_f[:], 1.0), writes=[Bc])
            fl_sb = C.sb([128, 2], F32)
            S.dma("sp", fl_sb[:], flags, writes=[Bc])
            for k in range(4):
                S.dma("sp", Sst[32 * k:32 * (k + 1)], S_src[k], reads=[BSg], writes=[BS])
            for i8 in range(8):
                S.op("dve", lambda e, i8=i8: e.tensor_scalar(Sst[:, i8, :], Sst[:, i8, :], fl_sb[:, 1:2], None, ALU.mult),
                     reads=[BS, Bc], writes=[BS])
            for i8 in range(8):
                S.op("pool", lambda e, i8=i8: e.tensor_copy(Sb[:, i8, :], Sst[:, i8, :]), reads=[BS], writes=[BSb])
            hv = h2T.rearrange("(c p) t -> p c t", p=128)
            wiv = wi_b.rearrange("(c p) o -> p c o", p=128)
            yv = yT_s.rearrange("(c p) t -> p c t", p=128)
            ctr = [0]
            wci = 0
            sci = 0
            oi = 0
            for t in range(NTH if L3_STAGE >= 2 else 1):
                c0 = t * TW
                H = h2b[t % 2]; BH = Bh2b[t % 2]
                Y = yT[t % 2]; BY = ByTt[t % 2]
                S.dma("pool", H[:], hv[:, :, c0:c0 + TW], reads=[Bh2], writes=[BH])
                if t == 0:
                    for fc in range(8):
                        S.op("dve", lambda e, H=H, fc=fc: e.tensor_tensor(H[:, fc, :], H[:, fc, :], tm_sb[:], ALU.mult),
                             reads=[BH, Bc], writes=[BH])
                for ch in range(L3_NCH):
                    W = wc[wci % 2]; BW = Bwc[wci % 2]
                    wci += 1
                    S.dma("sp", W[:], wiv[:, :, ch * 512:(ch + 1) * 512], reads=[Bwb], writes=[BW])
                    if ch < 4 or ch >= 8:
                        for c4 in range(4):
                            P = pk[ctr[0] % 2]; BP = Bpk[ctr[0] % 2]
                            ctr[0] += 1
                            for fc in range(8):
                                S.op("pe", lambda e, P=P, W=W, fc=fc, c4=c4, H=H: e.matmul(
                                    P[:, 0:TW], W[:, fc, c4 * 128:(c4 + 1) * 128], H[:, fc, :], start=(fc == 0), stop=(fc == 7)),
                                    reads=[BW, BH], writes=[BP], inc=(fc == 7))
                            if ch < 2:
                                ci = ch * 4 + c4
                                h = ci // 2
                                S.op("act", lambda e, P=P, ci=ci: e.copy(qT[:, ci, :], P[:, 0:TW]), reads=[BP], writes=[BqT])
                                S.op("dve", lambda e, P=P, ci=ci, h=h: e.tensor_tensor(qdT[:, ci, :], P[:, 0:TW], qd_sb[:, h * TW:(h + 1) * TW], ALU.mult),
                                     reads=[BP, Bc], writes=[BqdT])
                            elif ch < 4:
                                ci = (ch - 2) * 4 + c4
                                S.op("act", lambda e, P=P, ci=ci: e.copy(kT[:, ci, :], P[:, 0:TW]), reads=[BP], writes=[BkT])
                            else:
                                gi = (ch - 8) * 4 + c4
                                S.op("act", lambda e, P=P, gi=gi: e.activation(sg[:, gi, :], P[:, 0:TW], AF.Silu), reads=[BP], writes=[Bsg])
                    if 2 <= ch < 4:
                        half = ch - 2
                        for bl in range(3):
                            P = pk[ctr[0] % 2]; BP = Bpk[ctr[0] % 2]
                            ctr[0] += 1
                            for fc in range(8):
                                S.op("pe", lambda e, P=P, W=W, fc=fc, bl=bl, H=H: e.matmul(
                                    P[:], H[:, fc, bl * 128:(bl + 1) * 128], W[:, fc, :], start=(fc == 0), stop=(fc == 7)),
                                    reads=[BH, BW], writes=[BP], inc=(fc == 7))
                            for hh in range(2):
                                h = half * 2 + hh
                                S.op("dve", lambda e, P=P, bl=bl, h=h, hh=hh: e.tensor_scalar(
                                    kts[:, bl, h * 256:(h + 1) * 256], P[:, hh * 256:(hh + 1) * 256], kdec_sb[:, bl * 4 + h:bl * 4 + h + 1], None, ALU.mult),
                                    reads=[BP, Bc], writes=[Bkts])
                    if 4 <= ch < 8:
                        h = ch - 4
                        for bl in range(3):
                            P = pk[ctr[0] % 2]; BP = Bpk[ctr[0] % 2]
                            ctr[0] += 1
                            for fc in range(8):
                                S.op("pe", lambda e, P=P, W=W, fc=fc, bl=bl, H=H: e.matmul(
                                    P[:], H[:, fc, bl * 128:(bl + 1) * 128], W[:, fc, :], start=(fc == 0), stop=(fc == 7)),
                                    reads=[BH, BW], writes=[BP], inc=(fc == 7))
                            S.op("act", lambda e, P=P, bl=bl, h=h: e.copy(vts[:, bl, h * 512:(h + 1) * 512], P[:]), reads=[BP], writes=[Bvts])
                def st1(h):
                    nonlocal sci
                    ST = sTb[h]; BST = BsTb[h]
                    for jb in range(3):
                        P = psc[sci % 2]; BP = Bpsc[sci % 2]
                        sci += 1
                        for dc in range(2):
                            S.op("pe", lambda e, P=P, h=h, dc=dc, jb=jb: e.matmul(
                                P[:, 0:TW], kT[:, h * 2 + dc, jb * 128:(jb + 1) * 128], qT[:, h * 2 + dc, :], start=(dc == 0), stop=(dc == 1)),
                                reads=[BkT, BqT], writes=[BP], inc=(dc == 1))
                        S.op("dve", lambda e, P=P, ST=ST, jb=jb, h=h: e.tensor_tensor(
                            ST[:, jb, :], P[:, 0:TW], dm_sb[:, (h * 3 + jb) * TW:(h * 3 + jb + 1) * TW], ALU.mult),
                            reads=[BP, Bc], writes=[BST])

                def st2(h):
                    nonlocal oi
                    ST = sTb[h]; BST = BsTb[h]
                    O32 = o32[h % 2]; OSQ = osq[h % 2]
                    for ec in range(4):
                        P = po[oi % 2]; BP = Bpo[oi % 2]
                        oi += 1
                        for jb in range(3):
                            S.op("pe", lambda e, P=P, h=h, ec=ec, jb=jb, ST=ST: e.matmul(
                                P[:, 0:TW], vts[:, jb, h * 512 + ec * 128:h * 512 + (ec + 1) * 128], ST[:, jb, :], start=(jb == 0), stop=False),
                                reads=[Bvts, BST], writes=[BP], inc=False)
                        for dc in range(2):
                            S.op("pe", lambda e, P=P, h=h, ec=ec, dc=dc: e.matmul(
                                P[:, 0:TW], Sb[:, h * 2 + dc, ec * 128:(ec + 1) * 128], qdT[:, h * 2 + dc, :], start=False, stop=(dc == 1)),
                                reads=[BSb, BqdT], writes=[BP], inc=(dc == 1))
                        S.op("act", lambda e, P=P, ec=ec, O32=O32: e.copy(O32[:, ec, :], P[:, 0:TW]), reads=[BP], writes=[Bo32[h % 2]])
                        S.op("act", lambda e, P=P, ec=ec, OSQ=OSQ: e.activation(OSQ[:, ec, :], P[:, 0:TW], AF.Square), reads=[BP], writes=[Bosq[h % 2]])

                def st3(h, Y=Y, BY=BY):
                    O32 = o32[h % 2]; OSQ = osq[h % 2]; R = rstd[h % 2]; BR = Brstd[h % 2]
                    for ec in range(4):
                        S.op("pe", lambda e, ec=ec, OSQ=OSQ: e.matmul(pss[:, 0:TW], ones_f[:], OSQ[:, ec, :], start=(ec == 0), stop=(ec == 3)),
                             reads=[Bc, Bosq[h % 2]], writes=[Bpss], inc=(ec == 3))
                    S.op("act", lambda e, R=R: e.activation(R[:], pss[:, 0:TW], AF.Ln, bias=RMS_EPS, scale=1.0 / 512.0), reads=[Bpss], writes=[BR])
                    S.op("act", lambda e, R=R: e.activation(R[:], R[:], AF.Exp, scale=-0.5), reads=[BR], writes=[BR])
                    for ec in range(4):
                        S.op("dve", lambda e, ec=ec, O32=O32, R=R: e.tensor_tensor(tmpo[:], O32[:, ec, :], R[:], ALU.mult), reads=[Bo32[h % 2], BR], writes=[Btmpo])
                        S.op("dve", lambda e, ec=ec, h=h, Y=Y: e.tensor_tensor(Y[:, h * 4 + ec, :], tmpo[:], sg[:, h * 4 + ec, :], ALU.mult),
                             reads=[Btmpo, Bsg], writes=[BY])

                if L3_SUB >= 2:
                    for stg_fn, hh in ((st1, 0), (st1, 1), (st2, 0), (st1, 2), (st2, 1), (st3, 0), (st1, 3), (st2, 2), (st3, 1),
                                       (st2, 3), (st3, 2), (st3, 3)):
                        stg_fn(hh)
                if L3_SUB >= 3:
                    emit_state_update(S, kts, Bkts, vts, Bvts, Sst, BS, Sb, BSb, pk, Bpk, ctr)
                if L3_SUB >= 2:
                    S.dma("sp", yv[:, :, c0:c0 + TW], Y[:], reads=[BY], writes=[ByT])
            S.barrier()
            S.flush()

def l3_consts():
    p = np.arange(128, dtype=np.float64)[:, None]
    i = np.arange(TW, dtype=np.float64)[None, :]
    dm = np.zeros((128, 4, 3, TW), np.float32)
    qd = np.zeros((128, 4, TW), np.float32)
    for h in range(4):
        g = RET_GAMMA[h]
        for jb in range(3):
            j = jb * 128 + p
            same = (j // 64) == (i // 64)
            before = (j // 64) < (i // 64)
            val = np.where(same, g ** np.abs(i - j), np.where(before, g ** np.maximum(i - j, 0), 0.0)) / 16.0
            dm[:, h, jb, :] = val
        qd[:, h, :] = g ** (i + 1.0)
    return dm.reshape(128, 12 * TW), qd.reshape(128, 4 * TW)


PAIRS = [[0, 1], [2, 3], [4, 5], [6, 7]]


def build_fused():
    nc = bass.Bass("TRN2", target_bir_lowering=False)

    def din(name, shape):
        return nc.dram_tensor(name, list(shape), F32, kind="ExternalInput").ap()

    T1 = {"xT": din("xT", [D, LR]), "wq": din("wq", [D, 512]), "wk": din("wk", [D, 512]), "wv": din("wv", [D, 512]),
          "wf": din("wf", [D, 4]), "fbias": din("fbias", [4, 1]), "lamv": din("lamv", [128, 256]), "gsub": din("gsub", [128, 1]),
          "dbias": din("dbias", [128, 2 * 66]), "Fd": din("Fd", [128, 2 * 3 * TW]), "Ff": din("Ff", [128, 3 * TW]),
          "qaug": din("qaug", [2, 3, LR]), "odt": BF16}
    hres = din("hres", [D, HALF])
    flags = din("flags", [128, 2])
    w_out0 = din("w_out0", [D, D])
    w1_0 = din("w1_0", [D, 4 * D])
    w2_0 = din("w2_0", [4 * D, D])
    lnp0 = din("lnp0", [128, 32])
    w_in1 = din("w_in1", [D, 6144])
    w_out1 = din("w_out1", [2048, D])
    w1_1 = din("w1_1", [D, 4 * D])
    w2_1 = din("w2_1", [4 * D, D])
    lnp1 = din("lnp1", [128, 32])
    kdec = din("kdec", [128, 12])
    tmask = din("tmask", [128, TW])
    dmask = din("dmask", [128, 12 * TW])
    qdec = din("qdec", [128, 4 * TW])
    outT = nc.dram_tensor("outT", [D, HALF], F32, kind="ExternalOutput").ap()
    wo0_b = nc.dram_tensor("wo0_b", [D, D], BF16).ap()
    w10_b = nc.dram_tensor("w10_b", [D, 4 * D], BF16).ap()
    w20_b = nc.dram_tensor("w20_b", [4 * D, D], BF16).ap()
    wi_b = nc.dram_tensor("wi_b", [D, 6144], BF16).ap()
    wo1_b = nc.dram_tensor("wo1_b", [2048, D], BF16).ap()
    w11_b = nc.dram_tensor("w11_b", [D, 4 * D], BF16).ap()
    w21_b = nc.dram_tensor("w21_b", [4 * D, D], BF16).ap()
    NCH = NT // 2
    oT_c = [nc.dram_tensor("oT_c%d" % k, [512, 2 * TW], BF16) for k in range(NCH)]
    G_c = [nc.dram_tensor("G_c%d" % k, [1024, 2 * TW], BF16) for k in range(NCH)]
    h2T_i = nc.dram_tensor("h2T_i", [D, HALF], F32).ap()
    Se_c = [nc.dram_tensor("Se_c%d" % k, [256, 512], F32) for k in range(4)]
    Sg_c = [nc.dram_tensor("Sg_c%d" % k, [512, 512], F32) for k in range(4)]
    yT_s = nc.dram_tensor("yT_s", [2048, HALF], BF16).ap()
    T1["oT_tile"] = lambda r0, r1, t: oT_c[t // 2].ap()[r0:r1, (t % 2) * TW:(t % 2 + 1) * TW]

    def g_tile(k, t):
        gt = k * NTH + t
        return G_c[gt // 2].ap().rearrange("(c p) t -> p c t", p=128)[:, :, (gt % 2) * TW:(gt % 2 + 1) * TW]
    with contextlib.ExitStack() as outer:
        S = Sched(nc, outer)
        Bwb = Buf(); Bo = Buf(); BG = Buf(); Bh2 = Buf(); BSe = Buf(); BSg = Buf(); ByT = Buf(); Bout = Buf()
        def hookA(C):
            stg = [C.sb([128, 2048], BF16) for _ in range(2)]
            Bs = [Buf(), Buf()]
            for (s_, d_, r_, c_) in ():
                yield from iter_convert(S, stg, Bs, s_, d_, Bwb, r_, c_)

        def hookB(C):
            stg = [C.sb([128, 2048], BF16) for _ in range(2)]
            Bs = [Buf(), Buf()]
            for (s_, d_, r_, c_) in ((w_out0, wo0_b, D, D), (w1_0, w10_b, D, 4 * D), (w2_0, w20_b, 4 * D, D),
                                     (w_in1, wi_b, D, 6144), (w_out1, wo1_b, 2048, D), (w1_1, w11_b, D, 4 * D), (w2_1, w21_b, 4 * D, D)):
                yield from iter_convert(S, stg, Bs, s_, d_, Bwb, r_, c_)
        T1["hookA"] = hookA
        T1["hookB"] = hookB
        emit_l1(nc, S, T1, Bo)
        for k in range(NCH):
            S.op("pool", lambda e, k=k: e.collective_compute("AllGather", ALU.bypass, replica_groups=PAIRS,
                                                             ins=[oT_c[k].ap().opt()], outs=[G_c[k].ap().opt()]), reads=[Bo], writes=[BG])
        with contextlib.ExitStack() as st:
            emit_post(nc, S, st, NTH, 8, g_tile, "blend", BG, hres, wo0_b, w10_b, w20_b, Bwb, lnp0, h2T_i, Bh2, flags=flags)
            S.barrier()
            S.flush()
        Tp = {"h2T": h2T_i, "wk": wi_b[:, 1024:2048], "wv": wi_b[:, 2048:4096], "kdec": kdec, "tmask": tmask,
              "S_end": [Se_c[k].ap().rearrange("(p i) f -> p i f", i=8) for k in range(4)], "wqueue": "sp", "Bwb": Bwb}
        emit_prepass(nc, S, Tp, Bh2, BSe)
        for k in range(4):
            S.op("pool", lambda e, k=k: e.collective_compute("AllGather", ALU.bypass, replica_groups=PAIRS,
                                                             ins=[Se_c[k].ap().opt()], outs=[Sg_c[k].ap().opt()]), reads=[BSe], writes=[BSg])
        Tr = {"h2T": h2T_i, "wi_b": wi_b, "kdec": kdec, "tmask": tmask, "dmask": dmask, "qdec": qdec, "yT_s": yT_s,
              "S_src": [Sg_c[k].ap()[0:256, :].rearrange("(p i) f -> p i f", i=8) for k in range(4)], "flags": flags}
        with contextlib.ExitStack() as st:
            emit_retention(nc, S, st, Tr, Bwb, Bh2, BSg, ByT)
        with contextlib.ExitStack() as st:
            emit_post(nc, S, st, NTH, 16, yT_s, "bf16", ByT, h2T_i, wo1_b, w11_b, w21_b, Bwb, lnp1, outT, Bout)
            S.barrier()
            S.flush()
    return nc


def _token_mask(u):
    tm = np.ones((128, TW), np.float32)
    if u == 0:
        tm[:, 0:FPAD] = 0.0
    return tm


def kernel(x, meta_tokens, even_w_in, even_f_bias, diff_lambda, diff_subln_g, even_w_out,
           ret_w_in, ret_w_out, ln_g, ln_b, ffn_w1, ffn_w2):
    x = np.asarray(x, np.float32)
    f32 = lambda a: np.ascontiguousarray(np.asarray(a, np.float32))
    meta = f32(meta_tokens)
    B = x.shape[0]
    cores = list(range(8))
    kdec = ret_consts()
    dm, qd = l3_consts()
    wo = f32(even_w_out[0])
    wo_perm = np.ascontiguousarray(np.concatenate([wo[0:256], wo[512:768], wo[256:512], wo[768:1024]], axis=0))
    shared = {"w_out0": wo_perm, "w1_0": f32(ffn_w1[0]), "w2_0": f32(ffn_w2[0]), "lnp0": lnp_table(f32(ln_g), f32(ln_b), 0),
              "w_in1": f32(ret_w_in[0]), "w_out1": f32(ret_w_out[0]), "w1_1": f32(ffn_w1[1]), "w2_1": f32(ffn_w2[1]),
              "lnp1": lnp_table(f32(ln_g), f32(ln_b), 1), "kdec": kdec, "dmask": dm, "qdec": qd}
    in_maps = []
    for c in cores:
        b, u = c // 2, c % 2
        hp = np.zeros((LR, D), np.float32)
        hp[FPAD:FPAD + NMETA] = meta
        hp[FPAD + NMETA:FPAD + NMETA + SEQ] = x[b]
        xT = np.ascontiguousarray(hp.T)
        m = l1_inputs(xT, f32(even_w_in[0]), f32(even_f_bias[0]), f32(diff_lambda[0]), f32(diff_subln_g[0]), u)
        m["hres"] = np.ascontiguousarray(xT[:, u * HALF:(u + 1) * HALF])
        fl = np.zeros((128, 2), np.float32)
        fl[:, u] = 1.0
        m["flags"] = fl
        m["tmask"] = _token_mask(u)
        m.update(shared)
        in_maps.append(m)
    res = run_bass_kernel_spmd(build_fused(), in_maps, core_ids=cores).results
    out = np.empty((B, SEQ, D), np.float32)
    for b in range(B):
        hT = np.concatenate([res[2 * b]["outT"], res[2 * b + 1]["outT"]], axis=1)
        out[b] = hT[:, FPAD + NMETA:FPAD + NMETA + SEQ].T
    return out
```

```python
import contextlib
import math
import numpy as np
import concourse.bass as bass
import concourse.mybir as mybir
from concourse.bass_utils import run_bass_kernel_spmd

F32 = mybir.dt.float32
BF16 = mybir.dt.bfloat16
AF = mybir.ActivationFunctionType
ALU = mybir.AluOpType

D = 1024
SEQ = 8192
NMETA = 16
FPAD = 48
LR = 8448
TW = 384
NT = LR // TW
NB = LR // 128
HALF = LR // 2
NTH = HALF // TW
ALPHA = 4 ** 0.25
LN_EPS = 1e-5
RMS_EPS = 1e-6
LAM_INIT0 = 0.8 - 0.6 * math.exp(-0.3 * 0)
NEG = -30000.0
REAL_END = FPAD + NMETA + SEQ

ENGS = ("pe", "act", "dve", "pool", "sp")


class Buf:
    __slots__ = ("name", "w", "r", "ex")

    def __init__(self, name="", ex=False):
        self.name = name
        self.w = None
        self.r = []
        self.ex = ex


def PB():
    return Buf(ex=True)


class Sched:
    def __init__(self, nc, stack, n_dma_sems=16):
        self.nc = nc
        self.streams = {e: [] for e in ENGS}
        self.cnt = {e: 0 for e in ENGS}
        self.seen = {e: {} for e in ENGS}
        self.n_dma = n_dma_sems
        self.dma_k = 0
        self.sems = {e: stack.enter_context(nc.semaphore("s_" + e)) for e in ENGS}
        self.dsems = [stack.enter_context(nc.semaphore("d_%d" % i)) for i in range(n_dma_sems)]
        self.dlast = [0] * n_dma_sems

    def _need(self, eng, tok, waits):
        if tok is None:
            return
        kind, key, val = tok
        if kind == "e" and key == eng and eng in ("pe", "sp"):
            return
        k = (kind, key)
        if self.seen[eng].get(k, 0) >= val:
            return
        if val > waits.get(k, 0):
            waits[k] = val

    def _emit_waits(self, eng, waits):
        for k, val in waits.items():
            self.seen[eng][k] = val
            self.streams[eng].append(("wait", k, val))

    def _deps(self, eng, reads, writes):
        waits = {}
        for b in reads:
            self._need(eng, b.w, waits)
        for b in writes:
            self._need(eng, b.w, waits)
            for t in b.r:
                self._need(eng, t, waits)
        return waits

    def _mark(self, tok, reads, writes):
        for b in reads:
            b.r.append(tok)
            if len(b.r) > 24:
                b.r = b.r[-24:]
        for b in writes:
            b.w = tok
            b.r = []

    def op(self, eng, fn, reads=(), writes=(), inc=True):
        if any(b.ex for b in reads):
            writes = list(writes) + [b for b in reads if b.ex]
            reads = [b for b in reads if not b.ex]
        waits = self._deps(eng, reads, writes)
        self._emit_waits(eng, waits)
        if inc:
            self.cnt[eng] += 1
            tok = ("e", eng, self.cnt[eng])
        else:
            tok = ("e", eng, self.cnt[eng] + 1)
        self.streams[eng].append(("op", fn, inc))
        self._mark(tok, reads, writes)
        return tok

    def dma(self, q, out_ap, in_ap, reads=(), writes=(), **kw):
        waits = self._deps(q, reads, writes)
        s = self.dma_k % self.n_dma
        v = self.dlast[s] + 16
        self.dma_k += 1
        if v > 16:
            self._need(q, ("d", s, v - 16), waits)
        self._emit_waits(q, waits)
        self.dlast[s] = v
        tok = ("d", s, v)
        self.streams[q].append(("dma", out_ap, in_ap, s, kw))
        self._mark(tok, reads, writes)
        return tok

    def barrier(self):
        toks = [("e", e, self.cnt[e]) for e in ENGS if self.cnt[e] > 0]
        toks += [("d", s, self.dlast[s]) for s in range(self.n_dma) if self.dlast[s] > 0]
        for e in ENGS:
            waits = {}
            for t in toks:
                if t[0] == "e" and t[1] == e:
                    continue
                self._need(e, t, waits)
            self._emit_waits(e, waits)

    def flush(self):
        nc = self.nc
        with nc.Block() as block:
            def run(e, engobj):
                for item in self.streams[e]:
                    if item[0] == "wait":
                        (kind, key), val = item[1], item[2]
                        sem = self.sems[key] if kind == "e" else self.dsems[key]
                        engobj.wait_ge(sem, val)
                    elif item[0] == "op":
                        ins = item[1](engobj)
                        if item[2]:
                            ins.then_inc(self.sems[e], 1)
                    else:
                        _, o, i, s, kw = item
                        engobj.dma_start(out=o, in_=i, **kw).then_inc(self.dsems[s], 16)

            @block.tensor
            def _(eng):
                run("pe", eng)

            @block.scalar
            def _(eng):
                run("act", eng)

            @block.vector
            def _(eng):
                run("dve", eng)

            @block.gpsimd
            def _(eng):
                run("pool", eng)

            @block.sync
            def _(eng):
                run("sp", eng)
        self.streams = {e: [] for e in ENGS}


class Ctx:
    K = [0]

    def __init__(self, nc, stack):
        self.nc = nc
        self.st = stack

    def sb(self, shape, dt, name=None):
        Ctx.K[0] += 1
        return self.st.enter_context(self.nc.sbuf_tensor(name or ("t%d" % Ctx.K[0]), list(shape), dt))

    def ps(self, shape, dt=F32, name=None):
        Ctx.K[0] += 1
        return self.st.enter_context(self.nc.psum_tensor(name or ("p%d" % Ctx.K[0]), list(shape), dt))


SCALE = 0.125
DEBUG_A = False


def emit_l1(nc, S, T, Bo):
    xT = T["xT"]; wq = T["wq"]; wk = T["wk"]; wv = T["wv"]; wf = T["wf"]; fbias = T["fbias"]
    lamv = T["lamv"]; gsub = T["gsub"]; dbias = T["dbias"]; Fd = T["Fd"]; Ff = T["Ff"]; qaug = T["qaug"]
    ODT = T["odt"]
    QT_s = nc.dram_tensor("QT_s", [8, 64, LR], BF16).ap()
    KT_s = nc.dram_tensor("KT_s", [8, 64, LR], BF16).ap()
    V_s = nc.dram_tensor("V_s", [LR, 512], BF16).ap()
    AQ_s = nc.dram_tensor("AQ_s", [4, 3, LR], BF16).ap()
    AQd_s = nc.dram_tensor("AQd_s", [6, LR], BF16).ap()
    out_toks = []
    if True:
        BQT = [Buf() for _ in range(8)]
        BKT = [Buf() for _ in range(8)]
        BV = Buf()
        BAQ = Buf()
        with contextlib.ExitStack() as st:
            C = Ctx(nc, st)
            wq_sb = C.sb([128, 8, 512], BF16)
            wk_sb = C.sb([128, 8, 512], BF16)
            wv_sb = C.sb([128, 8, 512], BF16)
            wf_sb = C.sb([128, 8, 4], BF16)
            fb_sb = C.sb([4, 1], F32)
            nfb_sb = C.sb([4, 1], F32)
            ident = C.sb([128, 128], F32)
            xt = [C.sb([128, 8, TW], BF16) for _ in range(2)]
            stq = [C.sb([128, TW], BF16) for _ in range(4)]
            stv = [C.sb([128, 512], BF16) for _ in range(2)]
            lf = C.sb([4, LR], F32)
            cs = C.sb([4, LR], F32)
            ones4 = C.sb([4, LR // 4], F32)
            e_t = C.sb([4, TW], F32)
            hi = C.sb([4, LR], BF16)
            mid = C.sb([4, LR], BF16)
            lo = C.sb([4, LR], BF16)
            pq = [C.ps([128, 512]) for _ in range(4)]
            pv = [C.ps([128, 512]) for _ in range(2)]
            pf = C.ps([128, 512])
            Bw = Buf(); Bxt = [Buf(), Buf()]; Bstq = [Buf() for _ in range(4)]; Bstv = [Buf(), Buf()]
            Bpq = [PB() for _ in range(4)]; Bpv = [PB(), PB()]; Bpf = PB(); Blf = Buf(); Bcs = Buf()
            Bet = Buf(); Bfb = Buf(); Bo4 = Buf(); Bhi = Buf(); Bmid = Buf(); Blo = Buf()

            wr = "(c p) o -> p c o"
            S.dma("pool", wq_sb[:], wq.rearrange(wr, p=128), writes=[Bw])
            for fc in range(8):
                S.dma("pool", wk_sb[:, fc, :], wk[fc * 128:(fc + 1) * 128, :], writes=[Bw])
                S.dma("pool", wv_sb[:, fc, :], wv[fc * 128:(fc + 1) * 128, :], writes=[Bw])
            S.dma("pool", wf_sb[:], wf.rearrange(wr, p=128), writes=[Bw])
            S.dma("sp", fb_sb[:], fbias, writes=[Bfb])
            S.op("dve", lambda e: e.tensor_scalar(nfb_sb[:], fb_sb[:], -1.0, None, ALU.mult), reads=[Bfb], writes=[Bfb])
            S.op("dve", lambda e: e.memset(ones4[:], 1.0), writes=[Bo4])
            xTr = xT.rearrange("(c p) t -> p c t", p=128)
            conv_it = T["hookA"](C) if T.get("hookA") else None
            qi = 0
            vi = 0
            for t in range(NT):
                c0 = t * TW
                X = xt[t % 2]; BX = Bxt[t % 2]
                S.dma("pool", X[:], xTr[:, :, c0:c0 + TW], writes=[BX])
                for which, (w_sb, dst, BD) in enumerate(((wq_sb, QT_s, BQT), (wk_sb, KT_s, BKT))):
                    for g in range(4):
                        P = pq[qi % 4]; BP = Bpq[qi % 4]; ST = stq[qi % 4]; BS = Bstq[qi % 4]
                        qi += 1
                        for c in range(8):
                            S.op("pe", lambda e, P=P, w_sb=w_sb, c=c, g=g, X=X: e.matmul(
                                P[:, 0:TW], w_sb[:, c, g * 128:(g + 1) * 128], X[:, c, :], start=(c == 0), stop=(c == 7)),
                                reads=[Bw, BX], writes=[BP], inc=(c == 7))
                        eng = "act" if (g % 2 == 0) else "dve"
                        if eng == "act":
                            S.op("act", lambda e, ST=ST, P=P: e.copy(ST[:], P[:, 0:TW]), reads=[BP], writes=[BS])
                        else:
                            S.op("dve", lambda e, ST=ST, P=P: e.tensor_copy(ST[:], P[:, 0:TW]), reads=[BP], writes=[BS])
                        S.dma("sp", dst[2 * g:2 * g + 2].rearrange("u r t -> (u r) t")[:, c0:c0 + TW], ST[:],
                              reads=[BS], writes=[BD[2 * g], BD[2 * g + 1]])
                for bl in range(3):
                    P = pv[vi % 2]; BP = Bpv[vi % 2]; ST = stv[vi % 2]; BS = Bstv[vi % 2]
                    vi += 1
                    for c in range(8):
                        S.op("pe", lambda e, P=P, c=c, bl=bl, X=X: e.matmul(
                            P[:], X[:, c, bl * 128:(bl + 1) * 128], wv_sb[:, c, :], start=(c == 0), stop=(c == 7)),
                            reads=[Bw, BX], writes=[BP], inc=(c == 7))
                    S.op("dve", lambda e, ST=ST, P=P: e.tensor_copy(ST[:], P[:]), reads=[BP], writes=[BS])
                    r0 = c0 + bl * 128
                    S.dma("sp", V_s[r0:r0 + 128, :], ST[:], reads=[BS], writes=[BV])
                for c in range(8):
                    S.op("pe", lambda e, c=c, X=X: e.matmul(pf[0:4, 0:TW], wf_sb[:, c, :], X[:, c, :], start=(c == 0), stop=(c == 7)),
                         reads=[Bw, BX], writes=[Bpf], inc=(c == 7))
                S.op("act", lambda e: e.activation(e_t[:], pf[0:4, 0:TW], AF.Exp, bias=nfb_sb[:, 0:1], scale=-1.0),
                     reads=[Bpf, Bfb], writes=[Bet])
                S.op("act", lambda e, c0=c0: e.activation(lf[:, c0:c0 + TW], e_t[:], AF.Ln, bias=1.0, scale=1.0),
                     reads=[Bet], writes=[Blf])
                if conv_it is not None:
                    for _ in range(3):
                        next(conv_it, None)
            S.op("dve", lambda e: e.memset(lf[:, 0:FPAD], 0.0), reads=[Blf], writes=[Blf])
            S.op("dve", lambda e: e.memset(lf[:, REAL_END:LR], 0.0), reads=[Blf], writes=[Blf])
            CH = LR // 4
            for k in range(4):
                a = k * CH
                if k == 0:
                    S.op("dve", lambda e, a=a: e.tensor_tensor_scan(cs[:, a:a + CH], ones4[:], lf[:, a:a + CH], 0.0, ALU.mult, ALU.add),
                         reads=[Blf, Bo4], writes=[Bcs])
                else:
                    S.op("dve", lambda e, a=a: e.tensor_tensor_scan(cs[:, a:a + CH], ones4[:], lf[:, a:a + CH], cs[:, a - 1:a], ALU.mult, ALU.add),
                         reads=[Blf, Bo4, Bcs], writes=[Bcs])
            CS_s = nc.dram_tensor("CS_s", [4, LR], F32).ap()
            BCS = Buf()
            S.dma("sp", CS_s, cs[:], reads=[Bcs], writes=[BCS])
            for t in range(NT):
                c0 = t * TW
                S.op("dve", lambda e, c0=c0: e.tensor_scalar(lf[:, c0:c0 + TW], cs[:, c0:c0 + TW], cs[:, c0:c0 + 1], -8.0, ALU.subtract, ALU.mult),
                     reads=[Bcs, Blf], writes=[Blf])
            S.op("dve", lambda e: e.tensor_copy(hi[:], lf[:]), reads=[Blf], writes=[Bhi])
            S.op("dve", lambda e: e.tensor_tensor(lf[:], lf[:], hi[:], ALU.subtract), reads=[Blf, Bhi], writes=[Blf])
            S.op("dve", lambda e: e.tensor_copy(mid[:], lf[:]), reads=[Blf], writes=[Bmid])
            S.op("dve", lambda e: e.tensor_tensor(lf[:], lf[:], mid[:], ALU.subtract), reads=[Blf, Bmid], writes=[Blf])
            S.op("dve", lambda e: e.tensor_copy(lo[:], lf[:]), reads=[Blf], writes=[Blo])
            qa_sb = C.sb([6, LR], BF16)
            Bqa = Buf()
            S.dma("pool", qa_sb[:], qaug.rearrange("h r t -> (h r) t"), writes=[Bqa])
            S.dma("sp", AQd_s, qa_sb[:], reads=[Bqa], writes=[BAQ])
            S.dma("sp", AQ_s[:, 0, :], hi[:], reads=[Bhi], writes=[BAQ])
            S.dma("sp", AQ_s[:, 1, :], mid[:], reads=[Bmid], writes=[BAQ])
            S.dma("sp", AQ_s[:, 2, :], lo[:], reads=[Blo], writes=[BAQ])
            if conv_it is not None:
                for _ in conv_it:
                    pass
            S.barrier()
            S.flush()

        with contextlib.ExitStack() as st:
            C = Ctx(nc, st)
            kt = [C.sb([128, LR], BF16) for _ in range(2)]
            qt = [C.sb([128, LR], BF16) for _ in range(2)]
            vt = [C.sb([128, NB, 128], BF16) for _ in range(2)]
            Fd_sb = C.sb([128, 2 * 3 * TW], F32)
            Ff_sb = C.sb([128, 3 * TW], F32)
            dbias_sb = C.sb([128, 2 * 66], F32)
            csk = C.sb([128, 4, NB], F32)
            csT0 = C.sb([128, 4, NT], F32)
            bkt = [C.sb([128, NB], F32) for _ in range(2)]
            ones_bf = C.sb([128, 128], BF16)
            ones_b0 = C.sb([128, 128], BF16)
            ones_f = C.sb([128, 128], F32)
            lam_sb = C.sb([128, 256], F32)
            lprod = C.sb([128, 128], F32)
            lsum = C.sb([128, 2], F32)
            lexp = C.sb([128, 2], F32)
            neglam = C.sb([128, 1], F32)
            gcol = C.sb([128, 1], F32)
            pT = [C.sb([128, TW], BF16) for _ in range(4)]
            tmp = [C.sb([128, TW], F32) for _ in range(2)]
            rl = C.sb([128, TW], F32)
            lsb = [C.sb([128, TW], F32) for _ in range(2)]
            l0b = [C.sb([128, TW], F32) for _ in range(2)]
            Blsb = [Buf(), Buf()]; Bl0b = [Buf(), Buf()]
            a0 = C.sb([128, TW], F32)
            a1 = C.sb([128, TW], F32)
            od = C.sb([128, TW], F32)
            sq = C.sb([128, TW], F32)
            rstd = C.sb([128, TW], F32)
            ofin = [C.sb([128, TW], ODT) for _ in range(2)]
            NS = 3
            NPT = 4
            ps_s = [C.ps([128, 512]) for _ in range(NS)]
            ps_o = [C.ps([128, 512]) for _ in range(2)]
            ps_l = [C.ps([128, 512]) for _ in range(2)]
            ps_x = C.ps([128, 512])
            Bkt = [Buf(), Buf()]; Bqt = [Buf(), Buf()]; Bvt = [Buf(), Buf()]
            Bc = Buf(); Bcsk = Buf(); BcsT0 = Buf(); Bbkt = [Buf(), Buf()]
            Bps_s = [PB(), PB(), PB()]; Bps_o = [PB(), PB()]; Bps_l = [PB(), PB()]; Bps_x = PB()
            BpT = [Buf() for _ in range(4)]; Btmp = [Buf(), Buf()]
            Brl = Buf(); Ba0 = Buf(); Ba1 = Buf(); Bod = Buf(); Bsq = Buf(); Brstd = Buf(); Bofin = [Buf(), Buf()]
            Blam = Buf()

            S.dma("sp", Fd_sb[:], Fd, writes=[Bc])
            S.dma("sp", Ff_sb[:], Ff, writes=[Bc])
            S.dma("sp", dbias_sb[:], dbias, writes=[Bc])
            S.dma("sp", lam_sb[:], lamv, writes=[Blam])
            S.dma("sp", gcol[:], gsub, writes=[Blam])
            for h in range(4):
                S.dma("sp", csk[:, h, :], CS_s[h].rearrange("(b p) -> p b", p=128), reads=[BCS], writes=[Bcsk], allow_slow_non_contiguous=True)
            for h in range(4):
                src = bass.AP(CS_s.tensor, CS_s.offset + h * LR, [[0, 128], [TW, NT]])
                S.dma("sp", csT0[:, h, :], src, reads=[BCS], writes=[BcsT0], allow_slow_non_contiguous=True)
            S.op("pool", lambda e: e.memset(ones_bf[:], 1.0), writes=[Bc])
            S.op("pool", lambda e: e.memset(ones_b0[:], 1.0), writes=[Bc])
            S.op("pool", lambda e: e.memset(ones_b0[0:FPAD, :], 0.0), reads=[Bc], writes=[Bc])
            S.op("pool", lambda e: e.memset(ones_f[:], 1.0), writes=[Bc])
            for b in range(2):
                S.op("pool", lambda e, b=b: e.memset(kt[b][64:67, :], 1.0), writes=[Bkt[b]])
            S.op("dve", lambda e: e.tensor_tensor(lprod[:, 0:64], lam_sb[:, 0:64], lam_sb[:, 64:128], ALU.mult), reads=[Blam], writes=[Blam])
            S.op("dve", lambda e: e.tensor_tensor(lprod[:, 64:128], lam_sb[:, 128:192], lam_sb[:, 192:256], ALU.mult), reads=[Blam], writes=[Blam])
            S.op("dve", lambda e: e.reduce_sum(lsum[:, 0:1], lprod[:, 0:64], mybir.AxisListType.X), reads=[Blam], writes=[Blam])
            S.op("dve", lambda e: e.reduce_sum(lsum[:, 1:2], lprod[:, 64:128], mybir.AxisListType.X), reads=[Blam], writes=[Blam])
            S.op("act", lambda e: e.activation(lexp[:], lsum[:], AF.Exp), reads=[Blam], writes=[Blam])
            S.op("dve", lambda e: e.scalar_tensor_tensor(neglam[:], lexp[:, 1:2], -LAM_INIT0, lexp[:, 0:1], ALU.add, ALU.subtract),
                 reads=[Blam], writes=[Blam])
            S.op("dve", lambda e: e.tensor_scalar(gcol[:], gcol[:], 1.0 - LAM_INIT0, None, ALU.mult), reads=[Blam], writes=[Blam])

            conv_itB = T["hookB"](C) if T.get("hookB") else None
            si = 0
            pi = 0
            ti = 0
            oi = 0
            fi = 0
            for g in range(4):
                isdiff = g < 2
                units = (2 * g, 2 * g + 1)
                dv = 128 if isdiff else 64
                for ui, u in enumerate(units):
                    S.dma("sp", kt[ui][0:64, :], KT_s[u], reads=[BKT[u]], writes=[Bkt[ui]])
                    S.dma("sp", qt[ui][0:64, :], QT_s[u], reads=[BQT[u]], writes=[Bqt[ui]])
                    if isdiff:
                        S.dma("sp", qt[ui][64:67, :], AQd_s[3 * g:3 * g + 3, :], reads=[BAQ], writes=[Bqt[ui]])
                    else:
                        S.dma("sp", qt[ui][64:67, :], AQ_s[u - 4], reads=[BAQ], writes=[Bqt[ui]])
                if isdiff:
                    S.dma("sp", vt[0][:], V_s[:, g * 128:(g + 1) * 128].rearrange("(b p) c -> p b c", p=128),
                          reads=[BV], writes=[Bvt[0]])
                else:
                    for ui, u in enumerate(units):
                        hf = u - 4
                        S.dma("sp", vt[ui][:, :, 0:64], V_s[:, 256 + hf * 64:256 + (hf + 1) * 64].rearrange("(b p) c -> p b c", p=128),
                              reads=[BV], writes=[Bvt[ui]])
                        S.op("pool", lambda e, ui=ui: e.memset(vt[ui][:, :, 64:128], 1.0), writes=[Bvt[ui]])
                        S.op("pool", lambda e, ui=ui: e.memset(vt[ui][0:FPAD, 0, 64:128], 0.0), reads=[Bvt[ui]], writes=[Bvt[ui]])
                items = []
                for t in range(NT):
                    for ui, u in enumerate(units):
                        for j in range(3 * t + 3):
                            items.append((t, ui, u, j))
                state = {}
                deferred = []

                def stage_a(t, ui, u, j):
                    nonlocal si, pi, ti, oi, fi
                    c0 = t * TW
                    nblk = 3 * t + 3
                    K = kt[ui]; Q = qt[ui]; BK = Bkt[ui]; BQ = Bqt[ui]
                    hf = u - 4
                    if j == 0:
                        PO = ps_o[oi % 2]; BPO = Bps_o[oi % 2]; PL = ps_l[oi % 2]; BPL = Bps_l[oi % 2]
                        oi += 1
                        BKk = None; BBK = None
                        if not isdiff:
                            BKk = bkt[fi % 2]; BBK = Bbkt[fi % 2]
                            fi += 1
                            S.op("dve", lambda e, BKk=BKk, nblk=nblk, hf=hf, t=t: e.tensor_scalar(
                                BKk[:, 0:nblk], csk[:, hf, 0:nblk], csT0[:, hf, t:t + 1], None, ALU.subtract),
                                reads=[Bcsk, BcsT0], writes=[BBK])
                        state[(t, ui)] = (PO, BPO, PL, BPL, BKk, BBK)
                    PO, BPO, PL, BPL, BKk, BBK = state[(t, ui)]
                    jj = j - 3 * t
                    PS = ps_s[si % NS]; BPS = Bps_s[si % NS]
                    si += 1
                    S.op("pe", lambda e, PS=PS, K=K, Q=Q, j=j, c0=c0: e.matmul(
                        PS[:, 0:TW], K[0:67, j * 128:(j + 1) * 128], Q[0:67, c0:c0 + TW], start=True, stop=True),
                        reads=[BK, BQ], writes=[BPS])
                    PT = pT[pi % NPT]; BPT = BpT[pi % NPT]
                    pi += 1
                    if isdiff:
                        col = g * 66 + (jj + 63)
                        bcol = dbias_sb[:, col:col + 1]
                        rb = [Bc]
                    else:
                        bcol = BKk[:, j:j + 1]
                        rb = [BBK]
                    if jj < 0:
                        S.op("act", lambda e, PT=PT, PS=PS, bcol=bcol: e.activation(PT[:], PS[:, 0:TW], AF.Exp, bias=bcol, scale=SCALE),
                             reads=[BPS] + rb, writes=[BPT])
                    else:
                        TM = tmp[ti % 2]; BTM = Btmp[ti % 2]
                        ti += 1
                        if isdiff:
                            Fap = Fd_sb[:, (g * 3 + jj) * TW:(g * 3 + jj + 1) * TW]
                        else:
                            Fap = Ff_sb[:, jj * TW:(jj + 1) * TW]
                        S.op("dve", lambda e, TM=TM, PS=PS, Fap=Fap: e.scalar_tensor_tensor(
                            TM[:], PS[:, 0:TW], SCALE, Fap, ALU.mult, ALU.add), reads=[BPS, Bc], writes=[BTM])
                        S.op("act", lambda e, PT=PT, TM=TM, bcol=bcol: e.activation(PT[:], TM[:], AF.Exp, bias=bcol, scale=1.0),
                             reads=[BTM] + rb, writes=[BPT])
                    return PT, BPT

                def stage_b(idx, t, ui, u, j, PT, BPT):
                    c0 = t * TW
                    nblk = 3 * t + 3
                    hf = u - 4
                    if isdiff:
                        VT = vt[0]; BVT = Bvt[0]
                    else:
                        VT = vt[ui]; BVT = Bvt[ui]
                    PO, BPO, PL, BPL, BKk, BBK = state[(t, ui)]
                    if not isdiff:
                        S.op("pe", lambda e, PO=PO, VT=VT, PT=PT, j=j, nblk=nblk: e.matmul(
                            PO[:, 0:TW], VT[:, j, :], PT[:], start=(j == 0), stop=(j == nblk - 1)),
                            reads=[BVT, BPT], writes=[BPO], inc=True)
                        if j != nblk - 1:
                            return
                        LS = lsb[(2 * t + ui) % 2]; BLS = Blsb[(2 * t + ui) % 2]
                        L0 = l0b[(2 * t + ui) % 2]; BL0 = Bl0b[(2 * t + ui) % 2]
                        S.op("dve", lambda e, LS=LS, PO=PO: e.tensor_scalar(LS[64:128, :], PO[64:128, 0:TW], 1e-30, None, ALU.add),
                             reads=[BPO], writes=[BLS])
                        S.dma("sp", L0[0:64, :], LS[64:128, :], reads=[BLS], writes=[BL0])
                        S.op("dve", lambda e, L0=L0: e.reciprocal(rl[0:64, :], L0[0:64, :]), reads=[BL0], writes=[Brl])
                        OF = ofin[(2 * t + ui) % 2]; BOF = Bofin[(2 * t + ui) % 2]
                        S.op("dve", lambda e, OF=OF, PO=PO: e.tensor_tensor(OF[0:64, :], PO[0:64, 0:TW], rl[0:64, :], ALU.mult),
                             reads=[BPO, Brl], writes=[BOF])
                        r0 = 256 + hf * 64
                        out_toks.append(S.dma("sp", T["oT_tile"](r0, r0 + 64, t), OF[0:64, :], reads=[BOF], writes=[Bo]))
                        return
                    S.op("pe", lambda e, PO=PO, VT=VT, PT=PT, j=j, nblk=nblk, dv=dv: e.matmul(
                        PO[0:dv, 0:TW], VT[:, j, 0:dv], PT[:], start=(j == 0), stop=(j == nblk - 1)),
                        reads=[BVT, BPT], writes=[BPO], inc=False)
                    on = ones_b0 if j == 0 else ones_bf
                    S.op("pe", lambda e, PL=PL, on=on, PT=PT, j=j, nblk=nblk, dv=dv: e.matmul(
                        PL[0:dv, 0:TW], on[:, 0:dv], PT[:], start=(j == 0), stop=(j == nblk - 1)),
                        reads=[Bc, BPT], writes=[BPL], inc=True)
                    if j != nblk - 1:
                        return
                    S.op("dve", lambda e, PL=PL, dv=dv: e.tensor_scalar(rl[0:dv, :], PL[0:dv, 0:TW], 1e-30, None, ALU.add), reads=[BPL], writes=[Brl])
                    S.op("dve", lambda e, dv=dv: e.reciprocal(rl[0:dv, :], rl[0:dv, :]), reads=[Brl], writes=[Brl])
                    if not isdiff:
                        OF = ofin[(2 * t + ui) % 2]; BOF = Bofin[(2 * t + ui) % 2]
                        S.op("dve", lambda e, OF=OF, PO=PO: e.tensor_tensor(OF[0:64, :], PO[0:64, 0:TW], rl[0:64, :], ALU.mult),
                             reads=[BPO, Brl], writes=[BOF])
                        r0 = 256 + hf * 64
                        out_toks.append(S.dma("sp", T["oT_tile"](r0, r0 + 64, t), OF[0:64, :], reads=[BOF], writes=[Bo]))
                        return
                    AA = a0 if ui == 0 else a1
                    BA = Ba0 if ui == 0 else Ba1
                    S.op("dve", lambda e, AA=AA, PO=PO: e.tensor_tensor(AA[:], PO[:, 0:TW], rl[:], ALU.mult),
                         reads=[BPO, Brl], writes=[BA])
                    if ui == 0:
                        return
                    OF = ofin[t % 2]; BOF = Bofin[t % 2]
                    S.op("dve", lambda e: e.scalar_tensor_tensor(od[:], a1[:], neglam[:, 0:1], a0[:], ALU.mult, ALU.add),
                         reads=[Ba0, Ba1, Blam], writes=[Bod])
                    S.op("pool", lambda e: e.tensor_tensor(sq[:], od[:], od[:], ALU.mult), reads=[Bod], writes=[Bsq])

                    def tail(OF=OF, BOF=BOF, t=t):
                        S.op("pe", lambda e: e.matmul(ps_x[:, 0:TW], ones_f[:], sq[:], start=True, stop=True),
                             reads=[Bc, Bsq], writes=[Bps_x])
                        S.op("act", lambda e: e.activation(rstd[:], ps_x[:, 0:TW], AF.Ln, bias=RMS_EPS, scale=1.0 / 128.0),
                             reads=[Bps_x], writes=[Brstd])
                        S.op("act", lambda e: e.activation(rstd[:], rstd[:], AF.Exp, scale=-0.5), reads=[Brstd], writes=[Brstd])
                        S.op("dve", lambda e, OF=OF: e.scalar_tensor_tensor(OF[:], od[:], gcol[:, 0:1], rstd[:], ALU.mult, ALU.mult),
                             reads=[Bod, Brstd, Blam], writes=[BOF])
                        out_toks.append(S.dma("sp", T["oT_tile"](g * 128, (g + 1) * 128, t), OF[:], reads=[BOF], writes=[Bo]))
                    deferred.append((idx + 3, tail))

                LA = 2
                pend = {}
                n_it = len(items)
                for i in range(n_it + LA):
                    if conv_itB is not None and i % 8 == 7:
                        next(conv_itB, None)
                    if i < n_it:
                        pend[i] = stage_a(*items[i])
                    k = i - LA
                    if k >= 0:
                        PT, BPT = pend.pop(k)
                        stage_b(k, *items[k], PT, BPT)
                    while deferred and deferred[0][0] <= k:
                        deferred.pop(0)[1]()
                while deferred:
                    deferred.pop(0)[1]()
            if conv_itB is not None:
                for _ in conv_itB:
                    pass
            S.barrier()
            S.flush()
    return out_toks


def l1_consts(s):
    p = np.arange(128, dtype=np.float64)[:, None]
    slopes = [2.0 ** (-8.0 * (h + 1) / 4) for h in (2 * s, 2 * s + 1)]
    dbias = np.zeros((128, 2, 66), np.float32)
    Fd = np.zeros((128, 2, 3, TW), np.float32)
    qaug = np.zeros((2, 3, LR), np.float32)
    i = np.arange(TW, dtype=np.float64)[None, :]
    r = np.arange(LR)
    w = r % TW
    for hl, sl in enumerate(slopes):
        for idx in range(66):
            dbias[:, hl, idx] = (sl * (128 * (idx - 63) + p))[:, 0]
        for jj in range(3):
            rk = 128 * jj + p
            vis = (rk // 64) <= (i // 64)
            f = np.where(i >= rk, 0.0, 2 * sl * (i - rk))
            Fd[:, hl, jj, :] = np.where(vis, f, NEG)
        qaug[hl, 0] = -(8 * sl) * 256 * (w // 256)
        qaug[hl, 1] = -(8 * sl) * (w % 256)
    Ff = np.zeros((128, 3, TW), np.float32)
    for jj in range(3):
        rk = 128 * jj + p
        Ff[:, jj, :] = np.where(rk <= i, 0.0, NEG)
    return (dbias.reshape(128, 132), Fd.reshape(128, 6 * TW), Ff.reshape(128, 3 * TW), qaug)


def l1_inputs(xT_b, even_w_in, even_f_bias, diff_lambda, diff_subln_g, s):
    w = even_w_in
    dq = [w[:, h * 128:(h + 1) * 128] for h in (2 * s, 2 * s + 1)]
    dk = [w[:, 512 + h * 128:512 + (h + 1) * 128] for h in (2 * s, 2 * s + 1)]
    dvv = [w[:, 1024 + h * 128:1024 + (h + 1) * 128] for h in (2 * s, 2 * s + 1)]
    fq = w[:, 1536 + 256 * s:1536 + 256 * (s + 1)]
    fk = w[:, 2048 + 256 * s:2048 + 256 * (s + 1)]
    fv = w[:, 2560 + 256 * s:2560 + 256 * (s + 1)]
    wf = w[:, 3072 + 4 * s:3072 + 4 * (s + 1)]
    dbias, Fd, Ff, qaug = l1_consts(s)
    return {
        "xT": xT_b,
        "wq": np.ascontiguousarray(np.concatenate(dq + [fq], axis=1)),
        "wk": np.ascontiguousarray(np.concatenate(dk + [fk], axis=1)),
        "wv": np.ascontiguousarray(np.concatenate(dvv + [fv], axis=1)),
        "wf": np.ascontiguousarray(wf),
        "fbias": np.ascontiguousarray(even_f_bias[4 * s:4 * (s + 1)].reshape(4, 1)),
        "lamv": np.ascontiguousarray(np.broadcast_to(diff_lambda.reshape(1, 256), (128, 256))),
        "gsub": np.ascontiguousarray(diff_subln_g.reshape(128, 1)),
        "dbias": dbias, "Fd": Fd, "Ff": Ff, "qaug": qaug,
    }


def emit_convert(S, C, src, dst, BD, rows, cols, tag, shared=None):
    if shared is None:
        stg = [C.sb([128, 2048], BF16) for _ in range(2)]
        Bs = [Buf(), Buf()]
    else:
        stg, Bs = shared
    for _ in iter_convert(S, stg, Bs, src, dst, BD, rows, cols):
        pass


def iter_convert(S, stg, Bs, src, dst, BD, rows, cols, ctr=[0]):
    for r0 in range(0, rows, 128):
        for c0 in range(0, cols, 2048):
            w = min(2048, cols - c0)
            T = stg[ctr[0] % 2]; B = Bs[ctr[0] % 2]
            ctr[0] += 1
            S.dma("pool", T[:, 0:w], src[r0:r0 + 128, c0:c0 + w], writes=[B])
            S.dma("sp", dst[r0:r0 + 128, c0:c0 + w], T[:, 0:w], reads=[B], writes=[BD])
            yield


def emit_ln(S, y, By, yb, Byb, sqb, Bsqb, tmpn, Btmpn, out_f, Bof, out_b, Bob, gcols, bcols, Bp, ones_b, Bc, ps1, Bps1, ps2, Bps2,
            mean, msq, rstd, Bst, nfeat_chunks=8):
    n = nfeat_chunks
    for c in range(n):
        S.op("act", lambda e, c=c: e.copy(yb[:, c, :], y[:, c, :]), reads=[By[c]], writes=[Byb[c]])
        S.op("act", lambda e, c=c: e.activation(sqb[:, c, :], y[:, c, :], AF.Square), reads=[By[c]], writes=[Bsqb[c]])
    for c in range(n):
        S.op("pe", lambda e, c=c: e.matmul(ps1[:, 0:TW], ones_b[:], yb[:, c, :], start=(c == 0), stop=(c == n - 1)),
             reads=[Bc, Byb[c]], writes=[Bps1], inc=(c == n - 1))
    for c in range(n):
        S.op("pe", lambda e, c=c: e.matmul(ps2[:, 0:TW], ones_b[:], sqb[:, c, :], start=(c == 0), stop=(c == n - 1)),
             reads=[Bc, Bsqb[c]], writes=[Bps2], inc=(c == n - 1))
    inv = 1.0 / (128.0 * n)
    S.op("dve", lambda e: e.tensor_scalar(mean[:], ps1[:, 0:TW], inv, None, ALU.mult), reads=[Bps1], writes=[Bst])
    S.op("dve", lambda e: e.tensor_tensor(msq[:], mean[:], mean[:], ALU.mult), reads=[Bst], writes=[Bst])
    S.op("dve", lambda e: e.scalar_tensor_tensor(msq[:], ps2[:, 0:TW], inv, msq[:], ALU.mult, ALU.subtract),
         reads=[Bps2, Bst], writes=[Bst])
    S.op("act", lambda e: e.activation(rstd[:], msq[:], AF.Ln, bias=LN_EPS, scale=1.0), reads=[Bst], writes=[Bst])
    S.op("act", lambda e: e.activation(rstd[:], rstd[:], AF.Exp, scale=-0.5), reads=[Bst], writes=[Bst])
    for c in range(n):
        eng = "dve" if c % 2 == 0 else "pool"
        k = (c % 2) * 2 + (c // 2) % 2
        TN = tmpn[k]; BTN = Btmpn[k]
        S.op(eng, lambda e, c=c, TN=TN: e.tensor_tensor(TN[:], y[:, c, :], mean[:], ALU.subtract), reads=[By[c], Bst], writes=[BTN])
        S.op(eng, lambda e, c=c, TN=TN: e.tensor_tensor(TN[:], TN[:], rstd[:], ALU.mult), reads=[Bst, BTN], writes=[BTN])
        S.op("dve", lambda e, c=c, TN=TN: e.tensor_scalar(out_f[:, c, :], TN[:], gcols[:, c:c + 1], bcols[:, c:c + 1], ALU.mult, ALU.add),
             reads=[BTN, Bp], writes=[Bof[c]])
        if out_b is not None:
            S.op("act", lambda e, c=c: e.copy(out_b[:, c, :], out_f[:, c, :]), reads=[Bof[c]], writes=[Bob[c]])


def emit_post(nc, S, st, ntiles, KC, src_o, src_mode, Bsrc, hres, wout_b, w1_b, w2_b, Bwb, lnp, hout, Bhout, flags=None):
    C = Ctx(nc, st)
    ot = [C.sb([128, KC, TW], BF16) for _ in range(2)]
    hr = [C.sb([128, 8, TW], F32)] * 2
    y = C.sb([128, 8, TW], F32)
    ysq = [C.sb([128, TW], F32) for _ in range(4)]
    h1 = [C.sb([128, 8, TW], F32) for _ in range(2)]
    h1b = C.sb([128, 8, TW], BF16)
    h2 = C.sb([128, 8, TW], F32)
    yb = C.sb([128, 8, TW], BF16)
    sqb = C.sb([128, 8, TW], BF16)
    hid = C.sb([128, 32, TW], BF16)
    rl = [C.sb([128, TW], F32) for _ in range(2)]
    WCOL = 4096 // KC
    NW = 8
    wring = [C.sb([128, 4096], BF16) for _ in range(NW)]
    Bring = [Buf() for _ in range(NW)]
    wk = [0]

    def wload(src_ap, c):
        i = wk[0] % NW
        wk[0] += 1
        W = wring[i][:].rearrange("p (c o) -> p c o", c=c)
        S.dma("sp", W, src_ap, reads=[Bwb], writes=[Bring[i]])
        return W, Bring[i]
    lnp_sb = C.sb([128, 32], F32)
    ones_f = C.sb([128, 128], BF16)
    if src_mode == "blend":
        cand = [C.sb([128, KC, TW], BF16) for _ in range(2)]
        fl_sb = C.sb([128, 2], F32)
        Bcand = [Buf(), Buf()]
        Bfl = Buf()
        S.dma("sp", fl_sb[:], flags, writes=[Bfl])
    mean = C.sb([128, TW], F32)
    msq = C.sb([128, TW], F32)
    rstd = C.sb([128, TW], F32)
    pa = [C.ps([128, 512]) for _ in range(2)]
    pb = [C.ps([128, 512]) for _ in range(4)]
    ps1 = C.ps([128, 512])
    ps2 = C.ps([128, 512])
    Bot = [Buf(), Buf()]; Bhr = [Buf()] * 2
    L8 = lambda: [Buf() for _ in range(8)]
    By = L8(); Bysq = [Buf() for _ in range(4)]; Bh1 = [L8(), L8()]; Bh1b = L8(); Bh2 = L8(); Byb = L8(); Bsqb = L8()
    Bhid = [Buf() for _ in range(32)]; Brl = [Buf(), Buf()]
    Bp = Buf(); Bc = Buf(); Bst = Buf(); Bpa = [PB(), PB()]; Bpb = [PB() for _ in range(4)]; Bps1 = PB(); Bps2 = PB()
    S.dma("sp", lnp_sb[:], lnp, writes=[Bp])
    S.op("pool", lambda e: e.memset(ones_f[:], 1.0), writes=[Bc])
    wr = "(c p) o -> p c o"
    wo_v = wout_b.rearrange(wr, p=128)
    w1_v = w1_b.rearrange(wr, p=128)
    w2_v = w2_b.rearrange(wr, p=128)
    so_v = None if src_mode == "blend" else src_o.rearrange("(c p) t -> p c t", p=128)
    hr_v = hres.rearrange("(c p) t -> p c t", p=128)
    ho_v = hout.rearrange("(c p) t -> p c t", p=128)
    ai = 0
    wi = 0
    w1i = 0
    w2i = 0
    ri = 0
    toks = []
    def stA(t):
        nonlocal ai, wi
        c0 = t * TW
        OT = ot[t % 2]; BOT = Bot[t % 2]; HR = hr[t % 2]; BHR = Bhr[t % 2]
        if src_mode == "blend":
            for k in range(2):
                S.dma("sp", cand[k][:], src_o(k, t), reads=[Bsrc], writes=[Bcand[k]])
            for kc in range(KC):
                S.op("dve", lambda e, kc=kc, OT=OT: e.tensor_scalar(OT[:, kc, :], cand[0][:, kc, :], fl_sb[:, 0:1], None, ALU.mult),
                     reads=[Bcand[0], Bfl], writes=[BOT])
                S.op("dve", lambda e, kc=kc, OT=OT: e.scalar_tensor_tensor(OT[:, kc, :], cand[1][:, kc, :], fl_sb[:, 1:2], OT[:, kc, :], ALU.mult, ALU.add),
                     reads=[Bcand[1], Bfl, BOT], writes=[BOT])
        else:
            S.dma("pool" if src_mode == "f32" else "sp", OT[:], so_v[:, :, c0:c0 + TW], reads=[Bsrc], writes=[BOT])
        for n2 in range(1024 // WCOL):
            WO, BWO = wload(wo_v[:, :, n2 * WCOL:(n2 + 1) * WCOL], KC)
            for c4 in range(WCOL // 128):
                cc = n2 * (WCOL // 128) + c4
                P = pa[ai % 2]; BP = Bpa[ai % 2]
                ai += 1
                for kc in range(KC):
                    S.op("pe", lambda e, P=P, WO=WO, kc=kc, c4=c4, OT=OT: e.matmul(
                        P[:, 0:TW], WO[:, kc, c4 * 128:(c4 + 1) * 128], OT[:, kc, :], start=(kc == 0), stop=(kc == KC - 1)),
                        reads=[BWO, BOT], writes=[BP], inc=(kc == KC - 1))
                S.op("dve", lambda e, P=P, cc=cc, HR=HR: e.scalar_tensor_tensor(
                    y[:, cc, :], HR[:, cc, :], ALPHA, P[:, 0:TW], ALU.mult, ALU.add), reads=[BP, BHR], writes=[By[cc]])

    def stB(t):
        emit_ln(S, y, By, yb, Byb, sqb, Bsqb, ysq, Bysq, h1[t % 2], Bh1[t % 2], h1b, Bh1b, lnp_sb[:, 0:8], lnp_sb[:, 8:16], Bp, ones_f, Bc,
                ps1, Bps1, ps2, Bps2, mean, msq, rstd, Bst)

    def stC(t):
        nonlocal ai, w1i, ri
        for hg in range(8):
            W1, BW1 = wload(w1_v[:, :, hg * 512:(hg + 1) * 512], 8)
            for h4 in range(4):
                hc = hg * 4 + h4
                P = pa[ai % 2]; BP = Bpa[ai % 2]
                ai += 1
                for fc in range(8):
                    S.op("pe", lambda e, P=P, W1=W1, fc=fc, h4=h4: e.matmul(
                        P[:, 0:TW], W1[:, fc, h4 * 128:(h4 + 1) * 128], h1b[:, fc, :], start=(fc == 0), stop=(fc == 7)),
                        reads=[BW1, Bh1b[fc]], writes=[BP], inc=(fc == 7))
                R = rl[ri % 2]; BR = Brl[ri % 2]
                ri += 1
                S.op("act", lambda e, R=R, P=P: e.activation(R[:], P[:, 0:TW], AF.Relu), reads=[BP], writes=[BR])
                S.op("pool", lambda e, R=R, hc=hc: e.tensor_tensor(hid[:, hc, :], R[:], R[:], ALU.mult), reads=[BR], writes=[Bhid[hc]])

    def stD(t, n2s):
        nonlocal w2i
        for n2 in n2s:
            for kg in range(4):
                W2, BW2 = wload(w2_v[:, kg * 8:(kg + 1) * 8, n2 * 512:(n2 + 1) * 512], 8)
                for c4 in range(4):
                    for k8 in range(8):
                        hc = kg * 8 + k8
                        S.op("pe", lambda e, c4=c4, W2=W2, k8=k8, hc=hc: e.matmul(
                            pb[c4][:, 0:TW], W2[:, k8, c4 * 128:(c4 + 1) * 128], hid[:, hc, :], start=(hc == 0), stop=(hc == 31)),
                            reads=[BW2, Bhid[hc]], writes=[Bpb[c4]], inc=(k8 == 7))
            for c4 in range(4):
                cc = n2 * 4 + c4
                S.op("dve", lambda e, c4=c4, cc=cc, t=t: e.scalar_tensor_tensor(
                    h2[:, cc, :], h1[t % 2][:, cc, :], ALPHA, pb[c4][:, 0:TW], ALU.mult, ALU.add), reads=[Bpb[c4], Bh1[t % 2][cc]], writes=[Bh2[cc]])

    def stE(t):
        c0 = t * TW
        emit_ln(S, h2, Bh2, yb, Byb, sqb, Bsqb, ysq, Bysq, h2, Bh2, None, None, lnp_sb[:, 16:24], lnp_sb[:, 24:32], Bp, ones_f, Bc,
                ps1, Bps1, ps2, Bps2, mean, msq, rstd, Bst)
        toks.append(S.dma("sp", ho_v[:, :, c0:c0 + TW], h2[:], reads=Bh2, writes=[Bhout]))

    def stH(t):
        S.dma("sp", hr[0][:], hr_v[:, :, t * TW:(t + 1) * TW], writes=[Bhr[t % 2]])

    stH(0)
    stA(0)
    stB(0)
    for t in range(ntiles):
        if t + 1 < ntiles:
            stH(t + 1)
        stC(t)
        if t + 1 < ntiles:
            stA(t + 1)
        stD(t, (0,))
        if t + 1 < ntiles:
            stB(t + 1)
        stD(t, (1,))
        stE(t)
    return toks


L2_STAGE = 4
L3_STAGE = 3
L3_SUB = 3
L3_NCH = 12
RET_GAMMA = [1.0 - 2.0 ** (-5.0 - h) for h in range(4)]


def emit_state_tile(S, h2b, Bh2b, wk_sb, wv_sb, Bw, kts, Bkts, vts, Bvts, kdec_sb, Bc, Sst, BS, pk, Bpk, ctr):
    for bl in range(3):
        for half in range(2):
            P = pk[ctr[0] % 2]; BP = Bpk[ctr[0] % 2]
            ctr[0] += 1
            for fc in range(8):
                S.op("pe", lambda e, P=P, fc=fc, bl=bl, half=half: e.matmul(
                    P[:], h2b[:, fc, bl * 128:(bl + 1) * 128], wk_sb[:, fc, half * 512:(half + 1) * 512], start=(fc == 0), stop=(fc == 7)),
                    reads=[Bh2b, Bw], writes=[BP], inc=(fc == 7))
            for hh in range(2):
                h = half * 2 + hh
                S.op("dve", lambda e, P=P, bl=bl, h=h, hh=hh: e.tensor_scalar(
                    kts[:, bl, h * 256:(h + 1) * 256], P[:, hh * 256:(hh + 1) * 256], kdec_sb[:, bl * 4 + h:bl * 4 + h + 1], None, ALU.mult),
                    reads=[BP, Bc], writes=[Bkts])
        for h in range(4):
            P = pk[ctr[0] % 2]; BP = Bpk[ctr[0] % 2]
            ctr[0] += 1
            for fc in range(8):
                S.op("pe", lambda e, P=P, fc=fc, bl=bl, h=h: e.matmul(
                    P[:], h2b[:, fc, bl * 128:(bl + 1) * 128], wv_sb[:, fc, h * 512:(h + 1) * 512], start=(fc == 0), stop=(fc == 7)),
                    reads=[Bh2b, Bw], writes=[BP], inc=(fc == 7))
            S.op("act", lambda e, P=P, bl=bl, h=h: e.copy(vts[:, bl, h * 512:(h + 1) * 512], P[:]), reads=[BP], writes=[Bvts])


def emit_state_update(S, kts, Bkts, vts, Bvts, Sst, BS, Sb, BSb, pk, Bpk, ctr):
    for h in range(4):
        c384 = RET_GAMMA[h] ** TW
        for dkc in range(2):
            P = pk[ctr[0] % 2]; BP = Bpk[ctr[0] % 2]
            ctr[0] += 1
            for bl in range(3):
                S.op("pe", lambda e, P=P, bl=bl, h=h, dkc=dkc: e.matmul(
                    P[:], kts[:, bl, h * 256 + dkc * 128:h * 256 + (dkc + 1) * 128], vts[:, bl, h * 512:(h + 1) * 512],
                    start=(bl == 0), stop=(bl == 2)), reads=[Bkts, Bvts], writes=[BP], inc=(bl == 2))
            idx = h * 2 + dkc
            S.op("dve", lambda e, P=P, idx=idx, c384=c384: e.scalar_tensor_tensor(
                Sst[:, idx, :], Sst[:, idx, :], c384, P[:], ALU.mult, ALU.add), reads=[BP, BS], writes=[BS])
            if Sb is not None:
                S.op("pool", lambda e, idx=idx: e.tensor_copy(Sb[:, idx, :], Sst[:, idx, :]), reads=[BS], writes=[BSb])


def emit_prepass(nc, S, T, Bhout, BSe):
    h2T = T["h2T"]; wk = T["wk"]; wv = T["wv"]; kdec = T["kdec"]; tmask = T["tmask"]; S_end = T["S_end"]
    if True:
        with contextlib.ExitStack() as st:
            C = Ctx(nc, st)
            wk_sb = C.sb([128, 8, 1024], BF16)
            wv_sb = C.sb([128, 8, 2048], BF16)
            h2b = [C.sb([128, 8, TW], BF16) for _ in range(2)]
            kts = C.sb([128, 3, 1024], BF16)
            vts = C.sb([128, 3, 2048], BF16)
            kdec_sb = C.sb([128, 12], F32)
            tm_sb = C.sb([128, TW], F32)
            Sst = C.sb([128, 8, 512], F32)
            pk = [C.ps([128, 512]) for _ in range(2)]
            Bw = Buf(); Bh2b = [Buf(), Buf()]; Bkts = Buf(); Bvts = Buf(); Bc = Buf(); BS = Buf(); Bpk = [PB(), PB()]
            wr = "(c p) o -> p c o"
            wqueue = T.get("wqueue", "pool")
            for fc in range(8):
                S.dma(wqueue, wk_sb[:, fc, :], wk[fc * 128:(fc + 1) * 128, :], reads=[T["Bwb"]], writes=[Bw])
                S.dma(wqueue, wv_sb[:, fc, :], wv[fc * 128:(fc + 1) * 128, :], reads=[T["Bwb"]], writes=[Bw])
            S.dma("sp", kdec_sb[:], kdec, writes=[Bc])
            S.dma("sp", tm_sb[:], tmask, writes=[Bc])
            for i8 in range(8):
                S.op("dve", lambda e, i8=i8: e.memset(Sst[:, i8, :], 0.0), writes=[BS])
            hv = h2T.rearrange("(c p) t -> p c t", p=128)
            ctr = [0]
            for t in range(NTH):
                c0 = t * TW
                H = h2b[t % 2]; BH = Bh2b[t % 2]
                S.dma("pool", H[:], hv[:, :, c0:c0 + TW], reads=[Bhout], writes=[BH])
                if t == 0:
                    for fc in range(8):
                        S.op("dve", lambda e, H=H, fc=fc: e.tensor_tensor(H[:, fc, :], H[:, fc, :], tm_sb[:], ALU.mult),
                             reads=[BH, Bc], writes=[BH])
                emit_state_tile(S, H, BH, wk_sb, wv_sb, Bw, kts, Bkts, vts, Bvts, kdec_sb, Bc, Sst, BS, pk, Bpk, ctr)
                emit_state_update(S, kts, Bkts, vts, Bvts, Sst, BS, None, None, pk, Bpk, ctr)
            for k in range(4):
                S.dma("sp", S_end[k], Sst[32 * k:32 * (k + 1)], reads=[BS], writes=[BSe])
            S.barrier()
            S.flush()

def ret_consts():
    p = np.arange(128, dtype=np.float64)
    kdec = np.zeros((128, 3, 4), np.float32)
    for h in range(4):
        g = RET_GAMMA[h]
        for bl in range(3):
            kdec[:, bl, h] = g ** (TW - 1 - (bl * 128 + p)) / 16.0
    return kdec.reshape(128, 12)


def lnp_table(ln_g, ln_b, layer):
    cols = []
    for k in range(2):
        cols.append(ln_g[layer, k].reshape(8, 128).T)
        cols.append(ln_b[layer, k].reshape(8, 128).T)
    return np.ascontiguousarray(np.concatenate(cols, axis=1).astype(np.float32))


def emit_retention(nc, S, st, T, Bwb, Bh2, BSg, ByT):
    h2T = T["h2T"]; wi_b = T["wi_b"]; kdec = T["kdec"]; tmask = T["tmask"]; dmask = T["dmask"]; qdec = T["qdec"]
    yT_s = T["yT_s"]; S_src = T["S_src"]; flags = T["flags"]
    C = Ctx(nc, st)
    if True:
        if True:
            h2b = [C.sb([128, 8, TW], BF16) for _ in range(2)]
            wc = [C.sb([128, 8, 512], BF16) for _ in range(2)]
            qT = C.sb([128, 8, TW], BF16)
            qdT = C.sb([128, 8, TW], BF16)
            kT = C.sb([128, 8, TW], BF16)
            kts = C.sb([128, 3, 1024], BF16)
            vts = C.sb([128, 3, 2048], BF16)
            sg = C.sb([128, 16, TW], BF16)
            sTb = [C.sb([128, 3, TW], BF16) for _ in range(4)]
            o32 = [C.sb([128, 4, TW], F32) for _ in range(2)]
            osq = [C.sb([128, 4, TW], BF16) for _ in range(2)]
            yT = [C.sb([128, 16, TW], BF16) for _ in range(2)]
            Sst = C.sb([128, 8, 512], F32)
            Sb = C.sb([128, 8, 512], BF16)
            dm_sb = C.sb([128, 12 * TW], F32)
            qd_sb = C.sb([128, 4 * TW], F32)
            kdec_sb = C.sb([128, 12], F32)
            tm_sb = C.sb([128, TW], F32)
            ones_f = C.sb([128, 128], BF16)
            rstd = [C.sb([128, TW], F32) for _ in range(2)]
            tmpo = C.sb([128, TW], F32)
            pk = [C.ps([128, 512]) for _ in range(2)]
            psc = [C.ps([128, 512]) for _ in range(2)]
            po = [C.ps([128, 512]) for _ in range(2)]
            pss = C.ps([128, 512])
            Bh2b = [Buf(), Buf()]; Bwc = [Buf(), Buf()]; BqT = Buf(); BqdT = Buf(); BkT = Buf(); Bkts = Buf(); Bvts = Buf()
            Bsg = Buf(); BsTb = [Buf() for _ in range(4)]; Bo32 = [Buf(), Buf()]; Bosq = [Buf(), Buf()]; ByTt = [Buf(), Buf()]; BS = Buf(); BSb = Buf()
            Bc = Buf(); Brstd = [Buf(), Buf()]; Btmpo = Buf(); Bpk = [PB(), PB()]; Bpsc = [PB(), PB()]; Bpo = [PB(), PB()]; Bpss = PB()
            S.dma("sp", dm_sb[:], dmask, writes=[Bc])
            S.dma("sp", qd_sb[:], qdec, writes=[Bc])
            S.dma("sp", kdec_sb[:], kdec, writes=[Bc])
            S.dma("sp", tm_sb[:], tmask, writes=[Bc])
            S.op("pool", lambda e: e.memset(ones_f[:], 1.0), writes=[Bc])
            fl_sb = C.sb([128, 2], F32)
            S.dma("sp", fl_sb[:], flags, writes=[Bc])
            for k in range(4):
                S.dma("sp", Sst[32 * k:32 * (k + 1)], S_src[k], reads=[BSg], writes=[BS])
            for i8 in range(8):
                S.op("dve", lambda e, i8=i8: e.tensor_scalar(Sst[:, i8, :], Sst[:, i8, :], fl_sb[:, 1:2], None, ALU.mult),
                     reads=[BS, Bc], writes=[BS])
            for i8 in range(8):
                S.op("pool", lambda e, i8=i8: e.tensor_copy(Sb[:, i8, :], Sst[:, i8, :]), reads=[BS], writes=[BSb])
            hv = h2T.rearrange("(c p) t -> p c t", p=128)
            wiv = wi_b.rearrange("(c p) o -> p c o", p=128)
            yv = yT_s.rearrange("(c p) t -> p c t", p=128)
            ctr = [0]
            wci = 0
            sci = 0
            oi = 0
            for t in range(NTH if L3_STAGE >= 2 else 1):
                c0 = t * TW
                H = h2b[t % 2]; BH = Bh2b[t % 2]
                Y = yT[t % 2]; BY = ByTt[t % 2]
                S.dma("pool", H[:], hv[:, :, c0:c0 + TW], reads=[Bh2], writes=[BH])
                if t == 0:
                    for fc in range(8):
                        S.op("dve", lambda e, H=H, fc=fc: e.tensor_tensor(H[:, fc, :], H[:, fc, :], tm_sb[:], ALU.mult),
                             reads=[BH, Bc], writes=[BH])
                for ch in range(L3_NCH):
                    W = wc[wci % 2]; BW = Bwc[wci % 2]
                    wci += 1
                    S.dma("sp", W[:], wiv[:, :, ch * 512:(ch + 1) * 512], reads=[Bwb], writes=[BW])
                    if ch < 4 or ch >= 8:
                        for c4 in range(4):
                            P = pk[ctr[0] % 2]; BP = Bpk[ctr[0] % 2]
                            ctr[0] += 1
                            for fc in range(8):
                                S.op("pe", lambda e, P=P, W=W, fc=fc, c4=c4, H=H: e.matmul(
                                    P[:, 0:TW], W[:, fc, c4 * 128:(c4 + 1) * 128], H[:, fc, :], start=(fc == 0), stop=(fc == 7)),
                                    reads=[BW, BH], writes=[BP], inc=(fc == 7))
                            if ch < 2:
                                ci = ch * 4 + c4
                                h = ci // 2
                                S.op("act", lambda e, P=P, ci=ci: e.copy(qT[:, ci, :], P[:, 0:TW]), reads=[BP], writes=[BqT])
                                S.op("dve", lambda e, P=P, ci=ci, h=h: e.tensor_tensor(qdT[:, ci, :], P[:, 0:TW], qd_sb[:, h * TW:(h + 1) * TW], ALU.mult),
                                     reads=[BP, Bc], writes=[BqdT])
                            elif ch < 4:
                                ci = (ch - 2) * 4 + c4
                                S.op("act", lambda e, P=P, ci=ci: e.copy(kT[:, ci, :], P[:, 0:TW]), reads=[BP], writes=[BkT])
                            else:
                                gi = (ch - 8) * 4 + c4
                                S.op("act", lambda e, P=P, gi=gi: e.activation(sg[:, gi, :], P[:, 0:TW], AF.Silu), reads=[BP], writes=[Bsg])
                    if 2 <= ch < 4:
                        half = ch - 2
                        for bl in range(3):
                            P = pk[ctr[0] % 2]; BP = Bpk[ctr[0] % 2]
                            ctr[0] += 1
                            for fc in range(8):
                                S.op("pe", lambda e, P=P, W=W, fc=fc, bl=bl, H=H: e.matmul(
                                    P[:], H[:, fc, bl * 128:(bl + 1) * 128], W[:, fc, :], start=(fc == 0), stop=(fc == 7)),
                                    reads=[BH, BW], writes=[BP], inc=(fc == 7))
                            for hh in range(2):
                                h = half * 2 + hh
                                S.op("dve", lambda e, P=P, bl=bl, h=h, hh=hh: e.tensor_scalar(
                                    kts[:, bl, h * 256:(h + 1) * 256], P[:, hh * 256:(hh + 1) * 256], kdec_sb[:, bl * 4 + h:bl * 4 + h + 1], None, ALU.mult),
                                    reads=[BP, Bc], writes=[Bkts])
                    if 4 <= ch < 8:
                        h = ch - 4
                        for bl in range(3):
                            P = pk[ctr[0] % 2]; BP = Bpk[ctr[0] % 2]
                            ctr[0] += 1
                            for fc in range(8):
                                S.op("pe", lambda e, P=P, W=W, fc=fc, bl=bl, H=H: e.matmul(
                                    P[:], H[:, fc, bl * 128:(bl + 1) * 128], W[:, fc, :], start=(fc == 0), stop=(fc == 7)),
                                    reads=[BH, BW], writes=[BP], inc=(fc == 7))
                            S.op("act", lambda e, P=P, bl=bl, h=h: e.copy(vts[:, bl, h * 512:(h + 1) * 512], P[:]), reads=[BP], writes=[Bvts])
                def st1(h):
                    nonlocal sci
                    ST = sTb[h]; BST = BsTb[h]
                    for jb in range(3):
                        P = psc[sci % 2]; BP = Bpsc[sci % 2]
                        sci += 1
                        for dc in range(2):
                            S.op("pe", lambda e, P=P, h=h, dc=dc, jb=jb: e.matmul(
                                P[:, 0:TW], kT[:, h * 2 + dc, jb * 128:(jb + 1) * 128], qT[:, h * 2 + dc, :], start=(dc == 0), stop=(dc == 1)),
                                reads=[BkT, BqT], writes=[BP], inc=(dc == 1))
                        S.op("dve", lambda e, P=P, ST=ST, jb=jb, h=h: e.tensor_tensor(
                            ST[:, jb, :], P[:, 0:TW], dm_sb[:, (h * 3 + jb) * TW:(h * 3 + jb + 1) * TW], ALU.mult),
                            reads=[BP, Bc], writes=[BST])

                def st2(h):
                    nonlocal oi
                    ST = sTb[h]; BST = BsTb[h]
                    O32 = o32[h % 2]; OSQ = osq[h % 2]
                    for ec in range(4):
                        P = po[oi % 2]; BP = Bpo[oi % 2]
                        oi += 1
                        for jb in range(3):
                            S.op("pe", lambda e, P=P, h=h, ec=ec, jb=jb, ST=ST: e.matmul(
                                P[:, 0:TW], vts[:, jb, h * 512 + ec * 128:h * 512 + (ec + 1) * 128], ST[:, jb, :], start=(jb == 0), stop=False),
                                reads=[Bvts, BST], writes=[BP], inc=False)
                        for dc in range(2):
                            S.op("pe", lambda e, P=P, h=h, ec=ec, dc=dc: e.matmul(
                                P[:, 0:TW], Sb[:, h * 2 + dc, ec * 128:(ec + 1) * 128], qdT[:, h * 2 + dc, :], start=False, stop=(dc == 1)),
                                reads=[BSb, BqdT], writes=[BP], inc=(dc == 1))
                        S.op("act", lambda e, P=P, ec=ec, O32=O32: e.copy(O32[:, ec, :], P[:, 0:TW]), reads=[BP], writes=[Bo32[h % 2]])
                        S.op("act", lambda e, P=P, ec=ec, OSQ=OSQ: e.activation(OSQ[:, ec, :], P[:, 0:TW], AF.Square), reads=[BP], writes=[Bosq[h % 2]])

                def st3(h, Y=Y, BY=BY):
                    O32 = o32[h % 2]; OSQ = osq[h % 2]; R = rstd[h % 2]; BR = Brstd[h % 2]
                    for ec in range(4):
                        S.op("pe", lambda e, ec=ec, OSQ=OSQ: e.matmul(pss[:, 0:TW], ones_f[:], OSQ[:, ec, :], start=(ec == 0), stop=(ec == 3)),
                             reads=[Bc, Bosq[h % 2]], writes=[Bpss], inc=(ec == 3))
                    S.op("act", lambda e, R=R: e.activation(R[:], pss[:, 0:TW], AF.Ln, bias=RMS_EPS, scale=1.0 / 512.0), reads=[Bpss], writes=[BR])
                    S.op("act", lambda e, R=R: e.activation(R[:], R[:], AF.Exp, scale=-0.5), reads=[BR], writes=[BR])
                    for ec in range(4):
                        S.op("dve", lambda e, ec=ec, O32=O32, R=R: e.tensor_tensor(tmpo[:], O32[:, ec, :], R[:], ALU.mult), reads=[Bo32[h % 2], BR], writes=[Btmpo])
                        S.op("dve", lambda e, ec=ec, h=h, Y=Y: e.tensor_tensor(Y[:, h * 4 + ec, :], tmpo[:], sg[:, h * 4 + ec, :], ALU.mult),
                             reads=[Btmpo, Bsg], writes=[BY])

                if L3_SUB >= 2:
                    for stg_fn, hh in ((st1, 0), (st1, 1), (st2, 0), (st1, 2), (st2, 1), (st3, 0), (st1, 3), (st2, 2), (st3, 1),
                                       (st2, 3), (st3, 2), (st3, 3)):
                        stg_fn(hh)
                if L3_SUB >= 3:
                    emit_state_update(S, kts, Bkts, vts, Bvts, Sst, BS, Sb, BSb, pk, Bpk, ctr)
                if L3_SUB >= 2:
                    S.dma("sp", yv[:, :, c0:c0 + TW], Y[:], reads=[BY], writes=[ByT])
            S.barrier()
            S.flush()

def l3_consts():
    p = np.arange(128, dtype=np.float64)[:, None]
    i = np.arange(TW, dtype=np.float64)[None, :]
    dm = np.zeros((128, 4, 3, TW), np.float32)
    qd = np.zeros((128, 4, TW), np.float32)
    for h in range(4):
        g = RET_GAMMA[h]
        for jb in range(3):
            j = jb * 128 + p
            same = (j // 64) == (i // 64)
            before = (j // 64) < (i // 64)
            val = np.where(same, g ** np.abs(i - j), np.where(before, g ** np.maximum(i - j, 0), 0.0)) / 16.0
            dm[:, h, jb, :] = val
        qd[:, h, :] = g ** (i + 1.0)
    return dm.reshape(128, 12 * TW), qd.reshape(128, 4 * TW)


PAIRS = [[0, 1], [2, 3], [4, 5], [6, 7]]


def build_fused():
    nc = bass.Bass("TRN2", target_bir_lowering=False)

    def din(name, shape):
        return nc.dram_tensor(name, list(shape), F32, kind="ExternalInput").ap()

    T1 = {"xT": din("xT", [D, LR]), "wq": din("wq", [D, 512]), "wk": din("wk", [D, 512]), "wv": din("wv", [D, 512]),
          "wf": din("wf", [D, 4]), "fbias": din("fbias", [4, 1]), "lamv": din("lamv", [128, 256]), "gsub": din("gsub", [128, 1]),
          "dbias": din("dbias", [128, 2 * 66]), "Fd": din("Fd", [128, 2 * 3 * TW]), "Ff": din("Ff", [128, 3 * TW]),
          "qaug": din("qaug", [2, 3, LR]), "odt": BF16}
    hres = din("hres", [D, HALF])
    flags = din("flags", [128, 2])
    w_out0 = din("w_out0", [D, D])
    w1_0 = din("w1_0", [D, 4 * D])
    w2_0 = din("w2_0", [4 * D, D])
    lnp0 = din("lnp0", [128, 32])
    w_in1 = din("w_in1", [D, 6144])
    w_out1 = din("w_out1", [2048, D])
    w1_1 = din("w1_1", [D, 4 * D])
    w2_1 = din("w2_1", [4 * D, D])
    lnp1 = din("lnp1", [128, 32])
    kdec = din("kdec", [128, 12])
    tmask = din("tmask", [128, TW])
    dmask = din("dmask", [128, 12 * TW])
    qdec = din("qdec", [128, 4 * TW])
    outT = nc.dram_tensor("outT", [D, HALF], F32, kind="ExternalOutput").ap()
    wo0_b = nc.dram_tensor("wo0_b", [D, D], BF16).ap()
    w10_b = nc.dram_tensor("w10_b", [D, 4 * D], BF16).ap()
    w20_b = nc.dram_tensor("w20_b", [4 * D, D], BF16).ap()
    wi_b = nc.dram_tensor("wi_b", [D, 6144], BF16).ap()
    wo1_b = nc.dram_tensor("wo1_b", [2048, D], BF16).ap()
    w11_b = nc.dram_tensor("w11_b", [D, 4 * D], BF16).ap()
    w21_b = nc.dram_tensor("w21_b", [4 * D, D], BF16).ap()
    NCH = NT // 2
    oT_c = [nc.dram_tensor("oT_c%d" % k, [512, 2 * TW], BF16) for k in range(NCH)]
    G_c = [nc.dram_tensor("G_c%d" % k, [1024, 2 * TW], BF16) for k in range(NCH)]
    h2T_i = nc.dram_tensor("h2T_i", [D, HALF], F32).ap()
    Se_c = [nc.dram_tensor("Se_c%d" % k, [256, 512], F32) for k in range(4)]
    Sg_c = [nc.dram_tensor("Sg_c%d" % k, [512, 512], F32) for k in range(4)]
    yT_s = nc.dram_tensor("yT_s", [2048, HALF], BF16).ap()
    T1["oT_tile"] = lambda r0, r1, t: oT_c[t // 2].ap()[r0:r1, (t % 2) * TW:(t % 2 + 1) * TW]

    def g_tile(k, t):
        gt = k * NTH + t
        return G_c[gt // 2].ap().rearrange("(c p) t -> p c t", p=128)[:, :, (gt % 2) * TW:(gt % 2 + 1) * TW]
    with contextlib.ExitStack() as outer:
        S = Sched(nc, outer)
        Bwb = Buf(); Bo = Buf(); BG = Buf(); Bh2 = Buf(); BSe = Buf(); BSg = Buf(); ByT = Buf(); Bout = Buf()
        def hookA(C):
            stg = [C.sb([128, 2048], BF16) for _ in range(2)]
            Bs = [Buf(), Buf()]
            for (s_, d_, r_, c_) in ():
                yield from iter_convert(S, stg, Bs, s_, d_, Bwb, r_, c_)

        def hookB(C):
            stg = [C.sb([128, 2048], BF16) for _ in range(2)]
            Bs = [Buf(), Buf()]
            for (s_, d_, r_, c_) in ((w_out0, wo0_b, D, D), (w1_0, w10_b, D, 4 * D), (w2_0, w20_b, 4 * D, D),
                                     (w_in1, wi_b, D, 6144), (w_out1, wo1_b, 2048, D), (w1_1, w11_b, D, 4 * D), (w2_1, w21_b, 4 * D, D)):
                yield from iter_convert(S, stg, Bs, s_, d_, Bwb, r_, c_)
        T1["hookA"] = hookA
        T1["hookB"] = hookB
        emit_l1(nc, S, T1, Bo)
        for k in range(NCH):
            S.op("pool", lambda e, k=k: e.collective_compute("AllGather", ALU.bypass, replica_groups=PAIRS,
                                                             ins=[oT_c[k].ap().opt()], outs=[G_c[k].ap().opt()]), reads=[Bo], writes=[BG])
        with contextlib.ExitStack() as st:
            emit_post(nc, S, st, NTH, 8, g_tile, "blend", BG, hres, wo0_b, w10_b, w20_b, Bwb, lnp0, h2T_i, Bh2, flags=flags)
            S.barrier()
            S.flush()
        Tp = {"h2T": h2T_i, "wk": wi_b[:, 1024:2048], "wv": wi_b[:, 2048:4096], "kdec": kdec, "tmask": tmask,
              "S_end": [Se_c[k].ap().rearrange("(p i) f -> p i f", i=8) for k in range(4)], "wqueue": "sp", "Bwb": Bwb}
        emit_prepass(nc, S, Tp, Bh2, BSe)
        for k in range(4):
            S.op("pool", lambda e, k=k: e.collective_compute("AllGather", ALU.bypass, replica_groups=PAIRS,
                                                             ins=[Se_c[k].ap().opt()], outs=[Sg_c[k].ap().opt()]), reads=[BSe], writes=[BSg])
        Tr = {"h2T": h2T_i, "wi_b": wi_b, "kdec": kdec, "tmask": tmask, "dmask": dmask, "qdec": qdec, "yT_s": yT_s,
              "S_src": [Sg_c[k].ap()[0:256, :].rearrange("(p i) f -> p i f", i=8) for k in range(4)], "flags": flags}
        with contextlib.ExitStack() as st:
            emit_retention(nc, S, st, Tr, Bwb, Bh2, BSg, ByT)
        with contextlib.ExitStack() as st:
            emit_post(nc, S, st, NTH, 16, yT_s, "bf16", ByT, h2T_i, wo1_b, w11_b, w21_b, Bwb, lnp1, outT, Bout)
            S.barrier()
            S.flush()
    return nc


def _token_mask(u):
    tm = np.ones((128, TW), np.float32)
    if u == 0:
        tm[:, 0:FPAD] = 0.0
    return tm


def kernel(x, meta_tokens, even_w_in, even_f_bias, diff_lambda, diff_subln_g, even_w_out,
           ret_w_in, ret_w_out, ln_g, ln_b, ffn_w1, ffn_w2):
    x = np.asarray(x, np.float32)
    f32 = lambda a: np.ascontiguousarray(np.asarray(a, np.float32))
    meta = f32(meta_tokens)
    B = x.shape[0]
    cores = list(range(8))
    kdec = ret_consts()
    dm, qd = l3_consts()
    wo = f32(even_w_out[0])
    wo_perm = np.ascontiguousarray(np.concatenate([wo[0:256], wo[512:768], wo[256:512], wo[768:1024]], axis=0))
    shared = {"w_out0": wo_perm, "w1_0": f32(ffn_w1[0]), "w2_0": f32(ffn_w2[0]), "lnp0": lnp_table(f32(ln_g), f32(ln_b), 0),
              "w_in1": f32(ret_w_in[0]), "w_out1": f32(ret_w_out[0]), "w1_1": f32(ffn_w1[1]), "w2_1": f32(ffn_w2[1]),
              "lnp1": lnp_table(f32(ln_g), f32(ln_b), 1), "kdec": kdec, "dmask": dm, "qdec": qd}
    in_maps = []
    for c in cores:
        b, u = c // 2, c % 2
        hp = np.zeros((LR, D), np.float32)
        hp[FPAD:FPAD + NMETA] = meta
        hp[FPAD + NMETA:FPAD + NMETA + SEQ] = x[b]
        xT = np.ascontiguousarray(hp.T)
        m = l1_inputs(xT, f32(even_w_in[0]), f32(even_f_bias[0]), f32(diff_lambda[0]), f32(diff_subln_g[0]), u)
        m["hres"] = np.ascontiguousarray(xT[:, u * HALF:(u + 1) * HALF])
        fl = np.zeros((128, 2), np.float32)
        fl[:, u] = 1.0
        m["flags"] = fl
        m["tmask"] = _token_mask(u)
        m.update(shared)
        in_maps.append(m)
    res = run_bass_kernel_spmd(build_fused(), in_maps, core_ids=cores).results
    out = np.empty((B, SEQ, D), np.float32)
    for b in range(B):
        hT = np.concatenate([res[2 * b]["outT"], res[2 * b + 1]["outT"]], axis=1)
        out[b] = hT[:, FPAD + NMETA:FPAD + NMETA + SEQ].T
    return out
```

```python
import contextlib
import math
import numpy as np
import concourse.bass as bass
import concourse.mybir as mybir
from concourse.bass_utils import run_bass_kernel_spmd

F32 = mybir.dt.float32
BF16 = mybir.dt.bfloat16
AF = mybir.ActivationFunctionType
ALU = mybir.AluOpType

D = 1024
SEQ = 8192
NMETA = 16
FPAD = 48
LR = 8448
TW = 384
NT = LR // TW
NB = LR // 128
HALF = LR // 2
NTH = HALF // TW
ALPHA = 4 ** 0.25
LN_EPS = 1e-5
RMS_EPS = 1e-6
LAM_INIT0 = 0.8 - 0.6 * math.exp(-0.3 * 0)
NEG = -30000.0
REAL_END = FPAD + NMETA + SEQ

ENGS = ("pe", "act", "dve", "pool", "sp")


class Buf:
    __slots__ = ("name", "w", "r", "ex")

    def __init__(self, name="", ex=False):
        self.name = name
        self.w = None
        self.r = []
        self.ex = ex


def PB():
    return Buf(ex=True)


class Sched:
    def __init__(self, nc, stack, n_dma_sems=16):
        self.nc = nc
        self.streams = {e: [] for e in ENGS}
        self.cnt = {e: 0 for e in ENGS}
        self.seen = {e: {} for e in ENGS}
        self.n_dma = n_dma_sems
        self.dma_k = 0
        self.sems = {e: stack.enter_context(nc.semaphore("s_" + e)) for e in ENGS}
        self.dsems = [stack.enter_context(nc.semaphore("d_%d" % i)) for i in range(n_dma_sems)]
        self.dlast = [0] * n_dma_sems

    def _need(self, eng, tok, waits):
        if tok is None:
            return
        kind, key, val = tok
        if kind == "e" and key == eng and eng in ("pe", "sp"):
            return
        k = (kind, key)
        if self.seen[eng].get(k, 0) >= val:
            return
        if val > waits.get(k, 0):
            waits[k] = val

    def _emit_waits(self, eng, waits):
        for k, val in waits.items():
            self.seen[eng][k] = val
            self.streams[eng].append(("wait", k, val))

    def _deps(self, eng, reads, writes):
        waits = {}
        for b in reads:
            self._need(eng, b.w, waits)
        for b in writes:
            self._need(eng, b.w, waits)
            for t in b.r:
                self._need(eng, t, waits)
        return waits

    def _mark(self, tok, reads, writes):
        for b in reads:
            b.r.append(tok)
            if len(b.r) > 24:
                b.r = b.r[-24:]
        for b in writes:
            b.w = tok
            b.r = []

    def op(self, eng, fn, reads=(), writes=(), inc=True):
        if any(b.ex for b in reads):
            writes = list(writes) + [b for b in reads if b.ex]
            reads = [b for b in reads if not b.ex]
        waits = self._deps(eng, reads, writes)
        self._emit_waits(eng, waits)
        if inc:
            self.cnt[eng] += 1
            tok = ("e", eng, self.cnt[eng])
        else:
            tok = ("e", eng, self.cnt[eng] + 1)
        self.streams[eng].append(("op", fn, inc))
        self._mark(tok, reads, writes)
        return tok

    def dma(self, q, out_ap, in_ap, reads=(), writes=(), **kw):
        waits = self._deps(q, reads, writes)
        s = self.dma_k % self.n_dma
        v = self.dlast[s] + 16
        self.dma_k += 1
        if v > 16:
            self._need(q, ("d", s, v - 16), waits)
        self._emit_waits(q, waits)
        self.dlast[s] = v
        tok = ("d", s, v)
        self.streams[q].append(("dma", out_ap, in_ap, s, kw))
        self._mark(tok, reads, writes)
        return tok

    def barrier(self):
        toks = [("e", e, self.cnt[e]) for e in ENGS if self.cnt[e] > 0]
        toks += [("d", s, self.dlast[s]) for s in range(self.n_dma) if self.dlast[s] > 0]
        for e in ENGS:
            waits = {}
            for t in toks:
                if t[0] == "e" and t[1] == e:
                    continue
                self._need(e, t, waits)
            self._emit_waits(e, waits)

    def flush(self):
        nc = self.nc
        with nc.Block() as block:
            def run(e, engobj):
                for item in self.streams[e]:
                    if item[0] == "wait":
                        (kind, key), val = item[1], item[2]
                        sem = self.sems[key] if kind == "e" else self.dsems[key]
                        engobj.wait_ge(sem, val)
                    elif item[0] == "op":
                        ins = item[1](engobj)
                        if item[2]:
                            ins.then_inc(self.sems[e], 1)
                    else:
                        _, o, i, s, kw = item
                        engobj.dma_start(out=o, in_=i, **kw).then_inc(self.dsems[s], 16)

            @block.tensor
            def _(eng):
                run("pe", eng)

            @block.scalar
            def _(eng):
                run("act", eng)

            @block.vector
            def _(eng):
                run("dve", eng)

            @block.gpsimd
            def _(eng):
                run("pool", eng)

            @block.sync
            def _(eng):
                run("sp", eng)
        self.streams = {e: [] for e in ENGS}


class Ctx:
    K = [0]

    def __init__(self, nc, stack):
        self.nc = nc
        self.st = stack

    def sb(self, shape, dt, name=None):
        Ctx.K[0] += 1
        return self.st.enter_context(self.nc.sbuf_tensor(name or ("t%d" % Ctx.K[0]), list(shape), dt))

    def ps(self, shape, dt=F32, name=None):
        Ctx.K[0] += 1
        return self.st.enter_context(self.nc.psum_tensor(name or ("p%d" % Ctx.K[0]), list(shape), dt))


SCALE = 0.125
DEBUG_A = False


def emit_l1(nc, S, T, Bo):
    xT = T["xT"]; wq = T["wq"]; wk = T["wk"]; wv = T["wv"]; wf = T["wf"]; fbias = T["fbias"]
    lamv = T["lamv"]; gsub = T["gsub"]; dbias = T["dbias"]; Fd = T["Fd"]; Ff = T["Ff"]; qaug = T["qaug"]
    ODT = T["odt"]
    QT_s = nc.dram_tensor("QT_s", [8, 64, LR], BF16).ap()
    KT_s = nc.dram_tensor("KT_s", [8, 64, LR], BF16).ap()
    V_s = nc.dram_tensor("V_s", [LR, 512], BF16).ap()
    AQ_s = nc.dram_tensor("AQ_s", [4, 3, LR], BF16).ap()
    AQd_s = nc.dram_tensor("AQd_s", [6, LR], BF16).ap()
    out_toks = []
    if True:
        BQT = [Buf() for _ in range(8)]
        BKT = [Buf() for _ in range(8)]
        BV = Buf()
        BAQ = Buf()
        with contextlib.ExitStack() as st:
            C = Ctx(nc, st)
            wq_sb = C.sb([128, 8, 512], BF16)
            wk_sb = C.sb([128, 8, 512], BF16)
            wv_sb = C.sb([128, 8, 512], BF16)
            wf_sb = C.sb([128, 8, 4], BF16)
            fb_sb = C.sb([4, 1], F32)
            nfb_sb = C.sb([4, 1], F32)
            ident = C.sb([128, 128], F32)
            xt = [C.sb([128, 8, TW], BF16) for _ in range(2)]
            stq = [C.sb([128, TW], BF16) for _ in range(4)]
            stv = [C.sb([128, 512], BF16) for _ in range(2)]
            lf = C.sb([4, LR], F32)
            cs = C.sb([4, LR], F32)
            ones4 = C.sb([4, LR // 4], F32)
            e_t = C.sb([4, TW], F32)
            hi = C.sb([4, LR], BF16)
            mid = C.sb([4, LR], BF16)
            lo = C.sb([4, LR], BF16)
            pq = [C.ps([128, 512]) for _ in range(4)]
            pv = [C.ps([128, 512]) for _ in range(2)]
            pf = C.ps([128, 512])
            Bw = Buf(); Bxt = [Buf(), Buf()]; Bstq = [Buf() for _ in range(4)]; Bstv = [Buf(), Buf()]
            Bpq = [PB() for _ in range(4)]; Bpv = [PB(), PB()]; Bpf = PB(); Blf = Buf(); Bcs = Buf()
            Bet = Buf(); Bfb = Buf(); Bo4 = Buf(); Bhi = Buf(); Bmid = Buf(); Blo = Buf()

            wr = "(c p) o -> p c o"
            S.dma("pool", wq_sb[:], wq.rearrange(wr, p=128), writes=[Bw])
            for fc in range(8):
                S.dma("pool", wk_sb[:, fc, :], wk[fc * 128:(fc + 1) * 128, :], writes=[Bw])
                S.dma("pool", wv_sb[:, fc, :], wv[fc * 128:(fc + 1) * 128, :], writes=[Bw])
            S.dma("pool", wf_sb[:], wf.rearrange(wr, p=128), writes=[Bw])
            S.dma("sp", fb_sb[:], fbias, writes=[Bfb])
            S.op("dve", lambda e: e.tensor_scalar(nfb_sb[:], fb_sb[:], -1.0, None, ALU.mult), reads=[Bfb], writes=[Bfb])
            S.op("dve", lambda e: e.memset(ones4[:], 1.0), writes=[Bo4])
            xTr = xT.rearrange("(c p) t -> p c t", p=128)
            conv_it = T["hookA"](C) if T.get("hookA") else None
            qi = 0
            vi = 0
            for t in range(NT):
                c0 = t * TW
                X = xt[t % 2]; BX = Bxt[t % 2]
                S.dma("pool", X[:], xTr[:, :, c0:c0 + TW], writes=[BX])
                for which, (w_sb, dst, BD) in enumerate(((wq_sb, QT_s, BQT), (wk_sb, KT_s, BKT))):
                    for g in range(4):
                        P = pq[qi % 4]; BP = Bpq[qi % 4]; ST = stq[qi % 4]; BS = Bstq[qi % 4]
                        qi += 1
                        for c in range(8):
                            S.op("pe", lambda e, P=P, w_sb=w_sb, c=c, g=g, X=X: e.matmul(
                                P[:, 0:TW], w_sb[:, c, g * 128:(g + 1) * 128], X[:, c, :], start=(c == 0), stop=(c == 7)),
                                reads=[Bw, BX], writes=[BP], inc=(c == 7))
                        eng = "act" if (g % 2 == 0) else "dve"
                        if eng == "act":
                            S.op("act", lambda e, ST=ST, P=P: e.copy(ST[:], P[:, 0:TW]), reads=[BP], writes=[BS])
                        else:
                            S.op("dve", lambda e, ST=ST, P=P: e.tensor_copy(ST[:], P[:, 0:TW]), reads=[BP], writes=[BS])
                        S.dma("sp", dst[2 * g:2 * g + 2].rearrange("u r t -> (u r) t")[:, c0:c0 + TW], ST[:],
                              reads=[BS], writes=[BD[2 * g], BD[2 * g + 1]])
                for bl in range(3):
                    P = pv[vi % 2]; BP = Bpv[vi % 2]; ST = stv[vi % 2]; BS = Bstv[vi % 2]
                    vi += 1
                    for c in range(8):
                        S.op("pe", lambda e, P=P, c=c, bl=bl, X=X: e.matmul(
                            P[:], X[:, c, bl * 128:(bl + 1) * 128], wv_sb[:, c, :], start=(c == 0), stop=(c == 7)),
                            reads=[Bw, BX], writes=[BP], inc=(c == 7))
                    S.op("dve", lambda e, ST=ST, P=P: e.tensor_copy(ST[:], P[:]), reads=[BP], writes=[BS])
                    r0 = c0 + bl * 128
                    S.dma("sp", V_s[r0:r0 + 128, :], ST[:], reads=[BS], writes=[BV])
                for c in range(8):
                    S.op("pe", lambda e, c=c, X=X: e.matmul(pf[0:4, 0:TW], wf_sb[:, c, :], X[:, c, :], start=(c == 0), stop=(c == 7)),
                         reads=[Bw, BX], writes=[Bpf], inc=(c == 7))
                S.op("act", lambda e: e.activation(e_t[:], pf[0:4, 0:TW], AF.Exp, bias=nfb_sb[:, 0:1], scale=-1.0),
                     reads=[Bpf, Bfb], writes=[Bet])
                S.op("act", lambda e, c0=c0: e.activation(lf[:, c0:c0 + TW], e_t[:], AF.Ln, bias=1.0, scale=1.0),
                     reads=[Bet], writes=[Blf])
                if conv_it is not None:
                    for _ in range(3):
                        next(conv_it, None)
            S.op("dve", lambda e: e.memset(lf[:, 0:FPAD], 0.0), reads=[Blf], writes=[Blf])
            S.op("dve", lambda e: e.memset(lf[:, REAL_END:LR], 0.0), reads=[Blf], writes=[Blf])
            CH = LR // 4
            for k in range(4):
                a = k * CH
                if k == 0:
                    S.op("dve", lambda e, a=a: e.tensor_tensor_scan(cs[:, a:a + CH], ones4[:], lf[:, a:a + CH], 0.0, ALU.mult, ALU.add),
                         reads=[Blf, Bo4], writes=[Bcs])
                else:
                    S.op("dve", lambda e, a=a: e.tensor_tensor_scan(cs[:, a:a + CH], ones4[:], lf[:, a:a + CH], cs[:, a - 1:a], ALU.mult, ALU.add),
                         reads=[Blf, Bo4, Bcs], writes=[Bcs])
            CS_s = nc.dram_tensor("CS_s", [4, LR], F32).ap()
            BCS = Buf()
            S.dma("sp", CS_s, cs[:], reads=[Bcs], writes=[BCS])
            for t in range(NT):
                c0 = t * TW
                S.op("dve", lambda e, c0=c0: e.tensor_scalar(lf[:, c0:c0 + TW], cs[:, c0:c0 + TW], cs[:, c0:c0 + 1], -8.0, ALU.subtract, ALU.mult),
                     reads=[Bcs, Blf], writes=[Blf])
            S.op("dve", lambda e: e.tensor_copy(hi[:], lf[:]), reads=[Blf], writes=[Bhi])
            S.op("dve", lambda e: e.tensor_tensor(lf[:], lf[:], hi[:], ALU.subtract), reads=[Blf, Bhi], writes=[Blf])
            S.op("dve", lambda e: e.tensor_copy(mid[:], lf[:]), reads=[Blf], writes=[Bmid])
            S.op("dve", lambda e: e.tensor_tensor(lf[:], lf[:], mid[:], ALU.subtract), reads=[Blf, Bmid], writes=[Blf])
            S.op("dve", lambda e: e.tensor_copy(lo[:], lf[:]), reads=[Blf], writes=[Blo])
            qa_sb = C.sb([6, LR], BF16)
            Bqa = Buf()
            S.dma("pool", qa_sb[:], qaug.rearrange("h r t -> (h r) t"), writes=[Bqa])
            S.dma("sp", AQd_s, qa_sb[:], reads=[Bqa], writes=[BAQ])
            S.dma("sp", AQ_s[:, 0, :], hi[:], reads=[Bhi], writes=[BAQ])
            S.dma("sp", AQ_s[:, 1, :], mid[:], reads=[Bmid], writes=[BAQ])
            S.dma("sp", AQ_s[:, 2, :], lo[:], reads=[Blo], writes=[BAQ])
            if conv_it is not None:
                for _ in conv_it:
                    pass
            S.barrier()
            S.flush()

        with contextlib.ExitStack() as st:
            C = Ctx(nc, st)
            kt = [C.sb([128, LR], BF16) for _ in range(2)]
            qt = [C.sb([128, LR], BF16) for _ in range(2)]
            vt = [C.sb([128, NB, 128], BF16) for _ in range(2)]
            Fd_sb = C.sb([128, 2 * 3 * TW], F32)
            Ff_sb = C.sb([128, 3 * TW], F32)
            dbias_sb = C.sb([128, 2 * 66], F32)
            csk = C.sb([128, 4, NB], F32)
            csT0 = C.sb([128, 4, NT], F32)
            bkt = [C.sb([128, NB], F32) for _ in range(2)]
            ones_bf = C.sb([128, 128], BF16)
            ones_b0 = C.sb([128, 128], BF16)
            ones_f = C.sb([128, 128], F32)
            lam_sb = C.sb([128, 256], F32)
            lprod = C.sb([128, 128], F32)
            lsum = C.sb([128, 2], F32)
            lexp = C.sb([128, 2], F32)
            neglam = C.sb([128, 1], F32)
            gcol = C.sb([128, 1], F32)
            pT = [C.sb([128, TW], BF16) for _ in range(4)]
            tmp = [C.sb([128, TW], F32) for _ in range(2)]
            rl = C.sb([128, TW], F32)
            lsb = [C.sb([128, TW], F32) for _ in range(2)]
            l0b = [C.sb([128, TW], F32) for _ in range(2)]
            Blsb = [Buf(), Buf()]; Bl0b = [Buf(), Buf()]
            a0 = C.sb([128, TW], F32)
            a1 = C.sb([128, TW], F32)
            od = C.sb([128, TW], F32)
            sq = C.sb([128, TW], F32)
            rstd = C.sb([128, TW], F32)
            ofin = [C.sb([128, TW], ODT) for _ in range(2)]
            NS = 3
            NPT = 4
            ps_s = [C.ps([128, 512]) for _ in range(NS)]
            ps_o = [C.ps([128, 512]) for _ in range(2)]
            ps_l = [C.ps([128, 512]) for _ in range(2)]
            ps_x = C.ps([128, 512])
            Bkt = [Buf(), Buf()]; Bqt = [Buf(), Buf()]; Bvt = [Buf(), Buf()]
            Bc = Buf(); Bcsk = Buf(); BcsT0 = Buf(); Bbkt = [Buf(), Buf()]
            Bps_s = [PB(), PB(), PB()]; Bps_o = [PB(), PB()]; Bps_l = [PB(), PB()]; Bps_x = PB()
            BpT = [Buf() for _ in range(4)]; Btmp = [Buf(), Buf()]
            Brl = Buf(); Ba0 = Buf(); Ba1 = Buf(); Bod = Buf(); Bsq = Buf(); Brstd = Buf(); Bofin = [Buf(), Buf()]
            Blam = Buf()

            S.dma("sp", Fd_sb[:], Fd, writes=[Bc])
            S.dma("sp", Ff_sb[:], Ff, writes=[Bc])
            S.dma("sp", dbias_sb[:], dbias, writes=[Bc])
            S.dma("sp", lam_sb[:], lamv, writes=[Blam])
            S.dma("sp", gcol[:], gsub, writes=[Blam])
            for h in range(4):
                S.dma("sp", csk[:, h, :], CS_s[h].rearrange("(b p) -> p b", p=128), reads=[BCS], writes=[Bcsk], allow_slow_non_contiguous=True)
            for h in range(4):
                src = bass.AP(CS_s.tensor, CS_s.offset + h * LR, [[0, 128], [TW, NT]])
                S.dma("sp", csT0[:, h, :], src, reads=[BCS], writes=[BcsT0], allow_slow_non_contiguous=True)
            S.op("pool", lambda e: e.memset(ones_bf[:], 1.0), writes=[Bc])
            S.op("pool", lambda e: e.memset(ones_b0[:], 1.0), writes=[Bc])
            S.op("pool", lambda e: e.memset(ones_b0[0:FPAD, :], 0.0), reads=[Bc], writes=[Bc])
            S.op("pool", lambda e: e.memset(ones_f[:], 1.0), writes=[Bc])
            for b in range(2):
                S.op("pool", lambda e, b=b: e.memset(kt[b][64:67, :], 1.0), writes=[Bkt[b]])
            S.op("dve", lambda e: e.tensor_tensor(lprod[:, 0:64], lam_sb[:, 0:64], lam_sb[:, 64:128], ALU.mult), reads=[Blam], writes=[Blam])
            S.op("dve", lambda e: e.tensor_tensor(lprod[:, 64:128], lam_sb[:, 128:192], lam_sb[:, 192:256], ALU.mult), reads=[Blam], writes=[Blam])
            S.op("dve", lambda e: e.reduce_sum(lsum[:, 0:1], lprod[:, 0:64], mybir.AxisListType.X), reads=[Blam], writes=[Blam])
            S.op("dve", lambda e: e.reduce_sum(lsum[:, 1:2], lprod[:, 64:128], mybir.AxisListType.X), reads=[Blam], writes=[Blam])
            S.op("act", lambda e: e.activation(lexp[:], lsum[:], AF.Exp), reads=[Blam], writes=[Blam])
            S.op("dve", lambda e: e.scalar_tensor_tensor(neglam[:], lexp[:, 1:2], -LAM_INIT0, lexp[:, 0:1], ALU.add, ALU.subtract),
                 reads=[Blam], writes=[Blam])
            S.op("dve", lambda e: e.tensor_scalar(gcol[:], gcol[:], 1.0 - LAM_INIT0, None, ALU.mult), reads=[Blam], writes=[Blam])

            conv_itB = T["hookB"](C) if T.get("hookB") else None
            si = 0
            pi = 0
            ti = 0
            oi = 0
            fi = 0
            for g in range(4):
                isdiff = g < 2
                units = (2 * g, 2 * g + 1)
                dv = 128 if isdiff else 64
                for ui, u in enumerate(units):
                    S.dma("sp", kt[ui][0:64, :], KT_s[u], reads=[BKT[u]], writes=[Bkt[ui]])
                    S.dma("sp", qt[ui][0:64, :], QT_s[u], reads=[BQT[u]], writes=[Bqt[ui]])
                    if isdiff:
                        S.dma("sp", qt[ui][64:67, :], AQd_s[3 * g:3 * g + 3, :], reads=[BAQ], writes=[Bqt[ui]])
                    else:
                        S.dma("sp", qt[ui][64:67, :], AQ_s[u - 4], reads=[BAQ], writes=[Bqt[ui]])
                if isdiff:
                    S.dma("sp", vt[0][:], V_s[:, g * 128:(g + 1) * 128].rearrange("(b p) c -> p b c", p=128),
                          reads=[BV], writes=[Bvt[0]])
                else:
                    for ui, u in enumerate(units):
                        hf = u - 4
                        S.dma("sp", vt[ui][:, :, 0:64], V_s[:, 256 + hf * 64:256 + (hf + 1) * 64].rearrange("(b p) c -> p b c", p=128),
                              reads=[BV], writes=[Bvt[ui]])
                        S.op("pool", lambda e, ui=ui: e.memset(vt[ui][:, :, 64:128], 1.0), writes=[Bvt[ui]])
                        S.op("pool", lambda e, ui=ui: e.memset(vt[ui][0:FPAD, 0, 64:128], 0.0), reads=[Bvt[ui]], writes=[Bvt[ui]])
                items = []
                for t in range(NT):
                    for ui, u in enumerate(units):
                        for j in range(3 * t + 3):
                            items.append((t, ui, u, j))
                state = {}
                deferred = []

                def stage_a(t, ui, u, j):
                    nonlocal si, pi, ti, oi, fi
                    c0 = t * TW
                    nblk = 3 * t + 3
                    K = kt[ui]; Q = qt[ui]; BK = Bkt[ui]; BQ = Bqt[ui]
                    hf = u - 4
                    if j == 0:
                        PO = ps_o[oi % 2]; BPO = Bps_o[oi % 2]; PL = ps_l[oi % 2]; BPL = Bps_l[oi % 2]
                        oi += 1
                        BKk = None; BBK = None
                        if not isdiff:
                            BKk = bkt[fi % 2]; BBK = Bbkt[fi % 2]
                            fi += 1
                            S.op("dve", lambda e, BKk=BKk, nblk=nblk, hf=hf, t=t: e.tensor_scalar(
                                BKk[:, 0:nblk], csk[:, hf, 0:nblk], csT0[:, hf, t:t + 1], None, ALU.subtract),
                                reads=[Bcsk, BcsT0], writes=[BBK])
                        state[(t, ui)] = (PO, BPO, PL, BPL, BKk, BBK)
                    PO, BPO, PL, BPL, BKk, BBK = state[(t, ui)]
                    jj = j - 3 * t
                    PS = ps_s[si % NS]; BPS = Bps_s[si % NS]
                    si += 1
                    S.op("pe", lambda e, PS=PS, K=K, Q=Q, j=j, c0=c0: e.matmul(
                        PS[:, 0:TW], K[0:67, j * 128:(j + 1) * 128], Q[0:67, c0:c0 + TW], start=True, stop=True),
                        reads=[BK, BQ], writes=[BPS])
                    PT = pT[pi % NPT]; BPT = BpT[pi % NPT]
                    pi += 1
                    if isdiff:
                        col = g * 66 + (jj + 63)
                        bcol = dbias_sb[:, col:col + 1]
                        rb = [Bc]
                    else:
                        bcol = BKk[:, j:j + 1]
                        rb = [BBK]
                    if jj < 0:
                        S.op("act", lambda e, PT=PT, PS=PS, bcol=bcol: e.activation(PT[:], PS[:, 0:TW], AF.Exp, bias=bcol, scale=SCALE),
                             reads=[BPS] + rb, writes=[BPT])
                    else:
                        TM = tmp[ti % 2]; BTM = Btmp[ti % 2]
                        ti += 1
                        if isdiff:
                            Fap = Fd_sb[:, (g * 3 + jj) * TW:(g * 3 + jj + 1) * TW]
                        else:
                            Fap = Ff_sb[:, jj * TW:(jj + 1) * TW]
                        S.op("dve", lambda e, TM=TM, PS=PS, Fap=Fap: e.scalar_tensor_tensor(
                            TM[:], PS[:, 0:TW], SCALE, Fap, ALU.mult, ALU.add), reads=[BPS, Bc], writes=[BTM])
                        S.op("act", lambda e, PT=PT, TM=TM, bcol=bcol: e.activation(PT[:], TM[:], AF.Exp, bias=bcol, scale=1.0),
                             reads=[BTM] + rb, writes=[BPT])
                    return PT, BPT

                def stage_b(idx, t, ui, u, j, PT, BPT):
                    c0 = t * TW
                    nblk = 3 * t + 3
                    hf = u - 4
                    if isdiff:
                        VT = vt[0]; BVT = Bvt[0]
                    else:
                        VT = vt[ui]; BVT = Bvt[ui]
                    PO, BPO, PL, BPL, BKk, BBK = state[(t, ui)]
                    if not isdiff:
                        S.op("pe", lambda e, PO=PO, VT=VT, PT=PT, j=j, nblk=nblk: e.matmul(
                            PO[:, 0:TW], VT[:, j, :], PT[:], start=(j == 0), stop=(j == nblk - 1)),
                            reads=[BVT, BPT], writes=[BPO], inc=True)
                        if j != nblk - 1:
                            return
                        LS = lsb[(2 * t + ui) % 2]; BLS = Blsb[(2 * t + ui) % 2]
                        L0 = l0b[(2 * t + ui) % 2]; BL0 = Bl0b[(2 * t + ui) % 2]
                        S.op("dve", lambda e, LS=LS, PO=PO: e.tensor_scalar(LS[64:128, :], PO[64:128, 0:TW], 1e-30, None, ALU.add),
                             reads=[BPO], writes=[BLS])
                        S.dma("sp", L0[0:64, :], LS[64:128, :], reads=[BLS], writes=[BL0])
                        S.op("dve", lambda e, L0=L0: e.reciprocal(rl[0:64, :], L0[0:64, :]), reads=[BL0], writes=[Brl])
                        OF = ofin[(2 * t + ui) % 2]; BOF = Bofin[(2 * t + ui) % 2]
                        S.op("dve", lambda e, OF=OF, PO=PO: e.tensor_tensor(OF[0:64, :], PO[0:64, 0:TW], rl[0:64, :], ALU.mult),
                             reads=[BPO, Brl], writes=[BOF])
                        r0 = 256 + hf * 64
                        out_toks.append(S.dma("sp", T["oT_tile"](r0, r0 + 64, t), OF[0:64, :], reads=[BOF], writes=[Bo]))
                        return
                    S.op("pe", lambda e, PO=PO, VT=VT, PT=PT, j=j, nblk=nblk, dv=dv: e.matmul(
                        PO[0:dv, 0:TW], VT[:, j, 0:dv], PT[:], start=(j == 0), stop=(j == nblk - 1)),
                        reads=[BVT, BPT], writes=[BPO], inc=False)
                    on = ones_b0 if j == 0 else ones_bf
                    S.op("pe", lambda e, PL=PL, on=on, PT=PT, j=j, nblk=nblk, dv=dv: e.matmul(
                        PL[0:dv, 0:TW], on[:, 0:dv], PT[:], start=(j == 0), stop=(j == nblk - 1)),
                        reads=[Bc, BPT], writes=[BPL], inc=True)
                    if j != nblk - 1:
                        return
                    S.op("dve", lambda e, PL=PL, dv=dv: e.tensor_scalar(rl[0:dv, :], PL[0:dv, 0:TW], 1e-30, None, ALU.add), reads=[BPL], writes=[Brl])
                    S.op("dve", lambda e, dv=dv: e.reciprocal(rl[0:dv, :], rl[0:dv, :]), reads=[Brl], writes=[Brl])
                    if not isdiff:
                        OF = ofin[(2 * t + ui) % 2]; BOF = Bofin[(2 * t + ui) % 2]
                        S.op("dve", lambda e, OF=OF, PO=PO: e.tensor_tensor(OF[0:64, :], PO[0:64, 0:TW], rl[0:64, :], ALU.mult),
                             reads=[BPO, Brl], writes=[BOF])
                        r0 = 256 + hf * 64
                        out_toks.append(S.dma("sp", T["oT_tile"](r0, r0 + 64, t), OF[0:64, :], reads=[BOF], writes=[Bo]))
                        return
                    AA = a0 if ui == 0 else a1
                    BA = Ba0 if ui == 0 else Ba1
                    S.op("dve", lambda e, AA=AA, PO=PO: e.tensor_tensor(AA[:], PO[:, 0:TW], rl[:], ALU.mult),
                         reads=[BPO, Brl], writes=[BA])
                    if ui == 0:
                        return
                    OF = ofin[t % 2]; BOF = Bofin[t % 2]
                    S.op("dve", lambda e: e.scalar_tensor_tensor(od[:], a1[:], neglam[:, 0:1], a0[:], ALU.mult, ALU.add),
                         reads=[Ba0, Ba1, Blam], writes=[Bod])
                    S.op("pool", lambda e: e.tensor_tensor(sq[:], od[:], od[:], ALU.mult), reads=[Bod], writes=[Bsq])

                    def tail(OF=OF, BOF=BOF, t=t):
                        S.op("pe", lambda e: e.matmul(ps_x[:, 0:TW], ones_f[:], sq[:], start=True, stop=True),
                             reads=[Bc, Bsq], writes=[Bps_x])
                        S.op("act", lambda e: e.activation(rstd[:], ps_x[:, 0:TW], AF.Ln, bias=RMS_EPS, scale=1.0 / 128.0),
                             reads=[Bps_x], writes=[Brstd])
                        S.op("act", lambda e: e.activation(rstd[:], rstd[:], AF.Exp, scale=-0.5), reads=[Brstd], writes=[Brstd])
                        S.op("dve", lambda e, OF=OF: e.scalar_tensor_tensor(OF[:], od[:], gcol[:, 0:1], rstd[:], ALU.mult, ALU.mult),
                             reads=[Bod, Brstd, Blam], writes=[BOF])
                        out_toks.append(S.dma("sp", T["oT_tile"](g * 128, (g + 1) * 128, t), OF[:], reads=[BOF], writes=[Bo]))
                    deferred.append((idx + 3, tail))

                LA = 2
                pend = {}
                n_it = len(items)
                for i in range(n_it + LA):
                    if conv_itB is not None and i % 8 == 7:
                        next(conv_itB, None)
                    if i < n_it:
                        pend[i] = stage_a(*items[i])
                    k = i - LA
                    if k >= 0:
                        PT, BPT = pend.pop(k)
                        stage_b(k, *items[k], PT, BPT)
                    while deferred and deferred[0][0] <= k:
                        deferred.pop(0)[1]()
                while deferred:
                    deferred.pop(0)[1]()
            if conv_itB is not None:
                for _ in conv_itB:
                    pass
            S.barrier()
            S.flush()
    return out_toks


def l1_consts(s):
    p = np.arange(128, dtype=np.float64)[:, None]
    slopes = [2.0 ** (-8.0 * (h + 1) / 4) for h in (2 * s, 2 * s + 1)]
    dbias = np.zeros((128, 2, 66), np.float32)
    Fd = np.zeros((128, 2, 3, TW), np.float32)
    qaug = np.zeros((2, 3, LR), np.float32)
    i = np.arange(TW, dtype=np.float64)[None, :]
    r = np.arange(LR)
    w = r % TW
    for hl, sl in enumerate(slopes):
        for idx in range(66):
            dbias[:, hl, idx] = (sl * (128 * (idx - 63) + p))[:, 0]
        for jj in range(3):
            rk = 128 * jj + p
            vis = (rk // 64) <= (i // 64)
            f = np.where(i >= rk, 0.0, 2 * sl * (i - rk))
            Fd[:, hl, jj, :] = np.where(vis, f, NEG)
        qaug[hl, 0] = -(8 * sl) * 256 * (w // 256)
        qaug[hl, 1] = -(8 * sl) * (w % 256)
    Ff = np.zeros((128, 3, TW), np.float32)
    for jj in range(3):
        rk = 128 * jj + p
        Ff[:, jj, :] = np.where(rk <= i, 0.0, NEG)
    return (dbias.reshape(128, 132), Fd.reshape(128, 6 * TW), Ff.reshape(128, 3 * TW), qaug)


def l1_inputs(xT_b, even_w_in, even_f_bias, diff_lambda, diff_subln_g, s):
    w = even_w_in
    dq = [w[:, h * 128:(h + 1) * 128] for h in (2 * s, 2 * s + 1)]
    dk = [w[:, 512 + h * 128:512 + (h + 1) * 128] for h in (2 * s, 2 * s + 1)]
    dvv = [w[:, 1024 + h * 128:1024 + (h + 1) * 128] for h in (2 * s, 2 * s + 1)]
    fq = w[:, 1536 + 256 * s:1536 + 256 * (s + 1)]
    fk = w[:, 2048 + 256 * s:2048 + 256 * (s + 1)]
    fv = w[:, 2560 + 256 * s:2560 + 256 * (s + 1)]
    wf = w[:, 3072 + 4 * s:3072 + 4 * (s + 1)]
    dbias, Fd, Ff, qaug = l1_consts(s)
    return {
        "xT": xT_b,
        "wq": np.ascontiguousarray(np.concatenate(dq + [fq], axis=1)),
        "wk": np.ascontiguousarray(np.concatenate(dk + [fk], axis=1)),
        "wv": np.ascontiguousarray(np.concatenate(dvv + [fv], axis=1)),
        "wf": np.ascontiguousarray(wf),
        "fbias": np.ascontiguousarray(even_f_bias[4 * s:4 * (s + 1)].reshape(4, 1)),
        "lamv": np.ascontiguousarray(np.broadcast_to(diff_lambda.reshape(1, 256), (128, 256))),
        "gsub": np.ascontiguousarray(diff_subln_g.reshape(128, 1)),
        "dbias": dbias, "Fd": Fd, "Ff": Ff, "qaug": qaug,
    }


def emit_convert(S, C, src, dst, BD, rows, cols, tag, shared=None):
    if shared is None:
        stg = [C.sb([128, 2048], BF16) for _ in range(2)]
        Bs = [Buf(), Buf()]
    else:
        stg, Bs = shared
    for _ in iter_convert(S, stg, Bs, src, dst, BD, rows, cols):
        pass


def iter_convert(S, stg, Bs, src, dst, BD, rows, cols, ctr=[0]):
    for r0 in range(0, rows, 128):
        for c0 in range(0, cols, 2048):
            w = min(2048, cols - c0)
            T = stg[ctr[0] % 2]; B = Bs[ctr[0] % 2]
            ctr[0] += 1
            S.dma("pool", T[:, 0:w], src[r0:r0 + 128, c0:c0 + w], writes=[B])
            S.dma("sp", dst[r0:r0 + 128, c0:c0 + w], T[:, 0:w], reads=[B], writes=[BD])
            yield


def emit_ln(S, y, By, yb, Byb, sqb, Bsqb, tmpn, Btmpn, out_f, Bof, out_b, Bob, gcols, bcols, Bp, ones_b, Bc, ps1, Bps1, ps2, Bps2,
            mean, msq, rstd, Bst, nfeat_chunks=8):
    n = nfeat_chunks
    for c in range(n):
        S.op("act", lambda e, c=c: e.copy(yb[:, c, :], y[:, c, :]), reads=[By[c]], writes=[Byb[c]])
        S.op("act", lambda e, c=c: e.activation(sqb[:, c, :], y[:, c, :], AF.Square), reads=[By[c]], writes=[Bsqb[c]])
    for c in range(n):
        S.op("pe", lambda e, c=c: e.matmul(ps1[:, 0:TW], ones_b[:], yb[:, c, :], start=(c == 0), stop=(c == n - 1)),
             reads=[Bc, Byb[c]], writes=[Bps1], inc=(c == n - 1))
    for c in range(n):
        S.op("pe", lambda e, c=c: e.matmul(ps2[:, 0:TW], ones_b[:], sqb[:, c, :], start=(c == 0), stop=(c == n - 1)),
             reads=[Bc, Bsqb[c]], writes=[Bps2], inc=(c == n - 1))
    inv = 1.0 / (128.0 * n)
    S.op("dve", lambda e: e.tensor_scalar(mean[:], ps1[:, 0:TW], inv, None, ALU.mult), reads=[Bps1], writes=[Bst])
    S.op("dve", lambda e: e.tensor_tensor(msq[:], mean[:], mean[:], ALU.mult), reads=[Bst], writes=[Bst])
    S.op("dve", lambda e: e.scalar_tensor_tensor(msq[:], ps2[:, 0:TW], inv, msq[:], ALU.mult, ALU.subtract),
         reads=[Bps2, Bst], writes=[Bst])
    S.op("act", lambda e: e.activation(rstd[:], msq[:], AF.Ln, bias=LN_EPS, scale=1.0), reads=[Bst], writes=[Bst])
    S.op("act", lambda e: e.activation(rstd[:], rstd[:], AF.Exp, scale=-0.5), reads=[Bst], writes=[Bst])
    for c in range(n):
        eng = "dve" if c % 2 == 0 else "pool"
        k = (c % 2) * 2 + (c // 2) % 2
        TN = tmpn[k]; BTN = Btmpn[k]
        S.op(eng, lambda e, c=c, TN=TN: e.tensor_tensor(TN[:], y[:, c, :], mean[:], ALU.subtract), reads=[By[c], Bst], writes=[BTN])
        S.op(eng, lambda e, c=c, TN=TN: e.tensor_tensor(TN[:], TN[:], rstd[:], ALU.mult), reads=[Bst, BTN], writes=[BTN])
        S.op("dve", lambda e, c=c, TN=TN: e.tensor_scalar(out_f[:, c, :], TN[:], gcols[:, c:c + 1], bcols[:, c:c + 1], ALU.mult, ALU.add),
             reads=[BTN, Bp], writes=[Bof[c]])
        if out_b is not None:
            S.op("act", lambda e, c=c: e.copy(out_b[:, c, :], out_f[:, c, :]), reads=[Bof[c]], writes=[Bob[c]])


def emit_post(nc, S, st, ntiles, KC, src_o, src_mode, Bsrc, hres, wout_b, w1_b, w2_b, Bwb, lnp, hout, Bhout, flags=None):
    C = Ctx(nc, st)
    ot = [C.sb([128, KC, TW], BF16) for _ in range(2)]
    hr = [C.sb([128, 8, TW], F32)] * 2
    y = C.sb([128, 8, TW], F32)
    ysq = [C.sb([128, TW], F32) for _ in range(4)]
    h1 = [C.sb([128, 8, TW], F32) for _ in range(2)]
    h1b = C.sb([128, 8, TW], BF16)
    h2 = C.sb([128, 8, TW], F32)
    yb = C.sb([128, 8, TW], BF16)
    sqb = C.sb([128, 8, TW], BF16)
    hid = C.sb([128, 32, TW], BF16)
    rl = [C.sb([128, TW], F32) for _ in range(2)]
    WCOL = 4096 // KC
    NW = 8
    wring = [C.sb([128, 4096], BF16) for _ in range(NW)]
    Bring = [Buf() for _ in range(NW)]
    wk = [0]

    def wload(src_ap, c):
        i = wk[0] % NW
        wk[0] += 1
        W = wring[i][:].rearrange("p (c o) -> p c o", c=c)
        S.dma("sp", W, src_ap, reads=[Bwb], writes=[Bring[i]])
        return W, Bring[i]
    lnp_sb = C.sb([128, 32], F32)
    ones_f = C.sb([128, 128], BF16)
    if src_mode == "blend":
        cand = [C.sb([128, KC, TW], BF16) for _ in range(2)]
        fl_sb = C.sb([128, 2], F32)
        Bcand = [Buf(), Buf()]
        Bfl = Buf()
        S.dma("sp", fl_sb[:], flags, writes=[Bfl])
    mean = C.sb([128, TW], F32)
    msq = C.sb([128, TW], F32)
    rstd = C.sb([128, TW], F32)
    pa = [C.ps([128, 512]) for _ in range(2)]
    pb = [C.ps([128, 512]) for _ in range(4)]
    ps1 = C.ps([128, 512])
    ps2 = C.ps([128, 512])
    Bot = [Buf(), Buf()]; Bhr = [Buf()] * 2
    L8 = lambda: [Buf() for _ in range(8)]
    By = L8(); Bysq = [Buf() for _ in range(4)]; Bh1 = [L8(), L8()]; Bh1b = L8(); Bh2 = L8(); Byb = L8(); Bsqb = L8()
    Bhid = [Buf() for _ in range(32)]; Brl = [Buf(), Buf()]
    Bp = Buf(); Bc = Buf(); Bst = Buf(); Bpa = [PB(), PB()]; Bpb = [PB() for _ in range(4)]; Bps1 = PB(); Bps2 = PB()
    S.dma("sp", lnp_sb[:], lnp, writes=[Bp])
    S.op("pool", lambda e: e.memset(ones_f[:], 1.0), writes=[Bc])
    wr = "(c p) o -> p c o"
    wo_v = wout_b.rearrange(wr, p=128)
    w1_v = w1_b.rearrange(wr, p=128)
    w2_v = w2_b.rearrange(wr, p=128)
    so_v = None if src_mode == "blend" else src_o.rearrange("(c p) t -> p c t", p=128)
    hr_v = hres.rearrange("(c p) t -> p c t", p=128)
    ho_v = hout.rearrange("(c p) t -> p c t", p=128)
    ai = 0
    wi = 0
    w1i = 0
    w2i = 0
    ri = 0
    toks = []
    def stA(t):
        nonlocal ai, wi
        c0 = t * TW
        OT = ot[t % 2]; BOT = Bot[t % 2]; HR = hr[t % 2]; BHR = Bhr[t % 2]
        if src_mode == "blend":
            for k in range(2):
                S.dma("sp", cand[k][:], src_o(k, t), reads=[Bsrc], writes=[Bcand[k]])
            for kc in range(KC):
                S.op("dve", lambda e, kc=kc, OT=OT: e.tensor_scalar(OT[:, kc, :], cand[0][:, kc, :], fl_sb[:, 0:1], None, ALU.mult),
                     reads=[Bcand[0], Bfl], writes=[BOT])
                S.op("dve", lambda e, kc=kc, OT=OT: e.scalar_tensor_tensor(OT[:, kc, :], cand[1][:, kc, :], fl_sb[:, 1:2], OT[:, kc, :], ALU.mult, ALU.add),
                     reads=[Bcand[1], Bfl, BOT], writes=[BOT])
        else:
            S.dma("pool" if src_mode == "f32" else "sp", OT[:], so_v[:, :, c0:c0 + TW], reads=[Bsrc], writes=[BOT])
        for n2 in range(1024 // WCOL):
            WO, BWO = wload(wo_v[:, :, n2 * WCOL:(n2 + 1) * WCOL], KC)
            for c4 in range(WCOL // 128):
                cc = n2 * (WCOL // 128) + c4
                P = pa[ai % 2]; BP = Bpa[ai % 2]
                ai += 1
                for kc in range(KC):
                    S.op("pe", lambda e, P=P, WO=WO, kc=kc, c4=c4, OT=OT: e.matmul(
                        P[:, 0:TW], WO[:, kc, c4 * 128:(c4 + 1) * 128], OT[:, kc, :], start=(kc == 0), stop=(kc == KC - 1)),
                        reads=[BWO, BOT], writes=[BP], inc=(kc == KC - 1))
                S.op("dve", lambda e, P=P, cc=cc, HR=HR: e.scalar_tensor_tensor(
                    y[:, cc, :], HR[:, cc, :], ALPHA, P[:, 0:TW], ALU.mult, ALU.add), reads=[BP, BHR], writes=[By[cc]])

    def stB(t):
        emit_ln(S, y, By, yb, Byb, sqb, Bsqb, ysq, Bysq, h1[t % 2], Bh1[t % 2], h1b, Bh1b, lnp_sb[:, 0:8], lnp_sb[:, 8:16], Bp, ones_f, Bc,
                ps1, Bps1, ps2, Bps2, mean, msq, rstd, Bst)

    def stC(t):
        nonlocal ai, w1i, ri
        for hg in range(8):
            W1, BW1 = wload(w1_v[:, :, hg * 512:(hg + 1) * 512], 8)
            for h4 in range(4):
                hc = hg * 4 + h4
                P = pa[ai % 2]; BP = Bpa[ai % 2]
                ai += 1
                for fc in range(8):
                    S.op("pe", lambda e, P=P, W1=W1, fc=fc, h4=h4: e.matmul(
                        P[:, 0:TW], W1[:, fc, h4 * 128:(h4 + 1) * 128], h1b[:, fc, :], start=(fc == 0), stop=(fc == 7)),
                        reads=[BW1, Bh1b[fc]], writes=[BP], inc=(fc == 7))
                R = rl[ri % 2]; BR = Brl[ri % 2]
                ri += 1
                S.op("act", lambda e, R=R, P=P: e.activation(R[:], P[:, 0:TW], AF.Relu), reads=[BP], writes=[BR])
                S.op("pool", lambda e, R=R, hc=hc: e.tensor_tensor(hid[:, hc, :], R[:], R[:], ALU.mult), reads=[BR], writes=[Bhid[hc]])

    def stD(t, n2s):
        nonlocal w2i
        for n2 in n2s:
            for kg in range(4):
                W2, BW2 = wload(w2_v[:, kg * 8:(kg + 1) * 8, n2 * 512:(n2 + 1) * 512], 8)
                for c4 in range(4):
                    for k8 in range(8):
                        hc = kg * 8 + k8
                        S.op("pe", lambda e, c4=c4, W2=W2, k8=k8, hc=hc: e.matmul(
                            pb[c4][:, 0:TW], W2[:, k8, c4 * 128:(c4 + 1) * 128], hid[:, hc, :], start=(hc == 0), stop=(hc == 31)),
                            reads=[BW2, Bhid[hc]], writes=[Bpb[c4]], inc=(k8 == 7))
            for c4 in range(4):
                cc = n2 * 4 + c4
                S.op("dve", lambda e, c4=c4, cc=cc, t=t: e.scalar_tensor_tensor(
                    h2[:, cc, :], h1[t % 2][:, cc, :], ALPHA, pb[c4][:, 0:TW], ALU.mult, ALU.add), reads=[Bpb[c4], Bh1[t % 2][cc]], writes=[Bh2[cc]])

    def stE(t):
        c0 = t * TW
        emit_ln(S, h2, Bh2, yb, Byb, sqb, Bsqb, ysq, Bysq, h2, Bh2, None, None, lnp_sb[:, 16:24], lnp_sb[:, 24:32], Bp, ones_f, Bc,
                ps1, Bps1, ps2, Bps2, mean, msq, rstd, Bst)
        toks.append(S.dma("sp", ho_v[:, :, c0:c0 + TW], h2[:], reads=Bh2, writes=[Bhout]))

    def stH(t):
        S.dma("sp", hr[0][:], hr_v[:, :, t * TW:(t + 1) * TW], writes=[Bhr[t % 2]])

    stH(0)
    stA(0)
    stB(0)
    for t in range(ntiles):
        if t + 1 < ntiles:
            stH(t + 1)
        stC(t)
        if t + 1 < ntiles:
            stA(t + 1)
        stD(t, (0,))
        if t + 1 < ntiles:
            stB(t + 1)
        stD(t, (1,))
        stE(t)
    return toks


L2_STAGE = 4
L3_STAGE = 3
L3_SUB = 3
L3_NCH = 12
RET_GAMMA = [1.0 - 2.0 ** (-5.0 - h) for h in range(4)]


def emit_state_tile(S, h2b, Bh2b, wk_sb, wv_sb, Bw, kts, Bkts, vts, Bvts, kdec_sb, Bc, Sst, BS, pk, Bpk, ctr):
    for bl in range(3):
        for half in range(2):
            P = pk[ctr[0] % 2]; BP = Bpk[ctr[0] % 2]
            ctr[0] += 1
            for fc in range(8):
                S.op("pe", lambda e, P=P, fc=fc, bl=bl, half=half: e.matmul(
                    P[:], h2b[:, fc, bl * 128:(bl + 1) * 128], wk_sb[:, fc, half * 512:(half + 1) * 512], start=(fc == 0), stop=(fc == 7)),
                    reads=[Bh2b, Bw], writes=[BP], inc=(fc == 7))
            for hh in range(2):
                h = half * 2 + hh
                S.op("dve", lambda e, P=P, bl=bl, h=h, hh=hh: e.tensor_scalar(
                    kts[:, bl, h * 256:(h + 1) * 256], P[:, hh * 256:(hh + 1) * 256], kdec_sb[:, bl * 4 + h:bl * 4 + h + 1], None, ALU.mult),
                    reads=[BP, Bc], writes=[Bkts])
        for h in range(4):
            P = pk[ctr[0] % 2]; BP = Bpk[ctr[0] % 2]
            ctr[0] += 1
            for fc in range(8):
                S.op("pe", lambda e, P=P, fc=fc, bl=bl, h=h: e.matmul(
                    P[:], h2b[:, fc, bl * 128:(bl + 1) * 128], wv_sb[:, fc, h * 512:(h + 1) * 512], start=(fc == 0), stop=(fc == 7)),
                    reads=[Bh2b, Bw], writes=[BP], inc=(fc == 7))
            S.op("act", lambda e, P=P, bl=bl, h=h: e.copy(vts[:, bl, h * 512:(h + 1) * 512], P[:]), reads=[BP], writes=[Bvts])


def emit_state_update(S, kts, Bkts, vts, Bvts, Sst, BS, Sb, BSb, pk, Bpk, ctr):
    for h in range(4):
        c384 = RET_GAMMA[h] ** TW
        for dkc in range(2):
            P = pk[ctr[0] % 2]; BP = Bpk[ctr[0] % 2]
            ctr[0] += 1
            for bl in range(3):
                S.op("pe", lambda e, P=P, bl=bl, h=h, dkc=dkc: e.matmul(
                    P[:], kts[:, bl, h * 256 + dkc * 128:h * 256 + (dkc + 1) * 128], vts[:, bl, h * 512:(h + 1) * 512],
                    start=(bl == 0), stop=(bl == 2)), reads=[Bkts] + (Bvts if isinstance(Bvts, list) else [Bvts]), writes=[BP], inc=(bl == 2))
            idx = h * 2 + dkc
            S.op("dve", lambda e, P=P, idx=idx, c384=c384: e.scalar_tensor_tensor(
                Sst[:, idx, :], Sst[:, idx, :], c384, P[:], ALU.mult, ALU.add), reads=[BP, BS], writes=[BS])
            if Sb is not None:
                S.op("pool", lambda e, idx=idx: e.tensor_copy(Sb[:, idx, :], Sst[:, idx, :]), reads=[BS], writes=[BSb])


def emit_prepass(nc, S, T, Bhout, BSe):
    h2T = T["h2T"]; wk = T["wk"]; wv = T["wv"]; kdec = T["kdec"]; tmask = T["tmask"]; S_end = T["S_end"]
    if True:
        with contextlib.ExitStack() as st:
            C = Ctx(nc, st)
            wk_sb = C.sb([128, 8, 1024], BF16)
            wv_sb = C.sb([128, 8, 2048], BF16)
            h2b = [C.sb([128, 8, TW], BF16) for _ in range(2)]
            kts = C.sb([128, 3, 1024], BF16)
            vts = C.sb([128, 3, 2048], BF16)
            kdec_sb = C.sb([128, 12], F32)
            tm_sb = C.sb([128, TW], F32)
            Sst = C.sb([128, 8, 512], F32)
            pk = [C.ps([128, 512]) for _ in range(2)]
            Bw = Buf(); Bh2b = [Buf(), Buf()]; Bkts = Buf(); Bvts = Buf(); Bc = Buf(); BS = Buf(); Bpk = [PB(), PB()]
            wr = "(c p) o -> p c o"
            wqueue = T.get("wqueue", "pool")
            for fc in range(8):
                S.dma(wqueue, wk_sb[:, fc, :], wk[fc * 128:(fc + 1) * 128, :], reads=[T["Bwb"]], writes=[Bw])
                S.dma(wqueue, wv_sb[:, fc, :], wv[fc * 128:(fc + 1) * 128, :], reads=[T["Bwb"]], writes=[Bw])
            S.dma("sp", kdec_sb[:], kdec, writes=[Bc])
            S.dma("sp", tm_sb[:], tmask, writes=[Bc])
            for i8 in range(8):
                S.op("dve", lambda e, i8=i8: e.memset(Sst[:, i8, :], 0.0), writes=[BS])
            hv = h2T.rearrange("(c p) t -> p c t", p=128)
            ctr = [0]
            for t in range(NTH):
                c0 = t * TW
                H = h2b[t % 2]; BH = Bh2b[t % 2]
                S.dma("pool", H[:], hv[:, :, c0:c0 + TW], reads=[Bhout], writes=[BH])
                if t == 0:
                    for fc in range(8):
                        S.op("dve", lambda e, H=H, fc=fc: e.tensor_tensor(H[:, fc, :], H[:, fc, :], tm_sb[:], ALU.mult),
                             reads=[BH, Bc], writes=[BH])
                emit_state_tile(S, H, BH, wk_sb, wv_sb, Bw, kts, Bkts, vts, Bvts, kdec_sb, Bc, Sst, BS, pk, Bpk, ctr)
                emit_state_update(S, kts, Bkts, vts, Bvts, Sst, BS, None, None, pk, Bpk, ctr)
            for k in range(4):
                S.dma("sp", S_end[k], Sst[32 * k:32 * (k + 1)], reads=[BS], writes=[BSe])
            S.barrier()
            S.flush()

def ret_consts():
    p = np.arange(128, dtype=np.float64)
    kdec = np.zeros((128, 3, 4), np.float32)
    for h in range(4):
        g = RET_GAMMA[h]
        for bl in range(3):
            kdec[:, bl, h] = g ** (TW - 1 - (bl * 128 + p)) / 16.0
    return kdec.reshape(128, 12)


def lnp_table(ln_g, ln_b, layer):
    cols = []
    for k in range(2):
        cols.append(ln_g[layer, k].reshape(8, 128).T)
        cols.append(ln_b[layer, k].reshape(8, 128).T)
    return np.ascontiguousarray(np.concatenate(cols, axis=1).astype(np.float32))


def emit_retention(nc, S, st, T, Bwb, Bh2, BSg, ByT):
    h2T = T["h2T"]; wi_b = T["wi_b"]; kdec = T["kdec"]; tmask = T["tmask"]; dmask = T["dmask"]; qdec = T["qdec"]
    yT_s = T["yT_s"]; S_src = T["S_src"]; flags = T["flags"]
    C = Ctx(nc, st)
    if True:
        if True:
            h2b = [C.sb([128, 8, TW], BF16) for _ in range(2)]
            wc = [C.sb([128, 8, 512], BF16) for _ in range(2)]
            qT = C.sb([128, 8, TW], BF16)
            qdT = C.sb([128, 8, TW], BF16)
            kT = C.sb([128, 8, TW], BF16)
            kts = C.sb([128, 3, 1024], BF16)
            vts = C.sb([128, 3, 2048], BF16)
            sg = C.sb([128, 16, TW], BF16)
            sTb = [C.sb([128, 3, TW], BF16) for _ in range(4)]
            o32 = [C.sb([128, 4, TW], F32) for _ in range(2)]
            osq = [C.sb([128, 4, TW], BF16) for _ in range(2)]
            yT = [C.sb([128, 16, TW], BF16) for _ in range(2)]
            Sst = C.sb([128, 8, 512], F32)
            Sb = C.sb([128, 8, 512], BF16)
            dm_sb = C.sb([128, 12 * TW], F32)
            qd_sb = C.sb([128, 4 * TW], F32)
            kdec_sb = C.sb([128, 12], F32)
            tm_sb = C.sb([128, TW], F32)
            ones_f = C.sb([128, 128], BF16)
            rstd = [C.sb([128, TW], F32) for _ in range(2)]
            tmpo = C.sb([128, TW], F32)
            pk = [C.ps([128, 512]) for _ in range(2)]
            psc = [C.ps([128, 512]) for _ in range(2)]
            po = [C.ps([128, 512]) for _ in range(2)]
            pss = C.ps([128, 512])
            Bh2b = [Buf(), Buf()]; Bwc = [Buf(), Buf()]; BqT = Buf(); BqdT = Buf(); BkT = Buf(); Bkts = Buf(); Bvts = [Buf() for _ in range(4)]
            Bsg = [Buf() for _ in range(4)]; BsTb = [Buf() for _ in range(4)]; Bo32 = [Buf(), Buf()]; Bosq = [Buf(), Buf()]; ByTt = [Buf(), Buf()]; BS = Buf(); BSb = Buf()
            Bc = Buf(); Brstd = [Buf(), Buf()]; Btmpo = Buf(); Bpk = [PB(), PB()]; Bpsc = [PB(), PB()]; Bpo = [PB(), PB()]; Bpss = PB()
            S.dma("sp", dm_sb[:], dmask, writes=[Bc])
            S.dma("sp", qd_sb[:], qdec, writes=[Bc])
            S.dma("sp", kdec_sb[:], kdec, writes=[Bc])
            S.dma("sp", tm_sb[:], tmask, writes=[Bc])
            S.op("pool", lambda e: e.memset(ones_f[:], 1.0), writes=[Bc])
            fl_sb = C.sb([128, 2], F32)
            S.dma("sp", fl_sb[:], flags, writes=[Bc])
            for k in range(4):
                S.dma("sp", Sst[32 * k:32 * (k + 1)], S_src[k], reads=[BSg], writes=[BS])
            for i8 in range(8):
                S.op("dve", lambda e, i8=i8: e.tensor_scalar(Sst[:, i8, :], Sst[:, i8, :], fl_sb[:, 1:2], None, ALU.mult),
                     reads=[BS, Bc], writes=[BS])
            for i8 in range(8):
                S.op("pool", lambda e, i8=i8: e.tensor_copy(Sb[:, i8, :], Sst[:, i8, :]), reads=[BS], writes=[BSb])
            hv = h2T.rearrange("(c p) t -> p c t", p=128)
            wiv = wi_b.rearrange("(c p) o -> p c o", p=128)
            yv = yT_s.rearrange("(c p) t -> p c t", p=128)
            ctr = [0]
            wci = 0
            sci = 0
            oi = 0
            for t in range(NTH if L3_STAGE >= 2 else 1):
                c0 = t * TW
                H = h2b[t % 2]; BH = Bh2b[t % 2]
                Y = yT[t % 2]; BY = ByTt[t % 2]
                S.dma("pool", H[:], hv[:, :, c0:c0 + TW], reads=[Bh2], writes=[BH])
                if t == 0:
                    for fc in range(8):
                        S.op("dve", lambda e, H=H, fc=fc: e.tensor_tensor(H[:, fc, :], H[:, fc, :], tm_sb[:], ALU.mult),
                             reads=[BH, Bc], writes=[BH])
                def proj_chunk(ch, H=H, BH=BH):
                    nonlocal wci
                    W = wc[wci % 2]; BW = Bwc[wci % 2]
                    wci += 1
                    S.dma("sp", W[:], wiv[:, :, ch * 512:(ch + 1) * 512], reads=[Bwb], writes=[BW])
                    if ch < 4 or ch >= 8:
                        for c4 in range(4):
                            P = pk[ctr[0] % 2]; BP = Bpk[ctr[0] % 2]
                            ctr[0] += 1
                            for fc in range(8):
                                S.op("pe", lambda e, P=P, W=W, fc=fc, c4=c4, H=H: e.matmul(
                                    P[:, 0:TW], W[:, fc, c4 * 128:(c4 + 1) * 128], H[:, fc, :], start=(fc == 0), stop=(fc == 7)),
                                    reads=[BW, BH], writes=[BP], inc=(fc == 7))
                            if ch < 2:
                                ci = ch * 4 + c4
                                h = ci // 2
                                S.op("act", lambda e, P=P, ci=ci: e.copy(qT[:, ci, :], P[:, 0:TW]), reads=[BP], writes=[BqT])
                                S.op("dve", lambda e, P=P, ci=ci, h=h: e.tensor_tensor(qdT[:, ci, :], P[:, 0:TW], qd_sb[:, h * TW:(h + 1) * TW], ALU.mult),
                                     reads=[BP, Bc], writes=[BqdT])
                            elif ch < 4:
                                ci = (ch - 2) * 4 + c4
                                S.op("act", lambda e, P=P, ci=ci: e.copy(kT[:, ci, :], P[:, 0:TW]), reads=[BP], writes=[BkT])
                            else:
                                gi = (ch - 8) * 4 + c4
                                S.op("act", lambda e, P=P, gi=gi: e.activation(sg[:, gi, :], P[:, 0:TW], AF.Silu), reads=[BP], writes=[Bsg[gi // 4]])
                    if 2 <= ch < 4:
                        half = ch - 2
                        for bl in range(3):
                            P = pk[ctr[0] % 2]; BP = Bpk[ctr[0] % 2]
                            ctr[0] += 1
                            for fc in range(8):
                                S.op("pe", lambda e, P=P, W=W, fc=fc, bl=bl, H=H: e.matmul(
                                    P[:], H[:, fc, bl * 128:(bl + 1) * 128], W[:, fc, :], start=(fc == 0), stop=(fc == 7)),
                                    reads=[BH, BW], writes=[BP], inc=(fc == 7))
                            for hh in range(2):
                                h = half * 2 + hh
                                S.op("dve", lambda e, P=P, bl=bl, h=h, hh=hh: e.tensor_scalar(
                                    kts[:, bl, h * 256:(h + 1) * 256], P[:, hh * 256:(hh + 1) * 256], kdec_sb[:, bl * 4 + h:bl * 4 + h + 1], None, ALU.mult),
                                    reads=[BP, Bc], writes=[Bkts])
                    if 4 <= ch < 8:
                        h = ch - 4
                        for bl in range(3):
                            P = pk[ctr[0] % 2]; BP = Bpk[ctr[0] % 2]
                            ctr[0] += 1
                            for fc in range(8):
                                S.op("pe", lambda e, P=P, W=W, fc=fc, bl=bl, H=H: e.matmul(
                                    P[:], H[:, fc, bl * 128:(bl + 1) * 128], W[:, fc, :], start=(fc == 0), stop=(fc == 7)),
                                    reads=[BH, BW], writes=[BP], inc=(fc == 7))
                            S.op("act", lambda e, P=P, bl=bl, h=h: e.copy(vts[:, bl, h * 512:(h + 1) * 512], P[:]), reads=[BP], writes=[Bvts[h]])
                def st1(h):
                    nonlocal sci
                    ST = sTb[h]; BST = BsTb[h]
                    for jb in range(3):
                        P = psc[sci % 2]; BP = Bpsc[sci % 2]
                        sci += 1
                        for dc in range(2):
                            S.op("pe", lambda e, P=P, h=h, dc=dc, jb=jb: e.matmul(
                                P[:, 0:TW], kT[:, h * 2 + dc, jb * 128:(jb + 1) * 128], qT[:, h * 2 + dc, :], start=(dc == 0), stop=(dc == 1)),
                                reads=[BkT, BqT], writes=[BP], inc=(dc == 1))
                        S.op("dve", lambda e, P=P, ST=ST, jb=jb, h=h: e.tensor_tensor(
                            ST[:, jb, :], P[:, 0:TW], dm_sb[:, (h * 3 + jb) * TW:(h * 3 + jb + 1) * TW], ALU.mult),
                            reads=[BP, Bc], writes=[BST])

                def st2(h):
                    nonlocal oi
                    ST = sTb[h]; BST = BsTb[h]
                    O32 = o32[h % 2]; OSQ = osq[h % 2]
                    for ec in range(4):
                        P = po[oi % 2]; BP = Bpo[oi % 2]
                        oi += 1
                        for jb in range(3):
                            S.op("pe", lambda e, P=P, h=h, ec=ec, jb=jb, ST=ST: e.matmul(
                                P[:, 0:TW], vts[:, jb, h * 512 + ec * 128:h * 512 + (ec + 1) * 128], ST[:, jb, :], start=(jb == 0), stop=False),
                                reads=[Bvts[h], BST], writes=[BP], inc=False)
                        for dc in range(2):
                            S.op("pe", lambda e, P=P, h=h, ec=ec, dc=dc: e.matmul(
                                P[:, 0:TW], Sb[:, h * 2 + dc, ec * 128:(ec + 1) * 128], qdT[:, h * 2 + dc, :], start=False, stop=(dc == 1)),
                                reads=[BSb, BqdT], writes=[BP], inc=(dc == 1))
                        S.op("act", lambda e, P=P, ec=ec, O32=O32: e.copy(O32[:, ec, :], P[:, 0:TW]), reads=[BP], writes=[Bo32[h % 2]])
                        S.op("act", lambda e, P=P, ec=ec, OSQ=OSQ: e.activation(OSQ[:, ec, :], P[:, 0:TW], AF.Square), reads=[BP], writes=[Bosq[h % 2]])

                def st3(h, Y=Y, BY=BY):
                    O32 = o32[h % 2]; OSQ = osq[h % 2]; R = rstd[h % 2]; BR = Brstd[h % 2]
                    for ec in range(4):
                        S.op("pe", lambda e, ec=ec, OSQ=OSQ: e.matmul(pss[:, 0:TW], ones_f[:], OSQ[:, ec, :], start=(ec == 0), stop=(ec == 3)),
                             reads=[Bc, Bosq[h % 2]], writes=[Bpss], inc=(ec == 3))
                    S.op("act", lambda e, R=R: e.activation(R[:], pss[:, 0:TW], AF.Ln, bias=RMS_EPS, scale=1.0 / 512.0), reads=[Bpss], writes=[BR])
                    S.op("act", lambda e, R=R: e.activation(R[:], R[:], AF.Exp, scale=-0.5), reads=[BR], writes=[BR])
                    for ec in range(4):
                        S.op("dve", lambda e, ec=ec, O32=O32, R=R: e.tensor_tensor(tmpo[:], O32[:, ec, :], R[:], ALU.mult), reads=[Bo32[h % 2], BR], writes=[Btmpo])
                        S.op("dve", lambda e, ec=ec, h=h, Y=Y: e.tensor_tensor(Y[:, h * 4 + ec, :], tmpo[:], sg[:, h * 4 + ec, :], ALU.mult),
                             reads=[Btmpo, Bsg[h]], writes=[BY])

                for stg_fn, hh in ((proj_chunk, 0), (proj_chunk, 1), (proj_chunk, 2), (proj_chunk, 3),
                                   (st1, 0), (proj_chunk, 4), (st1, 1), (proj_chunk, 5), (st2, 0), (proj_chunk, 6), (st1, 2), (st2, 1),
                                   (proj_chunk, 8), (st3, 0), (proj_chunk, 7), (st1, 3), (st2, 2), (proj_chunk, 9), (st3, 1), (st2, 3),
                                   (proj_chunk, 10), (st3, 2), (proj_chunk, 11), (st3, 3)):
                    stg_fn(hh)
                if L3_SUB >= 3:
                    emit_state_update(S, kts, Bkts, vts, Bvts, Sst, BS, Sb, BSb, pk, Bpk, ctr)
                if L3_SUB >= 2:
                    S.dma("sp", yv[:, :, c0:c0 + TW], Y[:], reads=[BY], writes=[ByT])
            S.barrier()
            S.flush()

def l3_consts():
    p = np.arange(128, dtype=np.float64)[:, None]
    i = np.arange(TW, dtype=np.float64)[None, :]
    dm = np.zeros((128, 4, 3, TW), np.float32)
    qd = np.zeros((128, 4, TW), np.float32)
    for h in range(4):
        g = RET_GAMMA[h]
        for jb in range(3):
            j = jb * 128 + p
            same = (j // 64) == (i // 64)
            before = (j // 64) < (i // 64)
            val = np.where(same, g ** np.abs(i - j), np.where(before, g ** np.maximum(i - j, 0), 0.0)) / 16.0
            dm[:, h, jb, :] = val
        qd[:, h, :] = g ** (i + 1.0)
    return dm.reshape(128, 12 * TW), qd.reshape(128, 4 * TW)


PAIRS = [[0, 1], [2, 3], [4, 5], [6, 7]]


def build_fused():
    nc = bass.Bass("TRN2", target_bir_lowering=False)

    def din(name, shape):
        return nc.dram_tensor(name, list(shape), F32, kind="ExternalInput").ap()

    T1 = {"xT": din("xT", [D, LR]), "wq": din("wq", [D, 512]), "wk": din("wk", [D, 512]), "wv": din("wv", [D, 512]),
          "wf": din("wf", [D, 4]), "fbias": din("fbias", [4, 1]), "lamv": din("lamv", [128, 256]), "gsub": din("gsub", [128, 1]),
          "dbias": din("dbias", [128, 2 * 66]), "Fd": din("Fd", [128, 2 * 3 * TW]), "Ff": din("Ff", [128, 3 * TW]),
          "qaug": din("qaug", [2, 3, LR]), "odt": BF16}
    hres = din("hres", [D, HALF])
    flags = din("flags", [128, 2])
    w_out0 = din("w_out0", [D, D])
    w1_0 = din("w1_0", [D, 4 * D])
    w2_0 = din("w2_0", [4 * D, D])
    lnp0 = din("lnp0", [128, 32])
    w_in1 = din("w_in1", [D, 6144])
    w_out1 = din("w_out1", [2048, D])
    w1_1 = din("w1_1", [D, 4 * D])
    w2_1 = din("w2_1", [4 * D, D])
    lnp1 = din("lnp1", [128, 32])
    kdec = din("kdec", [128, 12])
    tmask = din("tmask", [128, TW])
    dmask = din("dmask", [128, 12 * TW])
    qdec = din("qdec", [128, 4 * TW])
    outT = nc.dram_tensor("outT", [D, HALF], F32, kind="ExternalOutput").ap()
    wo0_b = nc.dram_tensor("wo0_b", [D, D], BF16).ap()
    w10_b = nc.dram_tensor("w10_b", [D, 4 * D], BF16).ap()
    w20_b = nc.dram_tensor("w20_b", [4 * D, D], BF16).ap()
    wi_b = nc.dram_tensor("wi_b", [D, 6144], BF16).ap()
    wo1_b = nc.dram_tensor("wo1_b", [2048, D], BF16).ap()
    w11_b = nc.dram_tensor("w11_b", [D, 4 * D], BF16).ap()
    w21_b = nc.dram_tensor("w21_b", [4 * D, D], BF16).ap()
    NCH = NT // 2
    oT_c = [nc.dram_tensor("oT_c%d" % k, [512, 2 * TW], BF16) for k in range(NCH)]
    G_c = [nc.dram_tensor("G_c%d" % k, [1024, 2 * TW], BF16) for k in range(NCH)]
    h2T_i = nc.dram_tensor("h2T_i", [D, HALF], F32).ap()
    Se_c = [nc.dram_tensor("Se_c%d" % k, [256, 512], F32) for k in range(4)]
    Sg_c = [nc.dram_tensor("Sg_c%d" % k, [512, 512], F32) for k in range(4)]
    yT_s = nc.dram_tensor("yT_s", [2048, HALF], BF16).ap()
    T1["oT_tile"] = lambda r0, r1, t: oT_c[t // 2].ap()[r0:r1, (t % 2) * TW:(t % 2 + 1) * TW]

    def g_tile(k, t):
        gt = k * NTH + t
        return G_c[gt // 2].ap().rearrange("(c p) t -> p c t", p=128)[:, :, (gt % 2) * TW:(gt % 2 + 1) * TW]
    with contextlib.ExitStack() as outer:
        S = Sched(nc, outer)
        Bwb = Buf(); Bo = Buf(); BG = Buf(); Bh2 = Buf(); BSe = Buf(); BSg = Buf(); ByT = Buf(); Bout = Buf()
        def hookA(C):
            stg = [C.sb([128, 2048], BF16) for _ in range(2)]
            Bs = [Buf(), Buf()]
            for (s_, d_, r_, c_) in ():
                yield from iter_convert(S, stg, Bs, s_, d_, Bwb, r_, c_)

        def hookB(C):
            stg = [C.sb([128, 2048], BF16) for _ in range(2)]
            Bs = [Buf(), Buf()]
            for (s_, d_, r_, c_) in ((w_out0, wo0_b, D, D), (w1_0, w10_b, D, 4 * D), (w2_0, w20_b, 4 * D, D),
                                     (w_in1, wi_b, D, 6144), (w_out1, wo1_b, 2048, D), (w1_1, w11_b, D, 4 * D), (w2_1, w21_b, 4 * D, D)):
                yield from iter_convert(S, stg, Bs, s_, d_, Bwb, r_, c_)
        T1["hookA"] = hookA
        T1["hookB"] = hookB
        emit_l1(nc, S, T1, Bo)
        for k in range(NCH):
            S.op("pool", lambda e, k=k: e.collective_compute("AllGather", ALU.bypass, replica_groups=PAIRS,
                                                             ins=[oT_c[k].ap().opt()], outs=[G_c[k].ap().opt()]), reads=[Bo], writes=[BG])
        with contextlib.ExitStack() as st:
            emit_post(nc, S, st, NTH, 8, g_tile, "blend", BG, hres, wo0_b, w10_b, w20_b, Bwb, lnp0, h2T_i, Bh2, flags=flags)
            S.barrier()
            S.flush()
        Tp = {"h2T": h2T_i, "wk": wi_b[:, 1024:2048], "wv": wi_b[:, 2048:4096], "kdec": kdec, "tmask": tmask,
              "S_end": [Se_c[k].ap().rearrange("(p i) f -> p i f", i=8) for k in range(4)], "wqueue": "sp", "Bwb": Bwb}
        emit_prepass(nc, S, Tp, Bh2, BSe)
        for k in range(4):
            S.op("pool", lambda e, k=k: e.collective_compute("AllGather", ALU.bypass, replica_groups=PAIRS,
                                                             ins=[Se_c[k].ap().opt()], outs=[Sg_c[k].ap().opt()]), reads=[BSe], writes=[BSg])
        Tr = {"h2T": h2T_i, "wi_b": wi_b, "kdec": kdec, "tmask": tmask, "dmask": dmask, "qdec": qdec, "yT_s": yT_s,
              "S_src": [Sg_c[k].ap()[0:256, :].rearrange("(p i) f -> p i f", i=8) for k in range(4)], "flags": flags}
        with contextlib.ExitStack() as st:
            emit_retention(nc, S, st, Tr, Bwb, Bh2, BSg, ByT)
        with contextlib.ExitStack() as st:
            emit_post(nc, S, st, NTH, 16, yT_s, "bf16", ByT, h2T_i, wo1_b, w11_b, w21_b, Bwb, lnp1, outT, Bout)
            S.barrier()
            S.flush()
    return nc


def _token_mask(u):
    tm = np.ones((128, TW), np.float32)
    if u == 0:
        tm[:, 0:FPAD] = 0.0
    return tm


def kernel(x, meta_tokens, even_w_in, even_f_bias, diff_lambda, diff_subln_g, even_w_out,
           ret_w_in, ret_w_out, ln_g, ln_b, ffn_w1, ffn_w2):
    x = np.asarray(x, np.float32)
    f32 = lambda a: np.ascontiguousarray(np.asarray(a, np.float32))
    meta = f32(meta_tokens)
    B = x.shape[0]
    cores = list(range(8))
    kdec = ret_consts()
    dm, qd = l3_consts()
    wo = f32(even_w_out[0])
    wo_perm = np.ascontiguousarray(np.concatenate([wo[0:256], wo[512:768], wo[256:512], wo[768:1024]], axis=0))
    shared = {"w_out0": wo_perm, "w1_0": f32(ffn_w1[0]), "w2_0": f32(ffn_w2[0]), "lnp0": lnp_table(f32(ln_g), f32(ln_b), 0),
              "w_in1": f32(ret_w_in[0]), "w_out1": f32(ret_w_out[0]), "w1_1": f32(ffn_w1[1]), "w2_1": f32(ffn_w2[1]),
              "lnp1": lnp_table(f32(ln_g), f32(ln_b), 1), "kdec": kdec, "dmask": dm, "qdec": qd}
    in_maps = []
    for c in cores:
        b, u = c // 2, c % 2
        hp = np.zeros((LR, D), np.float32)
        hp[FPAD:FPAD + NMETA] = meta
        hp[FPAD + NMETA:FPAD + NMETA + SEQ] = x[b]
        xT = np.ascontiguousarray(hp.T)
        m = l1_inputs(xT, f32(even_w_in[0]), f32(even_f_bias[0]), f32(diff_lambda[0]), f32(diff_subln_g[0]), u)
        m["hres"] = np.ascontiguousarray(xT[:, u * HALF:(u + 1) * HALF])
        fl = np.zeros((128, 2), np.float32)
        fl[:, u] = 1.0
        m["flags"] = fl
        m["tmask"] = _token_mask(u)
        m.update(shared)
        in_maps.append(m)
    res = run_bass_kernel_spmd(build_fused(), in_maps, core_ids=cores).results
    out = np.empty((B, SEQ, D), np.float32)
    for b in range(B):
        hT = np.concatenate([res[2 * b]["outT"], res[2 * b + 1]["outT"]], axis=1)
        out[b] = hT[:, FPAD + NMETA:FPAD + NMETA + SEQ].T
    return out
```
